# Optimizing a Trainium2 kernel written in Bass

```python
import math
import jax, jax.numpy as jnp
from jax import lax
import numpy as np

D_MODEL = 2048
BATCH = 2
SEQ = 4096
DEPTH = 1

N_META = 16
EPS = 1e-6
M_HEADS = 4
M_WIDTH = D_MODEL // 2
M_HEAD_DIM = M_WIDTH // M_HEADS
M_GATES = 4 * M_HEADS
M_CHUNK = 64
M_CONV = 3
PAD_LOG_I = -30.0
D_HEADS = 8
D_V_DIM = (D_MODEL // 2) // D_HEADS
D_QK_DIM = D_V_DIM // 2
D_WIDTH = D_HEADS * D_V_DIM
D_QK_WIDTH = D_HEADS * 2 * D_QK_DIM
Q_BLOCK = 128
ROPE_THETA = 500000.0
ROPE_DIM = D_QK_DIM // 4
N_BRANCH = 2
IN_WIDTH = 4 * M_WIDTH + M_GATES + 2 * D_QK_WIDTH + D_WIDTH + N_BRANCH * D_MODEL
FFN_DIM = 11 * D_MODEL // 4
FFN_CONV = 3

kernel_name = "hybrid_mlstm_diffattn_gated_merge"


def rmsnorm(x, g):
    xf = x.astype(jnp.float32)
    r = lax.rsqrt(jnp.mean(xf * xf, axis=-1, keepdims=True) + EPS)
    return (xf * r).astype(x.dtype) * g


def dwconv_centred(x, w):
    xp = jnp.pad(x, ((0, 0), (1, 1), (0, 0)))
    return xp[:, :-2] * w[0] + xp[:, 1:-1] * w[1] + xp[:, 2:] * w[2]


def rope_tables(L, dtype):
    inv_freq = ROPE_THETA ** (-jnp.arange(0, ROPE_DIM, 2, dtype=jnp.float32) / ROPE_DIM)
    ang = jnp.arange(L, dtype=jnp.float32)[:, None] * inv_freq[None, :]
    return jnp.cos(ang).astype(dtype), jnp.sin(ang).astype(dtype)


def rope_partial(x, cos, sin):
    half = ROPE_DIM // 2
    x1 = x[..., :half]
    x2 = x[..., half:ROPE_DIM]
    return jnp.concatenate([x1 * cos - x2 * sin, x2 * cos + x1 * sin, x[..., ROPE_DIM:]], axis=-1)


def mlstm_chunk_step(carry, inp):
    C, n, m = carry
    q, k, v, li, lf = inp
    T = q.shape[-2]
    lower = jnp.tril(jnp.ones((T, T), dtype=bool))
    b = jnp.cumsum(lf, axis=-1)
    g = b[..., -1]
    dmat = jnp.where(lower, b[..., :, None] - b[..., None, :] + li[..., None, :], -jnp.inf)
    inter = b + m[..., None]
    m_t = jnp.maximum(inter, jnp.max(dmat, axis=-1))
    s = jnp.einsum('bhtd,bhsd->bhts', q, k) * jnp.exp(dmat - m_t[..., None])
    a_t = jnp.exp(inter - m_t)
    num = jnp.einsum('bhts,bhsv->bhtv', s, v) + a_t[..., None] * jnp.einsum('bhvd,bhtd->bhtv', C, q)
    den = jnp.sum(s, axis=-1) + a_t * jnp.einsum('bhd,bhtd->bht', n, q)
    h = num / jnp.maximum(jnp.abs(den), jnp.exp(-m_t))[..., None]
    wlog = g[..., None] - b + li
    m_new = jnp.maximum(g + m, jnp.max(wlog, axis=-1))
    decay = jnp.exp(g + m - m_new)
    w = jnp.exp(wlog - m_new[..., None])
    C = decay[..., None, None] * C + jnp.einsum('bhs,bhsv,bhsd->bhvd', w, v, k)
    n = decay[..., None] * n + jnp.einsum('bhs,bhsd->bhd', w, k)
    return (C, n, m_new), h


def mlstm_scan(q, k, v, li, lf):
    B, H, Lp, dk = q.shape
    dv = v.shape[-1]
    nc = Lp // M_CHUNK

    def chunks(t):
        return jnp.moveaxis(t.reshape(t.shape[:2] + (nc, M_CHUNK) + t.shape[3:]), 2, 0)

    init = (jnp.zeros((B, H, dv, dk), jnp.float32), jnp.zeros((B, H, dk), jnp.float32),
            jnp.zeros((B, H), jnp.float32))
    _, h = lax.scan(mlstm_chunk_step, init, (chunks(q), chunks(k), chunks(v), chunks(li), chunks(lf)))
    return jnp.moveaxis(h, 0, 2).reshape(B, H, Lp, dv)


def mlstm_branch(m_qk, m_v, m_o, m_gates, conv_w, gate_bias, norm_g):
    B, L, _ = m_qk.shape
    dtype = m_qk.dtype
    qk = jax.nn.silu(dwconv_centred(m_qk, conv_w))
    q, k = jnp.split(qk, 2, axis=-1)

    def heads(t):
        return t.reshape(B, L, M_HEADS, M_HEAD_DIM).transpose(0, 2, 1, 3).astype(jnp.float32)

    q = heads(q)
    k = heads(k) * (M_HEAD_DIM ** -0.5)
    v = heads(m_v)
    gates = (m_gates + gate_bias).astype(jnp.float32).transpose(0, 2, 1)
    li_f = gates[:, 0:M_HEADS]
    lf_f = jax.nn.log_sigmoid(gates[:, M_HEADS:2 * M_HEADS])
    li_b = gates[:, 2 * M_HEADS:3 * M_HEADS]
    lf_b = jax.nn.log_sigmoid(gates[:, 3 * M_HEADS:4 * M_HEADS])
    pad = (-N_META) % M_CHUNK

    def padl(t, val):
        return jnp.pad(t, ((0, 0), (0, 0), (pad, 0)) + ((0, 0),) * (t.ndim - 3), constant_values=val)

    def flip(t):
        return jnp.flip(t, axis=2)

    qp, kp, vp = padl(q, 0.0), padl(k, 0.0), padl(v, 0.0)
    h_fwd = mlstm_scan(qp, kp, vp, padl(li_f, PAD_LOG_I), padl(lf_f, 0.0))
    h_bwd = flip(mlstm_scan(flip(qp), flip(kp), flip(vp), flip(padl(li_b, PAD_LOG_I)), flip(padl(lf_b, 0.0))))
    hs = (h_fwd + h_bwd)[:, :, pad:]
    mu = jnp.mean(hs, axis=-1, keepdims=True)
    var = jnp.mean(jnp.square(hs - mu), axis=-1, keepdims=True)
    hn = (hs - mu) * lax.rsqrt(var + EPS)
    hn = hn.astype(dtype) * norm_g.reshape(M_HEADS, 1, M_HEAD_DIM)
    hn = hn.transpose(0, 2, 1, 3).reshape(B, L, M_WIDTH)
    return jax.nn.sigmoid(m_o) * hn


def diff_attention_branch(d_q, d_k, d_v, lq1, lk1, lq2, lk2, subln_g, lam_init, cos, sin):
    B, L, _ = d_q.shape
    q = d_q.reshape(B, L, D_HEADS, 2, D_QK_DIM).transpose(0, 2, 3, 1, 4)
    k = d_k.reshape(B, L, D_HEADS, 2, D_QK_DIM).transpose(0, 2, 3, 1, 4)
    v = d_v.reshape(B, L, D_HEADS, D_V_DIM).transpose(0, 2, 1, 3)
    q = rope_partial(q, cos, sin)
    k = rope_partial(k, cos, sin)
    lam = (jnp.exp(jnp.sum(lq1.astype(jnp.float32) * lk1.astype(jnp.float32)))
           - jnp.exp(jnp.sum(lq2.astype(jnp.float32) * lk2.astype(jnp.float32))) + lam_init)
    LQ = -(-L // Q_BLOCK) * Q_BLOCK
    nb = LQ // Q_BLOCK
    qpad = jnp.pad(q, ((0, 0), (0, 0), (0, 0), (0, LQ - L), (0, 0)))
    qb = jnp.moveaxis(qpad.reshape(B, D_HEADS, 2, nb, Q_BLOCK, D_QK_DIM), 3, 0)
    scale = D_QK_DIM ** -0.5

    def attend(qblk):
        s = jnp.einsum('bhmqd,bhmkd->bhmqk', qblk, k).astype(jnp.float32) * scale
        p = jax.nn.softmax(s, axis=-1)
        a = p[:, :, 0] - lam * p[:, :, 1]
        return jnp.einsum('bhqk,bhkd->bhqd', a.astype(v.dtype), v)

    o = lax.map(attend, qb)
    o = jnp.moveaxis(o, 0, 2).reshape(B, D_HEADS, LQ, D_V_DIM)[:, :, :L]
    o = rmsnorm(o, subln_g) * (1.0 - lam_init)
    return o.transpose(0, 2, 1, 3).reshape(B, L, D_WIDTH)


def setup_inputs(seed: int = 0) -> dict:
    key = jax.random.key(seed)
    ks = jax.random.split(key, 24)
    f32 = jnp.float32

    def nrm(k, shape, scale):
        return jax.random.normal(k, shape, f32) * scale

    def gain(k, shape):
        return 1.0 + 0.02 * jax.random.normal(k, shape, f32)

    gk = jax.random.split(ks[5], 4)
    i_f = nrm(gk[0], (DEPTH, M_HEADS), 0.1)
    f_f = jnp.linspace(3.0, 6.0, M_HEADS, dtype=f32)[None] + nrm(gk[1], (DEPTH, M_HEADS), 0.1)
    i_b = nrm(gk[2], (DEPTH, M_HEADS), 0.1)
    f_b = jnp.linspace(3.0, 6.0, M_HEADS, dtype=f32)[None] + nrm(gk[3], (DEPTH, M_HEADS), 0.1)
    return {
        "x": nrm(ks[0], (BATCH, SEQ, D_MODEL), 1.0),
        "meta_tokens": nrm(ks[1], (N_META, D_MODEL), 1.0),
        "norm1_g": gain(ks[2], (DEPTH, D_MODEL)),
        "w_in": nrm(ks[3], (DEPTH, D_MODEL, IN_WIDTH), D_MODEL ** -0.5),
        "mlstm_conv_w": nrm(ks[4], (DEPTH, M_CONV, 2 * M_WIDTH), M_CONV ** -0.5),
        "mlstm_gate_bias": jnp.concatenate([i_f, f_f, i_b, f_b], axis=-1),
        "mlstm_norm_g": gain(ks[6], (DEPTH, M_WIDTH)),
        "lambda_q1": nrm(ks[7], (DEPTH, D_QK_DIM), 0.1),
        "lambda_k1": nrm(ks[8], (DEPTH, D_QK_DIM), 0.1),
        "lambda_q2": nrm(ks[9], (DEPTH, D_QK_DIM), 0.1),
        "lambda_k2": nrm(ks[10], (DEPTH, D_QK_DIM), 0.1),
        "diff_subln_g": gain(ks[11], (DEPTH, D_V_DIM)),
        "w_branch_m": nrm(ks[12], (DEPTH, M_WIDTH, D_MODEL), M_WIDTH ** -0.5),
        "w_branch_d": nrm(ks[13], (DEPTH, D_WIDTH, D_MODEL), D_WIDTH ** -0.5),
        "w_out": nrm(ks[14], (DEPTH, D_MODEL, D_MODEL), D_MODEL ** -0.5),
        "norm2_g": gain(ks[15], (DEPTH, D_MODEL)),
        "w_up": nrm(ks[16], (DEPTH, D_MODEL, 2 * FFN_DIM), D_MODEL ** -0.5),
        "ffn_conv_w": nrm(ks[17], (DEPTH, FFN_CONV, 2 * FFN_DIM), FFN_CONV ** -0.5),
        "w_down": nrm(ks[18], (DEPTH, FFN_DIM, D_MODEL), FFN_DIM ** -0.5),
        "norm_f_g": gain(ks[19], (D_MODEL,)),
    }


def reference(x, meta_tokens, norm1_g, w_in, mlstm_conv_w, mlstm_gate_bias, mlstm_norm_g,
              lambda_q1, lambda_k1, lambda_q2, lambda_k2, diff_subln_g, w_branch_m, w_branch_d,
              w_out, norm2_g, w_up, ffn_conv_w, w_down, norm_f_g):
    B = x.shape[0]
    meta = jnp.broadcast_to(meta_tokens[None].astype(x.dtype), (B, N_META, D_MODEL))
    h = jnp.concatenate([meta, x], axis=1)
    L = h.shape[1]
    cos, sin = rope_tables(L, x.dtype)
    split_at = np.cumsum([2 * M_WIDTH, M_WIDTH, M_WIDTH, M_GATES, D_QK_WIDTH, D_QK_WIDTH,
                          D_WIDTH, D_MODEL]).tolist()
    for layer in range(DEPTH):
        lam_init = 0.8 - 0.6 * math.exp(-0.3 * layer)
        u = rmsnorm(h, norm1_g[layer])
        proj = u @ w_in[layer]
        m_qk, m_v, m_o, m_gates, d_q, d_k, d_v, g_m, g_d = jnp.split(proj, split_at, axis=-1)
        a_out = mlstm_branch(m_qk, m_v, m_o, m_gates, mlstm_conv_w[layer],
                             mlstm_gate_bias[layer], mlstm_norm_g[layer])
        b_out = diff_attention_branch(d_q, d_k, d_v, lambda_q1[layer], lambda_k1[layer],
                                      lambda_q2[layer], lambda_k2[layer], diff_subln_g[layer],
                                      lam_init, cos, sin)
        merged = (jax.nn.sigmoid(g_m) * (a_out @ w_branch_m[layer])
                  + jax.nn.sigmoid(g_d) * (b_out @ w_branch_d[layer]))
        h = h + merged @ w_out[layer]
        up = dwconv_centred(rmsnorm(h, norm2_g[layer]) @ w_up[layer], ffn_conv_w[layer])
        gate, val = jnp.split(up, 2, axis=-1)
        h = h + (jax.nn.silu(gate) * val) @ w_down[layer]
    return rmsnorm(h, norm_f_g)[:, N_META:]
```

```python
import numpy as np
from contextlib import ExitStack
import concourse.bass as bass
import concourse.mybir as mybir
from concourse.bass_utils import run_bass_kernel_spmd

F32 = mybir.dt.float32
BF16 = mybir.dt.bfloat16
AF = mybir.ActivationFunctionType
ALU = mybir.AluOpType
AX = mybir.AxisListType

SAME_ENGINE_SYNC = True
DEBUG = False
STAGES = ("A1", "A2", "A3", "X", "B")

D = 2048
L = 4112
LP = 4160
NMETA = 16
NW = 1026
FFN = 5632
EPS = 1e-6


class _Stop(Exception):
    pass


STOP_AT = None


_PROG = []


def stop_here(tag):
    if STOP_AT == tag:
        _PROG[0].stopped = True


_FENCE = []


class Res:
    __slots__ = ("name", "w", "r")

    def __init__(self, name=""):
        self.name = name
        self.w = None
        self.r = dict(_FENCE)


class Prog:
    ENGS = ("pe", "act", "dve", "pool", "sp")

    def __init__(self, nc, stack, n_dma_sems=8):
        self.nc = nc
        self.stack = stack
        self.streams = {e: [] for e in self.ENGS}
        self.sems = {}
        self.count = {}
        self.waited = {e: {} for e in self.ENGS}
        for e in self.ENGS:
            self.sems["c_" + e] = stack.enter_context(nc.semaphore("c_" + e))
            self.count["c_" + e] = 0
        self.dma_pool = {}
        self.dma_next = {}
        for e in ("sp", "act", "pool"):
            keys = []
            for i in range(n_dma_sems):
                k = f"d_{e}{i}"
                self.sems[k] = stack.enter_context(nc.semaphore(k))
                self.count[k] = 0
                keys.append(k)
            self.dma_pool[e] = keys
            self.dma_next[e] = 0
        self.sems["cc"] = stack.enter_context(nc.semaphore("cc"))
        self.count["cc"] = 0
        self.stopped = False
        _PROG[:] = [self]
        _FENCE[:] = []

    def _need(self, eng, dep):
        if dep is None:
            return
        key, val = dep
        if key == "c_" + eng and (eng == "pe" or not SAME_ENGINE_SYNC):
            return
        if self.waited[eng].get(key, 0) >= val:
            return
        self.waited[eng][key] = val
        self.streams[eng].append(("wait", key, val))

    def _deps(self, eng, reads, writes):
        own = "c_" + eng
        for r in reads:
            self._need(eng, r.w)
            for k, v in r.r.items():
                if k != own:
                    self._need(eng, (k, v))
        for w in writes:
            self._need(eng, w.w)
            for k, v in w.r.items():
                self._need(eng, (k, v))

    def _commit(self, reads, writes, tok):
        for r in reads:
            if r.r.get(tok[0], 0) < tok[1]:
                r.r[tok[0]] = tok[1]
        for w in writes:
            w.w = tok
            w.r = {}

    def op(self, eng, fns, reads=(), writes=()):
        if self.stopped:
            return None
        if not isinstance(fns, (list, tuple)):
            fns = [fns]
        self._deps(eng, reads, writes)
        key = "c_" + eng
        self.count[key] += 1
        tok = (key, self.count[key])
        self.streams[eng].append(("op", fns, key, 1))
        self._commit(reads, writes, tok)
        return tok

    def dma(self, eng, fn, reads=(), writes=()):
        if self.stopped:
            return None
        pool = self.dma_pool[eng]
        key = pool[self.dma_next[eng] % len(pool)]
        self.dma_next[eng] += 1
        if self.count[key] > 0:
            self._need(eng, (key, self.count[key]))
        self._deps(eng, reads, writes)
        self.count[key] += 16
        tok = (key, self.count[key])
        self.streams[eng].append(("op", [fn], key, 16))
        self._commit(reads, writes, tok)
        return tok

    def fence(self):
        _FENCE[:] = [(k, c) for k, c in self.count.items() if c > 0]

    def cc(self, fn, reads=(), writes=()):
        if self.stopped:
            return None
        eng = "pool"
        self._deps(eng, reads, writes)
        self.count["cc"] += 1
        tok = ("cc", self.count["cc"])
        self.streams[eng].append(("cc", fn, "cc"))
        self._commit(reads, writes, tok)
        return tok

    def finish(self, final_tokens):
        for t in final_tokens:
            self._need("sp", t)
        for k, c in self.count.items():
            if c > 0:
                self._need("sp", (k, c))
        nc = self.nc
        with nc.Block() as block:
            def mk(ename):
                def body(e):
                    for item in self.streams[ename]:
                        if item[0] == "wait":
                            e.wait_ge(self.sems[item[1]], item[2])
                        elif item[0] == "op":
                            fns, key, inc = item[1], item[2], item[3]
                            for f in fns[:-1]:
                                f(e)
                            fns[-1](e).then_inc(self.sems[key], inc)
                        elif item[0] == "cc":
                            item[1](e).then_inc(self.sems[item[2]])
                return body
            block.tensor(mk("pe"))
            block.scalar(mk("act"))
            block.vector(mk("dve"))
            block.gpsimd(mk("pool"))
            block.sync(mk("sp"))


class Ring:
    def __init__(self, tiles):
        self.tiles = tiles
        self.res = [Res() for _ in tiles]
        self.i = 0

    def next(self):
        k = self.i % len(self.tiles)
        self.i += 1
        return self.tiles[k], self.res[k]


def build_program():
    nc = bass.Bass("TRN2", target_bir_lowering=False)
    dt_in = lambda name, shape, dt=F32: nc.dram_tensor(name, shape, dt, kind="ExternalInput").ap()
    dt_int = lambda name, shape, dt: nc.dram_tensor(name, shape, dt, kind="Internal").ap()

    hfull = dt_in("hfull", [L, D])
    xwin = dt_in("xwin", [NW, D])
    w_attn = dt_in("w_attn", [D, 1280])
    w_ml = dt_in("w_ml", [D, 1028])
    w_g = dt_in("w_g", [D, 4096] if "B" in STAGES else [128, 128])
    w_a = dt_in("w_a", [1024, D] if "B" in STAGES else [128, 128])
    w_b = dt_in("w_b", [1024, D] if "B" in STAGES else [128, 128])
    w_out = dt_in("w_out", [D, D] if "B" in STAGES else [128, 128])
    w_up = dt_in("w_up", [D, 2 * FFN] if "B" in STAGES else [128, 128])
    w_down = dt_in("w_down", [FFN, D] if "B" in STAGES else [128, 128])
    g1b = dt_in("g1b", [128, D])
    gfb = dt_in("gfb", [128, D])
    g2c = dt_in("g2c", [128, 16])
    cosf = dt_in("cosf", [128, L])
    sinf = dt_in("sinf", [128, L])
    mconv = dt_in("mconv", [128, 12])
    fconv = dt_in("fconv", [128, 88 * 3])
    gbias = dt_in("gbias", [128, 4])
    mng = dt_in("mng", [64, 256])
    lamv = dt_in("lamv", [128, 4 * 64])
    sublng = dt_in("sublng", [128, 128])
    ident_d = dt_in("ident", [128, 128])
    triu_d = dt_in("triu", [64, 64])
    tril_d = dt_in("tril", [64, 64])
    wmask_d = dt_in("wmask", [128, NW])
    out_d = nc.dram_tensor("out", [1024, D], F32, kind="ExternalOutput").ap()
    if DEBUG:
        dbg_cc = nc.dram_tensor("dbg_cc", [512, 4114], BF16, kind="ExternalOutput").ap()

    mqk_raw = dt_int("mqk_raw", [512, LP + 2], BF16)
    mv_s = dt_int("mv_s", [LP, 257], BF16)
    mo_s = dt_int("mo_s", [LP, 256], F32)
    gates_s = dt_int("gates_s", [LP, 4], F32)
    hf_s = dt_int("hf_s", [LP, 256], F32)
    hb_s = dt_int("hb_s", [LP, 256], F32)
    cc_in = dt_int("cc_in", [512, 4114], BF16)
    cc_win = dt_int("cc_win", [8 * 256, NW], BF16)
    cc_out = dt_int("cc_out", [8 * 1024, NW], BF16)

    with ExitStack() as st:
        P = Prog(nc, st)

        def sb(name, shape, dt, stack=st):
            return stack.enter_context(nc.sbuf_tensor(name, shape, dt))

        def ps(name, shape, dt, stack=st):
            return stack.enter_context(nc.psum_tensor(name, shape, dt))

        final_toks = []
        dbg_holder = []
        try:
            idf = sb("idf", [128, 128], F32); r_idf = Res()
            idb = sb("idb", [128, 128], BF16); r_idb = Res()
            zt = sb("zt", [128, 512], BF16); r_zt = Res()
            P.dma("sp", lambda e: e.dma_start(out=idf[:], in_=ident_d[:, :]), writes=[r_idf])
            P.op("dve", lambda e: e.tensor_copy(out=idb[:], in_=idf[:]), reads=[r_idf], writes=[r_idb])
            P.op("dve", lambda e: e.memset(zt[:], 0.0), writes=[r_zt])

            r_mqk_raw = Res(); r_mv_s = Res(); r_mo_s = Res(); r_gates_s = Res(); r_hf_s = Res(); r_hb_s = Res(); r_cc_in = Res(); r_cc_in_b = Res()
            r_cc_win = [Res(), Res()]; r_cc_out = [Res(), Res()]
            P.dma("sp", lambda e: e.dma_start(out=mqk_raw.rearrange("(c p) n -> p c n", p=128)[:, :, 0:49], in_=zt[:, 0:196].rearrange("p (c n) -> p c n", c=4)), reads=[r_zt], writes=[r_mqk_raw])
            P.dma("sp", lambda e: e.dma_start(out=mqk_raw.rearrange("(c p) n -> p c n", p=128)[:, :, LP + 1:LP + 2], in_=zt[:, 0:4].rearrange("p (c n) -> p c n", c=4), allow_slow_non_contiguous=True), reads=[r_zt], writes=[r_mqk_raw])
            P.dma("sp", lambda e: e.dma_start(out=mv_s[0:48, :], in_=zt[0:48, 0:257]), reads=[r_zt], writes=[r_mv_s])
            ztf = sb("ztf", [64, 256], F32); r_ztf = Res()
            P.op("dve", lambda e: e.memset(ztf[:], 0.0), writes=[r_ztf])
            P.dma("sp", lambda e: e.dma_start(out=mo_s[0:48, :], in_=ztf[0:48, :]), reads=[r_ztf], writes=[r_mo_s])
            P.dma("sp", lambda e: e.dma_start(out=gates_s[0:48, :], in_=ztf[0:48, 0:4]), reads=[r_ztf], writes=[r_gates_s])
            P.dma("sp", lambda e: e.dma_start(out=cc_in[:, 4112:4114].rearrange("(c p) n -> p c n", p=128), in_=zt[:, 0:8].rearrange("p (c n) -> p c n", c=4)), reads=[r_zt], writes=[r_cc_in, r_cc_in_b])


            def exchange_half(half):
                r_src = r_cc_in if half == 0 else r_cc_in_b
                for j in range(4):
                    i = j * 2 + half
                    P.dma("sp", lambda e, i=i, j=j, half=half: e.dma_start(out=cc_win[i * 256:(i + 1) * 256, :], in_=cc_in[half * 256:(half + 1) * 256, 15 + 1024 * j:15 + 1024 * j + NW]), reads=[r_src], writes=[r_cc_win[half]])
                for j in range(4):
                    i = j * 2 + half
                    P.cc(lambda e, i=i: e.collective_compute("AllGather", ALU.bypass, replica_groups=[[0, 1, 2, 3], [4, 5, 6, 7]], ins=[cc_win[i * 256:(i + 1) * 256, :]], outs=[cc_out[i * 1024:(i + 1) * 1024, :]]), reads=[r_cc_win[half]], writes=[r_cc_out[half]])

            stop_here("c0")
            with ExitStack() as sa:
                dqkT = sb("dqkT", [128, 4, L], BF16, sa); r_dqkT = Res()
                vatt = sb("vatt", [128, 33, 2, 129], BF16, sa); r_vatt = Res()
                with ExitStack() as s1:
                    wat = sb("wat", [128, 16, 1280], BF16, s1); r_wat = Res()
                    wml = sb("wml", [128, 16, 1028], BF16, s1); r_wml = Res()
                    g1t = sb("g1t", [128, D], F32, s1); r_g1t = Res()
                    gbt = sb("gbt", [128, 4], F32, s1); r_gbt = Res()
                    for kq in range(4):
                        P.dma("pool", lambda e, kq=kq: e.dma_start(out=wat[:, 4 * kq:4 * kq + 4, :], in_=w_attn[512 * kq:512 * kq + 512, :].rearrange("(k p) n -> p k n", p=128)), writes=[r_wat])
                        P.dma("pool", lambda e, kq=kq: e.dma_start(out=wml[:, 4 * kq:4 * kq + 4, :], in_=w_ml[512 * kq:512 * kq + 512, :].rearrange("(k p) n -> p k n", p=128)), writes=[r_wml])
                    P.dma("act", lambda e: e.dma_start(out=g1t[:], in_=g1b[:, :]), writes=[r_g1t])
                    P.dma("act", lambda e: e.dma_start(out=gbt[:], in_=gbias[:, :]), writes=[r_gbt])
                    P.op("dve", lambda e: e.memset(vatt[:, :, :, 128:129], 1.0), writes=[r_vatt])

                    xring = Ring([sb(f"xt{i}", [128, D], F32, s1) for i in range(2)])
                    junk = sb("junk", [128, D], BF16, s1); r_junk = Res()
                    ss = sb("ss", [128, 1], F32, s1); r_ss = Res()
                    u = sb("u", [128, D], BF16, s1); r_u = Res()
                    uT = sb("uT", [128, 16, 512], BF16, s1); r_uT = Res()
                    csr = Ring([sb(f"cs{i}", [128, 2, 512], F32, s1) for i in range(2)])
                    rt1 = sb("rt1", [128, 512], F32, s1); r_rt1 = Res()
                    rt2 = sb("rt2", [128, 512], F32, s1); r_rt2 = Res()
                    mstage = Ring([sb(f"mst{i}", [128, 4, 512], BF16, s1) for i in range(2)])
                    mvst = Ring([sb(f"mvst{i}", [128, 257], BF16, s1) for i in range(2)])
                    most = Ring([sb(f"most{i}", [128, 256], F32, s1) for i in range(2)])
                    gst = Ring([sb(f"gst{i}", [128, 4], F32, s1) for i in range(2)])
                    pT = ps("pT", [128, D], BF16, s1); r_pT = Res()
                    pa = ps("pa", [128, 512], F32, s1); r_pa = Res()
                    pb = ps("pb", [128, 512], F32, s1); r_pb = Res()
                    pm = ps("pm", [128, 512], F32, s1); r_pm = Res()
                    pv = ps("pv", [128, 512], F32, s1); r_pv = Res()
                    pvo = ps("pvo", [128, 512], F32, s1); r_pvo = Res()
                    pg = ps("pg", [128, 512], F32, s1); r_pg = Res()
                    for rr, rres in zip(mvst.tiles, mvst.res):
                        P.op("dve", lambda e, rr=rr: e.memset(rr[:, 256:257], 1.0), writes=[rres])

                    def norm_block(xt, r_xt, bs, gtile, r_g):
                        P.op("act", lambda e: e.activation(out=junk[:bs, :], in_=xt[:bs, :], func=AF.Square, accum_out=ss[:bs, :]), reads=[r_xt], writes=[r_junk, r_ss])
                        P.op("dve", lambda e: e.tensor_scalar(out=ss[:bs, :], in0=ss[:bs, :], scalar1=1.0 / D, scalar2=EPS, op0=ALU.mult, op1=ALU.add), reads=[r_ss], writes=[r_ss])
                        P.op("act", lambda e: e.activation(out=ss[:bs, :], in_=ss[:bs, :], func=AF.Sqrt), reads=[r_ss], writes=[r_ss])
                        P.op("dve", lambda e: e.reciprocal(out=ss[:bs, :], in_=ss[:bs, :]), reads=[r_ss], writes=[r_ss])
                        P.op("dve", lambda e: e.scalar_tensor_tensor(out=u[:bs, :], in0=xt[:bs, :], scalar=ss[:bs, 0:1], in1=gtile[:bs, :], op0=ALU.mult, op1=ALU.mult), reads=[r_xt, r_ss, r_g], writes=[r_u])

                    stop_here("a1w")
                    tiles = [(512 * i, 512) for i in range(8)] + [(4096, 16)]
                    for (p0, n) in (tiles if "A1" in STAGES else []):
                        nblk = (n + 127) // 128
                        cs, r_cs = csr.next()
                        P.dma("act", lambda e, cs=cs, p0=p0, n=n: e.dma_start(out=cs[:, 0, :n], in_=cosf[:, p0:p0 + n]), writes=[r_cs])
                        P.dma("act", lambda e, cs=cs, p0=p0, n=n: e.dma_start(out=cs[:, 1, :n], in_=sinf[:, p0:p0 + n]), writes=[r_cs])
                        for j in range(nblk):
                            bs = min(128, n - 128 * j)
                            xt, r_xt = xring.next()
                            P.dma("sp", lambda e, xt=xt, bs=bs, r0=p0 + 128 * j: e.dma_start(out=xt[:bs, :], in_=hfull[r0:r0 + bs, :]), writes=[r_xt])
                            norm_block(xt, r_xt, bs, g1t, r_g1t)
                            P.op("pe", [(lambda e, k=k, bs=bs: e.transpose(out=pT[:, k * 128:k * 128 + bs], in_=u[:bs, k * 128:(k + 1) * 128], identity=idb[:bs, :bs])) for k in range(16)], reads=[r_u, r_idb], writes=[r_pT])
                            P.op("act", lambda e, j=j, bs=bs: e.copy(out=uT[:, :, 128 * j:128 * j + bs], in_=pT[:].rearrange("p (k n) -> p k n", k=16)[:, :, :bs]), reads=[r_pT], writes=[r_uT])
                        stop_here("blk%d" % (p0 // 512))
                        for c in range(4):
                            cm = (c % 2) * 128 + (c // 2) * 512
                            cr = cm + 256
                            P.op("pe", [(lambda e, k=k, cm=cm, n=n: e.matmul(out=pa[:, :n], lhsT=wat[:, k, cm:cm + 128], rhs=uT[:, k, :n], start=(k == 0), stop=(k == 15))) for k in range(16)], reads=[r_wat, r_uT], writes=[r_pa])
                            P.op("pe", [(lambda e, k=k, cr=cr, n=n: e.matmul(out=pb[:, :n], lhsT=wat[:, k, cr:cr + 128], rhs=uT[:, k, :n], start=(k == 0), stop=(k == 15))) for k in range(16)], reads=[r_wat, r_uT], writes=[r_pb])
                            P.op("dve", lambda e, cs=cs, n=n: e.tensor_tensor(out=rt1[:, :n], in0=pa[:, :n], in1=cs[:, 0, :n], op=ALU.mult), reads=[r_pa, r_cs], writes=[r_rt1])
                            P.op("dve", lambda e, cs=cs, n=n: e.tensor_tensor(out=rt2[:, :n], in0=pb[:, :n], in1=cs[:, 1, :n], op=ALU.mult), reads=[r_pb, r_cs], writes=[r_rt2])
                            P.op("dve", lambda e, c=c, p0=p0, n=n: e.tensor_tensor(out=dqkT[:, c, p0:p0 + n], in0=rt1[:, :n], in1=rt2[:, :n], op=ALU.add), reads=[r_rt1, r_rt2], writes=[r_dqkT])
                        stop_here("fm%d" % (p0 // 512))
                        mst, r_mst = mstage.next()
                        for c in range(4):
                            P.op("pe", [(lambda e, k=k, c=c, n=n: e.matmul(out=pm[:, :n], lhsT=wml[:, k, c * 128:(c + 1) * 128], rhs=uT[:, k, :n], start=(k == 0), stop=(k == 15))) for k in range(16)], reads=[r_wml, r_uT], writes=[r_pm])
                            P.op("act", lambda e, mst=mst, c=c, n=n: e.copy(out=mst[:, c, :n], in_=pm[:, :n]), reads=[r_pm], writes=[r_mst])
                        P.dma("sp", lambda e, mst=mst, p0=p0, n=n: e.dma_start(out=mqk_raw.rearrange("(c p) n -> p c n", p=128)[:, :, 49 + p0:49 + p0 + n], in_=mst[:, :, :n]), reads=[r_mst], writes=[r_mqk_raw])
                        stop_here("ml%d" % (p0 // 512))
                        for j in range(nblk):
                            bs = min(128, n - 128 * j)
                            kb = (p0 + 128 * j) // 128
                            r0 = 48 + p0 + 128 * j
                            P.op("pe", [(lambda e, k=k, j=j, bs=bs: e.matmul(out=pv[:bs, 0:256], lhsT=uT[:, k, 128 * j:128 * j + bs], rhs=wat[:, k, 1024:1280], start=(k == 0), stop=(k == 15))) for k in range(16)], reads=[r_wat, r_uT], writes=[r_pv])
                            P.op("dve", lambda e, kb=kb, bs=bs: e.tensor_copy(out=vatt[:bs, kb, :, 0:128], in_=pv[:bs, 0:256].rearrange("p (h d) -> p h d", h=2)), reads=[r_pv], writes=[r_vatt])
                            stop_here("tv%d_%d" % (p0 // 512, j))
                            P.op("pe", [(lambda e, k=k, j=j, bs=bs: e.matmul(out=pvo[:bs, :], lhsT=uT[:, k, 128 * j:128 * j + bs], rhs=wml[:, k, 512:1024], start=(k == 0), stop=(k == 15))) for k in range(16)], reads=[r_wml, r_uT], writes=[r_pvo])
                            mvt, r_mvt = mvst.next()
                            mot, r_mot = most.next()
                            P.op("act", lambda e, mvt=mvt, bs=bs: e.copy(out=mvt[:bs, 0:256], in_=pvo[:bs, 0:256]), reads=[r_pvo], writes=[r_mvt])
                            P.op("act", lambda e, mot=mot, bs=bs: e.activation(out=mot[:bs, :], in_=pvo[:bs, 256:512], func=AF.Sigmoid), reads=[r_pvo], writes=[r_mot])
                            P.dma("sp", lambda e, mvt=mvt, bs=bs, r0=r0: e.dma_start(out=mv_s[r0:r0 + bs, :], in_=mvt[:bs, :]), reads=[r_mvt], writes=[r_mv_s])
                            P.dma("sp", lambda e, mot=mot, bs=bs, r0=r0: e.dma_start(out=mo_s[r0:r0 + bs, :], in_=mot[:bs, :]), reads=[r_mot], writes=[r_mo_s])
                            stop_here("tm%d_%d" % (p0 // 512, j))
                            P.op("pe", [(lambda e, k=k, j=j, bs=bs: e.matmul(out=pg[:bs, 0:64], lhsT=uT[:, k, 128 * j:128 * j + bs], rhs=wml[:, k, 964:1028], start=(k == 0), stop=(k == 15))) for k in range(16)], reads=[r_wml, r_uT], writes=[r_pg])
                            stop_here("tgm%d_%d" % (p0 // 512, j))
                            gt_, r_gt = gst.next()
                            P.op("dve", lambda e, gt_=gt_, bs=bs: e.tensor_tensor(out=gt_[:bs, :], in0=pg[:bs, 60:64], in1=gbt[:bs, :], op=ALU.add), reads=[r_pg, r_gbt], writes=[r_gt])
                            stop_here("tga%d_%d" % (p0 // 512, j))
                            P.dma("sp", lambda e, gt_=gt_, bs=bs, r0=r0: e.dma_start(out=gates_s[r0:r0 + bs, :], in_=gt_[:bs, :]), reads=[r_gt], writes=[r_gates_s])
                    stop_here("a1t%d" % (p0 // 512))

                P.fence()
                stop_here("a1")
                with ExitStack() as s2:
                    lam_t = sb("lam_t", [128, 256], F32, s2); r_lam = Res()
                    lamw = sb("lamw", [128, 8], F32, s2); r_lamw = Res()
                    sg_t = sb("sg_t", [128, 128], F32, s2); r_sg = Res()
                    P.dma("act", lambda e: e.dma_start(out=lam_t[:], in_=lamv[:, :]), writes=[r_lam])
                    P.dma("act", lambda e: e.dma_start(out=sg_t[:], in_=sublng[:, :]), writes=[r_sg])
                    ljunk = sb("ljunk", [128, 64], F32, s2); r_lj = Res()
                    for i in range(2):
                        P.op("dve", lambda e, i=i: e.tensor_tensor(out=ljunk[:], in0=lam_t[:, 128 * i:128 * i + 64], in1=lam_t[:, 128 * i + 64:128 * i + 128], op=ALU.mult), reads=[r_lam], writes=[r_lj])
                        P.op("dve", lambda e, i=i: e.reduce_sum(out=lamw[:, i:i + 1], in_=ljunk[:], axis=AX.X), reads=[r_lj], writes=[r_lamw])
                    P.op("act", lambda e: e.activation(out=lamw[:, 2:4], in_=lamw[:, 0:2], func=AF.Exp), reads=[r_lamw], writes=[r_lamw])
                    P.op("dve", lambda e: e.tensor_tensor(out=lamw[:, 4:5], in0=lamw[:, 2:3], in1=lamw[:, 3:4], op=ALU.subtract), reads=[r_lamw], writes=[r_lamw])
                    P.op("dve", lambda e: e.tensor_scalar(out=lamw[:, 5:6], in0=lamw[:, 4:5], scalar1=0.2, scalar2=-1.0, op0=ALU.add, op1=ALU.mult), reads=[r_lamw], writes=[r_lamw])
                    P.op("dve", lambda e: e.tensor_scalar(out=sg_t[:], in0=sg_t[:], scalar1=0.8, scalar2=None, op0=ALU.mult), reads=[r_sg], writes=[r_sg])

                    stop_here("a2s")
                    pss = [Ring([ps(f"ps{m}{i}", [128, 512], F32, s2) for i in range(2)]) for m in range(2)]
                    pacc = [ps(f"pacc{i}", [128, 512], F32, s2) for i in range(3)]
                    r_pacc = Res()
                    ptr = ps("ptr", [128, 512], BF16, s2); r_ptr = Res()
                    ptile = [Ring([sb(f"pt{m}{i}", [128, 512], BF16, s2) for i in range(3)]) for m in range(2)]
                    rc = sb("rc", [128, 2], F32, s2); r_rc = Res()
                    t2 = sb("t2", [128, 128], F32, s2); r_t2 = Res()
                    ot = sb("ot", [128, 128], F32, s2); r_ot = Res()
                    oj = sb("oj", [128, 128], F32, s2); r_oj = Res()
                    os_ = sb("os_", [128, 1], F32, s2); r_os = Res()
                    bo = sb("bo", [128, 128], BF16, s2); r_bo = Res()
                    boT = Ring([sb(f"boT{i}", [128, 512], BF16, s2) for i in range(2)])

                    def acc(m, sub):
                        i = m * 4 + sub
                        return pacc[i // 3][:, (i % 3) * 129:(i % 3) * 129 + 129]

                    qtiles = [(512 * i, 512) for i in range(8)] + [(4096, 16)]
                    kblocks = [(128 * i, 128) for i in range(32)] + [(4096, 16)]
                    for h in range(2 if "A2" in STAGES else 0):
                        for (q0, nq) in qtiles:
                            nsub = (nq + 127) // 128
                            P.op("dve", [(lambda e, i=i: e.memset(pacc[i][:], 0.0)) for i in range(3)], writes=[r_pacc])
                            pend = []
                            for kbi, (k0, nk) in enumerate(kblocks):
                                cur = []
                                for m in range(2):
                                    pst, r_pst = pss[m].next()
                                    P.op("pe", lambda e, pst=pst, m=m, h=h, k0=k0, nk=nk, q0=q0, nq=nq: e.matmul(out=pst[:nk, :nq], lhsT=dqkT[64 * m:64 * m + 64, 2 + h, k0:k0 + nk], rhs=dqkT[64 * m:64 * m + 64, h, q0:q0 + nq], start=True, stop=True), reads=[r_dqkT], writes=[r_pst])
                                    cur.append((m, pst, r_pst))
                                newpend = []
                                for (m, pst, r_pst) in cur:
                                    pt, r_pt = ptile[m].next()
                                    P.op("act", lambda e, pst=pst, pt=pt, nk=nk, nq=nq: e.activation(out=pt[:nk, :nq], in_=pst[:nk, :nq], func=AF.Exp, scale=0.125), reads=[r_pst], writes=[r_pt])
                                    def pvf(pt=pt, r_pt=r_pt, m=m, nk=nk, kbi=kbi):
                                        P.op("pe", [(lambda e, pt=pt, m=m, sub=sub, nk=nk, kbi=kbi, h=h, qs=min(128, nq - 128 * sub): e.matmul(out=acc(m, sub)[:qs, :], lhsT=pt[:nk, 128 * sub:128 * sub + qs], rhs=vatt[:nk, kbi, h, :], start=False, stop=False, skip_group_check=True)) for sub in range(nsub)], reads=[r_pt, r_vatt], writes=[r_pacc])
                                    newpend.append(pvf)
                                for f in pend:
                                    f()
                                pend = newpend
                            for f in pend:
                                f()
                            bT, r_bT = boT.next()
                            for sub in range(nsub):
                                qs = min(128, nq - 128 * sub)
                                a1 = acc(0, sub); a2 = acc(1, sub)
                                P.op("dve", lambda e, a1=a1, qs=qs: e.reciprocal(out=rc[:qs, 0:1], in_=a1[:qs, 128:129]), reads=[r_pacc], writes=[r_rc])
                                P.op("dve", lambda e, a2=a2, qs=qs: e.reciprocal(out=rc[:qs, 1:2], in_=a2[:qs, 128:129]), reads=[r_pacc], writes=[r_rc])
                                P.op("dve", lambda e, a2=a2, qs=qs: e.tensor_scalar(out=t2[:qs, :], in0=a2[:qs, 0:128], scalar1=rc[:qs, 1:2], scalar2=lamw[:qs, 5:6], op0=ALU.mult, op1=ALU.mult), reads=[r_pacc, r_rc, r_lamw], writes=[r_t2])
                                P.op("dve", lambda e, a1=a1, qs=qs: e.scalar_tensor_tensor(out=ot[:qs, :], in0=a1[:qs, 0:128], scalar=rc[:qs, 0:1], in1=t2[:qs, :], op0=ALU.mult, op1=ALU.add), reads=[r_pacc, r_rc, r_t2], writes=[r_ot])
                                P.op("act", lambda e, qs=qs: e.activation(out=oj[:qs, :], in_=ot[:qs, :], func=AF.Square, accum_out=os_[:qs, :]), reads=[r_ot], writes=[r_oj, r_os])
                                P.op("dve", lambda e, qs=qs: e.tensor_scalar(out=os_[:qs, :], in0=os_[:qs, :], scalar1=1.0 / 128, scalar2=EPS, op0=ALU.mult, op1=ALU.add), reads=[r_os], writes=[r_os])
                                P.op("act", lambda e, qs=qs: e.activation(out=os_[:qs, :], in_=os_[:qs, :], func=AF.Sqrt), reads=[r_os], writes=[r_os])
                                P.op("dve", lambda e, qs=qs: e.reciprocal(out=os_[:qs, :], in_=os_[:qs, :]), reads=[r_os], writes=[r_os])
                                P.op("dve", lambda e, qs=qs: e.scalar_tensor_tensor(out=bo[:qs, :], in0=ot[:qs, :], scalar=os_[:qs, 0:1], in1=sg_t[:qs, :], op0=ALU.mult, op1=ALU.mult), reads=[r_ot, r_os, r_sg], writes=[r_bo])
                                P.op("pe", lambda e, sub=sub, qs=qs: e.transpose(out=ptr[:, 128 * sub:128 * sub + qs], in_=bo[:qs, :], identity=idb[:qs, :qs]), reads=[r_bo, r_idb], writes=[r_ptr])
                                P.op("act", lambda e, bT=bT, sub=sub, qs=qs: e.copy(out=bT[:, 128 * sub:128 * sub + qs], in_=ptr[:, 128 * sub:128 * sub + qs]), reads=[r_ptr], writes=[r_bT])
                            P.dma("sp", lambda e, bT=bT, h=h, q0=q0, nq=nq: e.dma_start(out=cc_in[256 + 128 * h:256 + 128 * h + 128, q0:q0 + nq], in_=bT[:, :nq]), reads=[r_bT], writes=[r_cc_in_b])

            if "X" in STAGES:
                exchange_half(1)
            P.fence()
            with ExitStack() as s3:
                mqkT = sb("mqkT", [128, 4, LP], BF16, s3); r_mqkT = Res()
                with ExitStack() as s3a:
                    raw = sb("raw", [128, 4, LP + 2], BF16, s3a); r_raw = Res()
                    cacc = sb("cacc", [128, LP], F32, s3a); r_cacc = Res()
                    mcw = sb("mcw", [128, 12], F32, s3a); r_mcw = Res()
                    P.dma("act", lambda e: e.dma_start(out=mcw[:], in_=mconv[:, :]), writes=[r_mcw])
                    P.dma("sp", lambda e: e.dma_start(out=raw[:], in_=mqk_raw.rearrange("(c p) n -> p c n", p=128)), reads=[r_mqk_raw], writes=[r_raw])
                    for c in range(4):
                        P.op("dve", lambda e, c=c: e.tensor_scalar(out=cacc[:], in0=raw[:, c, 0:LP], scalar1=mcw[:, 3 * c:3 * c + 1], scalar2=None, op0=ALU.mult), reads=[r_raw, r_mcw], writes=[r_cacc])
                        P.op("dve", lambda e, c=c: e.scalar_tensor_tensor(out=cacc[:], in0=raw[:, c, 1:LP + 1], scalar=mcw[:, 3 * c + 1:3 * c + 2], in1=cacc[:], op0=ALU.mult, op1=ALU.add), reads=[r_raw, r_mcw, r_cacc], writes=[r_cacc])
                        P.op("dve", lambda e, c=c: e.scalar_tensor_tensor(out=cacc[:], in0=raw[:, c, 2:LP + 2], scalar=mcw[:, 3 * c + 2:3 * c + 3], in1=cacc[:], op0=ALU.mult, op1=ALU.add), reads=[r_raw, r_mcw, r_cacc], writes=[r_cacc])
                        P.op("act", lambda e, c=c: e.activation(out=mqkT[:, c, :], in_=cacc[:], func=AF.Silu), reads=[r_cacc], writes=[r_mqkT])
                    P.op("dve", lambda e: e.memset(mqkT[:, :, 0:48], 0.0), writes=[r_mqkT])

                P.fence()
                stop_here("a3conv")
                gt = sb("gt", [64, 65, 4], F32, s3); r_gtb = Res()
                gl = sb("gl", [64, 4, 65], F32, s3); r_gl = Res()
                tru = sb("tru", [64, 64], F32, s3); r_tru = Res()
                trl = sb("trl", [64, 64], F32, s3); r_trl = Res()
                mku = sb("mku", [64, 64], F32, s3); r_mku = Res()
                mkl = sb("mkl", [64, 64], F32, s3); r_mkl = Res()
                ones64 = sb("ones64", [64, 128], F32, s3); r_ones = Res()
                ebt = sb("ebt", [64, 2, 65], F32, s3); r_ebt = Res()
                rft = sb("rft", [64, 2, 65], F32, s3); r_rft = Res()
                egt = sb("egt", [128, 2, 65], F32, s3); r_egt = Res()
                mngt = sb("mngt", [64, 256], F32, s3); r_mngt = Res()
                s3g = s3.enter_context(ExitStack())
                pgx = ps("pgx", [128, 512], F32, s3g); r_pgx = Res()
                for c5 in range(5):
                    P.dma("sp", lambda e, c5=c5: e.dma_start(out=gt[:, 13 * c5:13 * c5 + 13, :], in_=gates_s.rearrange("(c t) g -> t c g", t=64)[:, 13 * c5:13 * c5 + 13, :]), reads=[r_gates_s], writes=[r_gtb])
                P.dma("act", lambda e: e.dma_start(out=tru[:], in_=triu_d[:, :]), writes=[r_tru])
                P.dma("act", lambda e: e.dma_start(out=trl[:], in_=tril_d[:, :]), writes=[r_trl])
                P.dma("act", lambda e: e.dma_start(out=mngt[:], in_=mng[:, :]), writes=[r_mngt])
                P.op("dve", lambda e: e.tensor_scalar(out=mku[:], in0=tru[:], scalar1=1.0 / 16, scalar2=None, op0=ALU.mult), reads=[r_tru], writes=[r_mku])
                P.op("dve", lambda e: e.tensor_scalar(out=mkl[:], in0=trl[:], scalar1=1.0 / 16, scalar2=None, op0=ALU.mult), reads=[r_trl], writes=[r_mkl])
                P.op("dve", lambda e: e.memset(ones64[:], 1.0), writes=[r_ones])
                stop_here("g1")
                for gi in range(2):
                    P.op("act", lambda e, gi=gi: e.activation(out=gl[:, gi, :], in_=gt[:, :, gi], func=AF.Exp, scale=-1.0), reads=[r_gtb], writes=[r_gl])
                P.op("dve", lambda e: e.tensor_scalar(out=gl[:, 0:2, :], in0=gl[:, 0:2, :], scalar1=1.0, scalar2=None, op0=ALU.add), reads=[r_gl], writes=[r_gl])
                P.op("act", lambda e: e.activation(out=gl[:, 0:2, :], in_=gl[:, 0:2, :], func=AF.Ln), reads=[r_gl], writes=[r_gl])
                P.op("dve", lambda e: e.tensor_scalar(out=gl[:, 0:2, :], in0=gl[:, 0:2, :], scalar1=-1.0, scalar2=None, op0=ALU.mult), reads=[r_gl], writes=[r_gl])
                for gi in range(2):
                    P.op("dve", lambda e, gi=gi: e.tensor_copy(out=gl[:, 2 + gi, :], in_=gt[:, :, 2 + gi]), reads=[r_gtb], writes=[r_gl])
                stop_here("g2")
                P.op("dve", lambda e: e.memset(gl[0:48, :, 0:1], 0.0), writes=[r_gl])
                stop_here("g3")
                P.op("pe", lambda e: e.matmul(out=pgx[:64, 0:65], lhsT=tru[:], rhs=gl[:, 0, :], start=True, stop=True), reads=[r_tru, r_gl], writes=[r_pgx])
                P.op("pe", lambda e: e.matmul(out=pgx[:64, 65:130], lhsT=trl[:], rhs=gl[:, 1, :], start=True, stop=True), reads=[r_trl, r_gl], writes=[r_pgx])
                P.op("pe", lambda e: e.matmul(out=pgx[:, 130:260], lhsT=ones64[:], rhs=gl[:, 0:2, :].rearrange("p a c -> p (a c)"), start=True, stop=True), reads=[r_ones, r_gl], writes=[r_pgx])
                stop_here("g4")
                P.op("act", lambda e: e.activation(out=ebt[:].rearrange("p a c -> p (a c)"), in_=pgx[:64, 0:130], func=AF.Exp), reads=[r_pgx], writes=[r_ebt])
                P.op("act", lambda e: e.activation(out=egt[:].rearrange("p a c -> p (a c)"), in_=pgx[:, 130:260], func=AF.Exp), reads=[r_pgx], writes=[r_egt])
                P.op("dve", lambda e: e.tensor_tensor(out=rft[:].rearrange("p a c -> p (a c)"), in0=gl[:, 2:4, :].rearrange("p a c -> p (a c)"), in1=pgx[:64, 0:130], op=ALU.subtract), reads=[r_pgx, r_gl], writes=[r_rft])
                P.op("act", lambda e: e.activation(out=rft[:].rearrange("p a c -> p (a c)"), in_=rft[:].rearrange("p a c -> p (a c)"), func=AF.Exp), reads=[r_rft], writes=[r_rft])

                stop_here("a3g")
                s3g.close()
                P.fence()
                Zt = [sb(f"Zt{d}", [128, 2, 257], F32, s3) for d in range(2)]; r_Z = [Res(), Res()]
                Zbt = [sb(f"Zbt{d}", [128, 2, 257], BF16, s3) for d in range(2)]; r_Zb = [Res(), Res()]
                ztt = [sb(f"ztt{d}", [128, 2, 257], F32, s3) for d in range(2)]; r_ztmp = [Res(), Res()]
                nebt = sb("nebt", [64, 2, 65], F32, s3); r_nebt = Res()
                P.op("dve", lambda e: e.tensor_scalar(out=nebt[:].rearrange("p a c -> p (a c)"), in0=ebt[:].rearrange("p a c -> p (a c)"), scalar1=-1.0, scalar2=None, op0=ALU.mult), reads=[r_ebt], writes=[r_nebt])
                vres = sb("vres", [64, 65, 257], BF16, s3); r_vres = Res()
                for c5 in range(5):
                    P.dma("act", lambda e, c5=c5: e.dma_start(out=vres[:, 13 * c5:13 * c5 + 13, :], in_=mv_s.rearrange("(c t) v -> t c v", t=64)[:, 13 * c5:13 * c5 + 13, :]), reads=[r_mv_s], writes=[r_vres])
                vtr = Ring([sb(f"vt{i}", [64, 257], BF16, s3) for i in range(4)])
                ptmr = Ring([sb(f"ptm{i}", [64, 64], BF16, s3) for i in range(4)])
                ktokr = Ring([sb(f"ktok{i}", [64, 256], BF16, s3) for i in range(4)])
                dnr = Ring([sb(f"dn{i}", [64, 4], F32, s3) for i in range(4)])
                hring = Ring([sb(f"hch{i}", [64, 256], F32, s3) for i in range(4)])
                with ExitStack() as s3s:
                    p_sr = Ring([ps(f"p_s{i}", [128, 512], F32, s3s) for i in range(2)])
                    p_kr = Ring([ps(f"p_k{i}", [128, 1024], BF16, s3s) for i in range(2)])
                    p_or = Ring([ps(f"p_o{i}", [128, 512], F32, s3s) for i in range(2)])
                    p_cc = ps("p_cc", [128, 1024], F32, s3s); r_p_c = Res()

                    def phase_I(d, c):
                        c0 = 64 * c
                        mask, r_mask = (mku, r_mku) if d == 0 else (mkl, r_mkl)
                        p_s, r_p_s = p_sr.next()
                        p_k, r_p_k = p_kr.next()
                        ptm, r_ptm = ptmr.next()
                        ktok, r_ktok = ktokr.next()
                        vt, r_vt = vtr.next()
                        P.op("pe", [(lambda e, dc=dc, c0=c0, p_s=p_s: e.matmul(out=p_s[:64, 0:64], lhsT=mqkT[:, 2 + dc, c0:c0 + 64], rhs=mqkT[:, dc, c0:c0 + 64], start=(dc == 0), stop=(dc == 1))) for dc in range(2)], reads=[r_mqkT], writes=[r_p_s])
                        P.op("dve", lambda e, mask=mask, p_s=p_s, ptm=ptm: e.tensor_tensor(out=ptm[:], in0=p_s[:64, 0:64], in1=mask[:], op=ALU.mult), reads=[r_p_s, r_mask], writes=[r_ptm])
                        P.op("pe", [(lambda e, dc=dc, c0=c0, p_k=p_k: e.transpose(out=p_k[:64, 128 * dc:128 * dc + 128], in_=mqkT[:, 2 + dc, c0:c0 + 64], identity=idb[:])) for dc in range(2)], reads=[r_mqkT, r_idb], writes=[r_p_k])
                        P.op("act", lambda e, ktok=ktok, p_k=p_k: e.copy(out=ktok[:], in_=p_k[:64, 0:256]), reads=[r_p_k], writes=[r_ktok])
                        P.op("dve", lambda e, vt=vt, d=d, c=c: e.tensor_scalar(out=vt[:], in0=vres[:, c, :], scalar1=rft[:, d, c:c + 1], scalar2=None, op0=ALU.mult), reads=[r_vres, r_rft], writes=[r_vt])
                        return dict(c=c, c0=c0, d=d, ptm=ptm, r_ptm=r_ptm, ktok=ktok, r_ktok=r_ktok, vt=vt, r_vt=r_vt)

                    def phase_D(x):
                        d, c, c0 = x["d"], x["c"], x["c0"]
                        ptm, r_ptm, ktok, r_ktok, vt, r_vt = x["ptm"], x["r_ptm"], x["ktok"], x["r_ktok"], x["vt"], x["r_vt"]
                        p_o, r_p_o = p_or.next()
                        for dc in range(2):
                            P.op("pe", lambda e, dc=dc, ktok=ktok, vt=vt: e.matmul(out=p_cc[:, 512 * dc:512 * dc + 257], lhsT=ktok[:, 128 * dc:128 * dc + 128], rhs=vt[:], start=True, stop=True), reads=[r_ktok, r_vt], writes=[r_p_c])
                        P.op("pe", [lambda e, ptm=ptm, vt=vt, p_o=p_o: e.matmul(out=p_o[:64, 0:257], lhsT=ptm[:], rhs=vt[:], start=True, stop=False)] +
                             [(lambda e, dc=dc, c0=c0, p_o=p_o, d=d: e.matmul(out=p_o[:64, 0:257], lhsT=mqkT[:, dc, c0:c0 + 64], rhs=Zbt[d][:, dc, :], start=False, stop=(dc == 1))) for dc in range(2)],
                             reads=[r_ptm, r_vt, r_mqkT, r_Zb[d]], writes=[r_p_o])
                        P.op("dve", lambda e, d=d: e.scalar_tensor_tensor(out=ztt[d][:], in0=p_cc[:].rearrange("p (a n) -> p a n", a=2)[:, :, 0:257], scalar=1.0 / 16, in1=Zt[d][:], op0=ALU.mult, op1=ALU.add), reads=[r_p_c, r_Z[d]], writes=[r_ztmp[d]])
                        P.op("act", lambda e, d=d, c=c: e.activation(out=Zbt[d][:], in_=ztt[d][:], func=AF.Identity, scale=egt[:, d, c:c + 1]), reads=[r_ztmp[d], r_egt], writes=[r_Zb[d]])
                        P.op("act", lambda e, d=d, c=c: e.activation(out=Zt[d][:], in_=ztt[d][:], func=AF.Identity, scale=egt[:, d, c:c + 1]), reads=[r_ztmp[d], r_egt], writes=[r_Z[d]])
                        hch, r_hch = hring.next()
                        dn, r_dn = dnr.next()
                        P.op("dve", lambda e, d=d, c=c, dn=dn, p_o=p_o: e.tensor_scalar(out=dn[:, 0:1], in0=p_o[:64, 256:257], scalar1=ebt[:, d, c:c + 1], scalar2=1.0, op0=ALU.mult, op1=ALU.max), reads=[r_p_o, r_ebt], writes=[r_dn])
                        P.op("dve", lambda e, d=d, c=c, dn=dn, p_o=p_o: e.scalar_tensor_tensor(out=dn[:, 1:2], in0=p_o[:64, 256:257], scalar=nebt[:, d, c:c + 1], in1=dn[:, 0:1], op0=ALU.mult, op1=ALU.max), reads=[r_p_o, r_nebt, r_dn], writes=[r_dn])
                        P.op("dve", lambda e, dn=dn: e.reciprocal(out=dn[:, 2:3], in_=dn[:, 1:2]), reads=[r_dn], writes=[r_dn])
                        P.op("dve", lambda e, hch=hch, dn=dn, p_o=p_o, d=d, c=c: e.tensor_scalar(out=hch[:], in0=p_o[:64, 0:256], scalar1=dn[:, 2:3], scalar2=ebt[:, d, c:c + 1], op0=ALU.mult, op1=ALU.mult), reads=[r_p_o, r_dn, r_ebt], writes=[r_hch])
                        dst, r_dst = (hf_s, r_hf_s) if d == 0 else (hb_s, r_hb_s)
                        P.dma("sp", lambda e, hch=hch, c0=c0, dst=dst: e.dma_start(out=dst[c0:c0 + 64, :], in_=hch[:]), reads=[r_hch], writes=[r_dst])

                    if "A3" in STAGES:
                        for d in range(2):
                            P.op("dve", lambda e, d=d: e.memset(Zt[d][:], 0.0), writes=[r_Z[d]])
                            P.op("dve", lambda e, d=d: e.memset(Zbt[d][:], 0.0), writes=[r_Zb[d]])
                        nxt = [phase_I(0, 0), phase_I(1, 64)]
                        for i in range(65):
                            cur = nxt
                            if i + 1 < 65:
                                nxt = [phase_I(0, i + 1), phase_I(1, 63 - i)]
                            phase_D(cur[0])
                            phase_D(cur[1])
                P.fence()
                with ExitStack() as s3e:
                    NR = 4
                    hfr = Ring([sb(f"ehf{i}", [128, 256], F32, s3e) for i in range(NR)])
                    hbr = Ring([sb(f"ehb{i}", [128, 256], F32, s3e) for i in range(NR)])
                    mor = Ring([sb(f"emo{i}", [128, 256], F32, s3e) for i in range(NR)])
                    hsr = Ring([sb(f"ehs{i}", [128, 256], F32, s3e) for i in range(NR)])
                    bsr = Ring([sb(f"ebs{i}", [128, 8], F32, s3e) for i in range(NR)])
                    aor = Ring([sb(f"eao{i}", [128, 256], BF16, s3e) for i in range(NR)])
                    aTr = Ring([sb(f"eaT{i}", [128, 2, 128], BF16, s3e) for i in range(NR)])
                    p_tr = Ring([ps(f"ep_t{i}", [128, 1024], BF16, s3e) for i in range(2)])
                    mng128 = sb("mng128", [128, 256], F32, s3e); r_mng128 = Res()
                    P.dma("act", lambda e: e.dma_start(out=mng128[0:64, :], in_=mng[:, :]), writes=[r_mng128])
                    P.dma("act", lambda e: e.dma_start(out=mng128[64:128, :], in_=mng[:, :]), writes=[r_mng128])
                    eblocks = [(128 * j, 128) for j in range(32)] + [(4096, 64)]

                    def ep1(j):
                        r0, n = eblocks[j]
                        hf, r_hf = hfr.next(); hb, r_hb = hbr.next(); mo, r_mo = mor.next()
                        hs, r_hs = hsr.next(); bs_, r_bs = bsr.next()
                        P.dma("sp", lambda e, hf=hf, r0=r0, n=n: e.dma_start(out=hf[:n, :], in_=hf_s[r0:r0 + n, :]), reads=[r_hf_s], writes=[r_hf])
                        P.dma("sp", lambda e, hb=hb, r0=r0, n=n: e.dma_start(out=hb[:n, :], in_=hb_s[r0:r0 + n, :]), reads=[r_hb_s], writes=[r_hb])
                        P.dma("sp", lambda e, mo=mo, r0=r0, n=n: e.dma_start(out=mo[:n, :], in_=mo_s[r0:r0 + n, :]), reads=[r_mo_s], writes=[r_mo])
                        P.op("pool", lambda e, hf=hf, hb=hb, hs=hs, n=n: e.tensor_tensor(out=hs[:n, :], in0=hf[:n, :], in1=hb[:n, :], op=ALU.add), reads=[r_hf, r_hb], writes=[r_hs])
                        P.op("dve", lambda e, hs=hs, bs_=bs_, n=n: e.bn_stats(out=bs_[:n, 0:6], in_=hs[:n, :]), reads=[r_hs], writes=[r_bs])
                        P.op("dve", lambda e, bs_=bs_, n=n: e.bn_aggr(out=bs_[:n, 6:8], in_=bs_[:n, 0:6]), reads=[r_bs], writes=[r_bs])
                        P.op("dve", lambda e, bs_=bs_, n=n: e.tensor_scalar(out=bs_[:n, 7:8], in0=bs_[:n, 7:8], scalar1=EPS, scalar2=None, op0=ALU.add), reads=[r_bs], writes=[r_bs])
                        return dict(r0=r0, n=n, hs=hs, r_hs=r_hs, bs=bs_, r_bs=r_bs, mo=mo, r_mo=r_mo)

                    def ep2(x):
                        n, bs_, r_bs, hs, r_hs = x["n"], x["bs"], x["r_bs"], x["hs"], x["r_hs"]
                        P.op("act", lambda e, bs_=bs_, n=n: e.activation(out=bs_[:n, 7:8], in_=bs_[:n, 7:8], func=AF.Sqrt), reads=[r_bs], writes=[r_bs])
                        P.op("dve", lambda e, bs_=bs_, n=n: e.reciprocal(out=bs_[:n, 7:8], in_=bs_[:n, 7:8]), reads=[r_bs], writes=[r_bs])
                        P.op("dve", lambda e, bs_=bs_, hs=hs, n=n: e.tensor_scalar(out=hs[:n, :], in0=hs[:n, :], scalar1=bs_[:n, 6:7], scalar2=bs_[:n, 7:8], op0=ALU.subtract, op1=ALU.mult), reads=[r_hs, r_bs], writes=[r_hs])

                    def ep3(x):
                        r0, n, hs, r_hs, mo, r_mo = x["r0"], x["n"], x["hs"], x["r_hs"], x["mo"], x["r_mo"]
                        ao, r_ao = aor.next(); aT, r_aT = aTr.next(); p_t, r_p_t = p_tr.next()
                        P.op("pool", lambda e, hs=hs, n=n: e.tensor_tensor(out=hs[:n, :], in0=hs[:n, :], in1=mng128[:n, :], op=ALU.mult), reads=[r_hs, r_mng128], writes=[r_hs])
                        P.op("pool", lambda e, hs=hs, mo=mo, ao=ao, n=n: e.tensor_tensor(out=ao[:n, :], in0=hs[:n, :], in1=mo[:n, :], op=ALU.mult), reads=[r_hs, r_mo], writes=[r_ao])
                        P.op("pe", [(lambda e, dc=dc, ao=ao, p_t=p_t, n=n: e.transpose(out=p_t[:, 128 * dc:128 * dc + n], in_=ao[:n, 128 * dc:128 * dc + 128], identity=idb[:n, :n])) for dc in range(2)], reads=[r_ao, r_idb], writes=[r_p_t])
                        P.op("act", lambda e, aT=aT, p_t=p_t, n=n: e.copy(out=aT[:, :, :n], in_=p_t[:, 0:256].rearrange("p (a t) -> p a t", a=2)[:, :, :n]), reads=[r_p_t], writes=[r_aT])
                        if r0 == 0:
                            P.dma("sp", lambda e, aT=aT: e.dma_start(out=cc_in[0:256, 0:80].rearrange("(a p) t -> p a t", p=128), in_=aT[:, :, 48:128]), reads=[r_aT], writes=[r_cc_in])
                        else:
                            pos0 = r0 - 48
                            P.dma("sp", lambda e, aT=aT, pos0=pos0, n=n: e.dma_start(out=cc_in[0:256, pos0:pos0 + n].rearrange("(a p) t -> p a t", p=128), in_=aT[:, :, :n]), reads=[r_aT], writes=[r_cc_in])

                    if "A3" in STAGES:
                        xs = {}
                        nb = len(eblocks)
                        for i in range(nb + 2):
                            if i < nb:
                                xs[i] = ep1(i)
                            if 0 <= i - 1 < nb:
                                ep2(xs[i - 1])
                            if 0 <= i - 2 < nb:
                                ep3(xs.pop(i - 2))

            P.fence()
            if "X" in STAGES:
                exchange_half(0)
            if DEBUG:
                P.stopped = False
                dbg_holder.append(P.dma("pool", lambda e: e.dma_start(out=dbg_cc[:, :], in_=cc_in[:, :]), reads=[r_cc_in, r_cc_in_b]))
                stop_here(STOP_AT)

            if "B" in STAGES:
                TS = [(342 * i, 342) for i in range(3)]
                with ExitStack() as sB:
                    hT = sb("hT", [128, 16, NW], F32, sB); r_hT = Res()
                    uT2 = sb("uT2", [128, 16, NW], BF16, sB); r_uT2 = Res()
                    g2t = sb("g2t", [128, 16], F32, sB); r_g2t = Res()
                    P.dma("act", lambda e: e.dma_start(out=g2t[:], in_=g2c[:, :]), writes=[r_g2t])
                    onesf = sb("onesf", [128, 128], F32, sB); r_onesf = Res()
                    P.op("dve", lambda e: e.memset(onesf[:], 1.0), writes=[r_onesf])
                    with ExitStack() as sB1:
                        abT = sb("abT", [128, 16, NW], BF16, sB1); r_abT = Res()
                        mT = sb("mT", [128, 16, NW], BF16, sB1); r_mT = Res()
                        rank_cache = {}
                        for k in range(16):
                            def load_ab(e, k=k):
                                if "r" not in rank_cache:
                                    rank_cache["r"] = e.partition_id() % 4
                                rank = rank_cache["r"]
                                return e.dma_start(out=abT[:, k:k + 1, :], in_=cc_out.rearrange("(j r) t -> r j t", j=4)[k * 128:(k + 1) * 128, bass.ds(rank, 1), :])
                            P.dma("pool", load_ab, reads=[r_cc_out[0 if k < 8 else 1]], writes=[r_abT])
                        with ExitStack() as sB0:
                            g1t_b = sb("g1t2", [128, D], F32, sB0); r_g1t_b = Res()
                            P.dma("act", lambda e: e.dma_start(out=g1t_b[:], in_=g1b[:, :]), writes=[r_g1t_b])
                            xring_b = Ring([sb(f"xw{i}", [128, D], F32, sB0) for i in range(2)])
                            junk_b = sb("junk2", [128, D], BF16, sB0); r_junk_b = Res()
                            ss_b = sb("ss2", [128, 1], F32, sB0); r_ss_b = Res()
                            u_b = sb("u2", [128, D], BF16, sB0); r_u_b = Res()
                            pT_b = ps("pT2", [128, D], BF16, sB0); r_pT_b = Res()
                            pX = [ps(f"pX{i}", [128, 1024], F32, sB0) for i in range(2)]; r_pX = [Res(), Res()]
                            for j in range(9):
                                bs = 128 if j < 8 else 2
                                xt_b, r_xt_b = xring_b.next()
                                P.dma("sp", lambda e, xt_b=xt_b, bs=bs, j=j: e.dma_start(out=xt_b[:bs, :], in_=xwin[128 * j:128 * j + bs, :]), writes=[r_xt_b])
                                P.op("act", lambda e, xt_b=xt_b, bs=bs: e.activation(out=junk_b[:bs, :], in_=xt_b[:bs, :], func=AF.Square, accum_out=ss_b[:bs, :]), reads=[r_xt_b], writes=[r_junk_b, r_ss_b])
                                P.op("dve", lambda e, bs=bs: e.tensor_scalar(out=ss_b[:bs, :], in0=ss_b[:bs, :], scalar1=1.0 / D, scalar2=EPS, op0=ALU.mult, op1=ALU.add), reads=[r_ss_b], writes=[r_ss_b])
                                P.op("act", lambda e, bs=bs: e.activation(out=ss_b[:bs, :], in_=ss_b[:bs, :], func=AF.Sqrt), reads=[r_ss_b], writes=[r_ss_b])
                                P.op("dve", lambda e, bs=bs: e.reciprocal(out=ss_b[:bs, :], in_=ss_b[:bs, :]), reads=[r_ss_b], writes=[r_ss_b])
                                P.op("dve", lambda e, xt_b=xt_b, bs=bs: e.scalar_tensor_tensor(out=u_b[:bs, :], in0=xt_b[:bs, :], scalar=ss_b[:bs, 0:1], in1=g1t_b[:bs, :], op0=ALU.mult, op1=ALU.mult), reads=[r_xt_b, r_ss_b, r_g1t_b], writes=[r_u_b])
                                P.op("pe", [(lambda e, k=k, bs=bs: e.transpose(out=pT_b[:, k * 128:k * 128 + bs], in_=u_b[:bs, k * 128:(k + 1) * 128], identity=idb[:bs, :bs])) for k in range(16)], reads=[r_u_b, r_idb], writes=[r_pT_b])
                                P.op("act", lambda e, j=j, bs=bs: e.copy(out=uT2[:, :, 128 * j:128 * j + bs], in_=pT_b[:].rearrange("p (k n) -> p k n", k=16)[:, :, :bs]), reads=[r_pT_b], writes=[r_uT2])
                                for hh in range(2):
                                    P.op("pe", [(lambda e, k=k, hh=hh, xt_b=xt_b, bs=bs: e.transpose(out=pX[hh][:, (k % 8) * 128:(k % 8) * 128 + bs], in_=xt_b[:bs, k * 128:(k + 1) * 128], identity=idf[:bs, :bs])) for k in range(8 * hh, 8 * hh + 8)], reads=[r_xt_b, r_idf], writes=[r_pX[hh]])
                                    P.op("dve", lambda e, hh=hh, j=j, bs=bs: e.tensor_copy(out=hT[:, 8 * hh:8 * hh + 8, 128 * j:128 * j + bs], in_=pX[hh][:].rearrange("p (k n) -> p k n", k=8)[:, :, :bs]), reads=[r_pX[hh]], writes=[r_hT])

                        P.fence()
                        with ExitStack() as sB1b:
                            wgr = Ring([sb(f"wg{i}", [128, 16, 256], BF16, sB1b) for i in range(2)])
                            war = Ring([sb(f"wa{i}", [128, 8, 256], BF16, sB1b) for i in range(2)])
                            sgm = sb("sgm", [128, 342], F32, sB1b); r_sgm = Res()
                            sgd = sb("sgd", [128, 342], F32, sB1b); r_sgd = Res()
                            tA = sb("tA", [128, 342], F32, sB1b); r_tA = Res()
                            tB = sb("tB", [128, 342], F32, sB1b); r_tB = Res()
                            pq = [Ring([ps(f"pq{q}{i}", [128, 512], F32, sB1b) for i in range(2)]) for q in range(4)]
                            for c in range(16):
                                wg, r_wg = wgr.next()
                                P.dma("pool", lambda e, wg=wg, c=c: e.dma_start(out=wg[:, :, 0:128], in_=w_g[:, 128 * c:128 * c + 128].rearrange("(k p) n -> p k n", p=128)), writes=[r_wg])
                                for (t0, tn) in TS:
                                    p0_, r0_ = pq[0].next()
                                    P.op("pe", [(lambda e, k=k, p0_=p0_, wg=wg, t0=t0, tn=tn: e.matmul(out=p0_[:, :tn], lhsT=wg[:, k, 0:128], rhs=uT2[:, k, t0:t0 + tn], start=(k == 0), stop=(k == 15))) for k in range(16)], reads=[r_wg, r_uT2], writes=[r0_])
                                    P.op("act", lambda e, p0_=p0_, c=c, t0=t0, tn=tn: e.activation(out=mT[:, c, t0:t0 + tn], in_=p0_[:, :tn], func=AF.Sigmoid), reads=[r0_], writes=[r_mT])
                            for c in range(16):
                                wg, r_wg = wgr.next()
                                wa, r_wa = war.next()
                                P.dma("pool", lambda e, wg=wg, c=c: e.dma_start(out=wg[:, :, 128:256], in_=w_g[:, 2048 + 128 * c:2048 + 128 * c + 128].rearrange("(k p) n -> p k n", p=128)), writes=[r_wg])
                                P.dma("pool", lambda e, wa=wa, c=c: e.dma_start(out=wa[:, :, 0:128], in_=w_a[:, 128 * c:128 * c + 128].rearrange("(k p) n -> p k n", p=128)), writes=[r_wa])
                                P.dma("pool", lambda e, wa=wa, c=c: e.dma_start(out=wa[:, :, 128:256], in_=w_b[:, 128 * c:128 * c + 128].rearrange("(k p) n -> p k n", p=128)), writes=[r_wa])
                                for (t0, tn) in TS:
                                    p1_, r1_ = pq[1].next(); p2_, r2_ = pq[2].next(); p3_, r3_ = pq[3].next()
                                    P.op("pe", [(lambda e, k=k, p2_=p2_, wg=wg, t0=t0, tn=tn: e.matmul(out=p2_[:, :tn], lhsT=wg[:, k, 128:256], rhs=uT2[:, k, t0:t0 + tn], start=(k == 0), stop=(k == 15))) for k in range(16)], reads=[r_wg, r_uT2], writes=[r2_])
                                    P.op("pe", [(lambda e, k=k, p1_=p1_, wa=wa, t0=t0, tn=tn: e.matmul(out=p1_[:, :tn], lhsT=wa[:, k, 0:128], rhs=abT[:, k, t0:t0 + tn], start=(k == 0), stop=(k == 7))) for k in range(8)], reads=[r_wa, r_abT], writes=[r1_])
                                    P.op("pe", [(lambda e, k=k, p3_=p3_, wa=wa, t0=t0, tn=tn: e.matmul(out=p3_[:, :tn], lhsT=wa[:, k, 128:256], rhs=abT[:, 8 + k, t0:t0 + tn], start=(k == 0), stop=(k == 7))) for k in range(8)], reads=[r_wa, r_abT], writes=[r3_])
                                    P.op("act", lambda e, p2_=p2_, tn=tn: e.activation(out=sgd[:, :tn], in_=p2_[:, :tn], func=AF.Sigmoid), reads=[r2_], writes=[r_sgd])
                                    P.op("dve", lambda e, p1_=p1_, c=c, t0=t0, tn=tn: e.tensor_tensor(out=tA[:, :tn], in0=p1_[:, :tn], in1=mT[:, c, t0:t0 + tn], op=ALU.mult), reads=[r1_, r_mT], writes=[r_tA])
                                    P.op("dve", lambda e, p3_=p3_, tn=tn: e.tensor_tensor(out=tB[:, :tn], in0=p3_[:, :tn], in1=sgd[:, :tn], op=ALU.mult), reads=[r3_, r_sgd], writes=[r_tB])
                                    P.op("dve", lambda e, c=c, t0=t0, tn=tn: e.tensor_tensor(out=mT[:, c, t0:t0 + tn], in0=tA[:, :tn], in1=tB[:, :tn], op=ALU.add), reads=[r_tA, r_tB], writes=[r_mT])
                        P.fence()
                        with ExitStack() as sB2:
                            wor = Ring([sb(f"wo{i}", [128, 16, 128], BF16, sB2) for i in range(2)])
                            po = Ring([ps(f"po{i}", [128, 512], F32, sB2) for i in range(4)])
                            for c in range(16):
                                wo, r_wo = wor.next()
                                P.dma("pool", lambda e, wo=wo, c=c: e.dma_start(out=wo[:], in_=w_out[:, 128 * c:128 * c + 128].rearrange("(k p) n -> p k n", p=128)), writes=[r_wo])
                                for (t0, tn) in TS:
                                    pp, rp = po.next()
                                    P.op("pe", [(lambda e, k=k, pp=pp, wo=wo, t0=t0, tn=tn: e.matmul(out=pp[:, :tn], lhsT=wo[:, k, :], rhs=mT[:, k, t0:t0 + tn], start=(k == 0), stop=(k == 15))) for k in range(16)], reads=[r_wo, r_mT], writes=[rp])
                                    P.op("dve", lambda e, pp=pp, c=c, t0=t0, tn=tn: e.tensor_tensor(out=hT[:, c, t0:t0 + tn], in0=hT[:, c, t0:t0 + tn], in1=pp[:, :tn], op=ALU.add), reads=[rp, r_hT], writes=[r_hT])

                    P.fence()
                    with ExitStack() as sB3:
                        sq = Ring([sb(f"sq{i}", [128, 342], F32, sB3) for i in range(2)])
                        rstd = sb("rstd", [128, NW], F32, sB3); r_rstd = Res()
                        wm = sb("wm", [128, NW], F32, sB3); r_wm = Res()
                        P.dma("act", lambda e: e.dma_start(out=wm[:], in_=wmask_d[:, :]), writes=[r_wm])
                        pss3 = [ps(f"pss3{i}", [128, 512], F32, sB3) for i in range(3)]; r_pss3 = [Res() for _ in range(3)]
                        for ti, (t0, tn) in enumerate(TS):
                            fns = []
                            for c in range(16):
                                s_, r_s = sq.next()
                                P.op("act", lambda e, s_=s_, c=c, t0=t0, tn=tn: e.activation(out=s_[:, :tn], in_=hT[:, c, t0:t0 + tn], func=AF.Square), reads=[r_hT], writes=[r_s])
                                P.op("pe", lambda e, s_=s_, c=c, ti=ti, tn=tn: e.matmul(out=pss3[ti][:, :tn], lhsT=onesf[:], rhs=s_[:, :tn], start=(c == 0), stop=(c == 15), skip_group_check=True), reads=[r_s, r_onesf], writes=[r_pss3[ti]])
                            P.op("dve", lambda e, ti=ti, t0=t0, tn=tn: e.tensor_scalar(out=rstd[:, t0:t0 + tn], in0=pss3[ti][:, :tn], scalar1=1.0 / D, scalar2=EPS, op0=ALU.mult, op1=ALU.add), reads=[r_pss3[ti]], writes=[r_rstd])
                        P.op("act", lambda e: e.activation(out=rstd[:], in_=rstd[:], func=AF.Sqrt), reads=[r_rstd], writes=[r_rstd])
                        P.op("dve", lambda e: e.reciprocal(out=rstd[:], in_=rstd[:]), reads=[r_rstd], writes=[r_rstd])
                        P.op("dve", lambda e: e.tensor_tensor(out=rstd[:], in0=rstd[:], in1=wm[:], op=ALU.mult), reads=[r_rstd, r_wm], writes=[r_rstd])
                        for c in range(16):
                            P.op("dve", lambda e, c=c: e.scalar_tensor_tensor(out=uT2[:, c, :], in0=hT[:, c, :], scalar=g2t[:, c:c + 1], in1=rstd[:], op0=ALU.mult, op1=ALU.mult), reads=[r_hT, r_g2t, r_rstd], writes=[r_uT2])

                    P.fence()
                    with ExitStack() as sB4:
                        fcw = sb("fcw", [128, 264], F32, sB4); r_fcw = Res()
                        P.dma("act", lambda e: e.dma_start(out=fcw[:], in_=fconv[:, :]), writes=[r_fcw])
                        actT = sb("actT", [128, 22, 1024], BF16, sB4); r_actT = Res()
                        wur = Ring([sb(f"wu{i}", [128, 16, 256], BF16, sB4) for i in range(2)])
                        wdr = Ring([sb(f"wd{i}", [128, 22, 128], BF16, sB4) for i in range(2)])
                        upg = sb("upg", [128, NW], F32, sB4); r_upg = Res()
                        upv = sb("upv", [128, NW], F32, sB4); r_upv = Res()
                        cg = sb("cg", [128, 1024], F32, sB4); r_cg = Res()
                        cv = sb("cv", [128, 1024], F32, sB4); r_cv = Res()
                        sgl = sb("sgl", [128, 1024], F32, sB4); r_sgl = Res()
                        pu = Ring([ps(f"pu{i}", [128, 512], F32, sB4) for i in range(4)])
                        pd = Ring([ps(f"pd{i}", [128, 512], F32, sB4) for i in range(4)])
                        for half in range(2):
                            for fc in range(22):
                                f = half * 22 + fc
                                wu, r_wu = wur.next()
                                P.dma("pool", lambda e, wu=wu, f=f: e.dma_start(out=wu[:, :, 0:128], in_=w_up[:, 128 * f:128 * f + 128].rearrange("(k p) n -> p k n", p=128)), writes=[r_wu])
                                P.dma("pool", lambda e, wu=wu, f=f: e.dma_start(out=wu[:, :, 128:256], in_=w_up[:, FFN + 128 * f:FFN + 128 * f + 128].rearrange("(k p) n -> p k n", p=128)), writes=[r_wu])
                                for (t0, tn) in TS:
                                    pg_, rg_ = pu.next()
                                    P.op("pe", [(lambda e, k=k, pg_=pg_, wu=wu, t0=t0, tn=tn: e.matmul(out=pg_[:, :tn], lhsT=wu[:, k, 0:128], rhs=uT2[:, k, t0:t0 + tn], start=(k == 0), stop=(k == 15))) for k in range(16)], reads=[r_wu, r_uT2], writes=[rg_])
                                    P.op("act", lambda e, pg_=pg_, t0=t0, tn=tn: e.copy(out=upg[:, t0:t0 + tn], in_=pg_[:, :tn]), reads=[rg_], writes=[r_upg])
                                    pv_, rv_ = pu.next()
                                    P.op("pe", [(lambda e, k=k, pv_=pv_, wu=wu, t0=t0, tn=tn: e.matmul(out=pv_[:, :tn], lhsT=wu[:, k, 128:256], rhs=uT2[:, k, t0:t0 + tn], start=(k == 0), stop=(k == 15))) for k in range(16)], reads=[r_wu, r_uT2], writes=[rv_])
                                    P.op("act", lambda e, pv_=pv_, t0=t0, tn=tn: e.copy(out=upv[:, t0:t0 + tn], in_=pv_[:, :tn]), reads=[rv_], writes=[r_upv])
                                for (src, r_src, dst, r_dst, ci) in ((upg, r_upg, cg, r_cg, f), (upv, r_upv, cv, r_cv, 44 + f)):
                                    P.op("dve", lambda e, src=src, dst=dst, ci=ci: e.tensor_scalar(out=dst[:], in0=src[:, 0:1024], scalar1=fcw[:, 3 * ci:3 * ci + 1], scalar2=None, op0=ALU.mult), reads=[r_src, r_fcw], writes=[r_dst])
                                    P.op("dve", lambda e, src=src, dst=dst, ci=ci: e.scalar_tensor_tensor(out=dst[:], in0=src[:, 1:1025], scalar=fcw[:, 3 * ci + 1:3 * ci + 2], in1=dst[:], op0=ALU.mult, op1=ALU.add), reads=[r_src, r_fcw, r_dst], writes=[r_dst])
                                    P.op("dve", lambda e, src=src, dst=dst, ci=ci: e.scalar_tensor_tensor(out=dst[:], in0=src[:, 2:1026], scalar=fcw[:, 3 * ci + 2:3 * ci + 3], in1=dst[:], op0=ALU.mult, op1=ALU.add), reads=[r_src, r_fcw, r_dst], writes=[r_dst])
                                P.op("act", lambda e: e.activation(out=sgl[:], in_=cg[:], func=AF.Silu), reads=[r_cg], writes=[r_sgl])
                                P.op("dve", lambda e, fc=fc: e.tensor_tensor(out=actT[:, fc, :], in0=sgl[:], in1=cv[:], op=ALU.mult), reads=[r_sgl, r_cv], writes=[r_actT])
                            for c in range(16):
                                wd, r_wd = wdr.next()
                                P.dma("pool", lambda e, wd=wd, c=c, half=half: e.dma_start(out=wd[:], in_=w_down[2816 * half:2816 * half + 2816, 128 * c:128 * c + 128].rearrange("(k p) n -> p k n", p=128)), writes=[r_wd])
                                for t2_ in range(2):
                                    pp, rp = pd.next()
                                    P.op("pe", [(lambda e, k=k, pp=pp, wd=wd, t2_=t2_: e.matmul(out=pp[:, :], lhsT=wd[:, k, :], rhs=actT[:, k, 512 * t2_:512 * t2_ + 512], start=(k == 0), stop=(k == 21))) for k in range(22)], reads=[r_wd, r_actT], writes=[rp])
                                    P.op("dve", lambda e, pp=pp, c=c, t2_=t2_: e.tensor_tensor(out=hT[:, c, 1 + 512 * t2_:1 + 512 * t2_ + 512], in0=hT[:, c, 1 + 512 * t2_:1 + 512 * t2_ + 512], in1=pp[:, :], op=ALU.add), reads=[rp, r_hT], writes=[r_hT])

                    P.fence()
                    with ExitStack() as sB5:
                        gft = sb("gft", [128, D], F32, sB5); r_gft = Res()
                        P.dma("act", lambda e: e.dma_start(out=gft[:], in_=gfb[:, :]), writes=[r_gft])
                        pF = [ps(f"pF{i}", [128, 1024], F32, sB5) for i in range(2)]; r_pF = [Res(), Res()]
                        oring = Ring([sb(f"ob{i}", [128, D], F32, sB5) for i in range(2)])
                        fj = sb("fj", [128, 1024], F32, sB5); r_fj = Res()
                        fs = sb("fs", [128, 4], F32, sB5); r_fs = Res()
                        for j in range(8):
                            for hh in range(2):
                                P.op("pe", [(lambda e, k=k, hh=hh, j=j: e.transpose(out=pF[hh][:, (k % 8) * 128:(k % 8) * 128 + 128], in_=hT[:, k, 1 + 128 * j:1 + 128 * j + 128], identity=idf[:])) for k in range(8 * hh, 8 * hh + 8)], reads=[r_hT, r_idf], writes=[r_pF[hh]])
                                P.op("act", lambda e, hh=hh: e.activation(out=fj[:], in_=pF[hh][:], func=AF.Square, accum_out=fs[:, hh:hh + 1]), reads=[r_pF[hh]], writes=[r_fj, r_fs])
                            P.op("dve", lambda e: e.tensor_tensor(out=fs[:, 2:3], in0=fs[:, 0:1], in1=fs[:, 1:2], op=ALU.add), reads=[r_fs], writes=[r_fs])
                            P.op("dve", lambda e: e.tensor_scalar(out=fs[:, 2:3], in0=fs[:, 2:3], scalar1=1.0 / D, scalar2=EPS, op0=ALU.mult, op1=ALU.add), reads=[r_fs], writes=[r_fs])
                            P.op("act", lambda e: e.activation(out=fs[:, 2:3], in_=fs[:, 2:3], func=AF.Sqrt), reads=[r_fs], writes=[r_fs])
                            P.op("dve", lambda e: e.reciprocal(out=fs[:, 3:4], in_=fs[:, 2:3]), reads=[r_fs], writes=[r_fs])
                            ob, r_ob = oring.next()
                            for hh in range(2):
                                P.op("dve", lambda e, hh=hh, ob=ob: e.scalar_tensor_tensor(out=ob[:, 1024 * hh:1024 * hh + 1024], in0=pF[hh][:], scalar=fs[:, 3:4], in1=gft[:, 1024 * hh:1024 * hh + 1024], op0=ALU.mult, op1=ALU.mult), reads=[r_pF[hh], r_fs, r_gft], writes=[r_ob])
                            final_toks.append(P.dma("sp", lambda e, ob=ob, j=j: e.dma_start(out=out_d[128 * j:128 * j + 128, :], in_=ob[:]), reads=[r_ob]))
        except _Stop:
            pass
        if True:
            if DEBUG and dbg_holder:
                final_toks.append(dbg_holder[0])
            P.finish(final_toks)
    return nc


_NC_CACHE = {}


def _rope_tables():
    inv_freq = (500000.0 ** (-np.arange(0, 16, 2, dtype=np.float32) / 16)).astype(np.float32)
    ang = np.arange(L, dtype=np.float32)[:, None] * inv_freq[None, :]
    cos = np.cos(ang).astype(np.float32).T
    sin = np.sin(ang).astype(np.float32).T
    cosF = np.ones((128, L), np.float32)
    sinF = np.zeros((128, L), np.float32)
    for mp in range(2):
        b0 = 64 * mp
        cosF[b0:b0 + 8] = cos
        cosF[b0 + 8:b0 + 16] = cos
        sinF[b0:b0 + 8] = -sin
        sinF[b0 + 8:b0 + 16] = sin
    return cosF, sinF


def kernel(x, meta_tokens, norm1_g, w_in, mlstm_conv_w, mlstm_gate_bias, mlstm_norm_g,
           lambda_q1, lambda_k1, lambda_q2, lambda_k2, diff_subln_g, w_branch_m, w_branch_d,
           w_out, norm2_g, w_up, ffn_conv_w, w_down, norm_f_g):
    f32 = np.float32
    x = np.asarray(x, f32)
    w_in0 = np.asarray(w_in, f32)[0]
    B = x.shape[0]
    cosF, sinF = _rope_tables()
    o_mqk, o_mv, o_mo, o_gates, o_dq, o_dk, o_dv, o_gm = 0, 2048, 3072, 4096, 4112, 5136, 6160, 7184
    rotperm = np.arange(128)
    for mp in range(2):
        b0 = 64 * mp
        rotperm[b0:b0 + 8] = np.arange(b0 + 8, b0 + 16)
        rotperm[b0 + 8:b0 + 16] = np.arange(b0, b0 + 8)
    ident = np.eye(128, dtype=f32)
    triu = np.triu(np.ones((64, 64), f32))
    tril = np.tril(np.ones((64, 64), f32))
    common = {
        "w_g": np.ascontiguousarray(w_in0[:, o_gm:o_gm + 4096]),
        "w_a": np.ascontiguousarray(np.asarray(w_branch_m, f32)[0]),
        "w_b": np.ascontiguousarray(np.asarray(w_branch_d, f32)[0]),
        "w_out": np.ascontiguousarray(np.asarray(w_out, f32)[0]),
        "w_up": np.ascontiguousarray(np.asarray(w_up, f32)[0]),
        "w_down": np.ascontiguousarray(np.asarray(w_down, f32)[0]),
        "g1b": np.ascontiguousarray(np.broadcast_to(np.asarray(norm1_g, f32)[0], (128, D))),
        "gfb": np.ascontiguousarray(np.broadcast_to(np.asarray(norm_f_g, f32), (128, D))),
        "g2c": np.ascontiguousarray(np.asarray(norm2_g, f32)[0].reshape(16, 128).T),
        "cosf": cosF, "sinf": sinF,
        "fconv": np.ascontiguousarray(np.asarray(ffn_conv_w, f32)[0].reshape(3, 88, 128).transpose(2, 1, 0).reshape(128, 264)),
        "lamv": np.ascontiguousarray(np.broadcast_to(np.concatenate([np.asarray(a, f32)[0] for a in (lambda_q1, lambda_k1, lambda_q2, lambda_k2)]), (128, 256))),
        "sublng": np.ascontiguousarray(np.broadcast_to(np.asarray(diff_subln_g, f32)[0], (128, 128))),
        "ident": ident, "triu": triu, "tril": tril,
    }
    if "B" not in STAGES:
        for nm in ("w_g", "w_a", "w_b", "w_out", "w_up", "w_down"):
            common[nm] = np.zeros((128, 128), f32)
    in_maps = []
    mcw_full = np.asarray(mlstm_conv_w, f32)[0]
    gb_full = np.asarray(mlstm_gate_bias, f32)[0]
    for c in range(8):
        b, g = c // 4, c % 4
        hfull = np.concatenate([np.asarray(meta_tokens, f32), x[b]], axis=0)
        s0 = 15 + 1024 * g
        xwin = np.zeros((NW, D), f32)
        e0 = min(s0 + NW, L)
        xwin[:e0 - s0] = hfull[s0:e0]
        wmask = np.ones((128, NW), f32)
        if e0 - s0 < NW:
            wmask[:, e0 - s0:] = 0.0
        cols = []
        for base in (o_dq, o_dk):
            for hh in range(2):
                head = 2 * g + hh
                cols.append(base + 128 * head + np.arange(128))
            for hh in range(2):
                head = 2 * g + hh
                cols.append(base + 128 * head + rotperm)
        for hh in range(2):
            head = 2 * g + hh
            cols.append(o_dv + 128 * head + np.arange(128))
        w_attn = np.ascontiguousarray(w_in0[:, np.concatenate(cols)])
        qc = o_mqk + 256 * g + np.arange(256)
        kc = o_mqk + 1024 + 256 * g + np.arange(256)
        vc = o_mv + 256 * g + np.arange(256)
        oc = o_mo + 256 * g + np.arange(256)
        gc = o_gates + np.array([4 + g, 12 + g, 0 + g, 8 + g])
        w_ml = np.ascontiguousarray(w_in0[:, np.concatenate([qc, kc, vc, oc, gc])])
        mconv = np.ascontiguousarray(mcw_full[:, np.concatenate([qc, kc])].reshape(3, 4, 128).transpose(2, 1, 0).reshape(128, 12))
        gbias = np.ascontiguousarray(np.broadcast_to(gb_full[[4 + g, 12 + g, 0 + g, 8 + g]], (128, 4)))
        mngv = np.ascontiguousarray(np.broadcast_to(np.asarray(mlstm_norm_g, f32)[0][256 * g:256 * g + 256], (64, 256)))
        m = dict(common)
        m.update({"hfull": hfull, "xwin": xwin, "w_attn": w_attn, "w_ml": w_ml, "mconv": mconv,
                  "gbias": gbias, "mng": mngv, "wmask": wmask})
        in_maps.append(m)
    if "nc" not in _NC_CACHE:
        _NC_CACHE["nc"] = build_program()
    nc = _NC_CACHE["nc"]
    res = run_bass_kernel_spmd(nc, in_maps, core_ids=list(range(8)))
    out = np.empty((B, 4096, D), f32)
    for c in range(8):
        b, g = c // 4, c % 4
        out[b, 1024 * g:1024 * g + 1024] = res.results[c]["out"]
    if DEBUG:
        kernel.dbg = [res.results[c] for c in range(8)]
    return out
```

```python
import numpy as np
from contextlib import ExitStack
import concourse.bass as bass
import concourse.mybir as mybir
from concourse.bass_utils import run_bass_kernel_spmd

F32 = mybir.dt.float32
BF16 = mybir.dt.bfloat16
AF = mybir.ActivationFunctionType
ALU = mybir.AluOpType
AX = mybir.AxisListType

SAME_ENGINE_SYNC = True
DEBUG = False
STAGES = ("A1", "A2", "A3", "X", "B")

D = 2048
L = 4112
LP = 4160
NMETA = 16
NW = 1026
FFN = 5632
EPS = 1e-6


class _Stop(Exception):
    pass


STOP_AT = None


_PROG = []


def stop_here(tag):
    if STOP_AT == tag:
        _PROG[0].stopped = True


_FENCE = []


class Res:
    __slots__ = ("name", "w", "r")

    def __init__(self, name=""):
        self.name = name
        self.w = None
        self.r = dict(_FENCE)


class Prog:
    ENGS = ("pe", "act", "dve", "pool", "sp")

    def __init__(self, nc, stack, n_dma_sems=8):
        self.nc = nc
        self.stack = stack
        self.streams = {e: [] for e in self.ENGS}
        self.sems = {}
        self.count = {}
        self.waited = {e: {} for e in self.ENGS}
        for e in self.ENGS:
            self.sems["c_" + e] = stack.enter_context(nc.semaphore("c_" + e))
            self.count["c_" + e] = 0
        self.dma_pool = {}
        self.dma_next = {}
        for e in ("sp", "act", "pool"):
            keys = []
            for i in range(n_dma_sems):
                k = f"d_{e}{i}"
                self.sems[k] = stack.enter_context(nc.semaphore(k))
                self.count[k] = 0
                keys.append(k)
            self.dma_pool[e] = keys
            self.dma_next[e] = 0
        self.sems["cc"] = stack.enter_context(nc.semaphore("cc"))
        self.count["cc"] = 0
        self.stopped = False
        _PROG[:] = [self]
        _FENCE[:] = []

    def _need(self, eng, dep):
        if dep is None:
            return
        key, val = dep
        if key == "c_" + eng and (eng == "pe" or not SAME_ENGINE_SYNC):
            return
        if self.waited[eng].get(key, 0) >= val:
            return
        self.waited[eng][key] = val
        self.streams[eng].append(("wait", key, val))

    def _deps(self, eng, reads, writes):
        own = "c_" + eng
        for r in reads:
            self._need(eng, r.w)
            for k, v in r.r.items():
                if k != own:
                    self._need(eng, (k, v))
        for w in writes:
            self._need(eng, w.w)
            for k, v in w.r.items():
                self._need(eng, (k, v))

    def _commit(self, reads, writes, tok):
        for r in reads:
            if r.r.get(tok[0], 0) < tok[1]:
                r.r[tok[0]] = tok[1]
        for w in writes:
            w.w = tok
            w.r = {}

    def op(self, eng, fns, reads=(), writes=()):
        if self.stopped:
            return None
        if not isinstance(fns, (list, tuple)):
            fns = [fns]
        self._deps(eng, reads, writes)
        key = "c_" + eng
        self.count[key] += 1
        tok = (key, self.count[key])
        self.streams[eng].append(("op", fns, key, 1))
        self._commit(reads, writes, tok)
        return tok

    def dma(self, eng, fn, reads=(), writes=()):
        if self.stopped:
            return None
        pool = self.dma_pool[eng]
        key = pool[self.dma_next[eng] % len(pool)]
        self.dma_next[eng] += 1
        if self.count[key] > 0:
            self._need(eng, (key, self.count[key]))
        self._deps(eng, reads, writes)
        self.count[key] += 16
        tok = (key, self.count[key])
        self.streams[eng].append(("op", [fn], key, 16))
        self._commit(reads, writes, tok)
        return tok

    def fence(self):
        _FENCE[:] = [(k, c) for k, c in self.count.items() if c > 0 and k != "cc"]

    def cc(self, fn, reads=(), writes=()):
        if self.stopped:
            return None
        eng = "pool"
        self._deps(eng, reads, writes)
        self.count["cc"] += 1
        tok = ("cc", self.count["cc"])
        self.streams[eng].append(("cc", fn, "cc"))
        self._commit(reads, writes, tok)
        return tok

    def finish(self, final_tokens):
        for t in final_tokens:
            self._need("sp", t)
        for k, c in self.count.items():
            if c > 0:
                self._need("sp", (k, c))
        nc = self.nc
        with nc.Block() as block:
            def mk(ename):
                def body(e):
                    for item in self.streams[ename]:
                        if item[0] == "wait":
                            e.wait_ge(self.sems[item[1]], item[2])
                        elif item[0] == "op":
                            fns, key, inc = item[1], item[2], item[3]
                            for f in fns[:-1]:
                                f(e)
                            fns[-1](e).then_inc(self.sems[key], inc)
                        elif item[0] == "cc":
                            item[1](e).then_inc(self.sems[item[2]])
                return body
            block.tensor(mk("pe"))
            block.scalar(mk("act"))
            block.vector(mk("dve"))
            block.gpsimd(mk("pool"))
            block.sync(mk("sp"))


class Ring:
    def __init__(self, tiles):
        self.tiles = tiles
        self.res = [Res() for _ in tiles]
        self.i = 0

    def next(self):
        k = self.i % len(self.tiles)
        self.i += 1
        return self.tiles[k], self.res[k]


def build_program():
    nc = bass.Bass("TRN2", target_bir_lowering=False)
    dt_in = lambda name, shape, dt=F32: nc.dram_tensor(name, shape, dt, kind="ExternalInput").ap()
    dt_int = lambda name, shape, dt: nc.dram_tensor(name, shape, dt, kind="Internal").ap()

    hfull = dt_in("hfull", [L, D])
    xwin = dt_in("xwin", [NW, D])
    w_attn = dt_in("w_attn", [D, 1280])
    w_ml = dt_in("w_ml", [D, 1028])
    w_g = dt_in("w_g", [D, 4096] if "B" in STAGES else [128, 128])
    w_a = dt_in("w_a", [1024, D] if "B" in STAGES else [128, 128])
    w_b = dt_in("w_b", [1024, D] if "B" in STAGES else [128, 128])
    w_out = dt_in("w_out", [D, D] if "B" in STAGES else [128, 128])
    w_up = dt_in("w_up", [D, 2 * FFN] if "B" in STAGES else [128, 128])
    w_down = dt_in("w_down", [FFN, D] if "B" in STAGES else [128, 128])
    g1b = dt_in("g1b", [128, D])
    gfb = dt_in("gfb", [128, D])
    g2c = dt_in("g2c", [128, 16])
    cosf = dt_in("cosf", [128, L])
    sinf = dt_in("sinf", [128, L])
    mconv = dt_in("mconv", [128, 12])
    fconv = dt_in("fconv", [128, 88 * 3])
    gbias = dt_in("gbias", [128, 4])
    mng = dt_in("mng", [64, 256])
    lamv = dt_in("lamv", [128, 4 * 64])
    sublng = dt_in("sublng", [128, 128])
    ident_d = dt_in("ident", [128, 128])
    triu_d = dt_in("triu", [64, 64])
    tril_d = dt_in("tril", [64, 64])
    wmask_d = dt_in("wmask", [128, NW])
    out_d = nc.dram_tensor("out", [1024, D], F32, kind="ExternalOutput").ap()
    if DEBUG:
        dbg_cc = nc.dram_tensor("dbg_cc", [512, 4114], BF16, kind="ExternalOutput").ap()

    mqk_raw = dt_int("mqk_raw", [512, LP + 2], BF16)
    mv_s = dt_int("mv_s", [LP, 257], BF16)
    mo_s = dt_int("mo_s", [LP, 256], F32)
    gates_s = dt_int("gates_s", [LP, 4], F32)
    hf_s = dt_int("hf_s", [LP, 256], F32)
    hb_s = dt_int("hb_s", [LP, 256], F32)
    cc_in = dt_int("cc_in", [512, 4114], BF16)
    cc_win = dt_int("cc_win", [8 * 256, NW], BF16)
    cc_out = dt_int("cc_out", [8 * 1024, NW], BF16)

    with ExitStack() as st:
        P = Prog(nc, st)

        def sb(name, shape, dt, stack=st):
            return stack.enter_context(nc.sbuf_tensor(name, shape, dt))

        def ps(name, shape, dt, stack=st):
            return stack.enter_context(nc.psum_tensor(name, shape, dt))

        final_toks = []
        dbg_holder = []
        try:
            idf = sb("idf", [128, 128], F32); r_idf = Res()
            idb = sb("idb", [128, 128], BF16); r_idb = Res()
            zt = sb("zt", [128, 512], BF16); r_zt = Res()
            P.dma("sp", lambda e: e.dma_start(out=idf[:], in_=ident_d[:, :]), writes=[r_idf])
            P.op("dve", lambda e: e.tensor_copy(out=idb[:], in_=idf[:]), reads=[r_idf], writes=[r_idb])
            P.op("dve", lambda e: e.memset(zt[:], 0.0), writes=[r_zt])

            r_mqk_raw = Res(); r_mv_s = Res(); r_mo_s = Res(); r_gates_s = Res(); r_hf_s = Res(); r_hb_s = Res(); r_cc_in = Res(); r_cc_in_b = Res()
            r_cc_win = [Res(), Res()]; r_cc_out = [Res(), Res()]
            P.dma("sp", lambda e: e.dma_start(out=mqk_raw.rearrange("(c p) n -> p c n", p=128)[:, :, 0:49], in_=zt[:, 0:196].rearrange("p (c n) -> p c n", c=4)), reads=[r_zt], writes=[r_mqk_raw])
            P.dma("sp", lambda e: e.dma_start(out=mqk_raw.rearrange("(c p) n -> p c n", p=128)[:, :, LP + 1:LP + 2], in_=zt[:, 0:4].rearrange("p (c n) -> p c n", c=4), allow_slow_non_contiguous=True), reads=[r_zt], writes=[r_mqk_raw])
            P.dma("sp", lambda e: e.dma_start(out=mv_s[0:48, :], in_=zt[0:48, 0:257]), reads=[r_zt], writes=[r_mv_s])
            ztf = sb("ztf", [64, 256], F32); r_ztf = Res()
            P.op("dve", lambda e: e.memset(ztf[:], 0.0), writes=[r_ztf])
            P.dma("sp", lambda e: e.dma_start(out=mo_s[0:48, :], in_=ztf[0:48, :]), reads=[r_ztf], writes=[r_mo_s])
            P.dma("sp", lambda e: e.dma_start(out=gates_s[0:48, :], in_=ztf[0:48, 0:4]), reads=[r_ztf], writes=[r_gates_s])
            P.dma("sp", lambda e: e.dma_start(out=cc_in[:, 4112:4114].rearrange("(c p) n -> p c n", p=128), in_=zt[:, 0:8].rearrange("p (c n) -> p c n", c=4)), reads=[r_zt], writes=[r_cc_in, r_cc_in_b])


            def exchange_half(half):
                r_src = r_cc_in if half == 0 else r_cc_in_b
                for j in range(4):
                    i = j * 2 + half
                    P.dma("sp", lambda e, i=i, j=j, half=half: e.dma_start(out=cc_win[i * 256:(i + 1) * 256, :], in_=cc_in[half * 256:(half + 1) * 256, 15 + 1024 * j:15 + 1024 * j + NW]), reads=[r_src], writes=[r_cc_win[half]])
                for j in range(4):
                    i = j * 2 + half
                    P.cc(lambda e, i=i: e.collective_compute("AllGather", ALU.bypass, replica_groups=[[0, 1, 2, 3], [4, 5, 6, 7]], ins=[cc_win[i * 256:(i + 1) * 256, :]], outs=[cc_out[i * 1024:(i + 1) * 1024, :]]), reads=[r_cc_win[half]], writes=[r_cc_out[half]])

            stop_here("c0")
            with ExitStack() as sa:
                dqkT = sb("dqkT", [128, 4, L], BF16, sa); r_dqkT = Res()
                vatt = sb("vatt", [128, 33, 2, 129], BF16, sa); r_vatt = Res()
                with ExitStack() as s1:
                    wat = sb("wat", [128, 16, 1280], BF16, s1); r_wat = Res()
                    wml = sb("wml", [128, 16, 1028], BF16, s1); r_wml = Res()
                    g1t = sb("g1t", [128, D], F32, s1); r_g1t = Res()
                    gbt = sb("gbt", [128, 4], F32, s1); r_gbt = Res()
                    for kq in range(4):
                        P.dma("pool", lambda e, kq=kq: e.dma_start(out=wat[:, 4 * kq:4 * kq + 4, :], in_=w_attn[512 * kq:512 * kq + 512, :].rearrange("(k p) n -> p k n", p=128)), writes=[r_wat])
                        P.dma("pool", lambda e, kq=kq: e.dma_start(out=wml[:, 4 * kq:4 * kq + 4, :], in_=w_ml[512 * kq:512 * kq + 512, :].rearrange("(k p) n -> p k n", p=128)), writes=[r_wml])
                    P.dma("act", lambda e: e.dma_start(out=g1t[:], in_=g1b[:, :]), writes=[r_g1t])
                    P.dma("act", lambda e: e.dma_start(out=gbt[:], in_=gbias[:, :]), writes=[r_gbt])
                    P.op("dve", lambda e: e.memset(vatt[:, :, :, 128:129], 1.0), writes=[r_vatt])

                    xring = Ring([sb(f"xt{i}", [128, D], F32, s1) for i in range(2)])
                    junk = sb("junk", [128, D], BF16, s1); r_junk = Res()
                    ss = sb("ss", [128, 1], F32, s1); r_ss = Res()
                    u = sb("u", [128, D], BF16, s1); r_u = Res()
                    uT = sb("uT", [128, 16, 512], BF16, s1); r_uT = Res()
                    csr = Ring([sb(f"cs{i}", [128, 2, 512], F32, s1) for i in range(2)])
                    rt1 = sb("rt1", [128, 512], F32, s1); r_rt1 = Res()
                    rt2 = sb("rt2", [128, 512], F32, s1); r_rt2 = Res()
                    mstage = Ring([sb(f"mst{i}", [128, 4, 512], BF16, s1) for i in range(2)])
                    mvst = Ring([sb(f"mvst{i}", [128, 257], BF16, s1) for i in range(2)])
                    most = Ring([sb(f"most{i}", [128, 256], F32, s1) for i in range(2)])
                    gst = Ring([sb(f"gst{i}", [128, 4], F32, s1) for i in range(2)])
                    pT = ps("pT", [128, D], BF16, s1); r_pT = Res()
                    pa = ps("pa", [128, 512], F32, s1); r_pa = Res()
                    pb = ps("pb", [128, 512], F32, s1); r_pb = Res()
                    pm = ps("pm", [128, 512], F32, s1); r_pm = Res()
                    pv = ps("pv", [128, 512], F32, s1); r_pv = Res()
                    pvo = ps("pvo", [128, 512], F32, s1); r_pvo = Res()
                    pg = ps("pg", [128, 512], F32, s1); r_pg = Res()
                    for rr, rres in zip(mvst.tiles, mvst.res):
                        P.op("dve", lambda e, rr=rr: e.memset(rr[:, 256:257], 1.0), writes=[rres])

                    def norm_block(xt, r_xt, bs, gtile, r_g):
                        P.op("act", lambda e: e.activation(out=junk[:bs, :], in_=xt[:bs, :], func=AF.Square, accum_out=ss[:bs, :]), reads=[r_xt], writes=[r_junk, r_ss])
                        P.op("dve", lambda e: e.tensor_scalar(out=ss[:bs, :], in0=ss[:bs, :], scalar1=1.0 / D, scalar2=EPS, op0=ALU.mult, op1=ALU.add), reads=[r_ss], writes=[r_ss])
                        P.op("act", lambda e: e.activation(out=ss[:bs, :], in_=ss[:bs, :], func=AF.Sqrt), reads=[r_ss], writes=[r_ss])
                        P.op("dve", lambda e: e.reciprocal(out=ss[:bs, :], in_=ss[:bs, :]), reads=[r_ss], writes=[r_ss])
                        P.op("dve", lambda e: e.scalar_tensor_tensor(out=u[:bs, :], in0=xt[:bs, :], scalar=ss[:bs, 0:1], in1=gtile[:bs, :], op0=ALU.mult, op1=ALU.mult), reads=[r_xt, r_ss, r_g], writes=[r_u])

                    stop_here("a1w")
                    tiles = [(512 * i, 512) for i in range(8)] + [(4096, 16)]
                    for (p0, n) in (tiles if "A1" in STAGES else []):
                        nblk = (n + 127) // 128
                        cs, r_cs = csr.next()
                        P.dma("act", lambda e, cs=cs, p0=p0, n=n: e.dma_start(out=cs[:, 0, :n], in_=cosf[:, p0:p0 + n]), writes=[r_cs])
                        P.dma("act", lambda e, cs=cs, p0=p0, n=n: e.dma_start(out=cs[:, 1, :n], in_=sinf[:, p0:p0 + n]), writes=[r_cs])
                        for j in range(nblk):
                            bs = min(128, n - 128 * j)
                            xt, r_xt = xring.next()
                            P.dma("sp", lambda e, xt=xt, bs=bs, r0=p0 + 128 * j: e.dma_start(out=xt[:bs, :], in_=hfull[r0:r0 + bs, :]), writes=[r_xt])
                            norm_block(xt, r_xt, bs, g1t, r_g1t)
                            P.op("pe", [(lambda e, k=k, bs=bs: e.transpose(out=pT[:, k * 128:k * 128 + bs], in_=u[:bs, k * 128:(k + 1) * 128], identity=idb[:bs, :bs])) for k in range(16)], reads=[r_u, r_idb], writes=[r_pT])
                            P.op("act", lambda e, j=j, bs=bs: e.copy(out=uT[:, :, 128 * j:128 * j + bs], in_=pT[:].rearrange("p (k n) -> p k n", k=16)[:, :, :bs]), reads=[r_pT], writes=[r_uT])
                        stop_here("blk%d" % (p0 // 512))
                        for c in range(4):
                            cm = (c % 2) * 128 + (c // 2) * 512
                            cr = cm + 256
                            P.op("pe", [(lambda e, k=k, cm=cm, n=n: e.matmul(out=pa[:, :n], lhsT=wat[:, k, cm:cm + 128], rhs=uT[:, k, :n], start=(k == 0), stop=(k == 15))) for k in range(16)], reads=[r_wat, r_uT], writes=[r_pa])
                            P.op("pe", [(lambda e, k=k, cr=cr, n=n: e.matmul(out=pb[:, :n], lhsT=wat[:, k, cr:cr + 128], rhs=uT[:, k, :n], start=(k == 0), stop=(k == 15))) for k in range(16)], reads=[r_wat, r_uT], writes=[r_pb])
                            P.op("dve", lambda e, cs=cs, n=n: e.tensor_tensor(out=rt1[:, :n], in0=pa[:, :n], in1=cs[:, 0, :n], op=ALU.mult), reads=[r_pa, r_cs], writes=[r_rt1])
                            P.op("dve", lambda e, cs=cs, n=n: e.tensor_tensor(out=rt2[:, :n], in0=pb[:, :n], in1=cs[:, 1, :n], op=ALU.mult), reads=[r_pb, r_cs], writes=[r_rt2])
                            P.op("dve", lambda e, c=c, p0=p0, n=n: e.tensor_tensor(out=dqkT[:, c, p0:p0 + n], in0=rt1[:, :n], in1=rt2[:, :n], op=ALU.add), reads=[r_rt1, r_rt2], writes=[r_dqkT])
                        stop_here("fm%d" % (p0 // 512))
                        mst, r_mst = mstage.next()
                        for c in range(4):
                            P.op("pe", [(lambda e, k=k, c=c, n=n: e.matmul(out=pm[:, :n], lhsT=wml[:, k, c * 128:(c + 1) * 128], rhs=uT[:, k, :n], start=(k == 0), stop=(k == 15))) for k in range(16)], reads=[r_wml, r_uT], writes=[r_pm])
                            P.op("act", lambda e, mst=mst, c=c, n=n: e.copy(out=mst[:, c, :n], in_=pm[:, :n]), reads=[r_pm], writes=[r_mst])
                        P.dma("sp", lambda e, mst=mst, p0=p0, n=n: e.dma_start(out=mqk_raw.rearrange("(c p) n -> p c n", p=128)[:, :, 49 + p0:49 + p0 + n], in_=mst[:, :, :n]), reads=[r_mst], writes=[r_mqk_raw])
                        stop_here("ml%d" % (p0 // 512))
                        for j in range(nblk):
                            bs = min(128, n - 128 * j)
                            kb = (p0 + 128 * j) // 128
                            r0 = 48 + p0 + 128 * j
                            P.op("pe", [(lambda e, k=k, j=j, bs=bs: e.matmul(out=pv[:bs, 0:256], lhsT=uT[:, k, 128 * j:128 * j + bs], rhs=wat[:, k, 1024:1280], start=(k == 0), stop=(k == 15))) for k in range(16)], reads=[r_wat, r_uT], writes=[r_pv])
                            P.op("dve", lambda e, kb=kb, bs=bs: e.tensor_copy(out=vatt[:bs, kb, :, 0:128], in_=pv[:bs, 0:256].rearrange("p (h d) -> p h d", h=2)), reads=[r_pv], writes=[r_vatt])
                            stop_here("tv%d_%d" % (p0 // 512, j))
                            P.op("pe", [(lambda e, k=k, j=j, bs=bs: e.matmul(out=pvo[:bs, :], lhsT=uT[:, k, 128 * j:128 * j + bs], rhs=wml[:, k, 512:1024], start=(k == 0), stop=(k == 15))) for k in range(16)], reads=[r_wml, r_uT], writes=[r_pvo])
                            mvt, r_mvt = mvst.next()
                            mot, r_mot = most.next()
                            P.op("act", lambda e, mvt=mvt, bs=bs: e.copy(out=mvt[:bs, 0:256], in_=pvo[:bs, 0:256]), reads=[r_pvo], writes=[r_mvt])
                            P.op("act", lambda e, mot=mot, bs=bs: e.activation(out=mot[:bs, :], in_=pvo[:bs, 256:512], func=AF.Sigmoid), reads=[r_pvo], writes=[r_mot])
                            P.dma("sp", lambda e, mvt=mvt, bs=bs, r0=r0: e.dma_start(out=mv_s[r0:r0 + bs, :], in_=mvt[:bs, :]), reads=[r_mvt], writes=[r_mv_s])
                            P.dma("sp", lambda e, mot=mot, bs=bs, r0=r0: e.dma_start(out=mo_s[r0:r0 + bs, :], in_=mot[:bs, :]), reads=[r_mot], writes=[r_mo_s])
                            stop_here("tm%d_%d" % (p0 // 512, j))
                            P.op("pe", [(lambda e, k=k, j=j, bs=bs: e.matmul(out=pg[:bs, 0:64], lhsT=uT[:, k, 128 * j:128 * j + bs], rhs=wml[:, k, 964:1028], start=(k == 0), stop=(k == 15))) for k in range(16)], reads=[r_wml, r_uT], writes=[r_pg])
                            stop_here("tgm%d_%d" % (p0 // 512, j))
                            gt_, r_gt = gst.next()
                            P.op("dve", lambda e, gt_=gt_, bs=bs: e.tensor_tensor(out=gt_[:bs, :], in0=pg[:bs, 60:64], in1=gbt[:bs, :], op=ALU.add), reads=[r_pg, r_gbt], writes=[r_gt])
                            stop_here("tga%d_%d" % (p0 // 512, j))
                            P.dma("sp", lambda e, gt_=gt_, bs=bs, r0=r0: e.dma_start(out=gates_s[r0:r0 + bs, :], in_=gt_[:bs, :]), reads=[r_gt], writes=[r_gates_s])
                    stop_here("a1t%d" % (p0 // 512))

                P.fence()
                stop_here("a1")
                with ExitStack() as s2:
                    lam_t = sb("lam_t", [128, 256], F32, s2); r_lam = Res()
                    lamw = sb("lamw", [128, 8], F32, s2); r_lamw = Res()
                    sg_t = sb("sg_t", [128, 128], F32, s2); r_sg = Res()
                    P.dma("act", lambda e: e.dma_start(out=lam_t[:], in_=lamv[:, :]), writes=[r_lam])
                    P.dma("act", lambda e: e.dma_start(out=sg_t[:], in_=sublng[:, :]), writes=[r_sg])
                    ljunk = sb("ljunk", [128, 64], F32, s2); r_lj = Res()
                    for i in range(2):
                        P.op("dve", lambda e, i=i: e.tensor_tensor(out=ljunk[:], in0=lam_t[:, 128 * i:128 * i + 64], in1=lam_t[:, 128 * i + 64:128 * i + 128], op=ALU.mult), reads=[r_lam], writes=[r_lj])
                        P.op("dve", lambda e, i=i: e.reduce_sum(out=lamw[:, i:i + 1], in_=ljunk[:], axis=AX.X), reads=[r_lj], writes=[r_lamw])
                    P.op("act", lambda e: e.activation(out=lamw[:, 2:4], in_=lamw[:, 0:2], func=AF.Exp), reads=[r_lamw], writes=[r_lamw])
                    P.op("dve", lambda e: e.tensor_tensor(out=lamw[:, 4:5], in0=lamw[:, 2:3], in1=lamw[:, 3:4], op=ALU.subtract), reads=[r_lamw], writes=[r_lamw])
                    P.op("dve", lambda e: e.tensor_scalar(out=lamw[:, 5:6], in0=lamw[:, 4:5], scalar1=0.2, scalar2=-1.0, op0=ALU.add, op1=ALU.mult), reads=[r_lamw], writes=[r_lamw])
                    P.op("dve", lambda e: e.tensor_scalar(out=sg_t[:], in0=sg_t[:], scalar1=0.8, scalar2=None, op0=ALU.mult), reads=[r_sg], writes=[r_sg])

                    stop_here("a2s")
                    pss = [Ring([ps(f"ps{m}{i}", [128, 512], F32, s2) for i in range(2)]) for m in range(2)]
                    pacc = [ps(f"pacc{i}", [128, 512], F32, s2) for i in range(3)]
                    r_pacc = Res()
                    ptr = ps("ptr", [128, 512], BF16, s2); r_ptr = Res()
                    ptile = [Ring([sb(f"pt{m}{i}", [128, 512], BF16, s2) for i in range(3)]) for m in range(2)]
                    rc = sb("rc", [128, 2], F32, s2); r_rc = Res()
                    t2 = sb("t2", [128, 128], F32, s2); r_t2 = Res()
                    ot = sb("ot", [128, 128], F32, s2); r_ot = Res()
                    oj = sb("oj", [128, 128], F32, s2); r_oj = Res()
                    os_ = sb("os_", [128, 1], F32, s2); r_os = Res()
                    bo = sb("bo", [128, 128], BF16, s2); r_bo = Res()
                    boT = Ring([sb(f"boT{i}", [128, 512], BF16, s2) for i in range(2)])

                    def acc(m, sub):
                        i = m * 4 + sub
                        return pacc[i // 3][:, (i % 3) * 129:(i % 3) * 129 + 129]

                    qtiles = [(512 * i, 512) for i in range(8)] + [(4096, 16)]
                    kblocks = [(128 * i, 128) for i in range(32)] + [(4096, 16)]
                    for h in range(2 if "A2" in STAGES else 0):
                        for (q0, nq) in qtiles:
                            nsub = (nq + 127) // 128
                            P.op("dve", [(lambda e, i=i: e.memset(pacc[i][:], 0.0)) for i in range(3)], writes=[r_pacc])
                            pend = []
                            for kbi, (k0, nk) in enumerate(kblocks):
                                cur = []
                                for m in range(2):
                                    pst, r_pst = pss[m].next()
                                    P.op("pe", lambda e, pst=pst, m=m, h=h, k0=k0, nk=nk, q0=q0, nq=nq: e.matmul(out=pst[:nk, :nq], lhsT=dqkT[64 * m:64 * m + 64, 2 + h, k0:k0 + nk], rhs=dqkT[64 * m:64 * m + 64, h, q0:q0 + nq], start=True, stop=True), reads=[r_dqkT], writes=[r_pst])
                                    cur.append((m, pst, r_pst))
                                newpend = []
                                for (m, pst, r_pst) in cur:
                                    pt, r_pt = ptile[m].next()
                                    P.op("act", lambda e, pst=pst, pt=pt, nk=nk, nq=nq: e.activation(out=pt[:nk, :nq], in_=pst[:nk, :nq], func=AF.Exp, scale=0.125), reads=[r_pst], writes=[r_pt])
                                    def pvf(pt=pt, r_pt=r_pt, m=m, nk=nk, kbi=kbi):
                                        P.op("pe", [(lambda e, pt=pt, m=m, sub=sub, nk=nk, kbi=kbi, h=h, qs=min(128, nq - 128 * sub): e.matmul(out=acc(m, sub)[:qs, :], lhsT=pt[:nk, 128 * sub:128 * sub + qs], rhs=vatt[:nk, kbi, h, :], start=False, stop=False, skip_group_check=True)) for sub in range(nsub)], reads=[r_pt, r_vatt], writes=[r_pacc])
                                    newpend.append(pvf)
                                for f in pend:
                                    f()
                                pend = newpend
                            for f in pend:
                                f()
                            bT, r_bT = boT.next()
                            for sub in range(nsub):
                                qs = min(128, nq - 128 * sub)
                                a1 = acc(0, sub); a2 = acc(1, sub)
                                P.op("dve", lambda e, a1=a1, qs=qs: e.reciprocal(out=rc[:qs, 0:1], in_=a1[:qs, 128:129]), reads=[r_pacc], writes=[r_rc])
                                P.op("dve", lambda e, a2=a2, qs=qs: e.reciprocal(out=rc[:qs, 1:2], in_=a2[:qs, 128:129]), reads=[r_pacc], writes=[r_rc])
                                P.op("dve", lambda e, a2=a2, qs=qs: e.tensor_scalar(out=t2[:qs, :], in0=a2[:qs, 0:128], scalar1=rc[:qs, 1:2], scalar2=lamw[:qs, 5:6], op0=ALU.mult, op1=ALU.mult), reads=[r_pacc, r_rc, r_lamw], writes=[r_t2])
                                P.op("dve", lambda e, a1=a1, qs=qs: e.scalar_tensor_tensor(out=ot[:qs, :], in0=a1[:qs, 0:128], scalar=rc[:qs, 0:1], in1=t2[:qs, :], op0=ALU.mult, op1=ALU.add), reads=[r_pacc, r_rc, r_t2], writes=[r_ot])
                                P.op("act", lambda e, qs=qs: e.activation(out=oj[:qs, :], in_=ot[:qs, :], func=AF.Square, accum_out=os_[:qs, :]), reads=[r_ot], writes=[r_oj, r_os])
                                P.op("dve", lambda e, qs=qs: e.tensor_scalar(out=os_[:qs, :], in0=os_[:qs, :], scalar1=1.0 / 128, scalar2=EPS, op0=ALU.mult, op1=ALU.add), reads=[r_os], writes=[r_os])
                                P.op("act", lambda e, qs=qs: e.activation(out=os_[:qs, :], in_=os_[:qs, :], func=AF.Sqrt), reads=[r_os], writes=[r_os])
                                P.op("dve", lambda e, qs=qs: e.reciprocal(out=os_[:qs, :], in_=os_[:qs, :]), reads=[r_os], writes=[r_os])
                                P.op("dve", lambda e, qs=qs: e.scalar_tensor_tensor(out=bo[:qs, :], in0=ot[:qs, :], scalar=os_[:qs, 0:1], in1=sg_t[:qs, :], op0=ALU.mult, op1=ALU.mult), reads=[r_ot, r_os, r_sg], writes=[r_bo])
                                P.op("pe", lambda e, sub=sub, qs=qs: e.transpose(out=ptr[:, 128 * sub:128 * sub + qs], in_=bo[:qs, :], identity=idb[:qs, :qs]), reads=[r_bo, r_idb], writes=[r_ptr])
                                P.op("act", lambda e, bT=bT, sub=sub, qs=qs: e.copy(out=bT[:, 128 * sub:128 * sub + qs], in_=ptr[:, 128 * sub:128 * sub + qs]), reads=[r_ptr], writes=[r_bT])
                            P.dma("sp", lambda e, bT=bT, h=h, q0=q0, nq=nq: e.dma_start(out=cc_in[256 + 128 * h:256 + 128 * h + 128, q0:q0 + nq], in_=bT[:, :nq]), reads=[r_bT], writes=[r_cc_in_b])

            if "X" in STAGES:
                exchange_half(1)
            P.fence()
            with ExitStack() as s3:
                mqkT = sb("mqkT", [128, 4, LP], BF16, s3); r_mqkT = Res()
                with ExitStack() as s3a:
                    raw = sb("raw", [128, 4, LP + 2], BF16, s3a); r_raw = Res()
                    cacc = sb("cacc", [128, LP], F32, s3a); r_cacc = Res()
                    mcw = sb("mcw", [128, 12], F32, s3a); r_mcw = Res()
                    P.dma("act", lambda e: e.dma_start(out=mcw[:], in_=mconv[:, :]), writes=[r_mcw])
                    P.dma("sp", lambda e: e.dma_start(out=raw[:], in_=mqk_raw.rearrange("(c p) n -> p c n", p=128)), reads=[r_mqk_raw], writes=[r_raw])
                    for c in range(4):
                        P.op("dve", lambda e, c=c: e.tensor_scalar(out=cacc[:], in0=raw[:, c, 0:LP], scalar1=mcw[:, 3 * c:3 * c + 1], scalar2=None, op0=ALU.mult), reads=[r_raw, r_mcw], writes=[r_cacc])
                        P.op("dve", lambda e, c=c: e.scalar_tensor_tensor(out=cacc[:], in0=raw[:, c, 1:LP + 1], scalar=mcw[:, 3 * c + 1:3 * c + 2], in1=cacc[:], op0=ALU.mult, op1=ALU.add), reads=[r_raw, r_mcw, r_cacc], writes=[r_cacc])
                        P.op("dve", lambda e, c=c: e.scalar_tensor_tensor(out=cacc[:], in0=raw[:, c, 2:LP + 2], scalar=mcw[:, 3 * c + 2:3 * c + 3], in1=cacc[:], op0=ALU.mult, op1=ALU.add), reads=[r_raw, r_mcw, r_cacc], writes=[r_cacc])
                        P.op("act", lambda e, c=c: e.activation(out=mqkT[:, c, :], in_=cacc[:], func=AF.Silu), reads=[r_cacc], writes=[r_mqkT])
                    P.op("dve", lambda e: e.memset(mqkT[:, :, 0:48], 0.0), writes=[r_mqkT])

                P.fence()
                stop_here("a3conv")
                gt = sb("gt", [64, 65, 4], F32, s3); r_gtb = Res()
                gl = sb("gl", [64, 4, 65], F32, s3); r_gl = Res()
                tru = sb("tru", [64, 64], F32, s3); r_tru = Res()
                trl = sb("trl", [64, 64], F32, s3); r_trl = Res()
                mku = sb("mku", [64, 64], F32, s3); r_mku = Res()
                mkl = sb("mkl", [64, 64], F32, s3); r_mkl = Res()
                ones64 = sb("ones64", [64, 128], F32, s3); r_ones = Res()
                ebt = sb("ebt", [64, 2, 65], F32, s3); r_ebt = Res()
                rft = sb("rft", [64, 2, 65], F32, s3); r_rft = Res()
                egt = sb("egt", [128, 2, 65], F32, s3); r_egt = Res()
                mngt = sb("mngt", [64, 256], F32, s3); r_mngt = Res()
                s3g = s3.enter_context(ExitStack())
                pgx = ps("pgx", [128, 512], F32, s3g); r_pgx = Res()
                for c5 in range(5):
                    P.dma("sp", lambda e, c5=c5: e.dma_start(out=gt[:, 13 * c5:13 * c5 + 13, :], in_=gates_s.rearrange("(c t) g -> t c g", t=64)[:, 13 * c5:13 * c5 + 13, :]), reads=[r_gates_s], writes=[r_gtb])
                P.dma("act", lambda e: e.dma_start(out=tru[:], in_=triu_d[:, :]), writes=[r_tru])
                P.dma("act", lambda e: e.dma_start(out=trl[:], in_=tril_d[:, :]), writes=[r_trl])
                P.dma("act", lambda e: e.dma_start(out=mngt[:], in_=mng[:, :]), writes=[r_mngt])
                P.op("dve", lambda e: e.tensor_scalar(out=mku[:], in0=tru[:], scalar1=1.0 / 16, scalar2=None, op0=ALU.mult), reads=[r_tru], writes=[r_mku])
                P.op("dve", lambda e: e.tensor_scalar(out=mkl[:], in0=trl[:], scalar1=1.0 / 16, scalar2=None, op0=ALU.mult), reads=[r_trl], writes=[r_mkl])
                P.op("dve", lambda e: e.memset(ones64[:], 1.0), writes=[r_ones])
                stop_here("g1")
                for gi in range(2):
                    P.op("act", lambda e, gi=gi: e.activation(out=gl[:, gi, :], in_=gt[:, :, gi], func=AF.Exp, scale=-1.0), reads=[r_gtb], writes=[r_gl])
                P.op("dve", lambda e: e.tensor_scalar(out=gl[:, 0:2, :], in0=gl[:, 0:2, :], scalar1=1.0, scalar2=None, op0=ALU.add), reads=[r_gl], writes=[r_gl])
                P.op("act", lambda e: e.activation(out=gl[:, 0:2, :], in_=gl[:, 0:2, :], func=AF.Ln), reads=[r_gl], writes=[r_gl])
                P.op("dve", lambda e: e.tensor_scalar(out=gl[:, 0:2, :], in0=gl[:, 0:2, :], scalar1=-1.0, scalar2=None, op0=ALU.mult), reads=[r_gl], writes=[r_gl])
                for gi in range(2):
                    P.op("dve", lambda e, gi=gi: e.tensor_copy(out=gl[:, 2 + gi, :], in_=gt[:, :, 2 + gi]), reads=[r_gtb], writes=[r_gl])
                stop_here("g2")
                P.op("dve", lambda e: e.memset(gl[0:48, :, 0:1], 0.0), writes=[r_gl])
                stop_here("g3")
                P.op("pe", lambda e: e.matmul(out=pgx[:64, 0:65], lhsT=tru[:], rhs=gl[:, 0, :], start=True, stop=True), reads=[r_tru, r_gl], writes=[r_pgx])
                P.op("pe", lambda e: e.matmul(out=pgx[:64, 65:130], lhsT=trl[:], rhs=gl[:, 1, :], start=True, stop=True), reads=[r_trl, r_gl], writes=[r_pgx])
                P.op("pe", lambda e: e.matmul(out=pgx[:, 130:260], lhsT=ones64[:], rhs=gl[:, 0:2, :].rearrange("p a c -> p (a c)"), start=True, stop=True), reads=[r_ones, r_gl], writes=[r_pgx])
                stop_here("g4")
                P.op("act", lambda e: e.activation(out=ebt[:].rearrange("p a c -> p (a c)"), in_=pgx[:64, 0:130], func=AF.Exp), reads=[r_pgx], writes=[r_ebt])
                P.op("act", lambda e: e.activation(out=egt[:].rearrange("p a c -> p (a c)"), in_=pgx[:, 130:260], func=AF.Exp), reads=[r_pgx], writes=[r_egt])
                P.op("dve", lambda e: e.tensor_tensor(out=rft[:].rearrange("p a c -> p (a c)"), in0=gl[:, 2:4, :].rearrange("p a c -> p (a c)"), in1=pgx[:64, 0:130], op=ALU.subtract), reads=[r_pgx, r_gl], writes=[r_rft])
                P.op("act", lambda e: e.activation(out=rft[:].rearrange("p a c -> p (a c)"), in_=rft[:].rearrange("p a c -> p (a c)"), func=AF.Exp), reads=[r_rft], writes=[r_rft])

                stop_here("a3g")
                s3g.close()
                P.fence()
                Zt = [sb(f"Zt{d}", [128, 2, 257], F32, s3) for d in range(2)]; r_Z = [Res(), Res()]
                Zbt = [sb(f"Zbt{d}", [128, 2, 257], BF16, s3) for d in range(2)]; r_Zb = [Res(), Res()]
                ztt = [sb(f"ztt{d}", [128, 2, 257], F32, s3) for d in range(2)]; r_ztmp = [Res(), Res()]
                nebt = sb("nebt", [64, 2, 65], F32, s3); r_nebt = Res()
                P.op("dve", lambda e: e.tensor_scalar(out=nebt[:].rearrange("p a c -> p (a c)"), in0=ebt[:].rearrange("p a c -> p (a c)"), scalar1=-1.0, scalar2=None, op0=ALU.mult), reads=[r_ebt], writes=[r_nebt])
                vres = sb("vres", [64, 65, 257], BF16, s3); r_vres = Res()
                for c5 in range(5):
                    P.dma("act", lambda e, c5=c5: e.dma_start(out=vres[:, 13 * c5:13 * c5 + 13, :], in_=mv_s.rearrange("(c t) v -> t c v", t=64)[:, 13 * c5:13 * c5 + 13, :]), reads=[r_mv_s], writes=[r_vres])
                vtr = Ring([sb(f"vt{i}", [64, 257], BF16, s3) for i in range(4)])
                ptmr = Ring([sb(f"ptm{i}", [64, 64], BF16, s3) for i in range(4)])
                ktokr = Ring([sb(f"ktok{i}", [64, 256], BF16, s3) for i in range(4)])
                dnr = Ring([sb(f"dn{i}", [64, 4], F32, s3) for i in range(4)])
                hring = Ring([sb(f"hch{i}", [64, 256], F32, s3) for i in range(4)])
                with ExitStack() as s3s:
                    p_sr = Ring([ps(f"p_s{i}", [128, 512], F32, s3s) for i in range(2)])
                    p_kr = Ring([ps(f"p_k{i}", [128, 1024], BF16, s3s) for i in range(2)])
                    p_or = Ring([ps(f"p_o{i}", [128, 512], F32, s3s) for i in range(2)])
                    p_cc = ps("p_cc", [128, 1024], F32, s3s); r_p_c = Res()

                    def phase_I(d, c):
                        c0 = 64 * c
                        mask, r_mask = (mku, r_mku) if d == 0 else (mkl, r_mkl)
                        p_s, r_p_s = p_sr.next()
                        p_k, r_p_k = p_kr.next()
                        ptm, r_ptm = ptmr.next()
                        ktok, r_ktok = ktokr.next()
                        vt, r_vt = vtr.next()
                        P.op("pe", [(lambda e, dc=dc, c0=c0, p_s=p_s: e.matmul(out=p_s[:64, 0:64], lhsT=mqkT[:, 2 + dc, c0:c0 + 64], rhs=mqkT[:, dc, c0:c0 + 64], start=(dc == 0), stop=(dc == 1))) for dc in range(2)], reads=[r_mqkT], writes=[r_p_s])
                        P.op("dve", lambda e, mask=mask, p_s=p_s, ptm=ptm: e.tensor_tensor(out=ptm[:], in0=p_s[:64, 0:64], in1=mask[:], op=ALU.mult), reads=[r_p_s, r_mask], writes=[r_ptm])
                        P.op("pe", [(lambda e, dc=dc, c0=c0, p_k=p_k: e.transpose(out=p_k[:64, 128 * dc:128 * dc + 128], in_=mqkT[:, 2 + dc, c0:c0 + 64], identity=idb[:])) for dc in range(2)], reads=[r_mqkT, r_idb], writes=[r_p_k])
                        P.op("act", lambda e, ktok=ktok, p_k=p_k: e.copy(out=ktok[:], in_=p_k[:64, 0:256]), reads=[r_p_k], writes=[r_ktok])
                        P.op("dve", lambda e, vt=vt, d=d, c=c: e.tensor_scalar(out=vt[:], in0=vres[:, c, :], scalar1=rft[:, d, c:c + 1], scalar2=None, op0=ALU.mult), reads=[r_vres, r_rft], writes=[r_vt])
                        return dict(c=c, c0=c0, d=d, ptm=ptm, r_ptm=r_ptm, ktok=ktok, r_ktok=r_ktok, vt=vt, r_vt=r_vt)

                    def phase_D(x):
                        d, c, c0 = x["d"], x["c"], x["c0"]
                        ptm, r_ptm, ktok, r_ktok, vt, r_vt = x["ptm"], x["r_ptm"], x["ktok"], x["r_ktok"], x["vt"], x["r_vt"]
                        p_o, r_p_o = p_or.next()
                        for dc in range(2):
                            P.op("pe", lambda e, dc=dc, ktok=ktok, vt=vt: e.matmul(out=p_cc[:, 512 * dc:512 * dc + 257], lhsT=ktok[:, 128 * dc:128 * dc + 128], rhs=vt[:], start=True, stop=True), reads=[r_ktok, r_vt], writes=[r_p_c])
                        P.op("pe", [lambda e, ptm=ptm, vt=vt, p_o=p_o: e.matmul(out=p_o[:64, 0:257], lhsT=ptm[:], rhs=vt[:], start=True, stop=False)] +
                             [(lambda e, dc=dc, c0=c0, p_o=p_o, d=d: e.matmul(out=p_o[:64, 0:257], lhsT=mqkT[:, dc, c0:c0 + 64], rhs=Zbt[d][:, dc, :], start=False, stop=(dc == 1))) for dc in range(2)],
                             reads=[r_ptm, r_vt, r_mqkT, r_Zb[d]], writes=[r_p_o])
                        P.op("dve", lambda e, d=d: e.scalar_tensor_tensor(out=ztt[d][:], in0=p_cc[:].rearrange("p (a n) -> p a n", a=2)[:, :, 0:257], scalar=1.0 / 16, in1=Zt[d][:], op0=ALU.mult, op1=ALU.add), reads=[r_p_c, r_Z[d]], writes=[r_ztmp[d]])
                        P.op("act", lambda e, d=d, c=c: e.activation(out=Zbt[d][:], in_=ztt[d][:], func=AF.Identity, scale=egt[:, d, c:c + 1]), reads=[r_ztmp[d], r_egt], writes=[r_Zb[d]])
                        P.op("act", lambda e, d=d, c=c: e.activation(out=Zt[d][:], in_=ztt[d][:], func=AF.Identity, scale=egt[:, d, c:c + 1]), reads=[r_ztmp[d], r_egt], writes=[r_Z[d]])
                        hch, r_hch = hring.next()
                        dn, r_dn = dnr.next()
                        P.op("dve", lambda e, d=d, c=c, dn=dn, p_o=p_o: e.tensor_scalar(out=dn[:, 0:1], in0=p_o[:64, 256:257], scalar1=ebt[:, d, c:c + 1], scalar2=1.0, op0=ALU.mult, op1=ALU.max), reads=[r_p_o, r_ebt], writes=[r_dn])
                        P.op("dve", lambda e, d=d, c=c, dn=dn, p_o=p_o: e.scalar_tensor_tensor(out=dn[:, 1:2], in0=p_o[:64, 256:257], scalar=nebt[:, d, c:c + 1], in1=dn[:, 0:1], op0=ALU.mult, op1=ALU.max), reads=[r_p_o, r_nebt, r_dn], writes=[r_dn])
                        P.op("dve", lambda e, dn=dn: e.reciprocal(out=dn[:, 2:3], in_=dn[:, 1:2]), reads=[r_dn], writes=[r_dn])
                        P.op("dve", lambda e, hch=hch, dn=dn, p_o=p_o, d=d, c=c: e.tensor_scalar(out=hch[:], in0=p_o[:64, 0:256], scalar1=dn[:, 2:3], scalar2=ebt[:, d, c:c + 1], op0=ALU.mult, op1=ALU.mult), reads=[r_p_o, r_dn, r_ebt], writes=[r_hch])
                        dst, r_dst = (hf_s, r_hf_s) if d == 0 else (hb_s, r_hb_s)
                        P.dma("sp", lambda e, hch=hch, c0=c0, dst=dst: e.dma_start(out=dst[c0:c0 + 64, :], in_=hch[:]), reads=[r_hch], writes=[r_dst])

                    if "A3" in STAGES:
                        for d in range(2):
                            P.op("dve", lambda e, d=d: e.memset(Zt[d][:], 0.0), writes=[r_Z[d]])
                            P.op("dve", lambda e, d=d: e.memset(Zbt[d][:], 0.0), writes=[r_Zb[d]])
                        nxt = [phase_I(0, 0), phase_I(1, 64)]
                        for i in range(65):
                            cur = nxt
                            if i + 1 < 65:
                                nxt = [phase_I(0, i + 1), phase_I(1, 63 - i)]
                            phase_D(cur[0])
                            phase_D(cur[1])
                P.fence()
                with ExitStack() as s3e:
                    NR = 4
                    hfr = Ring([sb(f"ehf{i}", [128, 256], F32, s3e) for i in range(NR)])
                    hbr = Ring([sb(f"ehb{i}", [128, 256], F32, s3e) for i in range(NR)])
                    mor = Ring([sb(f"emo{i}", [128, 256], F32, s3e) for i in range(NR)])
                    hsr = Ring([sb(f"ehs{i}", [128, 256], F32, s3e) for i in range(NR)])
                    bsr = Ring([sb(f"ebs{i}", [128, 8], F32, s3e) for i in range(NR)])
                    aor = Ring([sb(f"eao{i}", [128, 256], BF16, s3e) for i in range(NR)])
                    aTr = Ring([sb(f"eaT{i}", [128, 2, 128], BF16, s3e) for i in range(NR)])
                    p_tr = Ring([ps(f"ep_t{i}", [128, 1024], BF16, s3e) for i in range(2)])
                    mng128 = sb("mng128", [128, 256], F32, s3e); r_mng128 = Res()
                    P.dma("act", lambda e: e.dma_start(out=mng128[0:64, :], in_=mng[:, :]), writes=[r_mng128])
                    P.dma("act", lambda e: e.dma_start(out=mng128[64:128, :], in_=mng[:, :]), writes=[r_mng128])
                    eblocks = [(128 * j, 128) for j in range(32)] + [(4096, 64)]

                    def ep1(j):
                        r0, n = eblocks[j]
                        hf, r_hf = hfr.next(); hb, r_hb = hbr.next(); mo, r_mo = mor.next()
                        hs, r_hs = hsr.next(); bs_, r_bs = bsr.next()
                        P.dma("sp", lambda e, hf=hf, r0=r0, n=n: e.dma_start(out=hf[:n, :], in_=hf_s[r0:r0 + n, :]), reads=[r_hf_s], writes=[r_hf])
                        P.dma("sp", lambda e, hb=hb, r0=r0, n=n: e.dma_start(out=hb[:n, :], in_=hb_s[r0:r0 + n, :]), reads=[r_hb_s], writes=[r_hb])
                        P.dma("sp", lambda e, mo=mo, r0=r0, n=n: e.dma_start(out=mo[:n, :], in_=mo_s[r0:r0 + n, :]), reads=[r_mo_s], writes=[r_mo])
                        P.op("pool", lambda e, hf=hf, hb=hb, hs=hs, n=n: e.tensor_tensor(out=hs[:n, :], in0=hf[:n, :], in1=hb[:n, :], op=ALU.add), reads=[r_hf, r_hb], writes=[r_hs])
                        P.op("dve", lambda e, hs=hs, bs_=bs_, n=n: e.bn_stats(out=bs_[:n, 0:6], in_=hs[:n, :]), reads=[r_hs], writes=[r_bs])
                        P.op("dve", lambda e, bs_=bs_, n=n: e.bn_aggr(out=bs_[:n, 6:8], in_=bs_[:n, 0:6]), reads=[r_bs], writes=[r_bs])
                        P.op("dve", lambda e, bs_=bs_, n=n: e.tensor_scalar(out=bs_[:n, 7:8], in0=bs_[:n, 7:8], scalar1=EPS, scalar2=None, op0=ALU.add), reads=[r_bs], writes=[r_bs])
                        return dict(r0=r0, n=n, hs=hs, r_hs=r_hs, bs=bs_, r_bs=r_bs, mo=mo, r_mo=r_mo)

                    def ep2(x):
                        n, bs_, r_bs, hs, r_hs = x["n"], x["bs"], x["r_bs"], x["hs"], x["r_hs"]
                        P.op("act", lambda e, bs_=bs_, n=n: e.activation(out=bs_[:n, 7:8], in_=bs_[:n, 7:8], func=AF.Sqrt), reads=[r_bs], writes=[r_bs])
                        P.op("dve", lambda e, bs_=bs_, n=n: e.reciprocal(out=bs_[:n, 7:8], in_=bs_[:n, 7:8]), reads=[r_bs], writes=[r_bs])
                        P.op("dve", lambda e, bs_=bs_, hs=hs, n=n: e.tensor_scalar(out=hs[:n, :], in0=hs[:n, :], scalar1=bs_[:n, 6:7], scalar2=bs_[:n, 7:8], op0=ALU.subtract, op1=ALU.mult), reads=[r_hs, r_bs], writes=[r_hs])

                    def ep3(x):
                        r0, n, hs, r_hs, mo, r_mo = x["r0"], x["n"], x["hs"], x["r_hs"], x["mo"], x["r_mo"]
                        ao, r_ao = aor.next(); aT, r_aT = aTr.next(); p_t, r_p_t = p_tr.next()
                        P.op("pool", lambda e, hs=hs, n=n: e.tensor_tensor(out=hs[:n, :], in0=hs[:n, :], in1=mng128[:n, :], op=ALU.mult), reads=[r_hs, r_mng128], writes=[r_hs])
                        P.op("pool", lambda e, hs=hs, mo=mo, ao=ao, n=n: e.tensor_tensor(out=ao[:n, :], in0=hs[:n, :], in1=mo[:n, :], op=ALU.mult), reads=[r_hs, r_mo], writes=[r_ao])
                        P.op("pe", [(lambda e, dc=dc, ao=ao, p_t=p_t, n=n: e.transpose(out=p_t[:, 128 * dc:128 * dc + n], in_=ao[:n, 128 * dc:128 * dc + 128], identity=idb[:n, :n])) for dc in range(2)], reads=[r_ao, r_idb], writes=[r_p_t])
                        P.op("act", lambda e, aT=aT, p_t=p_t, n=n: e.copy(out=aT[:, :, :n], in_=p_t[:, 0:256].rearrange("p (a t) -> p a t", a=2)[:, :, :n]), reads=[r_p_t], writes=[r_aT])
                        if r0 == 0:
                            P.dma("sp", lambda e, aT=aT: e.dma_start(out=cc_in[0:256, 0:80].rearrange("(a p) t -> p a t", p=128), in_=aT[:, :, 48:128]), reads=[r_aT], writes=[r_cc_in])
                        else:
                            pos0 = r0 - 48
                            P.dma("sp", lambda e, aT=aT, pos0=pos0, n=n: e.dma_start(out=cc_in[0:256, pos0:pos0 + n].rearrange("(a p) t -> p a t", p=128), in_=aT[:, :, :n]), reads=[r_aT], writes=[r_cc_in])

                    if "A3" in STAGES:
                        xs = {}
                        nb = len(eblocks)
                        for i in range(nb + 2):
                            if i < nb:
                                xs[i] = ep1(i)
                            if 0 <= i - 1 < nb:
                                ep2(xs[i - 1])
                            if 0 <= i - 2 < nb:
                                ep3(xs.pop(i - 2))

            P.fence()
            if "X" in STAGES:
                exchange_half(0)
            if DEBUG:
                P.stopped = False
                dbg_holder.append(P.dma("pool", lambda e: e.dma_start(out=dbg_cc[:, :], in_=cc_in[:, :]), reads=[r_cc_in, r_cc_in_b]))
                stop_here(STOP_AT)

            if "B" in STAGES:
                TS = [(342 * i, 342) for i in range(3)]
                with ExitStack() as sB:
                    hT = sb("hT", [128, 16, NW], F32, sB); r_hT = Res()
                    uT2 = sb("uT2", [128, 16, NW], BF16, sB); r_uT2 = Res()
                    g2t = sb("g2t", [128, 16], F32, sB); r_g2t = Res()
                    P.dma("act", lambda e: e.dma_start(out=g2t[:], in_=g2c[:, :]), writes=[r_g2t])
                    onesf = sb("onesf", [128, 128], F32, sB); r_onesf = Res()
                    P.op("dve", lambda e: e.memset(onesf[:], 1.0), writes=[r_onesf])
                    with ExitStack() as sB1:
                        abT = sb("abT", [128, 16, NW], BF16, sB1); r_abT = Res()
                        mT = sb("mT", [128, 16, NW], BF16, sB1); r_mT = Res()
                        rank_cache = {}
                        for k in range(16):
                            def load_ab(e, k=k):
                                if "r" not in rank_cache:
                                    rank_cache["r"] = e.partition_id() % 4
                                rank = rank_cache["r"]
                                return e.dma_start(out=abT[:, k:k + 1, :], in_=cc_out.rearrange("(j r) t -> r j t", j=4)[k * 128:(k + 1) * 128, bass.ds(rank, 1), :])
                            P.dma("pool", load_ab, reads=[r_cc_out[0 if k < 8 else 1]], writes=[r_abT])
                        with ExitStack() as sB0:
                            g1t_b = sb("g1t2", [128, D], F32, sB0); r_g1t_b = Res()
                            P.dma("act", lambda e: e.dma_start(out=g1t_b[:], in_=g1b[:, :]), writes=[r_g1t_b])
                            xring_b = Ring([sb(f"xw{i}", [128, D], F32, sB0) for i in range(2)])
                            junk_b = sb("junk2", [128, D], BF16, sB0); r_junk_b = Res()
                            ss_b = sb("ss2", [128, 1], F32, sB0); r_ss_b = Res()
                            u_b = sb("u2", [128, D], BF16, sB0); r_u_b = Res()
                            pT_b = ps("pT2", [128, D], BF16, sB0); r_pT_b = Res()
                            pX = [ps(f"pX{i}", [128, 1024], F32, sB0) for i in range(2)]; r_pX = [Res(), Res()]
                            for j in range(9):
                                bs = 128 if j < 8 else 2
                                xt_b, r_xt_b = xring_b.next()
                                P.dma("sp", lambda e, xt_b=xt_b, bs=bs, j=j: e.dma_start(out=xt_b[:bs, :], in_=xwin[128 * j:128 * j + bs, :]), writes=[r_xt_b])
                                P.op("act", lambda e, xt_b=xt_b, bs=bs: e.activation(out=junk_b[:bs, :], in_=xt_b[:bs, :], func=AF.Square, accum_out=ss_b[:bs, :]), reads=[r_xt_b], writes=[r_junk_b, r_ss_b])
                                P.op("dve", lambda e, bs=bs: e.tensor_scalar(out=ss_b[:bs, :], in0=ss_b[:bs, :], scalar1=1.0 / D, scalar2=EPS, op0=ALU.mult, op1=ALU.add), reads=[r_ss_b], writes=[r_ss_b])
                                P.op("act", lambda e, bs=bs: e.activation(out=ss_b[:bs, :], in_=ss_b[:bs, :], func=AF.Sqrt), reads=[r_ss_b], writes=[r_ss_b])
                                P.op("dve", lambda e, bs=bs: e.reciprocal(out=ss_b[:bs, :], in_=ss_b[:bs, :]), reads=[r_ss_b], writes=[r_ss_b])
                                P.op("dve", lambda e, xt_b=xt_b, bs=bs: e.scalar_tensor_tensor(out=u_b[:bs, :], in0=xt_b[:bs, :], scalar=ss_b[:bs, 0:1], in1=g1t_b[:bs, :], op0=ALU.mult, op1=ALU.mult), reads=[r_xt_b, r_ss_b, r_g1t_b], writes=[r_u_b])
                                P.op("pe", [(lambda e, k=k, bs=bs: e.transpose(out=pT_b[:, k * 128:k * 128 + bs], in_=u_b[:bs, k * 128:(k + 1) * 128], identity=idb[:bs, :bs])) for k in range(16)], reads=[r_u_b, r_idb], writes=[r_pT_b])
                                P.op("act", lambda e, j=j, bs=bs: e.copy(out=uT2[:, :, 128 * j:128 * j + bs], in_=pT_b[:].rearrange("p (k n) -> p k n", k=16)[:, :, :bs]), reads=[r_pT_b], writes=[r_uT2])
                                for hh in range(2):
                                    P.op("pe", [(lambda e, k=k, hh=hh, xt_b=xt_b, bs=bs: e.transpose(out=pX[hh][:, (k % 8) * 128:(k % 8) * 128 + bs], in_=xt_b[:bs, k * 128:(k + 1) * 128], identity=idf[:bs, :bs])) for k in range(8 * hh, 8 * hh + 8)], reads=[r_xt_b, r_idf], writes=[r_pX[hh]])
                                    P.op("dve", lambda e, hh=hh, j=j, bs=bs: e.tensor_copy(out=hT[:, 8 * hh:8 * hh + 8, 128 * j:128 * j + bs], in_=pX[hh][:].rearrange("p (k n) -> p k n", k=8)[:, :, :bs]), reads=[r_pX[hh]], writes=[r_hT])

                        P.fence()
                        with ExitStack() as sB1b:
                            wgr = Ring([sb(f"wg{i}", [128, 16, 256], BF16, sB1b) for i in range(2)])
                            war = Ring([sb(f"wa{i}", [128, 8, 256], BF16, sB1b) for i in range(2)])
                            sgm = sb("sgm", [128, 342], F32, sB1b); r_sgm = Res()
                            sgd = sb("sgd", [128, 342], F32, sB1b); r_sgd = Res()
                            tA = sb("tA", [128, 342], F32, sB1b); r_tA = Res()
                            tB = sb("tB", [128, 342], F32, sB1b); r_tB = Res()
                            pq = [Ring([ps(f"pq{q}{i}", [128, 512], F32, sB1b) for i in range(2)]) for q in range(4)]
                            for c in range(16):
                                wg, r_wg = wgr.next()
                                P.dma("pool", lambda e, wg=wg, c=c: e.dma_start(out=wg[:, :, 0:128], in_=w_g[:, 128 * c:128 * c + 128].rearrange("(k p) n -> p k n", p=128)), writes=[r_wg])
                                for (t0, tn) in TS:
                                    p0_, r0_ = pq[0].next()
                                    P.op("pe", [(lambda e, k=k, p0_=p0_, wg=wg, t0=t0, tn=tn: e.matmul(out=p0_[:, :tn], lhsT=wg[:, k, 0:128], rhs=uT2[:, k, t0:t0 + tn], start=(k == 0), stop=(k == 15))) for k in range(16)], reads=[r_wg, r_uT2], writes=[r0_])
                                    P.op("act", lambda e, p0_=p0_, c=c, t0=t0, tn=tn: e.activation(out=mT[:, c, t0:t0 + tn], in_=p0_[:, :tn], func=AF.Sigmoid), reads=[r0_], writes=[r_mT])
                            for c in range(16):
                                wg, r_wg = wgr.next()
                                wa, r_wa = war.next()
                                P.dma("pool", lambda e, wg=wg, c=c: e.dma_start(out=wg[:, :, 128:256], in_=w_g[:, 2048 + 128 * c:2048 + 128 * c + 128].rearrange("(k p) n -> p k n", p=128)), writes=[r_wg])
                                P.dma("pool", lambda e, wa=wa, c=c: e.dma_start(out=wa[:, :, 0:128], in_=w_a[:, 128 * c:128 * c + 128].rearrange("(k p) n -> p k n", p=128)), writes=[r_wa])
                                P.dma("pool", lambda e, wa=wa, c=c: e.dma_start(out=wa[:, :, 128:256], in_=w_b[:, 128 * c:128 * c + 128].rearrange("(k p) n -> p k n", p=128)), writes=[r_wa])
                                for (t0, tn) in TS:
                                    p1_, r1_ = pq[1].next(); p2_, r2_ = pq[2].next(); p3_, r3_ = pq[3].next()
                                    P.op("pe", [(lambda e, k=k, p2_=p2_, wg=wg, t0=t0, tn=tn: e.matmul(out=p2_[:, :tn], lhsT=wg[:, k, 128:256], rhs=uT2[:, k, t0:t0 + tn], start=(k == 0), stop=(k == 15))) for k in range(16)], reads=[r_wg, r_uT2], writes=[r2_])
                                    P.op("pe", [(lambda e, k=k, p1_=p1_, wa=wa, t0=t0, tn=tn: e.matmul(out=p1_[:, :tn], lhsT=wa[:, k, 0:128], rhs=abT[:, k, t0:t0 + tn], start=(k == 0), stop=(k == 7))) for k in range(8)], reads=[r_wa, r_abT], writes=[r1_])
                                    P.op("pe", [(lambda e, k=k, p3_=p3_, wa=wa, t0=t0, tn=tn: e.matmul(out=p3_[:, :tn], lhsT=wa[:, k, 128:256], rhs=abT[:, 8 + k, t0:t0 + tn], start=(k == 0), stop=(k == 7))) for k in range(8)], reads=[r_wa, r_abT], writes=[r3_])
                                    P.op("act", lambda e, p2_=p2_, tn=tn: e.activation(out=sgd[:, :tn], in_=p2_[:, :tn], func=AF.Sigmoid), reads=[r2_], writes=[r_sgd])
                                    P.op("dve", lambda e, p1_=p1_, c=c, t0=t0, tn=tn: e.tensor_tensor(out=tA[:, :tn], in0=p1_[:, :tn], in1=mT[:, c, t0:t0 + tn], op=ALU.mult), reads=[r1_, r_mT], writes=[r_tA])
                                    P.op("dve", lambda e, p3_=p3_, tn=tn: e.tensor_tensor(out=tB[:, :tn], in0=p3_[:, :tn], in1=sgd[:, :tn], op=ALU.mult), reads=[r3_, r_sgd], writes=[r_tB])
                                    P.op("dve", lambda e, c=c, t0=t0, tn=tn: e.tensor_tensor(out=mT[:, c, t0:t0 + tn], in0=tA[:, :tn], in1=tB[:, :tn], op=ALU.add), reads=[r_tA, r_tB], writes=[r_mT])
                        P.fence()
                        with ExitStack() as sB2:
                            wor = Ring([sb(f"wo{i}", [128, 16, 128], BF16, sB2) for i in range(2)])
                            po = Ring([ps(f"po{i}", [128, 512], F32, sB2) for i in range(4)])
                            for c in range(16):
                                wo, r_wo = wor.next()
                                P.dma("pool", lambda e, wo=wo, c=c: e.dma_start(out=wo[:], in_=w_out[:, 128 * c:128 * c + 128].rearrange("(k p) n -> p k n", p=128)), writes=[r_wo])
                                for (t0, tn) in TS:
                                    pp, rp = po.next()
                                    P.op("pe", [(lambda e, k=k, pp=pp, wo=wo, t0=t0, tn=tn: e.matmul(out=pp[:, :tn], lhsT=wo[:, k, :], rhs=mT[:, k, t0:t0 + tn], start=(k == 0), stop=(k == 15))) for k in range(16)], reads=[r_wo, r_mT], writes=[rp])
                                    P.op("dve", lambda e, pp=pp, c=c, t0=t0, tn=tn: e.tensor_tensor(out=hT[:, c, t0:t0 + tn], in0=hT[:, c, t0:t0 + tn], in1=pp[:, :tn], op=ALU.add), reads=[rp, r_hT], writes=[r_hT])

                    P.fence()
                    with ExitStack() as sB3:
                        sq = Ring([sb(f"sq{i}", [128, 342], F32, sB3) for i in range(2)])
                        rstd = sb("rstd", [128, NW], F32, sB3); r_rstd = Res()
                        wm = sb("wm", [128, NW], F32, sB3); r_wm = Res()
                        P.dma("act", lambda e: e.dma_start(out=wm[:], in_=wmask_d[:, :]), writes=[r_wm])
                        pss3 = [ps(f"pss3{i}", [128, 512], F32, sB3) for i in range(3)]; r_pss3 = [Res() for _ in range(3)]
                        for ti, (t0, tn) in enumerate(TS):
                            fns = []
                            for c in range(16):
                                s_, r_s = sq.next()
                                P.op("act", lambda e, s_=s_, c=c, t0=t0, tn=tn: e.activation(out=s_[:, :tn], in_=hT[:, c, t0:t0 + tn], func=AF.Square), reads=[r_hT], writes=[r_s])
                                P.op("pe", lambda e, s_=s_, c=c, ti=ti, tn=tn: e.matmul(out=pss3[ti][:, :tn], lhsT=onesf[:], rhs=s_[:, :tn], start=(c == 0), stop=(c == 15), skip_group_check=True), reads=[r_s, r_onesf], writes=[r_pss3[ti]])
                            P.op("dve", lambda e, ti=ti, t0=t0, tn=tn: e.tensor_scalar(out=rstd[:, t0:t0 + tn], in0=pss3[ti][:, :tn], scalar1=1.0 / D, scalar2=EPS, op0=ALU.mult, op1=ALU.add), reads=[r_pss3[ti]], writes=[r_rstd])
                        P.op("act", lambda e: e.activation(out=rstd[:], in_=rstd[:], func=AF.Sqrt), reads=[r_rstd], writes=[r_rstd])
                        P.op("dve", lambda e: e.reciprocal(out=rstd[:], in_=rstd[:]), reads=[r_rstd], writes=[r_rstd])
                        P.op("dve", lambda e: e.tensor_tensor(out=rstd[:], in0=rstd[:], in1=wm[:], op=ALU.mult), reads=[r_rstd, r_wm], writes=[r_rstd])
                        for c in range(16):
                            P.op("dve", lambda e, c=c: e.scalar_tensor_tensor(out=uT2[:, c, :], in0=hT[:, c, :], scalar=g2t[:, c:c + 1], in1=rstd[:], op0=ALU.mult, op1=ALU.mult), reads=[r_hT, r_g2t, r_rstd], writes=[r_uT2])

                    P.fence()
                    with ExitStack() as sB4:
                        fcw = sb("fcw", [128, 264], F32, sB4); r_fcw = Res()
                        P.dma("act", lambda e: e.dma_start(out=fcw[:], in_=fconv[:, :]), writes=[r_fcw])
                        actT = sb("actT", [128, 22, 1024], BF16, sB4); r_actT = Res()
                        wur = Ring([sb(f"wu{i}", [128, 16, 256], BF16, sB4) for i in range(2)])
                        wdr = Ring([sb(f"wd{i}", [128, 22, 128], BF16, sB4) for i in range(2)])
                        upg = sb("upg", [128, NW], F32, sB4); r_upg = Res()
                        upv = sb("upv", [128, NW], F32, sB4); r_upv = Res()
                        cg = sb("cg", [128, 1024], F32, sB4); r_cg = Res()
                        cv = sb("cv", [128, 1024], F32, sB4); r_cv = Res()
                        sgl = sb("sgl", [128, 1024], F32, sB4); r_sgl = Res()
                        pu = Ring([ps(f"pu{i}", [128, 512], F32, sB4) for i in range(4)])
                        pd = Ring([ps(f"pd{i}", [128, 512], F32, sB4) for i in range(4)])
                        for half in range(2):
                            for fc in range(22):
                                f = half * 22 + fc
                                wu, r_wu = wur.next()
                                P.dma("pool", lambda e, wu=wu, f=f: e.dma_start(out=wu[:, :, 0:128], in_=w_up[:, 128 * f:128 * f + 128].rearrange("(k p) n -> p k n", p=128)), writes=[r_wu])
                                P.dma("pool", lambda e, wu=wu, f=f: e.dma_start(out=wu[:, :, 128:256], in_=w_up[:, FFN + 128 * f:FFN + 128 * f + 128].rearrange("(k p) n -> p k n", p=128)), writes=[r_wu])
                                for (t0, tn) in TS:
                                    pg_, rg_ = pu.next()
                                    P.op("pe", [(lambda e, k=k, pg_=pg_, wu=wu, t0=t0, tn=tn: e.matmul(out=pg_[:, :tn], lhsT=wu[:, k, 0:128], rhs=uT2[:, k, t0:t0 + tn], start=(k == 0), stop=(k == 15))) for k in range(16)], reads=[r_wu, r_uT2], writes=[rg_])
                                    P.op("act", lambda e, pg_=pg_, t0=t0, tn=tn: e.copy(out=upg[:, t0:t0 + tn], in_=pg_[:, :tn]), reads=[rg_], writes=[r_upg])
                                    pv_, rv_ = pu.next()
                                    P.op("pe", [(lambda e, k=k, pv_=pv_, wu=wu, t0=t0, tn=tn: e.matmul(out=pv_[:, :tn], lhsT=wu[:, k, 128:256], rhs=uT2[:, k, t0:t0 + tn], start=(k == 0), stop=(k == 15))) for k in range(16)], reads=[r_wu, r_uT2], writes=[rv_])
                                    P.op("act", lambda e, pv_=pv_, t0=t0, tn=tn: e.copy(out=upv[:, t0:t0 + tn], in_=pv_[:, :tn]), reads=[rv_], writes=[r_upv])
                                for (src, r_src, dst, r_dst, ci) in ((upg, r_upg, cg, r_cg, f), (upv, r_upv, cv, r_cv, 44 + f)):
                                    P.op("dve", lambda e, src=src, dst=dst, ci=ci: e.tensor_scalar(out=dst[:], in0=src[:, 0:1024], scalar1=fcw[:, 3 * ci:3 * ci + 1], scalar2=None, op0=ALU.mult), reads=[r_src, r_fcw], writes=[r_dst])
                                    P.op("dve", lambda e, src=src, dst=dst, ci=ci: e.scalar_tensor_tensor(out=dst[:], in0=src[:, 1:1025], scalar=fcw[:, 3 * ci + 1:3 * ci + 2], in1=dst[:], op0=ALU.mult, op1=ALU.add), reads=[r_src, r_fcw, r_dst], writes=[r_dst])
                                    P.op("dve", lambda e, src=src, dst=dst, ci=ci: e.scalar_tensor_tensor(out=dst[:], in0=src[:, 2:1026], scalar=fcw[:, 3 * ci + 2:3 * ci + 3], in1=dst[:], op0=ALU.mult, op1=ALU.add), reads=[r_src, r_fcw, r_dst], writes=[r_dst])
                                P.op("act", lambda e: e.activation(out=sgl[:], in_=cg[:], func=AF.Silu), reads=[r_cg], writes=[r_sgl])
                                P.op("dve", lambda e, fc=fc: e.tensor_tensor(out=actT[:, fc, :], in0=sgl[:], in1=cv[:], op=ALU.mult), reads=[r_sgl, r_cv], writes=[r_actT])
                            for c in range(16):
                                wd, r_wd = wdr.next()
                                P.dma("pool", lambda e, wd=wd, c=c, half=half: e.dma_start(out=wd[:], in_=w_down[2816 * half:2816 * half + 2816, 128 * c:128 * c + 128].rearrange("(k p) n -> p k n", p=128)), writes=[r_wd])
                                for t2_ in range(2):
                                    pp, rp = pd.next()
                                    P.op("pe", [(lambda e, k=k, pp=pp, wd=wd, t2_=t2_: e.matmul(out=pp[:, :], lhsT=wd[:, k, :], rhs=actT[:, k, 512 * t2_:512 * t2_ + 512], start=(k == 0), stop=(k == 21))) for k in range(22)], reads=[r_wd, r_actT], writes=[rp])
                                    P.op("dve", lambda e, pp=pp, c=c, t2_=t2_: e.tensor_tensor(out=hT[:, c, 1 + 512 * t2_:1 + 512 * t2_ + 512], in0=hT[:, c, 1 + 512 * t2_:1 + 512 * t2_ + 512], in1=pp[:, :], op=ALU.add), reads=[rp, r_hT], writes=[r_hT])

                    P.fence()
                    with ExitStack() as sB5:
                        gft = sb("gft", [128, D], F32, sB5); r_gft = Res()
                        P.dma("act", lambda e: e.dma_start(out=gft[:], in_=gfb[:, :]), writes=[r_gft])
                        pF = [ps(f"pF{i}", [128, 1024], F32, sB5) for i in range(2)]; r_pF = [Res(), Res()]
                        oring = Ring([sb(f"ob{i}", [128, D], F32, sB5) for i in range(2)])
                        fj = sb("fj", [128, 1024], F32, sB5); r_fj = Res()
                        fs = sb("fs", [128, 4], F32, sB5); r_fs = Res()
                        for j in range(8):
                            for hh in range(2):
                                P.op("pe", [(lambda e, k=k, hh=hh, j=j: e.transpose(out=pF[hh][:, (k % 8) * 128:(k % 8) * 128 + 128], in_=hT[:, k, 1 + 128 * j:1 + 128 * j + 128], identity=idf[:])) for k in range(8 * hh, 8 * hh + 8)], reads=[r_hT, r_idf], writes=[r_pF[hh]])
                                P.op("act", lambda e, hh=hh: e.activation(out=fj[:], in_=pF[hh][:], func=AF.Square, accum_out=fs[:, hh:hh + 1]), reads=[r_pF[hh]], writes=[r_fj, r_fs])
                            P.op("dve", lambda e: e.tensor_tensor(out=fs[:, 2:3], in0=fs[:, 0:1], in1=fs[:, 1:2], op=ALU.add), reads=[r_fs], writes=[r_fs])
                            P.op("dve", lambda e: e.tensor_scalar(out=fs[:, 2:3], in0=fs[:, 2:3], scalar1=1.0 / D, scalar2=EPS, op0=ALU.mult, op1=ALU.add), reads=[r_fs], writes=[r_fs])
                            P.op("act", lambda e: e.activation(out=fs[:, 2:3], in_=fs[:, 2:3], func=AF.Sqrt), reads=[r_fs], writes=[r_fs])
                            P.op("dve", lambda e: e.reciprocal(out=fs[:, 3:4], in_=fs[:, 2:3]), reads=[r_fs], writes=[r_fs])
                            ob, r_ob = oring.next()
                            for hh in range(2):
                                P.op("dve", lambda e, hh=hh, ob=ob: e.scalar_tensor_tensor(out=ob[:, 1024 * hh:1024 * hh + 1024], in0=pF[hh][:], scalar=fs[:, 3:4], in1=gft[:, 1024 * hh:1024 * hh + 1024], op0=ALU.mult, op1=ALU.mult), reads=[r_pF[hh], r_fs, r_gft], writes=[r_ob])
                            final_toks.append(P.dma("sp", lambda e, ob=ob, j=j: e.dma_start(out=out_d[128 * j:128 * j + 128, :], in_=ob[:]), reads=[r_ob]))
        except _Stop:
            pass
        if True:
            if DEBUG and dbg_holder:
                final_toks.append(dbg_holder[0])
            P.finish(final_toks)
    return nc


_NC_CACHE = {}


def _rope_tables():
    inv_freq = (500000.0 ** (-np.arange(0, 16, 2, dtype=np.float32) / 16)).astype(np.float32)
    ang = np.arange(L, dtype=np.float32)[:, None] * inv_freq[None, :]
    cos = np.cos(ang).astype(np.float32).T
    sin = np.sin(ang).astype(np.float32).T
    cosF = np.ones((128, L), np.float32)
    sinF = np.zeros((128, L), np.float32)
    for mp in range(2):
        b0 = 64 * mp
        cosF[b0:b0 + 8] = cos
        cosF[b0 + 8:b0 + 16] = cos
        sinF[b0:b0 + 8] = -sin
        sinF[b0 + 8:b0 + 16] = sin
    return cosF, sinF


def kernel(x, meta_tokens, norm1_g, w_in, mlstm_conv_w, mlstm_gate_bias, mlstm_norm_g,
           lambda_q1, lambda_k1, lambda_q2, lambda_k2, diff_subln_g, w_branch_m, w_branch_d,
           w_out, norm2_g, w_up, ffn_conv_w, w_down, norm_f_g):
    f32 = np.float32
    x = np.asarray(x, f32)
    w_in0 = np.asarray(w_in, f32)[0]
    B = x.shape[0]
    cosF, sinF = _rope_tables()
    o_mqk, o_mv, o_mo, o_gates, o_dq, o_dk, o_dv, o_gm = 0, 2048, 3072, 4096, 4112, 5136, 6160, 7184
    rotperm = np.arange(128)
    for mp in range(2):
        b0 = 64 * mp
        rotperm[b0:b0 + 8] = np.arange(b0 + 8, b0 + 16)
        rotperm[b0 + 8:b0 + 16] = np.arange(b0, b0 + 8)
    ident = np.eye(128, dtype=f32)
    triu = np.triu(np.ones((64, 64), f32))
    tril = np.tril(np.ones((64, 64), f32))
    common = {
        "w_g": np.ascontiguousarray(w_in0[:, o_gm:o_gm + 4096]),
        "w_a": np.ascontiguousarray(np.asarray(w_branch_m, f32)[0]),
        "w_b": np.ascontiguousarray(np.asarray(w_branch_d, f32)[0]),
        "w_out": np.ascontiguousarray(np.asarray(w_out, f32)[0]),
        "w_up": np.ascontiguousarray(np.asarray(w_up, f32)[0]),
        "w_down": np.ascontiguousarray(np.asarray(w_down, f32)[0]),
        "g1b": np.ascontiguousarray(np.broadcast_to(np.asarray(norm1_g, f32)[0], (128, D))),
        "gfb": np.ascontiguousarray(np.broadcast_to(np.asarray(norm_f_g, f32), (128, D))),
        "g2c": np.ascontiguousarray(np.asarray(norm2_g, f32)[0].reshape(16, 128).T),
        "cosf": cosF, "sinf": sinF,
        "fconv": np.ascontiguousarray(np.asarray(ffn_conv_w, f32)[0].reshape(3, 88, 128).transpose(2, 1, 0).reshape(128, 264)),
        "lamv": np.ascontiguousarray(np.broadcast_to(np.concatenate([np.asarray(a, f32)[0] for a in (lambda_q1, lambda_k1, lambda_q2, lambda_k2)]), (128, 256))),
        "sublng": np.ascontiguousarray(np.broadcast_to(np.asarray(diff_subln_g, f32)[0], (128, 128))),
        "ident": ident, "triu": triu, "tril": tril,
    }
    if "B" not in STAGES:
        for nm in ("w_g", "w_a", "w_b", "w_out", "w_up", "w_down"):
            common[nm] = np.zeros((128, 128), f32)
    in_maps = []
    mcw_full = np.asarray(mlstm_conv_w, f32)[0]
    gb_full = np.asarray(mlstm_gate_bias, f32)[0]
    for c in range(8):
        b, g = c // 4, c % 4
        hfull = np.concatenate([np.asarray(meta_tokens, f32), x[b]], axis=0)
        s0 = 15 + 1024 * g
        xwin = np.zeros((NW, D), f32)
        e0 = min(s0 + NW, L)
        xwin[:e0 - s0] = hfull[s0:e0]
        wmask = np.ones((128, NW), f32)
        if e0 - s0 < NW:
            wmask[:, e0 - s0:] = 0.0
        cols = []
        for base in (o_dq, o_dk):
            for hh in range(2):
                head = 2 * g + hh
                cols.append(base + 128 * head + np.arange(128))
            for hh in range(2):
                head = 2 * g + hh
                cols.append(base + 128 * head + rotperm)
        for hh in range(2):
            head = 2 * g + hh
            cols.append(o_dv + 128 * head + np.arange(128))
        w_attn = np.ascontiguousarray(w_in0[:, np.concatenate(cols)])
        qc = o_mqk + 256 * g + np.arange(256)
        kc = o_mqk + 1024 + 256 * g + np.arange(256)
        vc = o_mv + 256 * g + np.arange(256)
        oc = o_mo + 256 * g + np.arange(256)
        gc = o_gates + np.array([4 + g, 12 + g, 0 + g, 8 + g])
        w_ml = np.ascontiguousarray(w_in0[:, np.concatenate([qc, kc, vc, oc, gc])])
        mconv = np.ascontiguousarray(mcw_full[:, np.concatenate([qc, kc])].reshape(3, 4, 128).transpose(2, 1, 0).reshape(128, 12))
        gbias = np.ascontiguousarray(np.broadcast_to(gb_full[[4 + g, 12 + g, 0 + g, 8 + g]], (128, 4)))
        mngv = np.ascontiguousarray(np.broadcast_to(np.asarray(mlstm_norm_g, f32)[0][256 * g:256 * g + 256], (64, 256)))
        m = dict(common)
        m.update({"hfull": hfull, "xwin": xwin, "w_attn": w_attn, "w_ml": w_ml, "mconv": mconv,
                  "gbias": gbias, "mng": mngv, "wmask": wmask})
        in_maps.append(m)
    if "nc" not in _NC_CACHE:
        _NC_CACHE["nc"] = build_program()
    nc = _NC_CACHE["nc"]
    res = run_bass_kernel_spmd(nc, in_maps, core_ids=list(range(8)))
    out = np.empty((B, 4096, D), f32)
    for c in range(8):
        b, g = c // 4, c % 4
        out[b, 1024 * g:1024 * g + 1024] = res.results[c]["out"]
    if DEBUG:
        kernel.dbg = [res.results[c] for c in range(8)]
    return out
```

```python
import numpy as np
from contextlib import ExitStack
import concourse.bass as bass
import concourse.mybir as mybir
from concourse.bass_utils import run_bass_kernel_spmd

F32 = mybir.dt.float32
BF16 = mybir.dt.bfloat16
AF = mybir.ActivationFunctionType
ALU = mybir.AluOpType
AX = mybir.AxisListType

SAME_ENGINE_SYNC = True
DEBUG = False
STAGES = ("A1", "A2", "A3", "X", "B")

D = 2048
L = 4112
LP = 4160
NMETA = 16
NW = 1026
FFN = 5632
EPS = 1e-6


class _Stop(Exception):
    pass


STOP_AT = None


_PROG = []


def stop_here(tag):
    if STOP_AT == tag:
        _PROG[0].stopped = True


_FENCE = []


class Res:
    __slots__ = ("name", "w", "r")

    def __init__(self, name=""):
        self.name = name
        self.w = None
        self.r = dict(_FENCE)


class Prog:
    ENGS = ("pe", "act", "dve", "pool", "sp")

    def __init__(self, nc, stack, n_dma_sems=8):
        self.nc = nc
        self.stack = stack
        self.streams = {e: [] for e in self.ENGS}
        self.sems = {}
        self.count = {}
        self.waited = {e: {} for e in self.ENGS}
        for e in self.ENGS:
            self.sems["c_" + e] = stack.enter_context(nc.semaphore("c_" + e))
            self.count["c_" + e] = 0
        self.dma_pool = {}
        self.dma_next = {}
        for e in ("sp", "act", "pool"):
            keys = []
            for i in range(n_dma_sems):
                k = f"d_{e}{i}"
                self.sems[k] = stack.enter_context(nc.semaphore(k))
                self.count[k] = 0
                keys.append(k)
            self.dma_pool[e] = keys
            self.dma_next[e] = 0
        self.sems["cc"] = stack.enter_context(nc.semaphore("cc"))
        self.count["cc"] = 0
        self.stopped = False
        _PROG[:] = [self]
        _FENCE[:] = []

    def _need(self, eng, dep):
        if dep is None:
            return
        key, val = dep
        if key == "c_" + eng and (eng == "pe" or not SAME_ENGINE_SYNC):
            return
        if self.waited[eng].get(key, 0) >= val:
            return
        self.waited[eng][key] = val
        self.streams[eng].append(("wait", key, val))

    def _deps(self, eng, reads, writes):
        own = "c_" + eng
        for r in reads:
            self._need(eng, r.w)
            for k, v in r.r.items():
                if k != own:
                    self._need(eng, (k, v))
        for w in writes:
            self._need(eng, w.w)
            for k, v in w.r.items():
                self._need(eng, (k, v))

    def _commit(self, reads, writes, tok):
        for r in reads:
            if r.r.get(tok[0], 0) < tok[1]:
                r.r[tok[0]] = tok[1]
        for w in writes:
            w.w = tok
            w.r = {}

    def op(self, eng, fns, reads=(), writes=()):
        if self.stopped:
            return None
        if not isinstance(fns, (list, tuple)):
            fns = [fns]
        self._deps(eng, reads, writes)
        key = "c_" + eng
        self.count[key] += 1
        tok = (key, self.count[key])
        self.streams[eng].append(("op", fns, key, 1))
        self._commit(reads, writes, tok)
        return tok

    def dma(self, eng, fn, reads=(), writes=()):
        if self.stopped:
            return None
        pool = self.dma_pool[eng]
        key = pool[self.dma_next[eng] % len(pool)]
        self.dma_next[eng] += 1
        if self.count[key] > 0:
            self._need(eng, (key, self.count[key]))
        self._deps(eng, reads, writes)
        self.count[key] += 16
        tok = (key, self.count[key])
        self.streams[eng].append(("op", [fn], key, 16))
        self._commit(reads, writes, tok)
        return tok

    def fence(self):
        _FENCE[:] = [(k, c) for k, c in self.count.items() if c > 0 and k != "cc"]

    def cc(self, fn, reads=(), writes=()):
        if self.stopped:
            return None
        eng = "pool"
        self._deps(eng, reads, writes)
        self.count["cc"] += 1
        tok = ("cc", self.count["cc"])
        self.streams[eng].append(("cc", fn, "cc"))
        self._commit(reads, writes, tok)
        return tok

    def finish(self, final_tokens):
        for t in final_tokens:
            self._need("sp", t)
        for k, c in self.count.items():
            if c > 0:
                self._need("sp", (k, c))
        nc = self.nc
        with nc.Block() as block:
            def mk(ename):
                def body(e):
                    for item in self.streams[ename]:
                        if item[0] == "wait":
                            e.wait_ge(self.sems[item[1]], item[2])
                        elif item[0] == "op":
                            fns, key, inc = item[1], item[2], item[3]
                            for f in fns[:-1]:
                                f(e)
                            fns[-1](e).then_inc(self.sems[key], inc)
                        elif item[0] == "cc":
                            item[1](e).then_inc(self.sems[item[2]])
                return body
            block.tensor(mk("pe"))
            block.scalar(mk("act"))
            block.vector(mk("dve"))
            block.gpsimd(mk("pool"))
            block.sync(mk("sp"))


class Ring:
    def __init__(self, tiles):
        self.tiles = tiles
        self.res = [Res() for _ in tiles]
        self.i = 0

    def next(self):
        k = self.i % len(self.tiles)
        self.i += 1
        return self.tiles[k], self.res[k]


def build_program():
    nc = bass.Bass("TRN2", target_bir_lowering=False)
    dt_in = lambda name, shape, dt=F32: nc.dram_tensor(name, shape, dt, kind="ExternalInput").ap()
    dt_int = lambda name, shape, dt: nc.dram_tensor(name, shape, dt, kind="Internal").ap()

    hfull = dt_in("hfull", [L, D])
    xwin = dt_in("xwin", [NW, D])
    w_attn = dt_in("w_attn", [D, 1280])
    w_ml = dt_in("w_ml", [D, 1028])
    w_g = dt_in("w_g", [D, 4096] if "B" in STAGES else [128, 128])
    w_a = dt_in("w_a", [1024, D] if "B" in STAGES else [128, 128])
    w_b = dt_in("w_b", [1024, D] if "B" in STAGES else [128, 128])
    w_out = dt_in("w_out", [D, D] if "B" in STAGES else [128, 128])
    w_up = dt_in("w_up", [D, 2 * FFN] if "B" in STAGES else [128, 128])
    w_down = dt_in("w_down", [FFN, D] if "B" in STAGES else [128, 128])
    g1b = dt_in("g1b", [128, D])
    gfb = dt_in("gfb", [128, D])
    g2c = dt_in("g2c", [128, 16])
    cosf = dt_in("cosf", [128, L])
    sinf = dt_in("sinf", [128, L])
    mconv = dt_in("mconv", [128, 12])
    fconv = dt_in("fconv", [128, 88 * 3])
    gbias = dt_in("gbias", [128, 4])
    mng = dt_in("mng", [64, 256])
    lamv = dt_in("lamv", [128, 4 * 64])
    sublng = dt_in("sublng", [128, 128])
    ident_d = dt_in("ident", [128, 128])
    triu_d = dt_in("triu", [64, 64])
    tril_d = dt_in("tril", [64, 64])
    wmask_d = dt_in("wmask", [128, NW])
    out_d = nc.dram_tensor("out", [1024, D], F32, kind="ExternalOutput").ap()
    if DEBUG:
        dbg_cc = nc.dram_tensor("dbg_cc", [512, 4114], BF16, kind="ExternalOutput").ap()

    mqk_raw = dt_int("mqk_raw", [512, LP + 2], BF16)
    mv_s = dt_int("mv_s", [LP, 257], BF16)
    mo_s = dt_int("mo_s", [LP, 256], F32)
    gates_s = dt_int("gates_s", [LP, 4], F32)
    hf_s = dt_int("hf_s", [LP, 256], F32)
    hb_s = dt_int("hb_s", [LP, 256], F32)
    cc_in = dt_int("cc_in", [512, 4114], BF16)
    cc_win = dt_int("cc_win", [8 * 256, NW], BF16)
    cc_out = dt_int("cc_out", [8 * 1024, NW], BF16)

    with ExitStack() as st:
        P = Prog(nc, st)

        def sb(name, shape, dt, stack=st):
            return stack.enter_context(nc.sbuf_tensor(name, shape, dt))

        def ps(name, shape, dt, stack=st):
            return stack.enter_context(nc.psum_tensor(name, shape, dt))

        final_toks = []
        dbg_holder = []
        try:
            idf = sb("idf", [128, 128], F32); r_idf = Res()
            idb = sb("idb", [128, 128], BF16); r_idb = Res()
            zt = sb("zt", [128, 512], BF16); r_zt = Res()
            P.dma("sp", lambda e: e.dma_start(out=idf[:], in_=ident_d[:, :]), writes=[r_idf])
            P.op("dve", lambda e: e.tensor_copy(out=idb[:], in_=idf[:]), reads=[r_idf], writes=[r_idb])
            P.op("dve", lambda e: e.memset(zt[:], 0.0), writes=[r_zt])

            r_mqk_raw = Res(); r_mv_s = Res(); r_mo_s = Res(); r_gates_s = Res(); r_hf_s = Res(); r_hb_s = Res(); r_cc_in = Res(); r_cc_in_b = Res()
            r_cc_win = [Res(), Res()]; r_cc_out = [Res(), Res()]
            P.dma("sp", lambda e: e.dma_start(out=mqk_raw.rearrange("(c p) n -> p c n", p=128)[:, :, 0:49], in_=zt[:, 0:196].rearrange("p (c n) -> p c n", c=4)), reads=[r_zt], writes=[r_mqk_raw])
            P.dma("sp", lambda e: e.dma_start(out=mqk_raw.rearrange("(c p) n -> p c n", p=128)[:, :, LP + 1:LP + 2], in_=zt[:, 0:4].rearrange("p (c n) -> p c n", c=4), allow_slow_non_contiguous=True), reads=[r_zt], writes=[r_mqk_raw])
            P.dma("sp", lambda e: e.dma_start(out=mv_s[0:48, :], in_=zt[0:48, 0:257]), reads=[r_zt], writes=[r_mv_s])
            ztf = sb("ztf", [64, 256], F32); r_ztf = Res()
            P.op("dve", lambda e: e.memset(ztf[:], 0.0), writes=[r_ztf])
            P.dma("sp", lambda e: e.dma_start(out=mo_s[0:48, :], in_=ztf[0:48, :]), reads=[r_ztf], writes=[r_mo_s])
            P.dma("sp", lambda e: e.dma_start(out=gates_s[0:48, :], in_=ztf[0:48, 0:4]), reads=[r_ztf], writes=[r_gates_s])
            P.dma("sp", lambda e: e.dma_start(out=cc_in[:, 4112:4114].rearrange("(c p) n -> p c n", p=128), in_=zt[:, 0:8].rearrange("p (c n) -> p c n", c=4)), reads=[r_zt], writes=[r_cc_in, r_cc_in_b])


            def exchange_half(half):
                r_src = r_cc_in if half == 0 else r_cc_in_b
                for j in range(4):
                    i = j * 2 + half
                    P.dma("sp", lambda e, i=i, j=j, half=half: e.dma_start(out=cc_win[i * 256:(i + 1) * 256, :], in_=cc_in[half * 256:(half + 1) * 256, 15 + 1024 * j:15 + 1024 * j + NW]), reads=[r_src], writes=[r_cc_win[half]])
                for j in range(4):
                    i = j * 2 + half
                    P.cc(lambda e, i=i: e.collective_compute("AllGather", ALU.bypass, replica_groups=[[0, 1, 2, 3], [4, 5, 6, 7]], ins=[cc_win[i * 256:(i + 1) * 256, :]], outs=[cc_out[i * 1024:(i + 1) * 1024, :]]), reads=[r_cc_win[half]], writes=[r_cc_out[half]])

            stop_here("c0")
            with ExitStack() as sa:
                dqkT = sb("dqkT", [128, 4, L], BF16, sa); r_dqkT = Res()
                vatt = sb("vatt", [128, 33, 2, 129], BF16, sa); r_vatt = Res()
                with ExitStack() as s1:
                    wat = sb("wat", [128, 16, 1280], BF16, s1); r_wat = Res()
                    wml = sb("wml", [128, 16, 1028], BF16, s1); r_wml = Res()
                    g1t = sb("g1t", [128, D], F32, s1); r_g1t = Res()
                    gbt = sb("gbt", [128, 4], F32, s1); r_gbt = Res()
                    for kq in range(4):
                        P.dma("pool", lambda e, kq=kq: e.dma_start(out=wat[:, 4 * kq:4 * kq + 4, :], in_=w_attn[512 * kq:512 * kq + 512, :].rearrange("(k p) n -> p k n", p=128)), writes=[r_wat])
                        P.dma("pool", lambda e, kq=kq: e.dma_start(out=wml[:, 4 * kq:4 * kq + 4, :], in_=w_ml[512 * kq:512 * kq + 512, :].rearrange("(k p) n -> p k n", p=128)), writes=[r_wml])
                    P.dma("act", lambda e: e.dma_start(out=g1t[:], in_=g1b[:, :]), writes=[r_g1t])
                    P.dma("act", lambda e: e.dma_start(out=gbt[:], in_=gbias[:, :]), writes=[r_gbt])
                    P.op("dve", lambda e: e.memset(vatt[:, :, :, 128:129], 1.0), writes=[r_vatt])

                    xring = Ring([sb(f"xt{i}", [128, D], F32, s1) for i in range(2)])
                    junk = sb("junk", [128, D], BF16, s1); r_junk = Res()
                    ss = sb("ss", [128, 1], F32, s1); r_ss = Res()
                    u = sb("u", [128, D], BF16, s1); r_u = Res()
                    uT = sb("uT", [128, 16, 512], BF16, s1); r_uT = Res()
                    csr = Ring([sb(f"cs{i}", [128, 2, 512], F32, s1) for i in range(2)])
                    rt1 = sb("rt1", [128, 512], F32, s1); r_rt1 = Res()
                    rt2 = sb("rt2", [128, 512], F32, s1); r_rt2 = Res()
                    mstage = Ring([sb(f"mst{i}", [128, 4, 512], BF16, s1) for i in range(2)])
                    mvst = Ring([sb(f"mvst{i}", [128, 257], BF16, s1) for i in range(2)])
                    most = Ring([sb(f"most{i}", [128, 256], F32, s1) for i in range(2)])
                    gst = Ring([sb(f"gst{i}", [128, 4], F32, s1) for i in range(2)])
                    pT = ps("pT", [128, D], BF16, s1); r_pT = Res()
                    pa = ps("pa", [128, 512], F32, s1); r_pa = Res()
                    pb = ps("pb", [128, 512], F32, s1); r_pb = Res()
                    pm = ps("pm", [128, 512], F32, s1); r_pm = Res()
                    pv = ps("pv", [128, 512], F32, s1); r_pv = Res()
                    pvo = ps("pvo", [128, 512], F32, s1); r_pvo = Res()
                    pg = ps("pg", [128, 512], F32, s1); r_pg = Res()
                    for rr, rres in zip(mvst.tiles, mvst.res):
                        P.op("dve", lambda e, rr=rr: e.memset(rr[:, 256:257], 1.0), writes=[rres])

                    def norm_block(xt, r_xt, bs, gtile, r_g):
                        P.op("act", lambda e: e.activation(out=junk[:bs, :], in_=xt[:bs, :], func=AF.Square, accum_out=ss[:bs, :]), reads=[r_xt], writes=[r_junk, r_ss])
                        P.op("dve", lambda e: e.tensor_scalar(out=ss[:bs, :], in0=ss[:bs, :], scalar1=1.0 / D, scalar2=EPS, op0=ALU.mult, op1=ALU.add), reads=[r_ss], writes=[r_ss])
                        P.op("act", lambda e: e.activation(out=ss[:bs, :], in_=ss[:bs, :], func=AF.Sqrt), reads=[r_ss], writes=[r_ss])
                        P.op("dve", lambda e: e.reciprocal(out=ss[:bs, :], in_=ss[:bs, :]), reads=[r_ss], writes=[r_ss])
                        P.op("dve", lambda e: e.scalar_tensor_tensor(out=u[:bs, :], in0=xt[:bs, :], scalar=ss[:bs, 0:1], in1=gtile[:bs, :], op0=ALU.mult, op1=ALU.mult), reads=[r_xt, r_ss, r_g], writes=[r_u])

                    stop_here("a1w")
                    tiles = [(512 * i, 512) for i in range(8)] + [(4096, 16)]
                    for (p0, n) in (tiles if "A1" in STAGES else []):
                        nblk = (n + 127) // 128
                        cs, r_cs = csr.next()
                        P.dma("act", lambda e, cs=cs, p0=p0, n=n: e.dma_start(out=cs[:, 0, :n], in_=cosf[:, p0:p0 + n]), writes=[r_cs])
                        P.dma("act", lambda e, cs=cs, p0=p0, n=n: e.dma_start(out=cs[:, 1, :n], in_=sinf[:, p0:p0 + n]), writes=[r_cs])
                        for j in range(nblk):
                            bs = min(128, n - 128 * j)
                            xt, r_xt = xring.next()
                            P.dma("sp", lambda e, xt=xt, bs=bs, r0=p0 + 128 * j: e.dma_start(out=xt[:bs, :], in_=hfull[r0:r0 + bs, :]), writes=[r_xt])
                            norm_block(xt, r_xt, bs, g1t, r_g1t)
                            P.op("pe", [(lambda e, k=k, bs=bs: e.transpose(out=pT[:, k * 128:k * 128 + bs], in_=u[:bs, k * 128:(k + 1) * 128], identity=idb[:bs, :bs])) for k in range(16)], reads=[r_u, r_idb], writes=[r_pT])
                            P.op("act", lambda e, j=j, bs=bs: e.copy(out=uT[:, :, 128 * j:128 * j + bs], in_=pT[:].rearrange("p (k n) -> p k n", k=16)[:, :, :bs]), reads=[r_pT], writes=[r_uT])
                        stop_here("blk%d" % (p0 // 512))
                        for c in range(4):
                            cm = (c % 2) * 128 + (c // 2) * 512
                            cr = cm + 256
                            P.op("pe", [(lambda e, k=k, cm=cm, n=n: e.matmul(out=pa[:, :n], lhsT=wat[:, k, cm:cm + 128], rhs=uT[:, k, :n], start=(k == 0), stop=(k == 15))) for k in range(16)], reads=[r_wat, r_uT], writes=[r_pa])
                            P.op("pe", [(lambda e, k=k, cr=cr, n=n: e.matmul(out=pb[:, :n], lhsT=wat[:, k, cr:cr + 128], rhs=uT[:, k, :n], start=(k == 0), stop=(k == 15))) for k in range(16)], reads=[r_wat, r_uT], writes=[r_pb])
                            P.op("dve", lambda e, cs=cs, n=n: e.tensor_tensor(out=rt1[:, :n], in0=pa[:, :n], in1=cs[:, 0, :n], op=ALU.mult), reads=[r_pa, r_cs], writes=[r_rt1])
                            P.op("dve", lambda e, cs=cs, n=n: e.tensor_tensor(out=rt2[:, :n], in0=pb[:, :n], in1=cs[:, 1, :n], op=ALU.mult), reads=[r_pb, r_cs], writes=[r_rt2])
                            P.op("dve", lambda e, c=c, p0=p0, n=n: e.tensor_tensor(out=dqkT[:, c, p0:p0 + n], in0=rt1[:, :n], in1=rt2[:, :n], op=ALU.add), reads=[r_rt1, r_rt2], writes=[r_dqkT])
                        stop_here("fm%d" % (p0 // 512))
                        mst, r_mst = mstage.next()
                        for c in range(4):
                            P.op("pe", [(lambda e, k=k, c=c, n=n: e.matmul(out=pm[:, :n], lhsT=wml[:, k, c * 128:(c + 1) * 128], rhs=uT[:, k, :n], start=(k == 0), stop=(k == 15))) for k in range(16)], reads=[r_wml, r_uT], writes=[r_pm])
                            P.op("act", lambda e, mst=mst, c=c, n=n: e.copy(out=mst[:, c, :n], in_=pm[:, :n]), reads=[r_pm], writes=[r_mst])
                        P.dma("sp", lambda e, mst=mst, p0=p0, n=n: e.dma_start(out=mqk_raw.rearrange("(c p) n -> p c n", p=128)[:, :, 49 + p0:49 + p0 + n], in_=mst[:, :, :n]), reads=[r_mst], writes=[r_mqk_raw])
                        stop_here("ml%d" % (p0 // 512))
                        for j in range(nblk):
                            bs = min(128, n - 128 * j)
                            kb = (p0 + 128 * j) // 128
                            r0 = 48 + p0 + 128 * j
                            P.op("pe", [(lambda e, k=k, j=j, bs=bs: e.matmul(out=pv[:bs, 0:256], lhsT=uT[:, k, 128 * j:128 * j + bs], rhs=wat[:, k, 1024:1280], start=(k == 0), stop=(k == 15))) for k in range(16)], reads=[r_wat, r_uT], writes=[r_pv])
                            P.op("dve", lambda e, kb=kb, bs=bs: e.tensor_copy(out=vatt[:bs, kb, :, 0:128], in_=pv[:bs, 0:256].rearrange("p (h d) -> p h d", h=2)), reads=[r_pv], writes=[r_vatt])
                            stop_here("tv%d_%d" % (p0 // 512, j))
                            P.op("pe", [(lambda e, k=k, j=j, bs=bs: e.matmul(out=pvo[:bs, :], lhsT=uT[:, k, 128 * j:128 * j + bs], rhs=wml[:, k, 512:1024], start=(k == 0), stop=(k == 15))) for k in range(16)], reads=[r_wml, r_uT], writes=[r_pvo])
                            mvt, r_mvt = mvst.next()
                            mot, r_mot = most.next()
                            P.op("act", lambda e, mvt=mvt, bs=bs: e.copy(out=mvt[:bs, 0:256], in_=pvo[:bs, 0:256]), reads=[r_pvo], writes=[r_mvt])
                            P.op("act", lambda e, mot=mot, bs=bs: e.activation(out=mot[:bs, :], in_=pvo[:bs, 256:512], func=AF.Sigmoid), reads=[r_pvo], writes=[r_mot])
                            P.dma("sp", lambda e, mvt=mvt, bs=bs, r0=r0: e.dma_start(out=mv_s[r0:r0 + bs, :], in_=mvt[:bs, :]), reads=[r_mvt], writes=[r_mv_s])
                            P.dma("sp", lambda e, mot=mot, bs=bs, r0=r0: e.dma_start(out=mo_s[r0:r0 + bs, :], in_=mot[:bs, :]), reads=[r_mot], writes=[r_mo_s])
                            stop_here("tm%d_%d" % (p0 // 512, j))
                            P.op("pe", [(lambda e, k=k, j=j, bs=bs: e.matmul(out=pg[:bs, 0:64], lhsT=uT[:, k, 128 * j:128 * j + bs], rhs=wml[:, k, 964:1028], start=(k == 0), stop=(k == 15))) for k in range(16)], reads=[r_wml, r_uT], writes=[r_pg])
                            stop_here("tgm%d_%d" % (p0 // 512, j))
                            gt_, r_gt = gst.next()
                            P.op("dve", lambda e, gt_=gt_, bs=bs: e.tensor_tensor(out=gt_[:bs, :], in0=pg[:bs, 60:64], in1=gbt[:bs, :], op=ALU.add), reads=[r_pg, r_gbt], writes=[r_gt])
                            stop_here("tga%d_%d" % (p0 // 512, j))
                            P.dma("sp", lambda e, gt_=gt_, bs=bs, r0=r0: e.dma_start(out=gates_s[r0:r0 + bs, :], in_=gt_[:bs, :]), reads=[r_gt], writes=[r_gates_s])
                    stop_here("a1t%d" % (p0 // 512))

                P.fence()
                stop_here("a1")
                with ExitStack() as s2:
                    lam_t = sb("lam_t", [128, 256], F32, s2); r_lam = Res()
                    lamw = sb("lamw", [128, 8], F32, s2); r_lamw = Res()
                    sg_t = sb("sg_t", [128, 128], F32, s2); r_sg = Res()
                    P.dma("act", lambda e: e.dma_start(out=lam_t[:], in_=lamv[:, :]), writes=[r_lam])
                    P.dma("act", lambda e: e.dma_start(out=sg_t[:], in_=sublng[:, :]), writes=[r_sg])
                    ljunk = sb("ljunk", [128, 64], F32, s2); r_lj = Res()
                    for i in range(2):
                        P.op("dve", lambda e, i=i: e.tensor_tensor(out=ljunk[:], in0=lam_t[:, 128 * i:128 * i + 64], in1=lam_t[:, 128 * i + 64:128 * i + 128], op=ALU.mult), reads=[r_lam], writes=[r_lj])
                        P.op("dve", lambda e, i=i: e.reduce_sum(out=lamw[:, i:i + 1], in_=ljunk[:], axis=AX.X), reads=[r_lj], writes=[r_lamw])
                    P.op("act", lambda e: e.activation(out=lamw[:, 2:4], in_=lamw[:, 0:2], func=AF.Exp), reads=[r_lamw], writes=[r_lamw])
                    P.op("dve", lambda e: e.tensor_tensor(out=lamw[:, 4:5], in0=lamw[:, 2:3], in1=lamw[:, 3:4], op=ALU.subtract), reads=[r_lamw], writes=[r_lamw])
                    P.op("dve", lambda e: e.tensor_scalar(out=lamw[:, 5:6], in0=lamw[:, 4:5], scalar1=0.2, scalar2=-1.0, op0=ALU.add, op1=ALU.mult), reads=[r_lamw], writes=[r_lamw])
                    P.op("dve", lambda e: e.tensor_scalar(out=sg_t[:], in0=sg_t[:], scalar1=0.8, scalar2=None, op0=ALU.mult), reads=[r_sg], writes=[r_sg])

                    stop_here("a2s")
                    pss = [Ring([ps(f"ps{m}{i}", [128, 512], F32, s2) for i in range(2)]) for m in range(2)]
                    pacc = [ps(f"pacc{i}", [128, 512], F32, s2) for i in range(3)]
                    r_pacc = Res()
                    ptr = ps("ptr", [128, 512], BF16, s2); r_ptr = Res()
                    ptile = [Ring([sb(f"pt{m}{i}", [128, 512], BF16, s2) for i in range(3)]) for m in range(2)]
                    rc = sb("rc", [128, 2], F32, s2); r_rc = Res()
                    t2 = sb("t2", [128, 128], F32, s2); r_t2 = Res()
                    ot = sb("ot", [128, 128], F32, s2); r_ot = Res()
                    oj = sb("oj", [128, 128], F32, s2); r_oj = Res()
                    os_ = sb("os_", [128, 1], F32, s2); r_os = Res()
                    bo = sb("bo", [128, 128], BF16, s2); r_bo = Res()
                    boT = Ring([sb(f"boT{i}", [128, 512], BF16, s2) for i in range(2)])

                    def acc(m, sub):
                        i = m * 4 + sub
                        return pacc[i // 3][:, (i % 3) * 129:(i % 3) * 129 + 129]

                    qtiles = [(512 * i, 512) for i in range(8)] + [(4096, 16)]
                    kblocks = [(128 * i, 128) for i in range(32)] + [(4096, 16)]
                    for h in range(2 if "A2" in STAGES else 0):
                        for (q0, nq) in qtiles:
                            nsub = (nq + 127) // 128
                            P.op("dve", [(lambda e, i=i: e.memset(pacc[i][:], 0.0)) for i in range(3)], writes=[r_pacc])
                            pend = []
                            for kbi, (k0, nk) in enumerate(kblocks):
                                cur = []
                                for m in range(2):
                                    pst, r_pst = pss[m].next()
                                    P.op("pe", lambda e, pst=pst, m=m, h=h, k0=k0, nk=nk, q0=q0, nq=nq: e.matmul(out=pst[:nk, :nq], lhsT=dqkT[64 * m:64 * m + 64, 2 + h, k0:k0 + nk], rhs=dqkT[64 * m:64 * m + 64, h, q0:q0 + nq], start=True, stop=True), reads=[r_dqkT], writes=[r_pst])
                                    cur.append((m, pst, r_pst))
                                newpend = []
                                for (m, pst, r_pst) in cur:
                                    pt, r_pt = ptile[m].next()
                                    P.op("act", lambda e, pst=pst, pt=pt, nk=nk, nq=nq: e.activation(out=pt[:nk, :nq], in_=pst[:nk, :nq], func=AF.Exp, scale=0.125), reads=[r_pst], writes=[r_pt])
                                    def pvf(pt=pt, r_pt=r_pt, m=m, nk=nk, kbi=kbi):
                                        P.op("pe", [(lambda e, pt=pt, m=m, sub=sub, nk=nk, kbi=kbi, h=h, qs=min(128, nq - 128 * sub): e.matmul(out=acc(m, sub)[:qs, :], lhsT=pt[:nk, 128 * sub:128 * sub + qs], rhs=vatt[:nk, kbi, h, :], start=False, stop=False, skip_group_check=True)) for sub in range(nsub)], reads=[r_pt, r_vatt], writes=[r_pacc])
                                    newpend.append(pvf)
                                for f in pend:
                                    f()
                                pend = newpend
                            for f in pend:
                                f()
                            bT, r_bT = boT.next()
                            for sub in range(nsub):
                                qs = min(128, nq - 128 * sub)
                                a1 = acc(0, sub); a2 = acc(1, sub)
                                P.op("dve", lambda e, a1=a1, qs=qs: e.reciprocal(out=rc[:qs, 0:1], in_=a1[:qs, 128:129]), reads=[r_pacc], writes=[r_rc])
                                P.op("dve", lambda e, a2=a2, qs=qs: e.reciprocal(out=rc[:qs, 1:2], in_=a2[:qs, 128:129]), reads=[r_pacc], writes=[r_rc])
                                P.op("dve", lambda e, a2=a2, qs=qs: e.tensor_scalar(out=t2[:qs, :], in0=a2[:qs, 0:128], scalar1=rc[:qs, 1:2], scalar2=lamw[:qs, 5:6], op0=ALU.mult, op1=ALU.mult), reads=[r_pacc, r_rc, r_lamw], writes=[r_t2])
                                P.op("dve", lambda e, a1=a1, qs=qs: e.scalar_tensor_tensor(out=ot[:qs, :], in0=a1[:qs, 0:128], scalar=rc[:qs, 0:1], in1=t2[:qs, :], op0=ALU.mult, op1=ALU.add), reads=[r_pacc, r_rc, r_t2], writes=[r_ot])
                                P.op("act", lambda e, qs=qs: e.activation(out=oj[:qs, :], in_=ot[:qs, :], func=AF.Square, accum_out=os_[:qs, :]), reads=[r_ot], writes=[r_oj, r_os])
                                P.op("dve", lambda e, qs=qs: e.tensor_scalar(out=os_[:qs, :], in0=os_[:qs, :], scalar1=1.0 / 128, scalar2=EPS, op0=ALU.mult, op1=ALU.add), reads=[r_os], writes=[r_os])
                                P.op("act", lambda e, qs=qs: e.activation(out=os_[:qs, :], in_=os_[:qs, :], func=AF.Sqrt), reads=[r_os], writes=[r_os])
                                P.op("dve", lambda e, qs=qs: e.reciprocal(out=os_[:qs, :], in_=os_[:qs, :]), reads=[r_os], writes=[r_os])
                                P.op("dve", lambda e, qs=qs: e.scalar_tensor_tensor(out=bo[:qs, :], in0=ot[:qs, :], scalar=os_[:qs, 0:1], in1=sg_t[:qs, :], op0=ALU.mult, op1=ALU.mult), reads=[r_ot, r_os, r_sg], writes=[r_bo])
                                P.op("pe", lambda e, sub=sub, qs=qs: e.transpose(out=ptr[:, 128 * sub:128 * sub + qs], in_=bo[:qs, :], identity=idb[:qs, :qs]), reads=[r_bo, r_idb], writes=[r_ptr])
                                P.op("act", lambda e, bT=bT, sub=sub, qs=qs: e.copy(out=bT[:, 128 * sub:128 * sub + qs], in_=ptr[:, 128 * sub:128 * sub + qs]), reads=[r_ptr], writes=[r_bT])
                            P.dma("sp", lambda e, bT=bT, h=h, q0=q0, nq=nq: e.dma_start(out=cc_in[256 + 128 * h:256 + 128 * h + 128, q0:q0 + nq], in_=bT[:, :nq]), reads=[r_bT], writes=[r_cc_in_b])

            if "X" in STAGES:
                exchange_half(1)
            P.fence()
            with ExitStack() as s3:
                mqkT = sb("mqkT", [128, 4, LP], BF16, s3); r_mqkT = Res()
                with ExitStack() as s3a:
                    raw = sb("raw", [128, 4, LP + 2], BF16, s3a); r_raw = Res()
                    cacc = sb("cacc", [128, LP], F32, s3a); r_cacc = Res()
                    mcw = sb("mcw", [128, 12], F32, s3a); r_mcw = Res()
                    P.dma("act", lambda e: e.dma_start(out=mcw[:], in_=mconv[:, :]), writes=[r_mcw])
                    P.dma("sp", lambda e: e.dma_start(out=raw[:], in_=mqk_raw.rearrange("(c p) n -> p c n", p=128)), reads=[r_mqk_raw], writes=[r_raw])
                    for c in range(4):
                        P.op("dve", lambda e, c=c: e.tensor_scalar(out=cacc[:], in0=raw[:, c, 0:LP], scalar1=mcw[:, 3 * c:3 * c + 1], scalar2=None, op0=ALU.mult), reads=[r_raw, r_mcw], writes=[r_cacc])
                        P.op("dve", lambda e, c=c: e.scalar_tensor_tensor(out=cacc[:], in0=raw[:, c, 1:LP + 1], scalar=mcw[:, 3 * c + 1:3 * c + 2], in1=cacc[:], op0=ALU.mult, op1=ALU.add), reads=[r_raw, r_mcw, r_cacc], writes=[r_cacc])
                        P.op("dve", lambda e, c=c: e.scalar_tensor_tensor(out=cacc[:], in0=raw[:, c, 2:LP + 2], scalar=mcw[:, 3 * c + 2:3 * c + 3], in1=cacc[:], op0=ALU.mult, op1=ALU.add), reads=[r_raw, r_mcw, r_cacc], writes=[r_cacc])
                        P.op("act", lambda e, c=c: e.activation(out=mqkT[:, c, :], in_=cacc[:], func=AF.Silu), reads=[r_cacc], writes=[r_mqkT])
                    P.op("dve", lambda e: e.memset(mqkT[:, :, 0:48], 0.0), writes=[r_mqkT])

                P.fence()
                stop_here("a3conv")
                gt = sb("gt", [64, 65, 4], F32, s3); r_gtb = Res()
                gl = sb("gl", [64, 4, 65], F32, s3); r_gl = Res()
                tru = sb("tru", [64, 64], F32, s3); r_tru = Res()
                trl = sb("trl", [64, 64], F32, s3); r_trl = Res()
                mku = sb("mku", [64, 64], F32, s3); r_mku = Res()
                mkl = sb("mkl", [64, 64], F32, s3); r_mkl = Res()
                ones64 = sb("ones64", [64, 128], F32, s3); r_ones = Res()
                ebt = sb("ebt", [64, 2, 65], F32, s3); r_ebt = Res()
                rft = sb("rft", [64, 2, 65], F32, s3); r_rft = Res()
                egt = sb("egt", [128, 2, 65], F32, s3); r_egt = Res()
                mngt = sb("mngt", [64, 256], F32, s3); r_mngt = Res()
                s3g = s3.enter_context(ExitStack())
                pgx = ps("pgx", [128, 512], F32, s3g); r_pgx = Res()
                for c5 in range(5):
                    P.dma("sp", lambda e, c5=c5: e.dma_start(out=gt[:, 13 * c5:13 * c5 + 13, :], in_=gates_s.rearrange("(c t) g -> t c g", t=64)[:, 13 * c5:13 * c5 + 13, :]), reads=[r_gates_s], writes=[r_gtb])
                P.dma("act", lambda e: e.dma_start(out=tru[:], in_=triu_d[:, :]), writes=[r_tru])
                P.dma("act", lambda e: e.dma_start(out=trl[:], in_=tril_d[:, :]), writes=[r_trl])
                P.dma("act", lambda e: e.dma_start(out=mngt[:], in_=mng[:, :]), writes=[r_mngt])
                P.op("dve", lambda e: e.tensor_scalar(out=mku[:], in0=tru[:], scalar1=1.0 / 16, scalar2=None, op0=ALU.mult), reads=[r_tru], writes=[r_mku])
                P.op("dve", lambda e: e.tensor_scalar(out=mkl[:], in0=trl[:], scalar1=1.0 / 16, scalar2=None, op0=ALU.mult), reads=[r_trl], writes=[r_mkl])
                P.op("dve", lambda e: e.memset(ones64[:], 1.0), writes=[r_ones])
                stop_here("g1")
                for gi in range(2):
                    P.op("act", lambda e, gi=gi: e.activation(out=gl[:, gi, :], in_=gt[:, :, gi], func=AF.Exp, scale=-1.0), reads=[r_gtb], writes=[r_gl])
                P.op("dve", lambda e: e.tensor_scalar(out=gl[:, 0:2, :], in0=gl[:, 0:2, :], scalar1=1.0, scalar2=None, op0=ALU.add), reads=[r_gl], writes=[r_gl])
                P.op("act", lambda e: e.activation(out=gl[:, 0:2, :], in_=gl[:, 0:2, :], func=AF.Ln), reads=[r_gl], writes=[r_gl])
                P.op("dve", lambda e: e.tensor_scalar(out=gl[:, 0:2, :], in0=gl[:, 0:2, :], scalar1=-1.0, scalar2=None, op0=ALU.mult), reads=[r_gl], writes=[r_gl])
                for gi in range(2):
                    P.op("dve", lambda e, gi=gi: e.tensor_copy(out=gl[:, 2 + gi, :], in_=gt[:, :, 2 + gi]), reads=[r_gtb], writes=[r_gl])
                stop_here("g2")
                P.op("dve", lambda e: e.memset(gl[0:48, :, 0:1], 0.0), writes=[r_gl])
                stop_here("g3")
                P.op("pe", lambda e: e.matmul(out=pgx[:64, 0:65], lhsT=tru[:], rhs=gl[:, 0, :], start=True, stop=True), reads=[r_tru, r_gl], writes=[r_pgx])
                P.op("pe", lambda e: e.matmul(out=pgx[:64, 65:130], lhsT=trl[:], rhs=gl[:, 1, :], start=True, stop=True), reads=[r_trl, r_gl], writes=[r_pgx])
                P.op("pe", lambda e: e.matmul(out=pgx[:, 130:260], lhsT=ones64[:], rhs=gl[:, 0:2, :].rearrange("p a c -> p (a c)"), start=True, stop=True), reads=[r_ones, r_gl], writes=[r_pgx])
                stop_here("g4")
                P.op("act", lambda e: e.activation(out=ebt[:].rearrange("p a c -> p (a c)"), in_=pgx[:64, 0:130], func=AF.Exp), reads=[r_pgx], writes=[r_ebt])
                P.op("act", lambda e: e.activation(out=egt[:].rearrange("p a c -> p (a c)"), in_=pgx[:, 130:260], func=AF.Exp), reads=[r_pgx], writes=[r_egt])
                P.op("dve", lambda e: e.tensor_tensor(out=rft[:].rearrange("p a c -> p (a c)"), in0=gl[:, 2:4, :].rearrange("p a c -> p (a c)"), in1=pgx[:64, 0:130], op=ALU.subtract), reads=[r_pgx, r_gl], writes=[r_rft])
                P.op("act", lambda e: e.activation(out=rft[:].rearrange("p a c -> p (a c)"), in_=rft[:].rearrange("p a c -> p (a c)"), func=AF.Exp), reads=[r_rft], writes=[r_rft])

                stop_here("a3g")
                s3g.close()
                P.fence()
                Zt = [sb(f"Zt{d}", [128, 2, 257], F32, s3) for d in range(2)]; r_Z = [Res(), Res()]
                Zbt = [sb(f"Zbt{d}", [128, 2, 257], BF16, s3) for d in range(2)]; r_Zb = [Res(), Res()]
                ztt = [sb(f"ztt{d}", [128, 2, 257], F32, s3) for d in range(2)]; r_ztmp = [Res(), Res()]
                nebt = sb("nebt", [64, 2, 65], F32, s3); r_nebt = Res()
                P.op("dve", lambda e: e.tensor_scalar(out=nebt[:].rearrange("p a c -> p (a c)"), in0=ebt[:].rearrange("p a c -> p (a c)"), scalar1=-1.0, scalar2=None, op0=ALU.mult), reads=[r_ebt], writes=[r_nebt])
                vres = sb("vres", [64, 65, 257], BF16, s3); r_vres = Res()
                for c5 in range(5):
                    P.dma("act", lambda e, c5=c5: e.dma_start(out=vres[:, 13 * c5:13 * c5 + 13, :], in_=mv_s.rearrange("(c t) v -> t c v", t=64)[:, 13 * c5:13 * c5 + 13, :]), reads=[r_mv_s], writes=[r_vres])
                vtr = Ring([sb(f"vt{i}", [64, 257], BF16, s3) for i in range(4)])
                ptmr = Ring([sb(f"ptm{i}", [64, 64], BF16, s3) for i in range(4)])
                ktokr = Ring([sb(f"ktok{i}", [64, 256], BF16, s3) for i in range(4)])
                dnr = Ring([sb(f"dn{i}", [64, 4], F32, s3) for i in range(4)])
                hring = Ring([sb(f"hch{i}", [64, 256], F32, s3) for i in range(4)])
                with ExitStack() as s3s:
                    p_sr = Ring([ps(f"p_s{i}", [128, 512], F32, s3s) for i in range(2)])
                    p_kr = Ring([ps(f"p_k{i}", [128, 1024], BF16, s3s) for i in range(2)])
                    p_or = Ring([ps(f"p_o{i}", [128, 512], F32, s3s) for i in range(2)])
                    p_cc = ps("p_cc", [128, 1024], F32, s3s); r_p_c = Res()

                    def phase_I(d, c):
                        c0 = 64 * c
                        mask, r_mask = (mku, r_mku) if d == 0 else (mkl, r_mkl)
                        p_s, r_p_s = p_sr.next()
                        p_k, r_p_k = p_kr.next()
                        ptm, r_ptm = ptmr.next()
                        ktok, r_ktok = ktokr.next()
                        vt, r_vt = vtr.next()
                        P.op("pe", [(lambda e, dc=dc, c0=c0, p_s=p_s: e.matmul(out=p_s[:64, 0:64], lhsT=mqkT[:, 2 + dc, c0:c0 + 64], rhs=mqkT[:, dc, c0:c0 + 64], start=(dc == 0), stop=(dc == 1))) for dc in range(2)], reads=[r_mqkT], writes=[r_p_s])
                        P.op("dve", lambda e, mask=mask, p_s=p_s, ptm=ptm: e.tensor_tensor(out=ptm[:], in0=p_s[:64, 0:64], in1=mask[:], op=ALU.mult), reads=[r_p_s, r_mask], writes=[r_ptm])
                        P.op("pe", [(lambda e, dc=dc, c0=c0, p_k=p_k: e.transpose(out=p_k[:64, 128 * dc:128 * dc + 128], in_=mqkT[:, 2 + dc, c0:c0 + 64], identity=idb[:])) for dc in range(2)], reads=[r_mqkT, r_idb], writes=[r_p_k])
                        P.op("act", lambda e, ktok=ktok, p_k=p_k: e.copy(out=ktok[:], in_=p_k[:64, 0:256]), reads=[r_p_k], writes=[r_ktok])
                        P.op("dve", lambda e, vt=vt, d=d, c=c: e.tensor_scalar(out=vt[:], in0=vres[:, c, :], scalar1=rft[:, d, c:c + 1], scalar2=None, op0=ALU.mult), reads=[r_vres, r_rft], writes=[r_vt])
                        return dict(c=c, c0=c0, d=d, ptm=ptm, r_ptm=r_ptm, ktok=ktok, r_ktok=r_ktok, vt=vt, r_vt=r_vt)

                    def phase_D(x):
                        d, c, c0 = x["d"], x["c"], x["c0"]
                        ptm, r_ptm, ktok, r_ktok, vt, r_vt = x["ptm"], x["r_ptm"], x["ktok"], x["r_ktok"], x["vt"], x["r_vt"]
                        p_o, r_p_o = p_or.next()
                        for dc in range(2):
                            P.op("pe", lambda e, dc=dc, ktok=ktok, vt=vt: e.matmul(out=p_cc[:, 512 * dc:512 * dc + 257], lhsT=ktok[:, 128 * dc:128 * dc + 128], rhs=vt[:], start=True, stop=True), reads=[r_ktok, r_vt], writes=[r_p_c])
                        P.op("pe", [lambda e, ptm=ptm, vt=vt, p_o=p_o: e.matmul(out=p_o[:64, 0:257], lhsT=ptm[:], rhs=vt[:], start=True, stop=False)] +
                             [(lambda e, dc=dc, c0=c0, p_o=p_o, d=d: e.matmul(out=p_o[:64, 0:257], lhsT=mqkT[:, dc, c0:c0 + 64], rhs=Zbt[d][:, dc, :], start=False, stop=(dc == 1))) for dc in range(2)],
                             reads=[r_ptm, r_vt, r_mqkT, r_Zb[d]], writes=[r_p_o])
                        P.op("dve", lambda e, d=d: e.scalar_tensor_tensor(out=ztt[d][:], in0=p_cc[:].rearrange("p (a n) -> p a n", a=2)[:, :, 0:257], scalar=1.0 / 16, in1=Zt[d][:], op0=ALU.mult, op1=ALU.add), reads=[r_p_c, r_Z[d]], writes=[r_ztmp[d]])
                        P.op("act", lambda e, d=d, c=c: e.activation(out=Zbt[d][:], in_=ztt[d][:], func=AF.Identity, scale=egt[:, d, c:c + 1]), reads=[r_ztmp[d], r_egt], writes=[r_Zb[d]])
                        P.op("act", lambda e, d=d, c=c: e.activation(out=Zt[d][:], in_=ztt[d][:], func=AF.Identity, scale=egt[:, d, c:c + 1]), reads=[r_ztmp[d], r_egt], writes=[r_Z[d]])
                        hch, r_hch = hring.next()
                        dn, r_dn = dnr.next()
                        P.op("dve", lambda e, d=d, c=c, dn=dn, p_o=p_o: e.tensor_scalar(out=dn[:, 0:1], in0=p_o[:64, 256:257], scalar1=ebt[:, d, c:c + 1], scalar2=1.0, op0=ALU.mult, op1=ALU.max), reads=[r_p_o, r_ebt], writes=[r_dn])
                        P.op("dve", lambda e, d=d, c=c, dn=dn, p_o=p_o: e.scalar_tensor_tensor(out=dn[:, 1:2], in0=p_o[:64, 256:257], scalar=nebt[:, d, c:c + 1], in1=dn[:, 0:1], op0=ALU.mult, op1=ALU.max), reads=[r_p_o, r_nebt, r_dn], writes=[r_dn])
                        P.op("dve", lambda e, dn=dn: e.reciprocal(out=dn[:, 2:3], in_=dn[:, 1:2]), reads=[r_dn], writes=[r_dn])
                        P.op("dve", lambda e, hch=hch, dn=dn, p_o=p_o, d=d, c=c: e.tensor_scalar(out=hch[:], in0=p_o[:64, 0:256], scalar1=dn[:, 2:3], scalar2=ebt[:, d, c:c + 1], op0=ALU.mult, op1=ALU.mult), reads=[r_p_o, r_dn, r_ebt], writes=[r_hch])
                        dst, r_dst = (hf_s, r_hf_s) if d == 0 else (hb_s, r_hb_s)
                        P.dma("sp", lambda e, hch=hch, c0=c0, dst=dst: e.dma_start(out=dst[c0:c0 + 64, :], in_=hch[:]), reads=[r_hch], writes=[r_dst])

                    if "A3" in STAGES:
                        for d in range(2):
                            P.op("dve", lambda e, d=d: e.memset(Zt[d][:], 0.0), writes=[r_Z[d]])
                            P.op("dve", lambda e, d=d: e.memset(Zbt[d][:], 0.0), writes=[r_Zb[d]])
                        nxt = [phase_I(0, 0), phase_I(1, 64)]
                        for i in range(65):
                            cur = nxt
                            if i + 1 < 65:
                                nxt = [phase_I(0, i + 1), phase_I(1, 63 - i)]
                            phase_D(cur[0])
                            phase_D(cur[1])
                P.fence()
                with ExitStack() as s3e:
                    NR = 8
                    hfr = Ring([sb(f"ehf{i}", [128, 256], F32, s3e) for i in range(NR)])
                    hbr = Ring([sb(f"ehb{i}", [128, 256], F32, s3e) for i in range(NR)])
                    mor = Ring([sb(f"emo{i}", [128, 256], F32, s3e) for i in range(NR)])
                    hsr = Ring([sb(f"ehs{i}", [128, 256], F32, s3e) for i in range(NR)])
                    bsr = Ring([sb(f"ebs{i}", [128, 8], F32, s3e) for i in range(NR)])
                    aor = Ring([sb(f"eao{i}", [128, 256], BF16, s3e) for i in range(NR)])
                    aTr = Ring([sb(f"eaT{i}", [128, 2, 128], BF16, s3e) for i in range(NR)])
                    p_tr = Ring([ps(f"ep_t{i}", [128, 1024], BF16, s3e) for i in range(3)])
                    mng128 = sb("mng128", [128, 256], F32, s3e); r_mng128 = Res()
                    P.dma("act", lambda e: e.dma_start(out=mng128[0:64, :], in_=mng[:, :]), writes=[r_mng128])
                    P.dma("act", lambda e: e.dma_start(out=mng128[64:128, :], in_=mng[:, :]), writes=[r_mng128])
                    eblocks = [(128 * j, 128) for j in range(32)] + [(4096, 64)]

                    def st0(j):
                        r0, n = eblocks[j]
                        hf, r_hf = hfr.next(); hb, r_hb = hbr.next(); mo, r_mo = mor.next()
                        P.dma("sp", lambda e, hf=hf, r0=r0, n=n: e.dma_start(out=hf[:n, :], in_=hf_s[r0:r0 + n, :]), reads=[r_hf_s], writes=[r_hf])
                        P.dma("sp", lambda e, hb=hb, r0=r0, n=n: e.dma_start(out=hb[:n, :], in_=hb_s[r0:r0 + n, :]), reads=[r_hb_s], writes=[r_hb])
                        P.dma("act", lambda e, mo=mo, r0=r0, n=n: e.dma_start(out=mo[:n, :], in_=mo_s[r0:r0 + n, :]), reads=[r_mo_s], writes=[r_mo])
                        return dict(r0=r0, n=n, hf=hf, r_hf=r_hf, hb=hb, r_hb=r_hb, mo=mo, r_mo=r_mo)

                    def st1(x):
                        n = x["n"]
                        hs, r_hs = hsr.next(); bs_, r_bs = bsr.next()
                        hf, hb = x["hf"], x["hb"]
                        P.op("dve", lambda e, hf=hf, hb=hb, hs=hs, n=n: e.tensor_tensor(out=hs[:n, :], in0=hf[:n, :], in1=hb[:n, :], op=ALU.add), reads=[x["r_hf"], x["r_hb"]], writes=[r_hs])
                        P.op("dve", lambda e, hs=hs, bs_=bs_, n=n: e.bn_stats(out=bs_[:n, 0:6], in_=hs[:n, :]), reads=[r_hs], writes=[r_bs])
                        P.op("dve", lambda e, bs_=bs_, n=n: e.bn_aggr(out=bs_[:n, 6:8], in_=bs_[:n, 0:6]), reads=[r_bs], writes=[r_bs])
                        P.op("dve", lambda e, bs_=bs_, n=n: e.tensor_scalar(out=bs_[:n, 7:8], in0=bs_[:n, 7:8], scalar1=EPS, scalar2=None, op0=ALU.add), reads=[r_bs], writes=[r_bs])
                        x.update(hs=hs, r_hs=r_hs, bs=bs_, r_bs=r_bs)

                    def st2(x):
                        n, bs_, r_bs = x["n"], x["bs"], x["r_bs"]
                        P.op("act", lambda e, bs_=bs_, n=n: e.activation(out=bs_[:n, 7:8], in_=bs_[:n, 7:8], func=AF.Sqrt), reads=[r_bs], writes=[r_bs])

                    def st3(x):
                        n, bs_, r_bs, hs, r_hs = x["n"], x["bs"], x["r_bs"], x["hs"], x["r_hs"]
                        P.op("dve", lambda e, bs_=bs_, n=n: e.reciprocal(out=bs_[:n, 7:8], in_=bs_[:n, 7:8]), reads=[r_bs], writes=[r_bs])
                        P.op("dve", lambda e, bs_=bs_, hs=hs, n=n: e.tensor_scalar(out=hs[:n, :], in0=hs[:n, :], scalar1=bs_[:n, 6:7], scalar2=bs_[:n, 7:8], op0=ALU.subtract, op1=ALU.mult), reads=[r_hs, r_bs], writes=[r_hs])

                    def st4(x):
                        n, hs, r_hs, mo, r_mo = x["n"], x["hs"], x["r_hs"], x["mo"], x["r_mo"]
                        ao, r_ao = aor.next()
                        P.op("pool", lambda e, hs=hs, n=n: e.tensor_tensor(out=hs[:n, :], in0=hs[:n, :], in1=mng128[:n, :], op=ALU.mult), reads=[r_hs, r_mng128], writes=[r_hs])
                        P.op("pool", lambda e, hs=hs, mo=mo, ao=ao, n=n: e.tensor_tensor(out=ao[:n, :], in0=hs[:n, :], in1=mo[:n, :], op=ALU.mult), reads=[r_hs, r_mo], writes=[r_ao])
                        x.update(ao=ao, r_ao=r_ao)

                    def st5(x):
                        n, ao, r_ao = x["n"], x["ao"], x["r_ao"]
                        p_t, r_p_t = p_tr.next()
                        P.op("pe", [(lambda e, dc=dc, ao=ao, p_t=p_t, n=n: e.transpose(out=p_t[:, 128 * dc:128 * dc + n], in_=ao[:n, 128 * dc:128 * dc + 128], identity=idb[:n, :n])) for dc in range(2)], reads=[r_ao, r_idb], writes=[r_p_t])
                        x.update(p_t=p_t, r_p_t=r_p_t)

                    def st6(x):
                        r0, n, p_t, r_p_t = x["r0"], x["n"], x["p_t"], x["r_p_t"]
                        aT, r_aT = aTr.next()
                        P.op("act", lambda e, aT=aT, p_t=p_t, n=n: e.copy(out=aT[:, :, :n], in_=p_t[:, 0:256].rearrange("p (a t) -> p a t", a=2)[:, :, :n]), reads=[r_p_t], writes=[r_aT])
                        if r0 == 0:
                            P.dma("sp", lambda e, aT=aT: e.dma_start(out=cc_in[0:256, 0:80].rearrange("(a p) t -> p a t", p=128), in_=aT[:, :, 48:128]), reads=[r_aT], writes=[r_cc_in])
                        else:
                            pos0 = r0 - 48
                            P.dma("sp", lambda e, aT=aT, pos0=pos0, n=n: e.dma_start(out=cc_in[0:256, pos0:pos0 + n].rearrange("(a p) t -> p a t", p=128), in_=aT[:, :, :n]), reads=[r_aT], writes=[r_cc_in])

                    if "A3" in STAGES:
                        stages = [st1, st2, st3, st4, st5, st6]
                        xs = {}
                        nb = len(eblocks)
                        for i in range(nb + len(stages) + 1):
                            if i < nb:
                                xs[i] = st0(i)
                            for si, stf in enumerate(stages):
                                jj = i - 1 - si
                                if 0 <= jj < nb:
                                    stf(xs[jj])

            P.fence()
            if "X" in STAGES:
                exchange_half(0)
            if DEBUG:
                P.stopped = False
                dbg_holder.append(P.dma("pool", lambda e: e.dma_start(out=dbg_cc[:, :], in_=cc_in[:, :]), reads=[r_cc_in, r_cc_in_b]))
                stop_here(STOP_AT)

            if "B" in STAGES:
                TS = [(342 * i, 342) for i in range(3)]
                with ExitStack() as sB:
                    hT = sb("hT", [128, 16, NW], F32, sB); r_hT = Res()
                    uT2 = sb("uT2", [128, 16, NW], BF16, sB); r_uT2 = Res()
                    g2t = sb("g2t", [128, 16], F32, sB); r_g2t = Res()
                    P.dma("act", lambda e: e.dma_start(out=g2t[:], in_=g2c[:, :]), writes=[r_g2t])
                    onesf = sb("onesf", [128, 128], F32, sB); r_onesf = Res()
                    P.op("dve", lambda e: e.memset(onesf[:], 1.0), writes=[r_onesf])
                    with ExitStack() as sB1:
                        abT = sb("abT", [128, 16, NW], BF16, sB1); r_abT = Res()
                        mT = sb("mT", [128, 16, NW], BF16, sB1); r_mT = Res()
                        rank_cache = {}
                        for k in range(16):
                            def load_ab(e, k=k):
                                if "r" not in rank_cache:
                                    rank_cache["r"] = e.partition_id() % 4
                                rank = rank_cache["r"]
                                return e.dma_start(out=abT[:, k:k + 1, :], in_=cc_out.rearrange("(j r) t -> r j t", j=4)[k * 128:(k + 1) * 128, bass.ds(rank, 1), :])
                            P.dma("pool", load_ab, reads=[r_cc_out[0 if k < 8 else 1]], writes=[r_abT])
                        with ExitStack() as sB0:
                            g1t_b = sb("g1t2", [128, D], F32, sB0); r_g1t_b = Res()
                            P.dma("act", lambda e: e.dma_start(out=g1t_b[:], in_=g1b[:, :]), writes=[r_g1t_b])
                            xring_b = Ring([sb(f"xw{i}", [128, D], F32, sB0) for i in range(2)])
                            junk_b = sb("junk2", [128, D], BF16, sB0); r_junk_b = Res()
                            ss_b = sb("ss2", [128, 1], F32, sB0); r_ss_b = Res()
                            u_b = sb("u2", [128, D], BF16, sB0); r_u_b = Res()
                            pT_b = ps("pT2", [128, D], BF16, sB0); r_pT_b = Res()
                            pX = [ps(f"pX{i}", [128, 1024], F32, sB0) for i in range(2)]; r_pX = [Res(), Res()]
                            for j in range(9):
                                bs = 128 if j < 8 else 2
                                xt_b, r_xt_b = xring_b.next()
                                P.dma("sp", lambda e, xt_b=xt_b, bs=bs, j=j: e.dma_start(out=xt_b[:bs, :], in_=xwin[128 * j:128 * j + bs, :]), writes=[r_xt_b])
                                P.op("act", lambda e, xt_b=xt_b, bs=bs: e.activation(out=junk_b[:bs, :], in_=xt_b[:bs, :], func=AF.Square, accum_out=ss_b[:bs, :]), reads=[r_xt_b], writes=[r_junk_b, r_ss_b])
                                P.op("dve", lambda e, bs=bs: e.tensor_scalar(out=ss_b[:bs, :], in0=ss_b[:bs, :], scalar1=1.0 / D, scalar2=EPS, op0=ALU.mult, op1=ALU.add), reads=[r_ss_b], writes=[r_ss_b])
                                P.op("act", lambda e, bs=bs: e.activation(out=ss_b[:bs, :], in_=ss_b[:bs, :], func=AF.Sqrt), reads=[r_ss_b], writes=[r_ss_b])
                                P.op("dve", lambda e, bs=bs: e.reciprocal(out=ss_b[:bs, :], in_=ss_b[:bs, :]), reads=[r_ss_b], writes=[r_ss_b])
                                P.op("dve", lambda e, xt_b=xt_b, bs=bs: e.scalar_tensor_tensor(out=u_b[:bs, :], in0=xt_b[:bs, :], scalar=ss_b[:bs, 0:1], in1=g1t_b[:bs, :], op0=ALU.mult, op1=ALU.mult), reads=[r_xt_b, r_ss_b, r_g1t_b], writes=[r_u_b])
                                P.op("pe", [(lambda e, k=k, bs=bs: e.transpose(out=pT_b[:, k * 128:k * 128 + bs], in_=u_b[:bs, k * 128:(k + 1) * 128], identity=idb[:bs, :bs])) for k in range(16)], reads=[r_u_b, r_idb], writes=[r_pT_b])
                                P.op("act", lambda e, j=j, bs=bs: e.copy(out=uT2[:, :, 128 * j:128 * j + bs], in_=pT_b[:].rearrange("p (k n) -> p k n", k=16)[:, :, :bs]), reads=[r_pT_b], writes=[r_uT2])
                                for hh in range(2):
                                    P.op("pe", [(lambda e, k=k, hh=hh, xt_b=xt_b, bs=bs: e.transpose(out=pX[hh][:, (k % 8) * 128:(k % 8) * 128 + bs], in_=xt_b[:bs, k * 128:(k + 1) * 128], identity=idf[:bs, :bs])) for k in range(8 * hh, 8 * hh + 8)], reads=[r_xt_b, r_idf], writes=[r_pX[hh]])
                                    P.op("dve", lambda e, hh=hh, j=j, bs=bs: e.tensor_copy(out=hT[:, 8 * hh:8 * hh + 8, 128 * j:128 * j + bs], in_=pX[hh][:].rearrange("p (k n) -> p k n", k=8)[:, :, :bs]), reads=[r_pX[hh]], writes=[r_hT])

                        P.fence()
                        with ExitStack() as sB1b:
                            wgr = Ring([sb(f"wg{i}", [128, 16, 256], BF16, sB1b) for i in range(2)])
                            war = Ring([sb(f"wa{i}", [128, 8, 256], BF16, sB1b) for i in range(2)])
                            sgm = sb("sgm", [128, 342], F32, sB1b); r_sgm = Res()
                            sgd = sb("sgd", [128, 342], F32, sB1b); r_sgd = Res()
                            tA = sb("tA", [128, 342], F32, sB1b); r_tA = Res()
                            tB = sb("tB", [128, 342], F32, sB1b); r_tB = Res()
                            pq = [Ring([ps(f"pq{q}{i}", [128, 512], F32, sB1b) for i in range(2)]) for q in range(4)]
                            for c in range(16):
                                wg, r_wg = wgr.next()
                                P.dma("pool", lambda e, wg=wg, c=c: e.dma_start(out=wg[:, :, 0:128], in_=w_g[:, 128 * c:128 * c + 128].rearrange("(k p) n -> p k n", p=128)), writes=[r_wg])
                                for (t0, tn) in TS:
                                    p0_, r0_ = pq[0].next()
                                    P.op("pe", [(lambda e, k=k, p0_=p0_, wg=wg, t0=t0, tn=tn: e.matmul(out=p0_[:, :tn], lhsT=wg[:, k, 0:128], rhs=uT2[:, k, t0:t0 + tn], start=(k == 0), stop=(k == 15))) for k in range(16)], reads=[r_wg, r_uT2], writes=[r0_])
                                    P.op("act", lambda e, p0_=p0_, c=c, t0=t0, tn=tn: e.activation(out=mT[:, c, t0:t0 + tn], in_=p0_[:, :tn], func=AF.Sigmoid), reads=[r0_], writes=[r_mT])
                            for c in range(16):
                                wg, r_wg = wgr.next()
                                wa, r_wa = war.next()
                                P.dma("pool", lambda e, wg=wg, c=c: e.dma_start(out=wg[:, :, 128:256], in_=w_g[:, 2048 + 128 * c:2048 + 128 * c + 128].rearrange("(k p) n -> p k n", p=128)), writes=[r_wg])
                                P.dma("pool", lambda e, wa=wa, c=c: e.dma_start(out=wa[:, :, 0:128], in_=w_a[:, 128 * c:128 * c + 128].rearrange("(k p) n -> p k n", p=128)), writes=[r_wa])
                                P.dma("pool", lambda e, wa=wa, c=c: e.dma_start(out=wa[:, :, 128:256], in_=w_b[:, 128 * c:128 * c + 128].rearrange("(k p) n -> p k n", p=128)), writes=[r_wa])
                                for (t0, tn) in TS:
                                    p1_, r1_ = pq[1].next(); p2_, r2_ = pq[2].next(); p3_, r3_ = pq[3].next()
                                    P.op("pe", [(lambda e, k=k, p2_=p2_, wg=wg, t0=t0, tn=tn: e.matmul(out=p2_[:, :tn], lhsT=wg[:, k, 128:256], rhs=uT2[:, k, t0:t0 + tn], start=(k == 0), stop=(k == 15))) for k in range(16)], reads=[r_wg, r_uT2], writes=[r2_])
                                    P.op("pe", [(lambda e, k=k, p1_=p1_, wa=wa, t0=t0, tn=tn: e.matmul(out=p1_[:, :tn], lhsT=wa[:, k, 0:128], rhs=abT[:, k, t0:t0 + tn], start=(k == 0), stop=(k == 7))) for k in range(8)], reads=[r_wa, r_abT], writes=[r1_])
                                    P.op("pe", [(lambda e, k=k, p3_=p3_, wa=wa, t0=t0, tn=tn: e.matmul(out=p3_[:, :tn], lhsT=wa[:, k, 128:256], rhs=abT[:, 8 + k, t0:t0 + tn], start=(k == 0), stop=(k == 7))) for k in range(8)], reads=[r_wa, r_abT], writes=[r3_])
                                    P.op("act", lambda e, p2_=p2_, tn=tn: e.activation(out=sgd[:, :tn], in_=p2_[:, :tn], func=AF.Sigmoid), reads=[r2_], writes=[r_sgd])
                                    P.op("dve", lambda e, p1_=p1_, c=c, t0=t0, tn=tn: e.tensor_tensor(out=tA[:, :tn], in0=p1_[:, :tn], in1=mT[:, c, t0:t0 + tn], op=ALU.mult), reads=[r1_, r_mT], writes=[r_tA])
                                    P.op("dve", lambda e, p3_=p3_, tn=tn: e.tensor_tensor(out=tB[:, :tn], in0=p3_[:, :tn], in1=sgd[:, :tn], op=ALU.mult), reads=[r3_, r_sgd], writes=[r_tB])
                                    P.op("dve", lambda e, c=c, t0=t0, tn=tn: e.tensor_tensor(out=mT[:, c, t0:t0 + tn], in0=tA[:, :tn], in1=tB[:, :tn], op=ALU.add), reads=[r_tA, r_tB], writes=[r_mT])
                        P.fence()
                        with ExitStack() as sB2:
                            wor = Ring([sb(f"wo{i}", [128, 16, 128], BF16, sB2) for i in range(2)])
                            po = Ring([ps(f"po{i}", [128, 512], F32, sB2) for i in range(4)])
                            for c in range(16):
                                wo, r_wo = wor.next()
                                P.dma("pool", lambda e, wo=wo, c=c: e.dma_start(out=wo[:], in_=w_out[:, 128 * c:128 * c + 128].rearrange("(k p) n -> p k n", p=128)), writes=[r_wo])
                                for (t0, tn) in TS:
                                    pp, rp = po.next()
                                    P.op("pe", [(lambda e, k=k, pp=pp, wo=wo, t0=t0, tn=tn: e.matmul(out=pp[:, :tn], lhsT=wo[:, k, :], rhs=mT[:, k, t0:t0 + tn], start=(k == 0), stop=(k == 15))) for k in range(16)], reads=[r_wo, r_mT], writes=[rp])
                                    P.op("dve", lambda e, pp=pp, c=c, t0=t0, tn=tn: e.tensor_tensor(out=hT[:, c, t0:t0 + tn], in0=hT[:, c, t0:t0 + tn], in1=pp[:, :tn], op=ALU.add), reads=[rp, r_hT], writes=[r_hT])

                    P.fence()
                    with ExitStack() as sB3:
                        sq = Ring([sb(f"sq{i}", [128, 342], F32, sB3) for i in range(2)])
                        rstd = sb("rstd", [128, NW], F32, sB3); r_rstd = Res()
                        wm = sb("wm", [128, NW], F32, sB3); r_wm = Res()
                        P.dma("act", lambda e: e.dma_start(out=wm[:], in_=wmask_d[:, :]), writes=[r_wm])
                        pss3 = [ps(f"pss3{i}", [128, 512], F32, sB3) for i in range(3)]; r_pss3 = [Res() for _ in range(3)]
                        for ti, (t0, tn) in enumerate(TS):
                            fns = []
                            for c in range(16):
                                s_, r_s = sq.next()
                                P.op("act", lambda e, s_=s_, c=c, t0=t0, tn=tn: e.activation(out=s_[:, :tn], in_=hT[:, c, t0:t0 + tn], func=AF.Square), reads=[r_hT], writes=[r_s])
                                P.op("pe", lambda e, s_=s_, c=c, ti=ti, tn=tn: e.matmul(out=pss3[ti][:, :tn], lhsT=onesf[:], rhs=s_[:, :tn], start=(c == 0), stop=(c == 15), skip_group_check=True), reads=[r_s, r_onesf], writes=[r_pss3[ti]])
                            P.op("dve", lambda e, ti=ti, t0=t0, tn=tn: e.tensor_scalar(out=rstd[:, t0:t0 + tn], in0=pss3[ti][:, :tn], scalar1=1.0 / D, scalar2=EPS, op0=ALU.mult, op1=ALU.add), reads=[r_pss3[ti]], writes=[r_rstd])
                        P.op("act", lambda e: e.activation(out=rstd[:], in_=rstd[:], func=AF.Sqrt), reads=[r_rstd], writes=[r_rstd])
                        P.op("dve", lambda e: e.reciprocal(out=rstd[:], in_=rstd[:]), reads=[r_rstd], writes=[r_rstd])
                        P.op("dve", lambda e: e.tensor_tensor(out=rstd[:], in0=rstd[:], in1=wm[:], op=ALU.mult), reads=[r_rstd, r_wm], writes=[r_rstd])
                        for c in range(16):
                            P.op("dve", lambda e, c=c: e.scalar_tensor_tensor(out=uT2[:, c, :], in0=hT[:, c, :], scalar=g2t[:, c:c + 1], in1=rstd[:], op0=ALU.mult, op1=ALU.mult), reads=[r_hT, r_g2t, r_rstd], writes=[r_uT2])

                    P.fence()
                    with ExitStack() as sB4:
                        fcw = sb("fcw", [128, 264], F32, sB4); r_fcw = Res()
                        P.dma("act", lambda e: e.dma_start(out=fcw[:], in_=fconv[:, :]), writes=[r_fcw])
                        actT = sb("actT", [128, 22, 1024], BF16, sB4); r_actT = Res()
                        wur = Ring([sb(f"wu{i}", [128, 16, 256], BF16, sB4) for i in range(2)])
                        wdr = Ring([sb(f"wd{i}", [128, 22, 128], BF16, sB4) for i in range(2)])
                        upg = sb("upg", [128, NW], F32, sB4); r_upg = Res()
                        upv = sb("upv", [128, NW], F32, sB4); r_upv = Res()
                        cg = sb("cg", [128, 1024], F32, sB4); r_cg = Res()
                        cv = sb("cv", [128, 1024], F32, sB4); r_cv = Res()
                        sgl = sb("sgl", [128, 1024], F32, sB4); r_sgl = Res()
                        pu = Ring([ps(f"pu{i}", [128, 512], F32, sB4) for i in range(4)])
                        pd = Ring([ps(f"pd{i}", [128, 512], F32, sB4) for i in range(4)])
                        for half in range(2):
                            for fc in range(22):
                                f = half * 22 + fc
                                wu, r_wu = wur.next()
                                P.dma("pool", lambda e, wu=wu, f=f: e.dma_start(out=wu[:, :, 0:128], in_=w_up[:, 128 * f:128 * f + 128].rearrange("(k p) n -> p k n", p=128)), writes=[r_wu])
                                P.dma("pool", lambda e, wu=wu, f=f: e.dma_start(out=wu[:, :, 128:256], in_=w_up[:, FFN + 128 * f:FFN + 128 * f + 128].rearrange("(k p) n -> p k n", p=128)), writes=[r_wu])
                                for (t0, tn) in TS:
                                    pg_, rg_ = pu.next()
                                    P.op("pe", [(lambda e, k=k, pg_=pg_, wu=wu, t0=t0, tn=tn: e.matmul(out=pg_[:, :tn], lhsT=wu[:, k, 0:128], rhs=uT2[:, k, t0:t0 + tn], start=(k == 0), stop=(k == 15))) for k in range(16)], reads=[r_wu, r_uT2], writes=[rg_])
                                    P.op("act", lambda e, pg_=pg_, t0=t0, tn=tn: e.copy(out=upg[:, t0:t0 + tn], in_=pg_[:, :tn]), reads=[rg_], writes=[r_upg])
                                    pv_, rv_ = pu.next()
                                    P.op("pe", [(lambda e, k=k, pv_=pv_, wu=wu, t0=t0, tn=tn: e.matmul(out=pv_[:, :tn], lhsT=wu[:, k, 128:256], rhs=uT2[:, k, t0:t0 + tn], start=(k == 0), stop=(k == 15))) for k in range(16)], reads=[r_wu, r_uT2], writes=[rv_])
                                    P.op("act", lambda e, pv_=pv_, t0=t0, tn=tn: e.copy(out=upv[:, t0:t0 + tn], in_=pv_[:, :tn]), reads=[rv_], writes=[r_upv])
                                for (src, r_src, dst, r_dst, ci) in ((upg, r_upg, cg, r_cg, f), (upv, r_upv, cv, r_cv, 44 + f)):
                                    P.op("dve", lambda e, src=src, dst=dst, ci=ci: e.tensor_scalar(out=dst[:], in0=src[:, 0:1024], scalar1=fcw[:, 3 * ci:3 * ci + 1], scalar2=None, op0=ALU.mult), reads=[r_src, r_fcw], writes=[r_dst])
                                    P.op("dve", lambda e, src=src, dst=dst, ci=ci: e.scalar_tensor_tensor(out=dst[:], in0=src[:, 1:1025], scalar=fcw[:, 3 * ci + 1:3 * ci + 2], in1=dst[:], op0=ALU.mult, op1=ALU.add), reads=[r_src, r_fcw, r_dst], writes=[r_dst])
                                    P.op("dve", lambda e, src=src, dst=dst, ci=ci: e.scalar_tensor_tensor(out=dst[:], in0=src[:, 2:1026], scalar=fcw[:, 3 * ci + 2:3 * ci + 3], in1=dst[:], op0=ALU.mult, op1=ALU.add), reads=[r_src, r_fcw, r_dst], writes=[r_dst])
                                P.op("act", lambda e: e.activation(out=sgl[:], in_=cg[:], func=AF.Silu), reads=[r_cg], writes=[r_sgl])
                                P.op("dve", lambda e, fc=fc: e.tensor_tensor(out=actT[:, fc, :], in0=sgl[:], in1=cv[:], op=ALU.mult), reads=[r_sgl, r_cv], writes=[r_actT])
                            for c in range(16):
                                wd, r_wd = wdr.next()
                                P.dma("pool", lambda e, wd=wd, c=c, half=half: e.dma_start(out=wd[:], in_=w_down[2816 * half:2816 * half + 2816, 128 * c:128 * c + 128].rearrange("(k p) n -> p k n", p=128)), writes=[r_wd])
                                for t2_ in range(2):
                                    pp, rp = pd.next()
                                    P.op("pe", [(lambda e, k=k, pp=pp, wd=wd, t2_=t2_: e.matmul(out=pp[:, :], lhsT=wd[:, k, :], rhs=actT[:, k, 512 * t2_:512 * t2_ + 512], start=(k == 0), stop=(k == 21))) for k in range(22)], reads=[r_wd, r_actT], writes=[rp])
                                    P.op("dve", lambda e, pp=pp, c=c, t2_=t2_: e.tensor_tensor(out=hT[:, c, 1 + 512 * t2_:1 + 512 * t2_ + 512], in0=hT[:, c, 1 + 512 * t2_:1 + 512 * t2_ + 512], in1=pp[:, :], op=ALU.add), reads=[rp, r_hT], writes=[r_hT])

                    P.fence()
                    with ExitStack() as sB5:
                        gft = sb("gft", [128, D], F32, sB5); r_gft = Res()
                        P.dma("act", lambda e: e.dma_start(out=gft[:], in_=gfb[:, :]), writes=[r_gft])
                        pF = [ps(f"pF{i}", [128, 1024], F32, sB5) for i in range(2)]; r_pF = [Res(), Res()]
                        oring = Ring([sb(f"ob{i}", [128, D], F32, sB5) for i in range(2)])
                        fj = sb("fj", [128, 1024], F32, sB5); r_fj = Res()
                        fs = sb("fs", [128, 4], F32, sB5); r_fs = Res()
                        for j in range(8):
                            for hh in range(2):
                                P.op("pe", [(lambda e, k=k, hh=hh, j=j: e.transpose(out=pF[hh][:, (k % 8) * 128:(k % 8) * 128 + 128], in_=hT[:, k, 1 + 128 * j:1 + 128 * j + 128], identity=idf[:])) for k in range(8 * hh, 8 * hh + 8)], reads=[r_hT, r_idf], writes=[r_pF[hh]])
                                P.op("act", lambda e, hh=hh: e.activation(out=fj[:], in_=pF[hh][:], func=AF.Square, accum_out=fs[:, hh:hh + 1]), reads=[r_pF[hh]], writes=[r_fj, r_fs])
                            P.op("dve", lambda e: e.tensor_tensor(out=fs[:, 2:3], in0=fs[:, 0:1], in1=fs[:, 1:2], op=ALU.add), reads=[r_fs], writes=[r_fs])
                            P.op("dve", lambda e: e.tensor_scalar(out=fs[:, 2:3], in0=fs[:, 2:3], scalar1=1.0 / D, scalar2=EPS, op0=ALU.mult, op1=ALU.add), reads=[r_fs], writes=[r_fs])
                            P.op("act", lambda e: e.activation(out=fs[:, 2:3], in_=fs[:, 2:3], func=AF.Sqrt), reads=[r_fs], writes=[r_fs])
                            P.op("dve", lambda e: e.reciprocal(out=fs[:, 3:4], in_=fs[:, 2:3]), reads=[r_fs], writes=[r_fs])
                            ob, r_ob = oring.next()
                            for hh in range(2):
                                P.op("dve", lambda e, hh=hh, ob=ob: e.scalar_tensor_tensor(out=ob[:, 1024 * hh:1024 * hh + 1024], in0=pF[hh][:], scalar=fs[:, 3:4], in1=gft[:, 1024 * hh:1024 * hh + 1024], op0=ALU.mult, op1=ALU.mult), reads=[r_pF[hh], r_fs, r_gft], writes=[r_ob])
                            final_toks.append(P.dma("sp", lambda e, ob=ob, j=j: e.dma_start(out=out_d[128 * j:128 * j + 128, :], in_=ob[:]), reads=[r_ob]))
        except _Stop:
            pass
        if True:
            if DEBUG and dbg_holder:
                final_toks.append(dbg_holder[0])
            P.finish(final_toks)
    return nc


_NC_CACHE = {}


def _rope_tables():
    inv_freq = (500000.0 ** (-np.arange(0, 16, 2, dtype=np.float32) / 16)).astype(np.float32)
    ang = np.arange(L, dtype=np.float32)[:, None] * inv_freq[None, :]
    cos = np.cos(ang).astype(np.float32).T
    sin = np.sin(ang).astype(np.float32).T
    cosF = np.ones((128, L), np.float32)
    sinF = np.zeros((128, L), np.float32)
    for mp in range(2):
        b0 = 64 * mp
        cosF[b0:b0 + 8] = cos
        cosF[b0 + 8:b0 + 16] = cos
        sinF[b0:b0 + 8] = -sin
        sinF[b0 + 8:b0 + 16] = sin
    return cosF, sinF


def kernel(x, meta_tokens, norm1_g, w_in, mlstm_conv_w, mlstm_gate_bias, mlstm_norm_g,
           lambda_q1, lambda_k1, lambda_q2, lambda_k2, diff_subln_g, w_branch_m, w_branch_d,
           w_out, norm2_g, w_up, ffn_conv_w, w_down, norm_f_g):
    f32 = np.float32
    x = np.asarray(x, f32)
    w_in0 = np.asarray(w_in, f32)[0]
    B = x.shape[0]
    cosF, sinF = _rope_tables()
    o_mqk, o_mv, o_mo, o_gates, o_dq, o_dk, o_dv, o_gm = 0, 2048, 3072, 4096, 4112, 5136, 6160, 7184
    rotperm = np.arange(128)
    for mp in range(2):
        b0 = 64 * mp
        rotperm[b0:b0 + 8] = np.arange(b0 + 8, b0 + 16)
        rotperm[b0 + 8:b0 + 16] = np.arange(b0, b0 + 8)
    ident = np.eye(128, dtype=f32)
    triu = np.triu(np.ones((64, 64), f32))
    tril = np.tril(np.ones((64, 64), f32))
    common = {
        "w_g": np.ascontiguousarray(w_in0[:, o_gm:o_gm + 4096]),
        "w_a": np.ascontiguousarray(np.asarray(w_branch_m, f32)[0]),
        "w_b": np.ascontiguousarray(np.asarray(w_branch_d, f32)[0]),
        "w_out": np.ascontiguousarray(np.asarray(w_out, f32)[0]),
        "w_up": np.ascontiguousarray(np.asarray(w_up, f32)[0]),
        "w_down": np.ascontiguousarray(np.asarray(w_down, f32)[0]),
        "g1b": np.ascontiguousarray(np.broadcast_to(np.asarray(norm1_g, f32)[0], (128, D))),
        "gfb": np.ascontiguousarray(np.broadcast_to(np.asarray(norm_f_g, f32), (128, D))),
        "g2c": np.ascontiguousarray(np.asarray(norm2_g, f32)[0].reshape(16, 128).T),
        "cosf": cosF, "sinf": sinF,
        "fconv": np.ascontiguousarray(np.asarray(ffn_conv_w, f32)[0].reshape(3, 88, 128).transpose(2, 1, 0).reshape(128, 264)),
        "lamv": np.ascontiguousarray(np.broadcast_to(np.concatenate([np.asarray(a, f32)[0] for a in (lambda_q1, lambda_k1, lambda_q2, lambda_k2)]), (128, 256))),
        "sublng": np.ascontiguousarray(np.broadcast_to(np.asarray(diff_subln_g, f32)[0], (128, 128))),
        "ident": ident, "triu": triu, "tril": tril,
    }
    if "B" not in STAGES:
        for nm in ("w_g", "w_a", "w_b", "w_out", "w_up", "w_down"):
            common[nm] = np.zeros((128, 128), f32)
    in_maps = []
    mcw_full = np.asarray(mlstm_conv_w, f32)[0]
    gb_full = np.asarray(mlstm_gate_bias, f32)[0]
    for c in range(8):
        b, g = c // 4, c % 4
        hfull = np.concatenate([np.asarray(meta_tokens, f32), x[b]], axis=0)
        s0 = 15 + 1024 * g
        xwin = np.zeros((NW, D), f32)
        e0 = min(s0 + NW, L)
        xwin[:e0 - s0] = hfull[s0:e0]
        wmask = np.ones((128, NW), f32)
        if e0 - s0 < NW:
            wmask[:, e0 - s0:] = 0.0
        cols = []
        for base in (o_dq, o_dk):
            for hh in range(2):
                head = 2 * g + hh
                cols.append(base + 128 * head + np.arange(128))
            for hh in range(2):
                head = 2 * g + hh
                cols.append(base + 128 * head + rotperm)
        for hh in range(2):
            head = 2 * g + hh
            cols.append(o_dv + 128 * head + np.arange(128))
        w_attn = np.ascontiguousarray(w_in0[:, np.concatenate(cols)])
        qc = o_mqk + 256 * g + np.arange(256)
        kc = o_mqk + 1024 + 256 * g + np.arange(256)
        vc = o_mv + 256 * g + np.arange(256)
        oc = o_mo + 256 * g + np.arange(256)
        gc = o_gates + np.array([4 + g, 12 + g, 0 + g, 8 + g])
        w_ml = np.ascontiguousarray(w_in0[:, np.concatenate([qc, kc, vc, oc, gc])])
        mconv = np.ascontiguousarray(mcw_full[:, np.concatenate([qc, kc])].reshape(3, 4, 128).transpose(2, 1, 0).reshape(128, 12))
        gbias = np.ascontiguousarray(np.broadcast_to(gb_full[[4 + g, 12 + g, 0 + g, 8 + g]], (128, 4)))
        mngv = np.ascontiguousarray(np.broadcast_to(np.asarray(mlstm_norm_g, f32)[0][256 * g:256 * g + 256], (64, 256)))
        m = dict(common)
        m.update({"hfull": hfull, "xwin": xwin, "w_attn": w_attn, "w_ml": w_ml, "mconv": mconv,
                  "gbias": gbias, "mng": mngv, "wmask": wmask})
        in_maps.append(m)
    if "nc" not in _NC_CACHE:
        _NC_CACHE["nc"] = build_program()
    nc = _NC_CACHE["nc"]
    res = run_bass_kernel_spmd(nc, in_maps, core_ids=list(range(8)))
    out = np.empty((B, 4096, D), f32)
    for c in range(8):
        b, g = c // 4, c % 4
        out[b, 1024 * g:1024 * g + 1024] = res.results[c]["out"]
    if DEBUG:
        kernel.dbg = [res.results[c] for c in range(8)]
    return out
```

```python
import numpy as np
from contextlib import ExitStack
import concourse.bass as bass
import concourse.mybir as mybir
from concourse.bass_utils import run_bass_kernel_spmd

F32 = mybir.dt.float32
BF16 = mybir.dt.bfloat16
AF = mybir.ActivationFunctionType
ALU = mybir.AluOpType
AX = mybir.AxisListType

SAME_ENGINE_SYNC = True
DEBUG = False
STAGES = ("A1", "A2", "A3", "X", "B")

D = 2048
L = 4112
LP = 4160
NMETA = 16
NW = 1026
FFN = 5632
EPS = 1e-6


class _Stop(Exception):
    pass


STOP_AT = None


_PROG = []


def stop_here(tag):
    if STOP_AT == tag:
        _PROG[0].stopped = True


_FENCE = []


class Res:
    __slots__ = ("name", "w", "r")

    def __init__(self, name=""):
        self.name = name
        self.w = None
        self.r = dict(_FENCE)


class Prog:
    ENGS = ("pe", "act", "dve", "pool", "sp")

    def __init__(self, nc, stack, n_dma_sems=8):
        self.nc = nc
        self.stack = stack
        self.streams = {e: [] for e in self.ENGS}
        self.sems = {}
        self.count = {}
        self.waited = {e: {} for e in self.ENGS}
        for e in self.ENGS:
            self.sems["c_" + e] = stack.enter_context(nc.semaphore("c_" + e))
            self.count["c_" + e] = 0
        self.dma_pool = {}
        self.dma_next = {}
        for e in ("sp", "act", "pool"):
            keys = []
            for i in range(n_dma_sems):
                k = f"d_{e}{i}"
                self.sems[k] = stack.enter_context(nc.semaphore(k))
                self.count[k] = 0
                keys.append(k)
            self.dma_pool[e] = keys
            self.dma_next[e] = 0
        self.sems["cc"] = stack.enter_context(nc.semaphore("cc"))
        self.count["cc"] = 0
        self.stopped = False
        _PROG[:] = [self]
        _FENCE[:] = []

    def _need(self, eng, dep):
        if dep is None:
            return
        key, val = dep
        if key == "c_" + eng and (eng == "pe" or not SAME_ENGINE_SYNC):
            return
        if self.waited[eng].get(key, 0) >= val:
            return
        self.waited[eng][key] = val
        self.streams[eng].append(("wait", key, val))

    def _deps(self, eng, reads, writes):
        own = "c_" + eng
        for r in reads:
            self._need(eng, r.w)
            for k, v in r.r.items():
                if k != own:
                    self._need(eng, (k, v))
        for w in writes:
            self._need(eng, w.w)
            for k, v in w.r.items():
                self._need(eng, (k, v))

    def _commit(self, reads, writes, tok):
        for r in reads:
            if r.r.get(tok[0], 0) < tok[1]:
                r.r[tok[0]] = tok[1]
        for w in writes:
            w.w = tok
            w.r = {}

    def op(self, eng, fns, reads=(), writes=()):
        if self.stopped:
            return None
        if not isinstance(fns, (list, tuple)):
            fns = [fns]
        self._deps(eng, reads, writes)
        key = "c_" + eng
        self.count[key] += 1
        tok = (key, self.count[key])
        self.streams[eng].append(("op", fns, key, 1))
        self._commit(reads, writes, tok)
        return tok

    def dma(self, eng, fn, reads=(), writes=()):
        if self.stopped:
            return None
        pool = self.dma_pool[eng]
        key = pool[self.dma_next[eng] % len(pool)]
        self.dma_next[eng] += 1
        if self.count[key] > 0:
            self._need(eng, (key, self.count[key]))
        self._deps(eng, reads, writes)
        self.count[key] += 16
        tok = (key, self.count[key])
        self.streams[eng].append(("op", [fn], key, 16))
        self._commit(reads, writes, tok)
        return tok

    def fence(self):
        _FENCE[:] = [(k, c) for k, c in self.count.items() if c > 0 and k != "cc"]

    def cc(self, fn, reads=(), writes=()):
        if self.stopped:
            return None
        eng = "pool"
        self._deps(eng, reads, writes)
        self.count["cc"] += 1
        tok = ("cc", self.count["cc"])
        self.streams[eng].append(("cc", fn, "cc"))
        self._commit(reads, writes, tok)
        return tok

    def finish(self, final_tokens):
        for t in final_tokens:
            self._need("sp", t)
        for k, c in self.count.items():
            if c > 0:
                self._need("sp", (k, c))
        nc = self.nc
        with nc.Block() as block:
            def mk(ename):
                def body(e):
                    for item in self.streams[ename]:
                        if item[0] == "wait":
                            e.wait_ge(self.sems[item[1]], item[2])
                        elif item[0] == "op":
                            fns, key, inc = item[1], item[2], item[3]
                            for f in fns[:-1]:
                                f(e)
                            fns[-1](e).then_inc(self.sems[key], inc)
                        elif item[0] == "cc":
                            item[1](e).then_inc(self.sems[item[2]])
                return body
            block.tensor(mk("pe"))
            block.scalar(mk("act"))
            block.vector(mk("dve"))
            block.gpsimd(mk("pool"))
            block.sync(mk("sp"))


class Ring:
    def __init__(self, tiles):
        self.tiles = tiles
        self.res = [Res() for _ in tiles]
        self.i = 0

    def next(self):
        k = self.i % len(self.tiles)
        self.i += 1
        return self.tiles[k], self.res[k]


def build_program():
    nc = bass.Bass("TRN2", target_bir_lowering=False)
    dt_in = lambda name, shape, dt=F32: nc.dram_tensor(name, shape, dt, kind="ExternalInput").ap()
    dt_int = lambda name, shape, dt: nc.dram_tensor(name, shape, dt, kind="Internal").ap()

    hfull = dt_in("hfull", [L, D])
    xwin = dt_in("xwin", [NW, D])
    w_attn = dt_in("w_attn", [D, 1280])
    w_ml = dt_in("w_ml", [D, 1028])
    w_g = dt_in("w_g", [D, 4096] if "B" in STAGES else [128, 128])
    w_a = dt_in("w_a", [1024, D] if "B" in STAGES else [128, 128])
    w_b = dt_in("w_b", [1024, D] if "B" in STAGES else [128, 128])
    w_out = dt_in("w_out", [D, D] if "B" in STAGES else [128, 128])
    w_up = dt_in("w_up", [D, 2 * FFN] if "B" in STAGES else [128, 128])
    w_down = dt_in("w_down", [FFN, D] if "B" in STAGES else [128, 128])
    g1b = dt_in("g1b", [128, D])
    gfb = dt_in("gfb", [128, D])
    g2c = dt_in("g2c", [128, 16])
    cosf = dt_in("cosf", [128, L])
    sinf = dt_in("sinf", [128, L])
    mconv = dt_in("mconv", [128, 12])
    fconv = dt_in("fconv", [128, 88 * 3])
    gbias = dt_in("gbias", [128, 4])
    mng = dt_in("mng", [64, 256])
    lamv = dt_in("lamv", [128, 4 * 64])
    sublng = dt_in("sublng", [128, 128])
    ident_d = dt_in("ident", [128, 128])
    triu_d = dt_in("triu", [64, 64])
    tril_d = dt_in("tril", [64, 64])
    wmask_d = dt_in("wmask", [128, NW])
    out_d = nc.dram_tensor("out", [1024, D], F32, kind="ExternalOutput").ap()
    if DEBUG:
        dbg_cc = nc.dram_tensor("dbg_cc", [512, 4114], BF16, kind="ExternalOutput").ap()

    mqk_raw = dt_int("mqk_raw", [512, LP + 2], BF16)
    mv_s = dt_int("mv_s", [LP, 257], BF16)
    mo_s = dt_int("mo_s", [LP, 256], F32)
    gates_s = dt_int("gates_s", [LP, 4], F32)
    hf_s = dt_int("hf_s", [LP, 256], F32)
    hb_s = dt_int("hb_s", [LP, 256], F32)
    cc_in = dt_int("cc_in", [512, 4114], BF16)
    cc_win = dt_int("cc_win", [8 * 256, NW], BF16)
    cc_out = dt_int("cc_out", [8 * 1024, NW], BF16)

    with ExitStack() as st:
        P = Prog(nc, st)

        def sb(name, shape, dt, stack=st):
            return stack.enter_context(nc.sbuf_tensor(name, shape, dt))

        def ps(name, shape, dt, stack=st):
            return stack.enter_context(nc.psum_tensor(name, shape, dt))

        final_toks = []
        dbg_holder = []
        try:
            idf = sb("idf", [128, 128], F32); r_idf = Res()
            idb = sb("idb", [128, 128], BF16); r_idb = Res()
            zt = sb("zt", [128, 512], BF16); r_zt = Res()
            P.dma("sp", lambda e: e.dma_start(out=idf[:], in_=ident_d[:, :]), writes=[r_idf])
            P.op("dve", lambda e: e.tensor_copy(out=idb[:], in_=idf[:]), reads=[r_idf], writes=[r_idb])
            P.op("dve", lambda e: e.memset(zt[:], 0.0), writes=[r_zt])

            r_mqk_raw = Res(); r_mv_s = Res(); r_mo_s = Res(); r_gates_s = Res(); r_hf_s = Res(); r_hb_s = Res(); r_cc_in = Res(); r_cc_in_b = Res()
            r_cc_win = [Res(), Res()]; r_cc_out = [Res(), Res()]
            P.dma("sp", lambda e: e.dma_start(out=mqk_raw.rearrange("(c p) n -> p c n", p=128)[:, :, 0:49], in_=zt[:, 0:196].rearrange("p (c n) -> p c n", c=4)), reads=[r_zt], writes=[r_mqk_raw])
            P.dma("sp", lambda e: e.dma_start(out=mqk_raw.rearrange("(c p) n -> p c n", p=128)[:, :, LP + 1:LP + 2], in_=zt[:, 0:4].rearrange("p (c n) -> p c n", c=4), allow_slow_non_contiguous=True), reads=[r_zt], writes=[r_mqk_raw])
            P.dma("sp", lambda e: e.dma_start(out=mv_s[0:48, :], in_=zt[0:48, 0:257]), reads=[r_zt], writes=[r_mv_s])
            ztf = sb("ztf", [64, 256], F32); r_ztf = Res()
            P.op("dve", lambda e: e.memset(ztf[:], 0.0), writes=[r_ztf])
            P.dma("sp", lambda e: e.dma_start(out=mo_s[0:48, :], in_=ztf[0:48, :]), reads=[r_ztf], writes=[r_mo_s])
            P.dma("sp", lambda e: e.dma_start(out=gates_s[0:48, :], in_=ztf[0:48, 0:4]), reads=[r_ztf], writes=[r_gates_s])
            P.dma("sp", lambda e: e.dma_start(out=cc_in[:, 4112:4114].rearrange("(c p) n -> p c n", p=128), in_=zt[:, 0:8].rearrange("p (c n) -> p c n", c=4)), reads=[r_zt], writes=[r_cc_in, r_cc_in_b])


            def exchange_half(half, js=(0, 1, 2, 3)):
                r_src = r_cc_in if half == 0 else r_cc_in_b
                for j in js:
                    i = j * 2 + half
                    P.dma("sp", lambda e, i=i, j=j, half=half: e.dma_start(out=cc_win[i * 256:(i + 1) * 256, :], in_=cc_in[half * 256:(half + 1) * 256, 15 + 1024 * j:15 + 1024 * j + NW]), reads=[r_src], writes=[r_cc_win[half]])
                for j in js:
                    i = j * 2 + half
                    P.cc(lambda e, i=i: e.collective_compute("AllGather", ALU.bypass, replica_groups=[[0, 1, 2, 3], [4, 5, 6, 7]], ins=[cc_win[i * 256:(i + 1) * 256, :]], outs=[cc_out[i * 1024:(i + 1) * 1024, :]]), reads=[r_cc_win[half]], writes=[r_cc_out[half]])

            stop_here("c0")
            with ExitStack() as sa:
                dqkT = sb("dqkT", [128, 4, L], BF16, sa); r_dqkT = Res()
                vatt = sb("vatt", [128, 33, 2, 129], BF16, sa); r_vatt = Res()
                with ExitStack() as s1:
                    wat = sb("wat", [128, 16, 1280], BF16, s1); r_wat = Res()
                    wml = sb("wml", [128, 16, 1028], BF16, s1); r_wml = Res()
                    g1t = sb("g1t", [128, D], F32, s1); r_g1t = Res()
                    gbt = sb("gbt", [128, 4], F32, s1); r_gbt = Res()
                    for kq in range(4):
                        P.dma("pool", lambda e, kq=kq: e.dma_start(out=wat[:, 4 * kq:4 * kq + 4, :], in_=w_attn[512 * kq:512 * kq + 512, :].rearrange("(k p) n -> p k n", p=128)), writes=[r_wat])
                        P.dma("pool", lambda e, kq=kq: e.dma_start(out=wml[:, 4 * kq:4 * kq + 4, :], in_=w_ml[512 * kq:512 * kq + 512, :].rearrange("(k p) n -> p k n", p=128)), writes=[r_wml])
                    P.dma("act", lambda e: e.dma_start(out=g1t[:], in_=g1b[:, :]), writes=[r_g1t])
                    P.dma("act", lambda e: e.dma_start(out=gbt[:], in_=gbias[:, :]), writes=[r_gbt])
                    P.op("dve", lambda e: e.memset(vatt[:, :, :, 128:129], 1.0), writes=[r_vatt])

                    xring = Ring([sb(f"xt{i}", [128, D], F32, s1) for i in range(2)])
                    ss = sb("ss", [128, 1], F32, s1); r_ss = Res()
                    u = sb("u", [128, D], BF16, s1); r_u = Res()
                    uTr = Ring([sb(f"uT{i}", [128, 16, 512], BF16, s1) for i in range(2)])
                    csr = Ring([sb(f"cs{i}", [128, 2, 512], F32, s1) for i in range(2)])
                    rt1 = sb("rt1", [128, 512], F32, s1); r_rt1 = Res()
                    rt2 = sb("rt2", [128, 512], F32, s1); r_rt2 = Res()
                    mstage = Ring([sb(f"mst{i}", [128, 4, 512], BF16, s1) for i in range(1)])
                    mvst = Ring([sb(f"mvst{i}", [128, 257], BF16, s1) for i in range(2)])
                    most = Ring([sb(f"most{i}", [128, 256], F32, s1) for i in range(2)])
                    gst = Ring([sb(f"gst{i}", [128, 4], F32, s1) for i in range(2)])
                    pT = ps("pT", [128, D], BF16, s1); r_pT = Res()
                    pa = ps("pa", [128, 512], F32, s1); r_pa = Res()
                    pb = ps("pb", [128, 512], F32, s1); r_pb = Res()
                    pm = ps("pm", [128, 512], F32, s1); r_pm = Res()
                    pv = ps("pv", [128, 512], F32, s1); r_pv = Res()
                    pvo = ps("pvo", [128, 512], F32, s1); r_pvo = Res()
                    pg = ps("pg", [128, 512], F32, s1); r_pg = Res()
                    for rr, rres in zip(mvst.tiles, mvst.res):
                        P.op("dve", lambda e, rr=rr: e.memset(rr[:, 256:257], 1.0), writes=[rres])

                    def norm_block(xt, r_xt, bs, gtile, r_g):
                        P.op("act", lambda e: e.activation(out=u[:bs, :], in_=xt[:bs, :], func=AF.Square, accum_out=ss[:bs, :]), reads=[r_xt], writes=[r_u, r_ss])
                        P.op("dve", lambda e: e.tensor_scalar(out=ss[:bs, :], in0=ss[:bs, :], scalar1=1.0 / D, scalar2=EPS, op0=ALU.mult, op1=ALU.add), reads=[r_ss], writes=[r_ss])
                        P.op("act", lambda e: e.activation(out=ss[:bs, :], in_=ss[:bs, :], func=AF.Sqrt), reads=[r_ss], writes=[r_ss])
                        P.op("dve", lambda e: e.reciprocal(out=ss[:bs, :], in_=ss[:bs, :]), reads=[r_ss], writes=[r_ss])
                        P.op("dve", lambda e: e.scalar_tensor_tensor(out=u[:bs, :], in0=xt[:bs, :], scalar=ss[:bs, 0:1], in1=gtile[:bs, :], op0=ALU.mult, op1=ALU.mult), reads=[r_xt, r_ss, r_g], writes=[r_u])

                    stop_here("a1w")
                    tiles = [(512 * i, 512) for i in range(8)] + [(4096, 16)]
                    for (p0, n) in (tiles if "A1" in STAGES else []):
                        nblk = (n + 127) // 128
                        cs, r_cs = csr.next()
                        uT, r_uT = uTr.next()
                        P.dma("act", lambda e, cs=cs, p0=p0, n=n: e.dma_start(out=cs[:, 0, :n], in_=cosf[:, p0:p0 + n]), writes=[r_cs])
                        P.dma("act", lambda e, cs=cs, p0=p0, n=n: e.dma_start(out=cs[:, 1, :n], in_=sinf[:, p0:p0 + n]), writes=[r_cs])
                        for j in range(nblk):
                            bs = min(128, n - 128 * j)
                            xt, r_xt = xring.next()
                            P.dma("sp", lambda e, xt=xt, bs=bs, r0=p0 + 128 * j: e.dma_start(out=xt[:bs, :], in_=hfull[r0:r0 + bs, :]), writes=[r_xt])
                            norm_block(xt, r_xt, bs, g1t, r_g1t)
                            P.op("pe", [(lambda e, k=k, bs=bs: e.transpose(out=pT[:, k * 128:k * 128 + bs], in_=u[:bs, k * 128:(k + 1) * 128], identity=idb[:bs, :bs])) for k in range(16)], reads=[r_u, r_idb], writes=[r_pT])
                            P.op("act", lambda e, uT=uT, j=j, bs=bs: e.copy(out=uT[:, :, 128 * j:128 * j + bs], in_=pT[:].rearrange("p (k n) -> p k n", k=16)[:, :, :bs]), reads=[r_pT], writes=[r_uT])
                        stop_here("blk%d" % (p0 // 512))
                        for c in range(4):
                            cm = (c % 2) * 128 + (c // 2) * 512
                            cr = cm + 256
                            P.op("pe", [(lambda e, uT=uT, k=k, cm=cm, n=n: e.matmul(out=pa[:, :n], lhsT=wat[:, k, cm:cm + 128], rhs=uT[:, k, :n], start=(k == 0), stop=(k == 15))) for k in range(16)], reads=[r_wat, r_uT], writes=[r_pa])
                            P.op("pe", [(lambda e, uT=uT, k=k, cr=cr, n=n: e.matmul(out=pb[:, :n], lhsT=wat[:, k, cr:cr + 128], rhs=uT[:, k, :n], start=(k == 0), stop=(k == 15))) for k in range(16)], reads=[r_wat, r_uT], writes=[r_pb])
                            P.op("dve", lambda e, cs=cs, n=n: e.tensor_tensor(out=rt1[:, :n], in0=pa[:, :n], in1=cs[:, 0, :n], op=ALU.mult), reads=[r_pa, r_cs], writes=[r_rt1])
                            P.op("dve", lambda e, cs=cs, n=n: e.tensor_tensor(out=rt2[:, :n], in0=pb[:, :n], in1=cs[:, 1, :n], op=ALU.mult), reads=[r_pb, r_cs], writes=[r_rt2])
                            P.op("dve", lambda e, c=c, p0=p0, n=n: e.tensor_tensor(out=dqkT[:, c, p0:p0 + n], in0=rt1[:, :n], in1=rt2[:, :n], op=ALU.add), reads=[r_rt1, r_rt2], writes=[r_dqkT])
                        stop_here("fm%d" % (p0 // 512))
                        mst, r_mst = mstage.next()
                        for c in range(4):
                            P.op("pe", [(lambda e, uT=uT, k=k, c=c, n=n: e.matmul(out=pm[:, :n], lhsT=wml[:, k, c * 128:(c + 1) * 128], rhs=uT[:, k, :n], start=(k == 0), stop=(k == 15))) for k in range(16)], reads=[r_wml, r_uT], writes=[r_pm])
                            P.op("act", lambda e, mst=mst, c=c, n=n: e.copy(out=mst[:, c, :n], in_=pm[:, :n]), reads=[r_pm], writes=[r_mst])
                        P.dma("sp", lambda e, mst=mst, p0=p0, n=n: e.dma_start(out=mqk_raw.rearrange("(c p) n -> p c n", p=128)[:, :, 49 + p0:49 + p0 + n], in_=mst[:, :, :n]), reads=[r_mst], writes=[r_mqk_raw])
                        stop_here("ml%d" % (p0 // 512))
                        for j in range(nblk):
                            bs = min(128, n - 128 * j)
                            kb = (p0 + 128 * j) // 128
                            r0 = 48 + p0 + 128 * j
                            P.op("pe", [(lambda e, uT=uT, k=k, j=j, bs=bs: e.matmul(out=pv[:bs, 0:256], lhsT=uT[:, k, 128 * j:128 * j + bs], rhs=wat[:, k, 1024:1280], start=(k == 0), stop=(k == 15))) for k in range(16)], reads=[r_wat, r_uT], writes=[r_pv])
                            P.op("dve", lambda e, kb=kb, bs=bs: e.tensor_copy(out=vatt[:bs, kb, :, 0:128], in_=pv[:bs, 0:256].rearrange("p (h d) -> p h d", h=2)), reads=[r_pv], writes=[r_vatt])
                            stop_here("tv%d_%d" % (p0 // 512, j))
                            P.op("pe", [(lambda e, uT=uT, k=k, j=j, bs=bs: e.matmul(out=pvo[:bs, :], lhsT=uT[:, k, 128 * j:128 * j + bs], rhs=wml[:, k, 512:1024], start=(k == 0), stop=(k == 15))) for k in range(16)], reads=[r_wml, r_uT], writes=[r_pvo])
                            mvt, r_mvt = mvst.next()
                            mot, r_mot = most.next()
                            P.op("act", lambda e, mvt=mvt, bs=bs: e.copy(out=mvt[:bs, 0:256], in_=pvo[:bs, 0:256]), reads=[r_pvo], writes=[r_mvt])
                            P.op("act", lambda e, mot=mot, bs=bs: e.activation(out=mot[:bs, :], in_=pvo[:bs, 256:512], func=AF.Sigmoid), reads=[r_pvo], writes=[r_mot])
                            P.dma("sp", lambda e, mvt=mvt, bs=bs, r0=r0: e.dma_start(out=mv_s[r0:r0 + bs, :], in_=mvt[:bs, :]), reads=[r_mvt], writes=[r_mv_s])
                            P.dma("sp", lambda e, mot=mot, bs=bs, r0=r0: e.dma_start(out=mo_s[r0:r0 + bs, :], in_=mot[:bs, :]), reads=[r_mot], writes=[r_mo_s])
                            stop_here("tm%d_%d" % (p0 // 512, j))
                            P.op("pe", [(lambda e, uT=uT, k=k, j=j, bs=bs: e.matmul(out=pg[:bs, 0:64], lhsT=uT[:, k, 128 * j:128 * j + bs], rhs=wml[:, k, 964:1028], start=(k == 0), stop=(k == 15))) for k in range(16)], reads=[r_wml, r_uT], writes=[r_pg])
                            stop_here("tgm%d_%d" % (p0 // 512, j))
                            gt_, r_gt = gst.next()
                            P.op("dve", lambda e, gt_=gt_, bs=bs: e.tensor_tensor(out=gt_[:bs, :], in0=pg[:bs, 60:64], in1=gbt[:bs, :], op=ALU.add), reads=[r_pg, r_gbt], writes=[r_gt])
                            stop_here("tga%d_%d" % (p0 // 512, j))
                            P.dma("sp", lambda e, gt_=gt_, bs=bs, r0=r0: e.dma_start(out=gates_s[r0:r0 + bs, :], in_=gt_[:bs, :]), reads=[r_gt], writes=[r_gates_s])
                    stop_here("a1t%d" % (p0 // 512))

                P.fence()
                stop_here("a1")
                with ExitStack() as s2:
                    lam_t = sb("lam_t", [128, 256], F32, s2); r_lam = Res()
                    lamw = sb("lamw", [128, 8], F32, s2); r_lamw = Res()
                    sg_t = sb("sg_t", [128, 128], F32, s2); r_sg = Res()
                    P.dma("act", lambda e: e.dma_start(out=lam_t[:], in_=lamv[:, :]), writes=[r_lam])
                    P.dma("act", lambda e: e.dma_start(out=sg_t[:], in_=sublng[:, :]), writes=[r_sg])
                    ljunk = sb("ljunk", [128, 64], F32, s2); r_lj = Res()
                    for i in range(2):
                        P.op("dve", lambda e, i=i: e.tensor_tensor(out=ljunk[:], in0=lam_t[:, 128 * i:128 * i + 64], in1=lam_t[:, 128 * i + 64:128 * i + 128], op=ALU.mult), reads=[r_lam], writes=[r_lj])
                        P.op("dve", lambda e, i=i: e.reduce_sum(out=lamw[:, i:i + 1], in_=ljunk[:], axis=AX.X), reads=[r_lj], writes=[r_lamw])
                    P.op("act", lambda e: e.activation(out=lamw[:, 2:4], in_=lamw[:, 0:2], func=AF.Exp), reads=[r_lamw], writes=[r_lamw])
                    P.op("dve", lambda e: e.tensor_tensor(out=lamw[:, 4:5], in0=lamw[:, 2:3], in1=lamw[:, 3:4], op=ALU.subtract), reads=[r_lamw], writes=[r_lamw])
                    P.op("dve", lambda e: e.tensor_scalar(out=lamw[:, 5:6], in0=lamw[:, 4:5], scalar1=0.2, scalar2=-1.0, op0=ALU.add, op1=ALU.mult), reads=[r_lamw], writes=[r_lamw])
                    P.op("dve", lambda e: e.tensor_scalar(out=sg_t[:], in0=sg_t[:], scalar1=0.8, scalar2=None, op0=ALU.mult), reads=[r_sg], writes=[r_sg])

                    stop_here("a2s")
                    pss = [Ring([ps(f"ps{m}{i}", [128, 512], F32, s2) for i in range(2)]) for m in range(2)]
                    pacc = [ps(f"pacc{i}", [128, 512], F32, s2) for i in range(3)]
                    r_pacc = Res()
                    ptr = ps("ptr", [128, 512], BF16, s2); r_ptr = Res()
                    ptile = [Ring([sb(f"pt{m}{i}", [128, 512], BF16, s2) for i in range(3)]) for m in range(2)]
                    rc = sb("rc", [128, 2], F32, s2); r_rc = Res()
                    t2 = sb("t2", [128, 128], F32, s2); r_t2 = Res()
                    ot = sb("ot", [128, 128], F32, s2); r_ot = Res()
                    oj = sb("oj", [128, 128], F32, s2); r_oj = Res()
                    os_ = sb("os_", [128, 1], F32, s2); r_os = Res()
                    bo = sb("bo", [128, 128], BF16, s2); r_bo = Res()
                    boT = Ring([sb(f"boT{i}", [128, 512], BF16, s2) for i in range(2)])

                    def acc(m, sub):
                        i = m * 4 + sub
                        return pacc[i // 3][:, (i % 3) * 129:(i % 3) * 129 + 129]

                    qtiles = [(512 * i, 512) for i in range(8)] + [(4096, 16)]
                    kblocks = [(128 * i, 128) for i in range(32)] + [(4096, 16)]
                    for h in range(2 if "A2" in STAGES else 0):
                        for (q0, nq) in qtiles:
                            nsub = (nq + 127) // 128
                            P.op("dve", [(lambda e, i=i: e.memset(pacc[i][:], 0.0)) for i in range(3)], writes=[r_pacc])
                            pend = []
                            for kbi, (k0, nk) in enumerate(kblocks):
                                cur = []
                                for m in range(2):
                                    pst, r_pst = pss[m].next()
                                    P.op("pe", lambda e, pst=pst, m=m, h=h, k0=k0, nk=nk, q0=q0, nq=nq: e.matmul(out=pst[:nk, :nq], lhsT=dqkT[64 * m:64 * m + 64, 2 + h, k0:k0 + nk], rhs=dqkT[64 * m:64 * m + 64, h, q0:q0 + nq], start=True, stop=True), reads=[r_dqkT], writes=[r_pst])
                                    cur.append((m, pst, r_pst))
                                newpend = []
                                for (m, pst, r_pst) in cur:
                                    pt, r_pt = ptile[m].next()
                                    P.op("act", lambda e, pst=pst, pt=pt, nk=nk, nq=nq: e.activation(out=pt[:nk, :nq], in_=pst[:nk, :nq], func=AF.Exp, scale=0.125), reads=[r_pst], writes=[r_pt])
                                    def pvf(pt=pt, r_pt=r_pt, m=m, nk=nk, kbi=kbi):
                                        P.op("pe", [(lambda e, pt=pt, m=m, sub=sub, nk=nk, kbi=kbi, h=h, qs=min(128, nq - 128 * sub): e.matmul(out=acc(m, sub)[:qs, :], lhsT=pt[:nk, 128 * sub:128 * sub + qs], rhs=vatt[:nk, kbi, h, :], start=False, stop=False, skip_group_check=True)) for sub in range(nsub)], reads=[r_pt, r_vatt], writes=[r_pacc])
                                    newpend.append(pvf)
                                for f in pend:
                                    f()
                                pend = newpend
                            for f in pend:
                                f()
                            bT, r_bT = boT.next()
                            for sub in range(nsub):
                                qs = min(128, nq - 128 * sub)
                                a1 = acc(0, sub); a2 = acc(1, sub)
                                P.op("dve", lambda e, a1=a1, qs=qs: e.reciprocal(out=rc[:qs, 0:1], in_=a1[:qs, 128:129]), reads=[r_pacc], writes=[r_rc])
                                P.op("dve", lambda e, a2=a2, qs=qs: e.reciprocal(out=rc[:qs, 1:2], in_=a2[:qs, 128:129]), reads=[r_pacc], writes=[r_rc])
                                P.op("dve", lambda e, a2=a2, qs=qs: e.tensor_scalar(out=t2[:qs, :], in0=a2[:qs, 0:128], scalar1=rc[:qs, 1:2], scalar2=lamw[:qs, 5:6], op0=ALU.mult, op1=ALU.mult), reads=[r_pacc, r_rc, r_lamw], writes=[r_t2])
                                P.op("dve", lambda e, a1=a1, qs=qs: e.scalar_tensor_tensor(out=ot[:qs, :], in0=a1[:qs, 0:128], scalar=rc[:qs, 0:1], in1=t2[:qs, :], op0=ALU.mult, op1=ALU.add), reads=[r_pacc, r_rc, r_t2], writes=[r_ot])
                                P.op("act", lambda e, qs=qs: e.activation(out=oj[:qs, :], in_=ot[:qs, :], func=AF.Square, accum_out=os_[:qs, :]), reads=[r_ot], writes=[r_oj, r_os])
                                P.op("dve", lambda e, qs=qs: e.tensor_scalar(out=os_[:qs, :], in0=os_[:qs, :], scalar1=1.0 / 128, scalar2=EPS, op0=ALU.mult, op1=ALU.add), reads=[r_os], writes=[r_os])
                                P.op("act", lambda e, qs=qs: e.activation(out=os_[:qs, :], in_=os_[:qs, :], func=AF.Sqrt), reads=[r_os], writes=[r_os])
                                P.op("dve", lambda e, qs=qs: e.reciprocal(out=os_[:qs, :], in_=os_[:qs, :]), reads=[r_os], writes=[r_os])
                                P.op("dve", lambda e, qs=qs: e.scalar_tensor_tensor(out=bo[:qs, :], in0=ot[:qs, :], scalar=os_[:qs, 0:1], in1=sg_t[:qs, :], op0=ALU.mult, op1=ALU.mult), reads=[r_ot, r_os, r_sg], writes=[r_bo])
                                P.op("pe", lambda e, sub=sub, qs=qs: e.transpose(out=ptr[:, 128 * sub:128 * sub + qs], in_=bo[:qs, :], identity=idb[:qs, :qs]), reads=[r_bo, r_idb], writes=[r_ptr])
                                P.op("act", lambda e, bT=bT, sub=sub, qs=qs: e.copy(out=bT[:, 128 * sub:128 * sub + qs], in_=ptr[:, 128 * sub:128 * sub + qs]), reads=[r_ptr], writes=[r_bT])
                            P.dma("sp", lambda e, bT=bT, h=h, q0=q0, nq=nq: e.dma_start(out=cc_in[256 + 128 * h:256 + 128 * h + 128, q0:q0 + nq], in_=bT[:, :nq]), reads=[r_bT], writes=[r_cc_in_b])

            if "X" in STAGES:
                exchange_half(1)
            P.fence()
            with ExitStack() as s3:
                mqkT = sb("mqkT", [128, 4, LP], BF16, s3); r_mqkT = Res()
                with ExitStack() as s3a:
                    raw = sb("raw", [128, 4, LP + 2], BF16, s3a); r_raw = Res()
                    cacc = sb("cacc", [128, LP], F32, s3a); r_cacc = Res()
                    mcw = sb("mcw", [128, 12], F32, s3a); r_mcw = Res()
                    P.dma("act", lambda e: e.dma_start(out=mcw[:], in_=mconv[:, :]), writes=[r_mcw])
                    P.dma("sp", lambda e: e.dma_start(out=raw[:], in_=mqk_raw.rearrange("(c p) n -> p c n", p=128)), reads=[r_mqk_raw], writes=[r_raw])
                    for c in range(4):
                        P.op("dve", lambda e, c=c: e.tensor_scalar(out=cacc[:], in0=raw[:, c, 0:LP], scalar1=mcw[:, 3 * c:3 * c + 1], scalar2=None, op0=ALU.mult), reads=[r_raw, r_mcw], writes=[r_cacc])
                        P.op("dve", lambda e, c=c: e.scalar_tensor_tensor(out=cacc[:], in0=raw[:, c, 1:LP + 1], scalar=mcw[:, 3 * c + 1:3 * c + 2], in1=cacc[:], op0=ALU.mult, op1=ALU.add), reads=[r_raw, r_mcw, r_cacc], writes=[r_cacc])
                        P.op("dve", lambda e, c=c: e.scalar_tensor_tensor(out=cacc[:], in0=raw[:, c, 2:LP + 2], scalar=mcw[:, 3 * c + 2:3 * c + 3], in1=cacc[:], op0=ALU.mult, op1=ALU.add), reads=[r_raw, r_mcw, r_cacc], writes=[r_cacc])
                        P.op("act", lambda e, c=c: e.activation(out=mqkT[:, c, :], in_=cacc[:], func=AF.Silu), reads=[r_cacc], writes=[r_mqkT])
                    P.op("dve", lambda e: e.memset(mqkT[:, :, 0:48], 0.0), writes=[r_mqkT])

                P.fence()
                stop_here("a3conv")
                gt = sb("gt", [64, 65, 4], F32, s3); r_gtb = Res()
                gl = sb("gl", [64, 4, 65], F32, s3); r_gl = Res()
                tru = sb("tru", [64, 64], F32, s3); r_tru = Res()
                trl = sb("trl", [64, 64], F32, s3); r_trl = Res()
                mku = sb("mku", [64, 64], F32, s3); r_mku = Res()
                mkl = sb("mkl", [64, 64], F32, s3); r_mkl = Res()
                ones64 = sb("ones64", [64, 128], F32, s3); r_ones = Res()
                ebt = sb("ebt", [64, 2, 65], F32, s3); r_ebt = Res()
                rft = sb("rft", [64, 2, 65], F32, s3); r_rft = Res()
                egt = sb("egt", [128, 2, 65], F32, s3); r_egt = Res()
                mngt = sb("mngt", [64, 256], F32, s3); r_mngt = Res()
                s3g = s3.enter_context(ExitStack())
                pgx = ps("pgx", [128, 512], F32, s3g); r_pgx = Res()
                for c5 in range(5):
                    P.dma("sp", lambda e, c5=c5: e.dma_start(out=gt[:, 13 * c5:13 * c5 + 13, :], in_=gates_s.rearrange("(c t) g -> t c g", t=64)[:, 13 * c5:13 * c5 + 13, :]), reads=[r_gates_s], writes=[r_gtb])
                P.dma("act", lambda e: e.dma_start(out=tru[:], in_=triu_d[:, :]), writes=[r_tru])
                P.dma("act", lambda e: e.dma_start(out=trl[:], in_=tril_d[:, :]), writes=[r_trl])
                P.dma("act", lambda e: e.dma_start(out=mngt[:], in_=mng[:, :]), writes=[r_mngt])
                P.op("dve", lambda e: e.tensor_scalar(out=mku[:], in0=tru[:], scalar1=1.0 / 16, scalar2=None, op0=ALU.mult), reads=[r_tru], writes=[r_mku])
                P.op("dve", lambda e: e.tensor_scalar(out=mkl[:], in0=trl[:], scalar1=1.0 / 16, scalar2=None, op0=ALU.mult), reads=[r_trl], writes=[r_mkl])
                P.op("dve", lambda e: e.memset(ones64[:], 1.0), writes=[r_ones])
                stop_here("g1")
                for gi in range(2):
                    P.op("act", lambda e, gi=gi: e.activation(out=gl[:, gi, :], in_=gt[:, :, gi], func=AF.Exp, scale=-1.0), reads=[r_gtb], writes=[r_gl])
                P.op("dve", lambda e: e.tensor_scalar(out=gl[:, 0:2, :], in0=gl[:, 0:2, :], scalar1=1.0, scalar2=None, op0=ALU.add), reads=[r_gl], writes=[r_gl])
                P.op("act", lambda e: e.activation(out=gl[:, 0:2, :], in_=gl[:, 0:2, :], func=AF.Ln), reads=[r_gl], writes=[r_gl])
                P.op("dve", lambda e: e.tensor_scalar(out=gl[:, 0:2, :], in0=gl[:, 0:2, :], scalar1=-1.0, scalar2=None, op0=ALU.mult), reads=[r_gl], writes=[r_gl])
                for gi in range(2):
                    P.op("dve", lambda e, gi=gi: e.tensor_copy(out=gl[:, 2 + gi, :], in_=gt[:, :, 2 + gi]), reads=[r_gtb], writes=[r_gl])
                stop_here("g2")
                P.op("dve", lambda e: e.memset(gl[0:48, :, 0:1], 0.0), writes=[r_gl])
                stop_here("g3")
                P.op("pe", lambda e: e.matmul(out=pgx[:64, 0:65], lhsT=tru[:], rhs=gl[:, 0, :], start=True, stop=True), reads=[r_tru, r_gl], writes=[r_pgx])
                P.op("pe", lambda e: e.matmul(out=pgx[:64, 65:130], lhsT=trl[:], rhs=gl[:, 1, :], start=True, stop=True), reads=[r_trl, r_gl], writes=[r_pgx])
                P.op("pe", lambda e: e.matmul(out=pgx[:, 130:260], lhsT=ones64[:], rhs=gl[:, 0:2, :].rearrange("p a c -> p (a c)"), start=True, stop=True), reads=[r_ones, r_gl], writes=[r_pgx])
                stop_here("g4")
                P.op("act", lambda e: e.activation(out=ebt[:].rearrange("p a c -> p (a c)"), in_=pgx[:64, 0:130], func=AF.Exp), reads=[r_pgx], writes=[r_ebt])
                P.op("act", lambda e: e.activation(out=egt[:].rearrange("p a c -> p (a c)"), in_=pgx[:, 130:260], func=AF.Exp), reads=[r_pgx], writes=[r_egt])
                P.op("dve", lambda e: e.tensor_tensor(out=rft[:].rearrange("p a c -> p (a c)"), in0=gl[:, 2:4, :].rearrange("p a c -> p (a c)"), in1=pgx[:64, 0:130], op=ALU.subtract), reads=[r_pgx, r_gl], writes=[r_rft])
                P.op("act", lambda e: e.activation(out=rft[:].rearrange("p a c -> p (a c)"), in_=rft[:].rearrange("p a c -> p (a c)"), func=AF.Exp), reads=[r_rft], writes=[r_rft])

                stop_here("a3g")
                s3g.close()
                P.fence()
                Zt = [sb(f"Zt{d}", [128, 2, 257], F32, s3) for d in range(2)]; r_Z = [Res(), Res()]
                Zbt = [sb(f"Zbt{d}", [128, 2, 257], BF16, s3) for d in range(2)]; r_Zb = [Res(), Res()]
                ztt = [sb(f"ztt{d}", [128, 2, 257], F32, s3) for d in range(2)]; r_ztmp = [Res(), Res()]
                nebt = sb("nebt", [64, 2, 65], F32, s3); r_nebt = Res()
                P.op("dve", lambda e: e.tensor_scalar(out=nebt[:].rearrange("p a c -> p (a c)"), in0=ebt[:].rearrange("p a c -> p (a c)"), scalar1=-1.0, scalar2=None, op0=ALU.mult), reads=[r_ebt], writes=[r_nebt])
                vres = sb("vres", [64, 65, 257], BF16, s3); r_vres = Res()
                for c5 in range(5):
                    P.dma("act", lambda e, c5=c5: e.dma_start(out=vres[:, 13 * c5:13 * c5 + 13, :], in_=mv_s.rearrange("(c t) v -> t c v", t=64)[:, 13 * c5:13 * c5 + 13, :]), reads=[r_mv_s], writes=[r_vres])
                vtr = Ring([sb(f"vt{i}", [64, 257], BF16, s3) for i in range(4)])
                ptmr = Ring([sb(f"ptm{i}", [64, 64], BF16, s3) for i in range(4)])
                ktokr = Ring([sb(f"ktok{i}", [64, 256], BF16, s3) for i in range(4)])
                dnr = Ring([sb(f"dn{i}", [64, 4], F32, s3) for i in range(4)])
                hring = Ring([sb(f"hch{i}", [64, 256], F32, s3) for i in range(4)])
                with ExitStack() as s3s:
                    p_sr = Ring([ps(f"p_s{i}", [128, 512], F32, s3s) for i in range(2)])
                    p_kr = Ring([ps(f"p_k{i}", [128, 1024], BF16, s3s) for i in range(2)])
                    p_or = Ring([ps(f"p_o{i}", [128, 512], F32, s3s) for i in range(2)])
                    p_cc = ps("p_cc", [128, 1024], F32, s3s); r_p_c = Res()

                    def phase_I(d, c):
                        c0 = 64 * c
                        mask, r_mask = (mku, r_mku) if d == 0 else (mkl, r_mkl)
                        p_s, r_p_s = p_sr.next()
                        p_k, r_p_k = p_kr.next()
                        ptm, r_ptm = ptmr.next()
                        ktok, r_ktok = ktokr.next()
                        vt, r_vt = vtr.next()
                        P.op("pe", [(lambda e, dc=dc, c0=c0, p_s=p_s: e.matmul(out=p_s[:64, 0:64], lhsT=mqkT[:, 2 + dc, c0:c0 + 64], rhs=mqkT[:, dc, c0:c0 + 64], start=(dc == 0), stop=(dc == 1))) for dc in range(2)], reads=[r_mqkT], writes=[r_p_s])
                        P.op("dve", lambda e, mask=mask, p_s=p_s, ptm=ptm: e.tensor_tensor(out=ptm[:], in0=p_s[:64, 0:64], in1=mask[:], op=ALU.mult), reads=[r_p_s, r_mask], writes=[r_ptm])
                        P.op("pe", [(lambda e, dc=dc, c0=c0, p_k=p_k: e.transpose(out=p_k[:64, 128 * dc:128 * dc + 128], in_=mqkT[:, 2 + dc, c0:c0 + 64], identity=idb[:])) for dc in range(2)], reads=[r_mqkT, r_idb], writes=[r_p_k])
                        P.op("act", lambda e, ktok=ktok, p_k=p_k: e.copy(out=ktok[:], in_=p_k[:64, 0:256]), reads=[r_p_k], writes=[r_ktok])
                        P.op("dve", lambda e, vt=vt, d=d, c=c: e.tensor_scalar(out=vt[:], in0=vres[:, c, :], scalar1=rft[:, d, c:c + 1], scalar2=None, op0=ALU.mult), reads=[r_vres, r_rft], writes=[r_vt])
                        return dict(c=c, c0=c0, d=d, ptm=ptm, r_ptm=r_ptm, ktok=ktok, r_ktok=r_ktok, vt=vt, r_vt=r_vt)

                    def phase_D(x):
                        d, c, c0 = x["d"], x["c"], x["c0"]
                        ptm, r_ptm, ktok, r_ktok, vt, r_vt = x["ptm"], x["r_ptm"], x["ktok"], x["r_ktok"], x["vt"], x["r_vt"]
                        p_o, r_p_o = p_or.next()
                        for dc in range(2):
                            P.op("pe", lambda e, dc=dc, ktok=ktok, vt=vt: e.matmul(out=p_cc[:, 512 * dc:512 * dc + 257], lhsT=ktok[:, 128 * dc:128 * dc + 128], rhs=vt[:], start=True, stop=True), reads=[r_ktok, r_vt], writes=[r_p_c])
                        P.op("pe", [lambda e, ptm=ptm, vt=vt, p_o=p_o: e.matmul(out=p_o[:64, 0:257], lhsT=ptm[:], rhs=vt[:], start=True, stop=False)] +
                             [(lambda e, dc=dc, c0=c0, p_o=p_o, d=d: e.matmul(out=p_o[:64, 0:257], lhsT=mqkT[:, dc, c0:c0 + 64], rhs=Zbt[d][:, dc, :], start=False, stop=(dc == 1))) for dc in range(2)],
                             reads=[r_ptm, r_vt, r_mqkT, r_Zb[d]], writes=[r_p_o])
                        P.op("dve", lambda e, d=d: e.scalar_tensor_tensor(out=ztt[d][:], in0=p_cc[:].rearrange("p (a n) -> p a n", a=2)[:, :, 0:257], scalar=1.0 / 16, in1=Zt[d][:], op0=ALU.mult, op1=ALU.add), reads=[r_p_c, r_Z[d]], writes=[r_ztmp[d]])
                        P.op("act", lambda e, d=d, c=c: e.activation(out=Zbt[d][:], in_=ztt[d][:], func=AF.Identity, scale=egt[:, d, c:c + 1]), reads=[r_ztmp[d], r_egt], writes=[r_Zb[d]])
                        P.op("act", lambda e, d=d, c=c: e.activation(out=Zt[d][:], in_=ztt[d][:], func=AF.Identity, scale=egt[:, d, c:c + 1]), reads=[r_ztmp[d], r_egt], writes=[r_Z[d]])
                        hch, r_hch = hring.next()
                        dn, r_dn = dnr.next()
                        P.op("dve", lambda e, d=d, c=c, dn=dn, p_o=p_o: e.tensor_scalar(out=dn[:, 0:1], in0=p_o[:64, 256:257], scalar1=ebt[:, d, c:c + 1], scalar2=1.0, op0=ALU.mult, op1=ALU.max), reads=[r_p_o, r_ebt], writes=[r_dn])
                        P.op("dve", lambda e, d=d, c=c, dn=dn, p_o=p_o: e.scalar_tensor_tensor(out=dn[:, 1:2], in0=p_o[:64, 256:257], scalar=nebt[:, d, c:c + 1], in1=dn[:, 0:1], op0=ALU.mult, op1=ALU.max), reads=[r_p_o, r_nebt, r_dn], writes=[r_dn])
                        P.op("dve", lambda e, dn=dn: e.reciprocal(out=dn[:, 2:3], in_=dn[:, 1:2]), reads=[r_dn], writes=[r_dn])
                        P.op("dve", lambda e, hch=hch, dn=dn, p_o=p_o, d=d, c=c: e.tensor_scalar(out=hch[:], in0=p_o[:64, 0:256], scalar1=dn[:, 2:3], scalar2=ebt[:, d, c:c + 1], op0=ALU.mult, op1=ALU.mult), reads=[r_p_o, r_dn, r_ebt], writes=[r_hch])
                        dst, r_dst = (hf_s, r_hf_s) if d == 0 else (hb_s, r_hb_s)
                        P.dma("sp", lambda e, hch=hch, c0=c0, dst=dst: e.dma_start(out=dst[c0:c0 + 64, :], in_=hch[:]), reads=[r_hch], writes=[r_dst])

                    if "A3" in STAGES:
                        for d in range(2):
                            P.op("dve", lambda e, d=d: e.memset(Zt[d][:], 0.0), writes=[r_Z[d]])
                            P.op("dve", lambda e, d=d: e.memset(Zbt[d][:], 0.0), writes=[r_Zb[d]])
                        nxt = [phase_I(0, 0), phase_I(1, 64)]
                        for i in range(65):
                            cur = nxt
                            if i + 1 < 65:
                                nxt = [phase_I(0, i + 1), phase_I(1, 63 - i)]
                            phase_D(cur[0])
                            phase_D(cur[1])
                P.fence()
                with ExitStack() as s3e:
                    NR = 8
                    hfr = Ring([sb(f"ehf{i}", [128, 256], F32, s3e) for i in range(NR)])
                    hbr = Ring([sb(f"ehb{i}", [128, 256], F32, s3e) for i in range(NR)])
                    mor = Ring([sb(f"emo{i}", [128, 256], F32, s3e) for i in range(NR)])
                    hsr = Ring([sb(f"ehs{i}", [128, 256], F32, s3e) for i in range(NR)])
                    bsr = Ring([sb(f"ebs{i}", [128, 8], F32, s3e) for i in range(NR)])
                    aor = Ring([sb(f"eao{i}", [128, 256], BF16, s3e) for i in range(NR)])
                    aTr = Ring([sb(f"eaT{i}", [128, 2, 128], BF16, s3e) for i in range(NR)])
                    p_tr = Ring([ps(f"ep_t{i}", [128, 1024], BF16, s3e) for i in range(3)])
                    mng128 = sb("mng128", [128, 256], F32, s3e); r_mng128 = Res()
                    P.dma("act", lambda e: e.dma_start(out=mng128[0:64, :], in_=mng[:, :]), writes=[r_mng128])
                    P.dma("act", lambda e: e.dma_start(out=mng128[64:128, :], in_=mng[:, :]), writes=[r_mng128])
                    eblocks = [(128 * j, 128) for j in range(32)] + [(4096, 64)]

                    def st0(j):
                        r0, n = eblocks[j]
                        hf, r_hf = hfr.next(); hb, r_hb = hbr.next(); mo, r_mo = mor.next()
                        P.dma("sp", lambda e, hf=hf, r0=r0, n=n: e.dma_start(out=hf[:n, :], in_=hf_s[r0:r0 + n, :]), reads=[r_hf_s], writes=[r_hf])
                        P.dma("sp", lambda e, hb=hb, r0=r0, n=n: e.dma_start(out=hb[:n, :], in_=hb_s[r0:r0 + n, :]), reads=[r_hb_s], writes=[r_hb])
                        P.dma("act", lambda e, mo=mo, r0=r0, n=n: e.dma_start(out=mo[:n, :], in_=mo_s[r0:r0 + n, :]), reads=[r_mo_s], writes=[r_mo])
                        return dict(r0=r0, n=n, hf=hf, r_hf=r_hf, hb=hb, r_hb=r_hb, mo=mo, r_mo=r_mo)

                    def st1(x):
                        n = x["n"]
                        hs, r_hs = hsr.next(); bs_, r_bs = bsr.next()
                        hf, hb = x["hf"], x["hb"]
                        P.op("dve", lambda e, hf=hf, hb=hb, hs=hs, n=n: e.tensor_tensor(out=hs[:n, :], in0=hf[:n, :], in1=hb[:n, :], op=ALU.add), reads=[x["r_hf"], x["r_hb"]], writes=[r_hs])
                        P.op("dve", lambda e, hs=hs, bs_=bs_, n=n: e.bn_stats(out=bs_[:n, 0:6], in_=hs[:n, :]), reads=[r_hs], writes=[r_bs])
                        P.op("dve", lambda e, bs_=bs_, n=n: e.bn_aggr(out=bs_[:n, 6:8], in_=bs_[:n, 0:6]), reads=[r_bs], writes=[r_bs])
                        P.op("dve", lambda e, bs_=bs_, n=n: e.tensor_scalar(out=bs_[:n, 7:8], in0=bs_[:n, 7:8], scalar1=EPS, scalar2=None, op0=ALU.add), reads=[r_bs], writes=[r_bs])
                        x.update(hs=hs, r_hs=r_hs, bs=bs_, r_bs=r_bs)

                    def st2(x):
                        n, bs_, r_bs = x["n"], x["bs"], x["r_bs"]
                        P.op("act", lambda e, bs_=bs_, n=n: e.activation(out=bs_[:n, 7:8], in_=bs_[:n, 7:8], func=AF.Sqrt), reads=[r_bs], writes=[r_bs])

                    def st3(x):
                        n, bs_, r_bs, hs, r_hs = x["n"], x["bs"], x["r_bs"], x["hs"], x["r_hs"]
                        P.op("dve", lambda e, bs_=bs_, n=n: e.reciprocal(out=bs_[:n, 7:8], in_=bs_[:n, 7:8]), reads=[r_bs], writes=[r_bs])
                        P.op("dve", lambda e, bs_=bs_, hs=hs, n=n: e.tensor_scalar(out=hs[:n, :], in0=hs[:n, :], scalar1=bs_[:n, 6:7], scalar2=bs_[:n, 7:8], op0=ALU.subtract, op1=ALU.mult), reads=[r_hs, r_bs], writes=[r_hs])

                    def st4(x):
                        n, hs, r_hs, mo, r_mo = x["n"], x["hs"], x["r_hs"], x["mo"], x["r_mo"]
                        ao, r_ao = aor.next()
                        P.op("pool", lambda e, hs=hs, n=n: e.tensor_tensor(out=hs[:n, :], in0=hs[:n, :], in1=mng128[:n, :], op=ALU.mult), reads=[r_hs, r_mng128], writes=[r_hs])
                        P.op("pool", lambda e, hs=hs, mo=mo, ao=ao, n=n: e.tensor_tensor(out=ao[:n, :], in0=hs[:n, :], in1=mo[:n, :], op=ALU.mult), reads=[r_hs, r_mo], writes=[r_ao])
                        x.update(ao=ao, r_ao=r_ao)

                    def st5(x):
                        n, ao, r_ao = x["n"], x["ao"], x["r_ao"]
                        p_t, r_p_t = p_tr.next()
                        P.op("pe", [(lambda e, dc=dc, ao=ao, p_t=p_t, n=n: e.transpose(out=p_t[:, 128 * dc:128 * dc + n], in_=ao[:n, 128 * dc:128 * dc + 128], identity=idb[:n, :n])) for dc in range(2)], reads=[r_ao, r_idb], writes=[r_p_t])
                        x.update(p_t=p_t, r_p_t=r_p_t)

                    def st6(x):
                        r0, n, p_t, r_p_t = x["r0"], x["n"], x["p_t"], x["r_p_t"]
                        aT, r_aT = aTr.next()
                        P.op("act", lambda e, aT=aT, p_t=p_t, n=n: e.copy(out=aT[:, :, :n], in_=p_t[:, 0:256].rearrange("p (a t) -> p a t", a=2)[:, :, :n]), reads=[r_p_t], writes=[r_aT])
                        if r0 == 0:
                            P.dma("sp", lambda e, aT=aT: e.dma_start(out=cc_in[0:256, 0:80].rearrange("(a p) t -> p a t", p=128), in_=aT[:, :, 48:128]), reads=[r_aT], writes=[r_cc_in])
                        else:
                            pos0 = r0 - 48
                            P.dma("sp", lambda e, aT=aT, pos0=pos0, n=n: e.dma_start(out=cc_in[0:256, pos0:pos0 + n].rearrange("(a p) t -> p a t", p=128), in_=aT[:, :, :n]), reads=[r_aT], writes=[r_cc_in])

                    if "A3" in STAGES:
                        stages = [st1, st2, st3, st4, st5, st6]
                        xs = {}
                        nb = len(eblocks)
                        for i in range(nb + len(stages) + 1):
                            if i < nb:
                                xs[i] = st0(i)
                            for si, stf in enumerate(stages):
                                jj = i - 1 - si
                                if 0 <= jj < nb:
                                    stf(xs[jj])
                                    if stf is st6 and "X" in STAGES and jj in (8, 16, 24, 32):
                                        exchange_half(0, js=(jj // 8 - 1,))

            P.fence()
            if "X" in STAGES and "A3" not in STAGES:
                exchange_half(0)
            if DEBUG:
                P.stopped = False
                dbg_holder.append(P.dma("pool", lambda e: e.dma_start(out=dbg_cc[:, :], in_=cc_in[:, :]), reads=[r_cc_in, r_cc_in_b]))
                stop_here(STOP_AT)

            if "B" in STAGES:
                TS = [(342 * i, 342) for i in range(3)]
                with ExitStack() as sB:
                    hT = sb("hT", [128, 16, NW], F32, sB); r_hT = Res()
                    uT2 = sb("uT2", [128, 16, NW], BF16, sB); r_uT2 = Res()
                    g2t = sb("g2t", [128, 16], F32, sB); r_g2t = Res()
                    P.dma("act", lambda e: e.dma_start(out=g2t[:], in_=g2c[:, :]), writes=[r_g2t])
                    onesf = sb("onesf", [128, 128], F32, sB); r_onesf = Res()
                    P.op("dve", lambda e: e.memset(onesf[:], 1.0), writes=[r_onesf])
                    with ExitStack() as sB1:
                        abT = sb("abT", [128, 16, NW], BF16, sB1); r_abT = Res()
                        mT = sb("mT", [128, 16, NW], BF16, sB1); r_mT = Res()
                        rank_cache = {}
                        for k in range(16):
                            def load_ab(e, k=k):
                                if "r" not in rank_cache:
                                    rank_cache["r"] = e.partition_id() % 4
                                rank = rank_cache["r"]
                                return e.dma_start(out=abT[:, k:k + 1, :], in_=cc_out.rearrange("(j r) t -> r j t", j=4)[k * 128:(k + 1) * 128, bass.ds(rank, 1), :])
                            P.dma("pool", load_ab, reads=[r_cc_out[0 if k < 8 else 1]], writes=[r_abT])
                        with ExitStack() as sB0:
                            g1t_b = sb("g1t2", [128, D], F32, sB0); r_g1t_b = Res()
                            P.dma("act", lambda e: e.dma_start(out=g1t_b[:], in_=g1b[:, :]), writes=[r_g1t_b])
                            xring_b = Ring([sb(f"xw{i}", [128, D], F32, sB0) for i in range(2)])
                            junk_b = sb("junk2", [128, D], BF16, sB0); r_junk_b = Res()
                            ss_b = sb("ss2", [128, 1], F32, sB0); r_ss_b = Res()
                            u_b = sb("u2", [128, D], BF16, sB0); r_u_b = Res()
                            pT_b = ps("pT2", [128, D], BF16, sB0); r_pT_b = Res()
                            pX = [ps(f"pX{i}", [128, 1024], F32, sB0) for i in range(2)]; r_pX = [Res(), Res()]
                            for j in range(9):
                                bs = 128 if j < 8 else 2
                                xt_b, r_xt_b = xring_b.next()
                                P.dma("sp", lambda e, xt_b=xt_b, bs=bs, j=j: e.dma_start(out=xt_b[:bs, :], in_=xwin[128 * j:128 * j + bs, :]), writes=[r_xt_b])
                                P.op("act", lambda e, xt_b=xt_b, bs=bs: e.activation(out=junk_b[:bs, :], in_=xt_b[:bs, :], func=AF.Square, accum_out=ss_b[:bs, :]), reads=[r_xt_b], writes=[r_junk_b, r_ss_b])
                                P.op("dve", lambda e, bs=bs: e.tensor_scalar(out=ss_b[:bs, :], in0=ss_b[:bs, :], scalar1=1.0 / D, scalar2=EPS, op0=ALU.mult, op1=ALU.add), reads=[r_ss_b], writes=[r_ss_b])
                                P.op("act", lambda e, bs=bs: e.activation(out=ss_b[:bs, :], in_=ss_b[:bs, :], func=AF.Sqrt), reads=[r_ss_b], writes=[r_ss_b])
                                P.op("dve", lambda e, bs=bs: e.reciprocal(out=ss_b[:bs, :], in_=ss_b[:bs, :]), reads=[r_ss_b], writes=[r_ss_b])
                                P.op("dve", lambda e, xt_b=xt_b, bs=bs: e.scalar_tensor_tensor(out=u_b[:bs, :], in0=xt_b[:bs, :], scalar=ss_b[:bs, 0:1], in1=g1t_b[:bs, :], op0=ALU.mult, op1=ALU.mult), reads=[r_xt_b, r_ss_b, r_g1t_b], writes=[r_u_b])
                                P.op("pe", [(lambda e, k=k, bs=bs: e.transpose(out=pT_b[:, k * 128:k * 128 + bs], in_=u_b[:bs, k * 128:(k + 1) * 128], identity=idb[:bs, :bs])) for k in range(16)], reads=[r_u_b, r_idb], writes=[r_pT_b])
                                P.op("act", lambda e, j=j, bs=bs: e.copy(out=uT2[:, :, 128 * j:128 * j + bs], in_=pT_b[:].rearrange("p (k n) -> p k n", k=16)[:, :, :bs]), reads=[r_pT_b], writes=[r_uT2])
                                for hh in range(2):
                                    P.op("pe", [(lambda e, k=k, hh=hh, xt_b=xt_b, bs=bs: e.transpose(out=pX[hh][:, (k % 8) * 128:(k % 8) * 128 + bs], in_=xt_b[:bs, k * 128:(k + 1) * 128], identity=idf[:bs, :bs])) for k in range(8 * hh, 8 * hh + 8)], reads=[r_xt_b, r_idf], writes=[r_pX[hh]])
                                    P.op("dve", lambda e, hh=hh, j=j, bs=bs: e.tensor_copy(out=hT[:, 8 * hh:8 * hh + 8, 128 * j:128 * j + bs], in_=pX[hh][:].rearrange("p (k n) -> p k n", k=8)[:, :, :bs]), reads=[r_pX[hh]], writes=[r_hT])

                        P.fence()
                        with ExitStack() as sB1b:
                            wgr = Ring([sb(f"wg{i}", [128, 16, 256], BF16, sB1b) for i in range(2)])
                            war = Ring([sb(f"wa{i}", [128, 8, 256], BF16, sB1b) for i in range(2)])
                            sgm = sb("sgm", [128, 342], F32, sB1b); r_sgm = Res()
                            sgd = sb("sgd", [128, 342], F32, sB1b); r_sgd = Res()
                            tA = sb("tA", [128, 342], F32, sB1b); r_tA = Res()
                            tB = sb("tB", [128, 342], F32, sB1b); r_tB = Res()
                            pq = [Ring([ps(f"pq{q}{i}", [128, 512], F32, sB1b) for i in range(2)]) for q in range(4)]
                            for c in range(16):
                                wg, r_wg = wgr.next()
                                P.dma("pool", lambda e, wg=wg, c=c: e.dma_start(out=wg[:, :, 0:128], in_=w_g[:, 128 * c:128 * c + 128].rearrange("(k p) n -> p k n", p=128)), writes=[r_wg])
                                for (t0, tn) in TS:
                                    p0_, r0_ = pq[0].next()
                                    P.op("pe", [(lambda e, k=k, p0_=p0_, wg=wg, t0=t0, tn=tn: e.matmul(out=p0_[:, :tn], lhsT=wg[:, k, 0:128], rhs=uT2[:, k, t0:t0 + tn], start=(k == 0), stop=(k == 15))) for k in range(16)], reads=[r_wg, r_uT2], writes=[r0_])
                                    P.op("act", lambda e, p0_=p0_, c=c, t0=t0, tn=tn: e.activation(out=mT[:, c, t0:t0 + tn], in_=p0_[:, :tn], func=AF.Sigmoid), reads=[r0_], writes=[r_mT])
                            for c in range(16):
                                wg, r_wg = wgr.next()
                                wa, r_wa = war.next()
                                P.dma("pool", lambda e, wg=wg, c=c: e.dma_start(out=wg[:, :, 128:256], in_=w_g[:, 2048 + 128 * c:2048 + 128 * c + 128].rearrange("(k p) n -> p k n", p=128)), writes=[r_wg])
                                P.dma("pool", lambda e, wa=wa, c=c: e.dma_start(out=wa[:, :, 0:128], in_=w_a[:, 128 * c:128 * c + 128].rearrange("(k p) n -> p k n", p=128)), writes=[r_wa])
                                P.dma("pool", lambda e, wa=wa, c=c: e.dma_start(out=wa[:, :, 128:256], in_=w_b[:, 128 * c:128 * c + 128].rearrange("(k p) n -> p k n", p=128)), writes=[r_wa])
                                for (t0, tn) in TS:
                                    p1_, r1_ = pq[1].next(); p2_, r2_ = pq[2].next(); p3_, r3_ = pq[3].next()
                                    P.op("pe", [(lambda e, k=k, p2_=p2_, wg=wg, t0=t0, tn=tn: e.matmul(out=p2_[:, :tn], lhsT=wg[:, k, 128:256], rhs=uT2[:, k, t0:t0 + tn], start=(k == 0), stop=(k == 15))) for k in range(16)], reads=[r_wg, r_uT2], writes=[r2_])
                                    P.op("pe", [(lambda e, k=k, p1_=p1_, wa=wa, t0=t0, tn=tn: e.matmul(out=p1_[:, :tn], lhsT=wa[:, k, 0:128], rhs=abT[:, k, t0:t0 + tn], start=(k == 0), stop=(k == 7))) for k in range(8)], reads=[r_wa, r_abT], writes=[r1_])
                                    P.op("pe", [(lambda e, k=k, p3_=p3_, wa=wa, t0=t0, tn=tn: e.matmul(out=p3_[:, :tn], lhsT=wa[:, k, 128:256], rhs=abT[:, 8 + k, t0:t0 + tn], start=(k == 0), stop=(k == 7))) for k in range(8)], reads=[r_wa, r_abT], writes=[r3_])
                                    P.op("act", lambda e, p2_=p2_, tn=tn: e.activation(out=sgd[:, :tn], in_=p2_[:, :tn], func=AF.Sigmoid), reads=[r2_], writes=[r_sgd])
                                    P.op("dve", lambda e, p1_=p1_, c=c, t0=t0, tn=tn: e.tensor_tensor(out=tA[:, :tn], in0=p1_[:, :tn], in1=mT[:, c, t0:t0 + tn], op=ALU.mult), reads=[r1_, r_mT], writes=[r_tA])
                                    P.op("dve", lambda e, p3_=p3_, tn=tn: e.tensor_tensor(out=tB[:, :tn], in0=p3_[:, :tn], in1=sgd[:, :tn], op=ALU.mult), reads=[r3_, r_sgd], writes=[r_tB])
                                    P.op("dve", lambda e, c=c, t0=t0, tn=tn: e.tensor_tensor(out=mT[:, c, t0:t0 + tn], in0=tA[:, :tn], in1=tB[:, :tn], op=ALU.add), reads=[r_tA, r_tB], writes=[r_mT])
                        P.fence()
                        with ExitStack() as sB2:
                            wor = Ring([sb(f"wo{i}", [128, 16, 128], BF16, sB2) for i in range(2)])
                            po = Ring([ps(f"po{i}", [128, 512], F32, sB2) for i in range(4)])
                            for c in range(16):
                                wo, r_wo = wor.next()
                                P.dma("pool", lambda e, wo=wo, c=c: e.dma_start(out=wo[:], in_=w_out[:, 128 * c:128 * c + 128].rearrange("(k p) n -> p k n", p=128)), writes=[r_wo])
                                for (t0, tn) in TS:
                                    pp, rp = po.next()
                                    P.op("pe", [(lambda e, k=k, pp=pp, wo=wo, t0=t0, tn=tn: e.matmul(out=pp[:, :tn], lhsT=wo[:, k, :], rhs=mT[:, k, t0:t0 + tn], start=(k == 0), stop=(k == 15))) for k in range(16)], reads=[r_wo, r_mT], writes=[rp])
                                    P.op("dve", lambda e, pp=pp, c=c, t0=t0, tn=tn: e.tensor_tensor(out=hT[:, c, t0:t0 + tn], in0=hT[:, c, t0:t0 + tn], in1=pp[:, :tn], op=ALU.add), reads=[rp, r_hT], writes=[r_hT])

                    P.fence()
                    with ExitStack() as sB3:
                        sq = Ring([sb(f"sq{i}", [128, 342], F32, sB3) for i in range(2)])
                        rstd = sb("rstd", [128, NW], F32, sB3); r_rstd = Res()
                        wm = sb("wm", [128, NW], F32, sB3); r_wm = Res()
                        P.dma("act", lambda e: e.dma_start(out=wm[:], in_=wmask_d[:, :]), writes=[r_wm])
                        pss3 = [ps(f"pss3{i}", [128, 512], F32, sB3) for i in range(3)]; r_pss3 = [Res() for _ in range(3)]
                        for ti, (t0, tn) in enumerate(TS):
                            fns = []
                            for c in range(16):
                                s_, r_s = sq.next()
                                P.op("act", lambda e, s_=s_, c=c, t0=t0, tn=tn: e.activation(out=s_[:, :tn], in_=hT[:, c, t0:t0 + tn], func=AF.Square), reads=[r_hT], writes=[r_s])
                                P.op("pe", lambda e, s_=s_, c=c, ti=ti, tn=tn: e.matmul(out=pss3[ti][:, :tn], lhsT=onesf[:], rhs=s_[:, :tn], start=(c == 0), stop=(c == 15), skip_group_check=True), reads=[r_s, r_onesf], writes=[r_pss3[ti]])
                            P.op("dve", lambda e, ti=ti, t0=t0, tn=tn: e.tensor_scalar(out=rstd[:, t0:t0 + tn], in0=pss3[ti][:, :tn], scalar1=1.0 / D, scalar2=EPS, op0=ALU.mult, op1=ALU.add), reads=[r_pss3[ti]], writes=[r_rstd])
                        P.op("act", lambda e: e.activation(out=rstd[:], in_=rstd[:], func=AF.Sqrt), reads=[r_rstd], writes=[r_rstd])
                        P.op("dve", lambda e: e.reciprocal(out=rstd[:], in_=rstd[:]), reads=[r_rstd], writes=[r_rstd])
                        P.op("dve", lambda e: e.tensor_tensor(out=rstd[:], in0=rstd[:], in1=wm[:], op=ALU.mult), reads=[r_rstd, r_wm], writes=[r_rstd])
                        for c in range(16):
                            P.op("dve", lambda e, c=c: e.scalar_tensor_tensor(out=uT2[:, c, :], in0=hT[:, c, :], scalar=g2t[:, c:c + 1], in1=rstd[:], op0=ALU.mult, op1=ALU.mult), reads=[r_hT, r_g2t, r_rstd], writes=[r_uT2])

                    P.fence()
                    with ExitStack() as sB4:
                        fcw = sb("fcw", [128, 264], F32, sB4); r_fcw = Res()
                        P.dma("act", lambda e: e.dma_start(out=fcw[:], in_=fconv[:, :]), writes=[r_fcw])
                        actT = sb("actT", [128, 22, 1024], BF16, sB4); r_actT = Res()
                        wur = Ring([sb(f"wu{i}", [128, 16, 256], BF16, sB4) for i in range(2)])
                        wdr = Ring([sb(f"wd{i}", [128, 22, 128], BF16, sB4) for i in range(2)])
                        upg = sb("upg", [128, NW], F32, sB4); r_upg = Res()
                        upv = sb("upv", [128, NW], F32, sB4); r_upv = Res()
                        cg = sb("cg", [128, 1024], F32, sB4); r_cg = Res()
                        cv = sb("cv", [128, 1024], F32, sB4); r_cv = Res()
                        sgl = sb("sgl", [128, 1024], F32, sB4); r_sgl = Res()
                        pu = Ring([ps(f"pu{i}", [128, 512], F32, sB4) for i in range(4)])
                        pd = Ring([ps(f"pd{i}", [128, 512], F32, sB4) for i in range(4)])
                        for half in range(2):
                            for fc in range(22):
                                f = half * 22 + fc
                                wu, r_wu = wur.next()
                                P.dma("pool", lambda e, wu=wu, f=f: e.dma_start(out=wu[:, :, 0:128], in_=w_up[:, 128 * f:128 * f + 128].rearrange("(k p) n -> p k n", p=128)), writes=[r_wu])
                                P.dma("pool", lambda e, wu=wu, f=f: e.dma_start(out=wu[:, :, 128:256], in_=w_up[:, FFN + 128 * f:FFN + 128 * f + 128].rearrange("(k p) n -> p k n", p=128)), writes=[r_wu])
                                for (t0, tn) in TS:
                                    pg_, rg_ = pu.next()
                                    P.op("pe", [(lambda e, k=k, pg_=pg_, wu=wu, t0=t0, tn=tn: e.matmul(out=pg_[:, :tn], lhsT=wu[:, k, 0:128], rhs=uT2[:, k, t0:t0 + tn], start=(k == 0), stop=(k == 15))) for k in range(16)], reads=[r_wu, r_uT2], writes=[rg_])
                                    P.op("act", lambda e, pg_=pg_, t0=t0, tn=tn: e.copy(out=upg[:, t0:t0 + tn], in_=pg_[:, :tn]), reads=[rg_], writes=[r_upg])
                                    pv_, rv_ = pu.next()
                                    P.op("pe", [(lambda e, k=k, pv_=pv_, wu=wu, t0=t0, tn=tn: e.matmul(out=pv_[:, :tn], lhsT=wu[:, k, 128:256], rhs=uT2[:, k, t0:t0 + tn], start=(k == 0), stop=(k == 15))) for k in range(16)], reads=[r_wu, r_uT2], writes=[rv_])
                                    P.op("act", lambda e, pv_=pv_, t0=t0, tn=tn: e.copy(out=upv[:, t0:t0 + tn], in_=pv_[:, :tn]), reads=[rv_], writes=[r_upv])
                                for (src, r_src, dst, r_dst, ci) in ((upg, r_upg, cg, r_cg, f), (upv, r_upv, cv, r_cv, 44 + f)):
                                    P.op("dve", lambda e, src=src, dst=dst, ci=ci: e.tensor_scalar(out=dst[:], in0=src[:, 0:1024], scalar1=fcw[:, 3 * ci:3 * ci + 1], scalar2=None, op0=ALU.mult), reads=[r_src, r_fcw], writes=[r_dst])
                                    P.op("dve", lambda e, src=src, dst=dst, ci=ci: e.scalar_tensor_tensor(out=dst[:], in0=src[:, 1:1025], scalar=fcw[:, 3 * ci + 1:3 * ci + 2], in1=dst[:], op0=ALU.mult, op1=ALU.add), reads=[r_src, r_fcw, r_dst], writes=[r_dst])
                                    P.op("dve", lambda e, src=src, dst=dst, ci=ci: e.scalar_tensor_tensor(out=dst[:], in0=src[:, 2:1026], scalar=fcw[:, 3 * ci + 2:3 * ci + 3], in1=dst[:], op0=ALU.mult, op1=ALU.add), reads=[r_src, r_fcw, r_dst], writes=[r_dst])
                                P.op("act", lambda e: e.activation(out=sgl[:], in_=cg[:], func=AF.Silu), reads=[r_cg], writes=[r_sgl])
                                P.op("dve", lambda e, fc=fc: e.tensor_tensor(out=actT[:, fc, :], in0=sgl[:], in1=cv[:], op=ALU.mult), reads=[r_sgl, r_cv], writes=[r_actT])
                            for c in range(16):
                                wd, r_wd = wdr.next()
                                P.dma("pool", lambda e, wd=wd, c=c, half=half: e.dma_start(out=wd[:], in_=w_down[2816 * half:2816 * half + 2816, 128 * c:128 * c + 128].rearrange("(k p) n -> p k n", p=128)), writes=[r_wd])
                                for t2_ in range(2):
                                    pp, rp = pd.next()
                                    P.op("pe", [(lambda e, k=k, pp=pp, wd=wd, t2_=t2_: e.matmul(out=pp[:, :], lhsT=wd[:, k, :], rhs=actT[:, k, 512 * t2_:512 * t2_ + 512], start=(k == 0), stop=(k == 21))) for k in range(22)], reads=[r_wd, r_actT], writes=[rp])
                                    P.op("dve", lambda e, pp=pp, c=c, t2_=t2_: e.tensor_tensor(out=hT[:, c, 1 + 512 * t2_:1 + 512 * t2_ + 512], in0=hT[:, c, 1 + 512 * t2_:1 + 512 * t2_ + 512], in1=pp[:, :], op=ALU.add), reads=[rp, r_hT], writes=[r_hT])

                    P.fence()
                    with ExitStack() as sB5:
                        gft = sb("gft", [128, D], F32, sB5); r_gft = Res()
                        P.dma("act", lambda e: e.dma_start(out=gft[:], in_=gfb[:, :]), writes=[r_gft])
                        pF = [ps(f"pF{i}", [128, 1024], F32, sB5) for i in range(2)]; r_pF = [Res(), Res()]
                        oring = Ring([sb(f"ob{i}", [128, D], F32, sB5) for i in range(2)])
                        fj = sb("fj", [128, 1024], F32, sB5); r_fj = Res()
                        fs = sb("fs", [128, 4], F32, sB5); r_fs = Res()
                        for j in range(8):
                            for hh in range(2):
                                P.op("pe", [(lambda e, k=k, hh=hh, j=j: e.transpose(out=pF[hh][:, (k % 8) * 128:(k % 8) * 128 + 128], in_=hT[:, k, 1 + 128 * j:1 + 128 * j + 128], identity=idf[:])) for k in range(8 * hh, 8 * hh + 8)], reads=[r_hT, r_idf], writes=[r_pF[hh]])
                                P.op("act", lambda e, hh=hh: e.activation(out=fj[:], in_=pF[hh][:], func=AF.Square, accum_out=fs[:, hh:hh + 1]), reads=[r_pF[hh]], writes=[r_fj, r_fs])
                            P.op("dve", lambda e: e.tensor_tensor(out=fs[:, 2:3], in0=fs[:, 0:1], in1=fs[:, 1:2], op=ALU.add), reads=[r_fs], writes=[r_fs])
                            P.op("dve", lambda e: e.tensor_scalar(out=fs[:, 2:3], in0=fs[:, 2:3], scalar1=1.0 / D, scalar2=EPS, op0=ALU.mult, op1=ALU.add), reads=[r_fs], writes=[r_fs])
                            P.op("act", lambda e: e.activation(out=fs[:, 2:3], in_=fs[:, 2:3], func=AF.Sqrt), reads=[r_fs], writes=[r_fs])
                            P.op("dve", lambda e: e.reciprocal(out=fs[:, 3:4], in_=fs[:, 2:3]), reads=[r_fs], writes=[r_fs])
                            ob, r_ob = oring.next()
                            for hh in range(2):
                                P.op("dve", lambda e, hh=hh, ob=ob: e.scalar_tensor_tensor(out=ob[:, 1024 * hh:1024 * hh + 1024], in0=pF[hh][:], scalar=fs[:, 3:4], in1=gft[:, 1024 * hh:1024 * hh + 1024], op0=ALU.mult, op1=ALU.mult), reads=[r_pF[hh], r_fs, r_gft], writes=[r_ob])
                            final_toks.append(P.dma("sp", lambda e, ob=ob, j=j: e.dma_start(out=out_d[128 * j:128 * j + 128, :], in_=ob[:]), reads=[r_ob]))
        except _Stop:
            pass
        if True:
            if DEBUG and dbg_holder:
                final_toks.append(dbg_holder[0])
            P.finish(final_toks)
    return nc


_NC_CACHE = {}


def _rope_tables():
    inv_freq = (500000.0 ** (-np.arange(0, 16, 2, dtype=np.float32) / 16)).astype(np.float32)
    ang = np.arange(L, dtype=np.float32)[:, None] * inv_freq[None, :]
    cos = np.cos(ang).astype(np.float32).T
    sin = np.sin(ang).astype(np.float32).T
    cosF = np.ones((128, L), np.float32)
    sinF = np.zeros((128, L), np.float32)
    for mp in range(2):
        b0 = 64 * mp
        cosF[b0:b0 + 8] = cos
        cosF[b0 + 8:b0 + 16] = cos
        sinF[b0:b0 + 8] = -sin
        sinF[b0 + 8:b0 + 16] = sin
    return cosF, sinF


def kernel(x, meta_tokens, norm1_g, w_in, mlstm_conv_w, mlstm_gate_bias, mlstm_norm_g,
           lambda_q1, lambda_k1, lambda_q2, lambda_k2, diff_subln_g, w_branch_m, w_branch_d,
           w_out, norm2_g, w_up, ffn_conv_w, w_down, norm_f_g):
    f32 = np.float32
    x = np.asarray(x, f32)
    w_in0 = np.asarray(w_in, f32)[0]
    B = x.shape[0]
    cosF, sinF = _rope_tables()
    o_mqk, o_mv, o_mo, o_gates, o_dq, o_dk, o_dv, o_gm = 0, 2048, 3072, 4096, 4112, 5136, 6160, 7184
    rotperm = np.arange(128)
    for mp in range(2):
        b0 = 64 * mp
        rotperm[b0:b0 + 8] = np.arange(b0 + 8, b0 + 16)
        rotperm[b0 + 8:b0 + 16] = np.arange(b0, b0 + 8)
    ident = np.eye(128, dtype=f32)
    triu = np.triu(np.ones((64, 64), f32))
    tril = np.tril(np.ones((64, 64), f32))
    common = {
        "w_g": np.ascontiguousarray(w_in0[:, o_gm:o_gm + 4096]),
        "w_a": np.ascontiguousarray(np.asarray(w_branch_m, f32)[0]),
        "w_b": np.ascontiguousarray(np.asarray(w_branch_d, f32)[0]),
        "w_out": np.ascontiguousarray(np.asarray(w_out, f32)[0]),
        "w_up": np.ascontiguousarray(np.asarray(w_up, f32)[0]),
        "w_down": np.ascontiguousarray(np.asarray(w_down, f32)[0]),
        "g1b": np.ascontiguousarray(np.broadcast_to(np.asarray(norm1_g, f32)[0], (128, D))),
        "gfb": np.ascontiguousarray(np.broadcast_to(np.asarray(norm_f_g, f32), (128, D))),
        "g2c": np.ascontiguousarray(np.asarray(norm2_g, f32)[0].reshape(16, 128).T),
        "cosf": cosF, "sinf": sinF,
        "fconv": np.ascontiguousarray(np.asarray(ffn_conv_w, f32)[0].reshape(3, 88, 128).transpose(2, 1, 0).reshape(128, 264)),
        "lamv": np.ascontiguousarray(np.broadcast_to(np.concatenate([np.asarray(a, f32)[0] for a in (lambda_q1, lambda_k1, lambda_q2, lambda_k2)]), (128, 256))),
        "sublng": np.ascontiguousarray(np.broadcast_to(np.asarray(diff_subln_g, f32)[0], (128, 128))),
        "ident": ident, "triu": triu, "tril": tril,
    }
    if "B" not in STAGES:
        for nm in ("w_g", "w_a", "w_b", "w_out", "w_up", "w_down"):
            common[nm] = np.zeros((128, 128), f32)
    in_maps = []
    mcw_full = np.asarray(mlstm_conv_w, f32)[0]
    gb_full = np.asarray(mlstm_gate_bias, f32)[0]
    for c in range(8):
        b, g = c // 4, c % 4
        hfull = np.concatenate([np.asarray(meta_tokens, f32), x[b]], axis=0)
        s0 = 15 + 1024 * g
        xwin = np.zeros((NW, D), f32)
        e0 = min(s0 + NW, L)
        xwin[:e0 - s0] = hfull[s0:e0]
        wmask = np.ones((128, NW), f32)
        if e0 - s0 < NW:
            wmask[:, e0 - s0:] = 0.0
        cols = []
        for base in (o_dq, o_dk):
            for hh in range(2):
                head = 2 * g + hh
                cols.append(base + 128 * head + np.arange(128))
            for hh in range(2):
                head = 2 * g + hh
                cols.append(base + 128 * head + rotperm)
        for hh in range(2):
            head = 2 * g + hh
            cols.append(o_dv + 128 * head + np.arange(128))
        w_attn = np.ascontiguousarray(w_in0[:, np.concatenate(cols)])
        qc = o_mqk + 256 * g + np.arange(256)
        kc = o_mqk + 1024 + 256 * g + np.arange(256)
        vc = o_mv + 256 * g + np.arange(256)
        oc = o_mo + 256 * g + np.arange(256)
        gc = o_gates + np.array([4 + g, 12 + g, 0 + g, 8 + g])
        w_ml = np.ascontiguousarray(w_in0[:, np.concatenate([qc, kc, vc, oc, gc])])
        mconv = np.ascontiguousarray(mcw_full[:, np.concatenate([qc, kc])].reshape(3, 4, 128).transpose(2, 1, 0).reshape(128, 12))
        gbias = np.ascontiguousarray(np.broadcast_to(gb_full[[4 + g, 12 + g, 0 + g, 8 + g]], (128, 4)))
        mngv = np.ascontiguousarray(np.broadcast_to(np.asarray(mlstm_norm_g, f32)[0][256 * g:256 * g + 256], (64, 256)))
        m = dict(common)
        m.update({"hfull": hfull, "xwin": xwin, "w_attn": w_attn, "w_ml": w_ml, "mconv": mconv,
                  "gbias": gbias, "mng": mngv, "wmask": wmask})
        in_maps.append(m)
    if "nc" not in _NC_CACHE:
        _NC_CACHE["nc"] = build_program()
    nc = _NC_CACHE["nc"]
    res = run_bass_kernel_spmd(nc, in_maps, core_ids=list(range(8)))
    out = np.empty((B, 4096, D), f32)
    for c in range(8):
        b, g = c // 4, c % 4
        out[b, 1024 * g:1024 * g + 1024] = res.results[c]["out"]
    if DEBUG:
        kernel.dbg = [res.results[c] for c in range(8)]
    return out
```

```python
import numpy as np
from contextlib import ExitStack
import concourse.bass as bass
import concourse.mybir as mybir
from concourse.bass_utils import run_bass_kernel_spmd

F32 = mybir.dt.float32
BF16 = mybir.dt.bfloat16
AF = mybir.ActivationFunctionType
ALU = mybir.AluOpType
AX = mybir.AxisListType

SAME_ENGINE_SYNC = True
DEBUG = False
STAGES = ("A1", "A2", "A3", "X", "B")

D = 2048
L = 4112
LP = 4160
NMETA = 16
NW = 1026
FFN = 5632
EPS = 1e-6


class _Stop(Exception):
    pass


STOP_AT = None


_PROG = []


def stop_here(tag):
    if STOP_AT == tag:
        _PROG[0].stopped = True


_FENCE = []


class Res:
    __slots__ = ("name", "w", "r")

    def __init__(self, name=""):
        self.name = name
        self.w = None
        self.r = dict(_FENCE)


class Prog:
    ENGS = ("pe", "act", "dve", "pool", "sp")

    def __init__(self, nc, stack, n_dma_sems=8):
        self.nc = nc
        self.stack = stack
        self.streams = {e: [] for e in self.ENGS}
        self.sems = {}
        self.count = {}
        self.waited = {e: {} for e in self.ENGS}
        for e in self.ENGS:
            self.sems["c_" + e] = stack.enter_context(nc.semaphore("c_" + e))
            self.count["c_" + e] = 0
        self.dma_pool = {}
        self.dma_next = {}
        for e in ("sp", "act", "pool"):
            keys = []
            for i in range(n_dma_sems):
                k = f"d_{e}{i}"
                self.sems[k] = stack.enter_context(nc.semaphore(k))
                self.count[k] = 0
                keys.append(k)
            self.dma_pool[e] = keys
            self.dma_next[e] = 0
        self.sems["cc"] = stack.enter_context(nc.semaphore("cc"))
        self.count["cc"] = 0
        self.stopped = False
        _PROG[:] = [self]
        _FENCE[:] = []

    def _need(self, eng, dep):
        if dep is None:
            return
        key, val = dep
        if key == "c_" + eng and (eng == "pe" or not SAME_ENGINE_SYNC):
            return
        if self.waited[eng].get(key, 0) >= val:
            return
        self.waited[eng][key] = val
        self.streams[eng].append(("wait", key, val))

    def _deps(self, eng, reads, writes):
        own = "c_" + eng
        for r in reads:
            self._need(eng, r.w)
            for k, v in r.r.items():
                if k != own:
                    self._need(eng, (k, v))
        for w in writes:
            self._need(eng, w.w)
            for k, v in w.r.items():
                self._need(eng, (k, v))

    def _commit(self, reads, writes, tok):
        for r in reads:
            if r.r.get(tok[0], 0) < tok[1]:
                r.r[tok[0]] = tok[1]
        for w in writes:
            w.w = tok
            w.r = {}

    def op(self, eng, fns, reads=(), writes=()):
        if self.stopped:
            return None
        if not isinstance(fns, (list, tuple)):
            fns = [fns]
        self._deps(eng, reads, writes)
        key = "c_" + eng
        self.count[key] += 1
        tok = (key, self.count[key])
        self.streams[eng].append(("op", fns, key, 1))
        self._commit(reads, writes, tok)
        return tok

    def dma(self, eng, fn, reads=(), writes=()):
        if self.stopped:
            return None
        pool = self.dma_pool[eng]
        key = pool[self.dma_next[eng] % len(pool)]
        self.dma_next[eng] += 1
        if self.count[key] > 0:
            self._need(eng, (key, self.count[key]))
        self._deps(eng, reads, writes)
        self.count[key] += 16
        tok = (key, self.count[key])
        self.streams[eng].append(("op", [fn], key, 16))
        self._commit(reads, writes, tok)
        return tok

    def fence(self):
        _FENCE[:] = [(k, c) for k, c in self.count.items() if c > 0 and k != "cc"]

    def cc(self, fn, reads=(), writes=()):
        if self.stopped:
            return None
        eng = "pool"
        self._deps(eng, reads, writes)
        self.count["cc"] += 1
        tok = ("cc", self.count["cc"])
        self.streams[eng].append(("cc", fn, "cc"))
        self._commit(reads, writes, tok)
        return tok

    def finish(self, final_tokens):
        for t in final_tokens:
            self._need("sp", t)
        for k, c in self.count.items():
            if c > 0:
                self._need("sp", (k, c))
        nc = self.nc
        with nc.Block() as block:
            def mk(ename):
                def body(e):
                    for item in self.streams[ename]:
                        if item[0] == "wait":
                            e.wait_ge(self.sems[item[1]], item[2])
                        elif item[0] == "op":
                            fns, key, inc = item[1], item[2], item[3]
                            for f in fns[:-1]:
                                f(e)
                            fns[-1](e).then_inc(self.sems[key], inc)
                        elif item[0] == "cc":
                            item[1](e).then_inc(self.sems[item[2]])
                return body
            block.tensor(mk("pe"))
            block.scalar(mk("act"))
            block.vector(mk("dve"))
            block.gpsimd(mk("pool"))
            block.sync(mk("sp"))


class Ring:
    def __init__(self, tiles):
        self.tiles = tiles
        self.res = [Res() for _ in tiles]
        self.i = 0

    def next(self):
        k = self.i % len(self.tiles)
        self.i += 1
        return self.tiles[k], self.res[k]


def build_program():
    nc = bass.Bass("TRN2", target_bir_lowering=False)
    dt_in = lambda name, shape, dt=F32: nc.dram_tensor(name, shape, dt, kind="ExternalInput").ap()
    dt_int = lambda name, shape, dt: nc.dram_tensor(name, shape, dt, kind="Internal").ap()

    hfull = dt_in("hfull", [L, D])
    xwin = dt_in("xwin", [NW, D])
    w_attn = dt_in("w_attn", [D, 1280])
    w_ml = dt_in("w_ml", [D, 1028])
    w_g = dt_in("w_g", [D, 4096] if "B" in STAGES else [128, 128])
    w_a = dt_in("w_a", [1024, D] if "B" in STAGES else [128, 128])
    w_b = dt_in("w_b", [1024, D] if "B" in STAGES else [128, 128])
    w_out = dt_in("w_out", [D, D] if "B" in STAGES else [128, 128])
    w_up = dt_in("w_up", [D, 2 * FFN] if "B" in STAGES else [128, 128])
    w_down = dt_in("w_down", [FFN, D] if "B" in STAGES else [128, 128])
    g1b = dt_in("g1b", [128, D])
    gfb = dt_in("gfb", [128, D])
    g2c = dt_in("g2c", [128, 16])
    cosf = dt_in("cosf", [128, L])
    sinf = dt_in("sinf", [128, L])
    mconv = dt_in("mconv", [128, 12])
    fconv = dt_in("fconv", [128, 88 * 3])
    gbias = dt_in("gbias", [128, 4])
    mng = dt_in("mng", [64, 256])
    lamv = dt_in("lamv", [128, 4 * 64])
    sublng = dt_in("sublng", [128, 128])
    ident_d = dt_in("ident", [128, 128])
    triu_d = dt_in("triu", [64, 64])
    tril_d = dt_in("tril", [64, 64])
    wmask_d = dt_in("wmask", [128, NW])
    out_d = nc.dram_tensor("out", [1024, D], F32, kind="ExternalOutput").ap()
    if DEBUG:
        dbg_cc = nc.dram_tensor("dbg_cc", [512, 4114], BF16, kind="ExternalOutput").ap()

    mqk_raw = dt_int("mqk_raw", [512, LP + 2], BF16)
    mv_s = dt_int("mv_s", [LP, 257], BF16)
    mo_s = dt_int("mo_s", [LP, 256], F32)
    gates_s = dt_int("gates_s", [LP, 4], F32)
    hf_s = dt_int("hf_s", [LP, 256], F32)
    hb_s = dt_int("hb_s", [LP, 256], F32)
    cc_in = dt_int("cc_in", [512, 4114], BF16)
    cc_win = dt_int("cc_win", [8 * 256, NW], BF16)
    cc_out = dt_int("cc_out", [8 * 1024, NW], BF16)

    with ExitStack() as st:
        P = Prog(nc, st)

        def sb(name, shape, dt, stack=st):
            return stack.enter_context(nc.sbuf_tensor(name, shape, dt))

        def ps(name, shape, dt, stack=st):
            return stack.enter_context(nc.psum_tensor(name, shape, dt))

        final_toks = []
        dbg_holder = []
        try:
            idf = sb("idf", [128, 128], F32); r_idf = Res()
            idb = sb("idb", [128, 128], BF16); r_idb = Res()
            zt = sb("zt", [128, 512], BF16); r_zt = Res()
            P.dma("sp", lambda e: e.dma_start(out=idf[:], in_=ident_d[:, :]), writes=[r_idf])
            P.op("dve", lambda e: e.tensor_copy(out=idb[:], in_=idf[:]), reads=[r_idf], writes=[r_idb])
            P.op("dve", lambda e: e.memset(zt[:], 0.0), writes=[r_zt])

            r_mqk_raw = Res(); r_mv_s = Res(); r_mo_s = Res(); r_gates_s = Res(); r_hf_s = Res(); r_hb_s = Res(); r_cc_in = Res(); r_cc_in_b = Res()
            r_cc_win = [Res(), Res()]; r_cc_out = [Res(), Res()]
            P.dma("sp", lambda e: e.dma_start(out=mqk_raw.rearrange("(c p) n -> p c n", p=128)[:, :, 0:49], in_=zt[:, 0:196].rearrange("p (c n) -> p c n", c=4)), reads=[r_zt], writes=[r_mqk_raw])
            P.dma("sp", lambda e: e.dma_start(out=mqk_raw.rearrange("(c p) n -> p c n", p=128)[:, :, LP + 1:LP + 2], in_=zt[:, 0:4].rearrange("p (c n) -> p c n", c=4), allow_slow_non_contiguous=True), reads=[r_zt], writes=[r_mqk_raw])
            P.dma("sp", lambda e: e.dma_start(out=mv_s[0:48, :], in_=zt[0:48, 0:257]), reads=[r_zt], writes=[r_mv_s])
            ztf = sb("ztf", [64, 256], F32); r_ztf = Res()
            P.op("dve", lambda e: e.memset(ztf[:], 0.0), writes=[r_ztf])
            P.dma("sp", lambda e: e.dma_start(out=mo_s[0:48, :], in_=ztf[0:48, :]), reads=[r_ztf], writes=[r_mo_s])
            P.dma("sp", lambda e: e.dma_start(out=gates_s[0:48, :], in_=ztf[0:48, 0:4]), reads=[r_ztf], writes=[r_gates_s])
            P.dma("sp", lambda e: e.dma_start(out=cc_in[:, 4112:4114].rearrange("(c p) n -> p c n", p=128), in_=zt[:, 0:8].rearrange("p (c n) -> p c n", c=4)), reads=[r_zt], writes=[r_cc_in, r_cc_in_b])


            def exchange_half(half, js=(0, 1, 2, 3)):
                r_src = r_cc_in if half == 0 else r_cc_in_b
                for j in js:
                    i = j * 2 + half
                    P.dma("sp", lambda e, i=i, j=j, half=half: e.dma_start(out=cc_win[i * 256:(i + 1) * 256, :], in_=cc_in[half * 256:(half + 1) * 256, 15 + 1024 * j:15 + 1024 * j + NW]), reads=[r_src], writes=[r_cc_win[half]])
                for j in js:
                    i = j * 2 + half
                    P.cc(lambda e, i=i: e.collective_compute("AllGather", ALU.bypass, replica_groups=[[0, 1, 2, 3], [4, 5, 6, 7]], ins=[cc_win[i * 256:(i + 1) * 256, :]], outs=[cc_out[i * 1024:(i + 1) * 1024, :]]), reads=[r_cc_win[half]], writes=[r_cc_out[half]])

            stop_here("c0")
            with ExitStack() as sa:
                dqkT = sb("dqkT", [128, 4, L], BF16, sa); r_dqkT = Res()
                vatt = sb("vatt", [128, 33, 2, 129], BF16, sa); r_vatt = Res()
                with ExitStack() as s1:
                    wat = sb("wat", [128, 16, 1280], BF16, s1); r_wat = Res()
                    wml = sb("wml", [128, 16, 1028], BF16, s1); r_wml = Res()
                    g1t = sb("g1t", [128, D], F32, s1); r_g1t = Res()
                    gbt = sb("gbt", [128, 4], F32, s1); r_gbt = Res()
                    for kq in range(4):
                        P.dma("pool", lambda e, kq=kq: e.dma_start(out=wat[:, 4 * kq:4 * kq + 4, :], in_=w_attn[512 * kq:512 * kq + 512, :].rearrange("(k p) n -> p k n", p=128)), writes=[r_wat])
                        P.dma("pool", lambda e, kq=kq: e.dma_start(out=wml[:, 4 * kq:4 * kq + 4, :], in_=w_ml[512 * kq:512 * kq + 512, :].rearrange("(k p) n -> p k n", p=128)), writes=[r_wml])
                    P.dma("act", lambda e: e.dma_start(out=g1t[:], in_=g1b[:, :]), writes=[r_g1t])
                    P.dma("act", lambda e: e.dma_start(out=gbt[:], in_=gbias[:, :]), writes=[r_gbt])
                    P.op("dve", lambda e: e.memset(vatt[:, :, :, 128:129], 1.0), writes=[r_vatt])

                    xring = Ring([sb(f"xt{i}", [128, D], F32, s1) for i in range(3)])
                    ss = sb("ss", [128, 1], F32, s1); r_ss = Res()
                    u = sb("u", [128, D], BF16, s1); r_u = Res()
                    uT = sb("uT", [128, 16, 512], BF16, s1); r_uT = Res()
                    csr = Ring([sb(f"cs{i}", [128, 2, 512], F32, s1) for i in range(2)])
                    rt1 = sb("rt1", [128, 512], F32, s1); r_rt1 = Res()
                    rt2 = sb("rt2", [128, 512], F32, s1); r_rt2 = Res()
                    mstage = Ring([sb(f"mst{i}", [128, 4, 512], BF16, s1) for i in range(2)])
                    mvst = Ring([sb(f"mvst{i}", [128, 257], BF16, s1) for i in range(2)])
                    most = Ring([sb(f"most{i}", [128, 256], F32, s1) for i in range(2)])
                    gst = Ring([sb(f"gst{i}", [128, 4], F32, s1) for i in range(2)])
                    pT = ps("pT", [128, D], BF16, s1); r_pT = Res()
                    pa = ps("pa", [128, 512], F32, s1); r_pa = Res()
                    pb = ps("pb", [128, 512], F32, s1); r_pb = Res()
                    pm = ps("pm", [128, 512], F32, s1); r_pm = Res()
                    pv = ps("pv", [128, 512], F32, s1); r_pv = Res()
                    pvo = ps("pvo", [128, 512], F32, s1); r_pvo = Res()
                    pg = ps("pg", [128, 512], F32, s1); r_pg = Res()
                    for rr, rres in zip(mvst.tiles, mvst.res):
                        P.op("dve", lambda e, rr=rr: e.memset(rr[:, 256:257], 1.0), writes=[rres])

                    def norm_block(xt, r_xt, bs, gtile, r_g):
                        P.op("act", lambda e: e.activation(out=u[:bs, :], in_=xt[:bs, :], func=AF.Square, accum_out=ss[:bs, :]), reads=[r_xt], writes=[r_u, r_ss])
                        P.op("dve", lambda e: e.tensor_scalar(out=ss[:bs, :], in0=ss[:bs, :], scalar1=1.0 / D, scalar2=EPS, op0=ALU.mult, op1=ALU.add), reads=[r_ss], writes=[r_ss])
                        P.op("act", lambda e: e.activation(out=ss[:bs, :], in_=ss[:bs, :], func=AF.Sqrt), reads=[r_ss], writes=[r_ss])
                        P.op("dve", lambda e: e.reciprocal(out=ss[:bs, :], in_=ss[:bs, :]), reads=[r_ss], writes=[r_ss])
                        P.op("dve", lambda e: e.scalar_tensor_tensor(out=u[:bs, :], in0=xt[:bs, :], scalar=ss[:bs, 0:1], in1=gtile[:bs, :], op0=ALU.mult, op1=ALU.mult), reads=[r_xt, r_ss, r_g], writes=[r_u])

                    stop_here("a1w")
                    tiles = [(512 * i, 512) for i in range(8)] + [(4096, 16)]
                    for (p0, n) in (tiles if "A1" in STAGES else []):
                        nblk = (n + 127) // 128
                        cs, r_cs = csr.next()
                        P.dma("act", lambda e, cs=cs, p0=p0, n=n: e.dma_start(out=cs[:, 0, :n], in_=cosf[:, p0:p0 + n]), writes=[r_cs])
                        P.dma("act", lambda e, cs=cs, p0=p0, n=n: e.dma_start(out=cs[:, 1, :n], in_=sinf[:, p0:p0 + n]), writes=[r_cs])
                        for j in range(nblk):
                            bs = min(128, n - 128 * j)
                            xt, r_xt = xring.next()
                            P.dma("pool", lambda e, xt=xt, bs=bs, r0=p0 + 128 * j: e.dma_start(out=xt[:bs, :], in_=hfull[r0:r0 + bs, :]), writes=[r_xt])
                            norm_block(xt, r_xt, bs, g1t, r_g1t)
                            P.op("pe", [(lambda e, k=k, bs=bs: e.transpose(out=pT[:, k * 128:k * 128 + bs], in_=u[:bs, k * 128:(k + 1) * 128], identity=idb[:bs, :bs])) for k in range(16)], reads=[r_u, r_idb], writes=[r_pT])
                            P.op("act", lambda e, j=j, bs=bs: e.copy(out=uT[:, :, 128 * j:128 * j + bs], in_=pT[:].rearrange("p (k n) -> p k n", k=16)[:, :, :bs]), reads=[r_pT], writes=[r_uT])
                        stop_here("blk%d" % (p0 // 512))
                        for c in range(4):
                            cm = (c % 2) * 128 + (c // 2) * 512
                            cr = cm + 256
                            P.op("pe", [(lambda e, k=k, cm=cm, n=n: e.matmul(out=pa[:, :n], lhsT=wat[:, k, cm:cm + 128], rhs=uT[:, k, :n], start=(k == 0), stop=(k == 15))) for k in range(16)], reads=[r_wat, r_uT], writes=[r_pa])
                            P.op("pe", [(lambda e, k=k, cr=cr, n=n: e.matmul(out=pb[:, :n], lhsT=wat[:, k, cr:cr + 128], rhs=uT[:, k, :n], start=(k == 0), stop=(k == 15))) for k in range(16)], reads=[r_wat, r_uT], writes=[r_pb])
                            P.op("dve", lambda e, cs=cs, n=n: e.tensor_tensor(out=rt1[:, :n], in0=pa[:, :n], in1=cs[:, 0, :n], op=ALU.mult), reads=[r_pa, r_cs], writes=[r_rt1])
                            P.op("dve", lambda e, cs=cs, n=n: e.tensor_tensor(out=rt2[:, :n], in0=pb[:, :n], in1=cs[:, 1, :n], op=ALU.mult), reads=[r_pb, r_cs], writes=[r_rt2])
                            P.op("dve", lambda e, c=c, p0=p0, n=n: e.tensor_tensor(out=dqkT[:, c, p0:p0 + n], in0=rt1[:, :n], in1=rt2[:, :n], op=ALU.add), reads=[r_rt1, r_rt2], writes=[r_dqkT])
                        stop_here("fm%d" % (p0 // 512))
                        mst, r_mst = mstage.next()
                        for c in range(4):
                            P.op("pe", [(lambda e, k=k, c=c, n=n: e.matmul(out=pm[:, :n], lhsT=wml[:, k, c * 128:(c + 1) * 128], rhs=uT[:, k, :n], start=(k == 0), stop=(k == 15))) for k in range(16)], reads=[r_wml, r_uT], writes=[r_pm])
                            P.op("act", lambda e, mst=mst, c=c, n=n: e.copy(out=mst[:, c, :n], in_=pm[:, :n]), reads=[r_pm], writes=[r_mst])
                        P.dma("sp", lambda e, mst=mst, p0=p0, n=n: e.dma_start(out=mqk_raw.rearrange("(c p) n -> p c n", p=128)[:, :, 49 + p0:49 + p0 + n], in_=mst[:, :, :n]), reads=[r_mst], writes=[r_mqk_raw])
                        stop_here("ml%d" % (p0 // 512))
                        for j in range(nblk):
                            bs = min(128, n - 128 * j)
                            kb = (p0 + 128 * j) // 128
                            r0 = 48 + p0 + 128 * j
                            P.op("pe", [(lambda e, k=k, j=j, bs=bs: e.matmul(out=pv[:bs, 0:256], lhsT=uT[:, k, 128 * j:128 * j + bs], rhs=wat[:, k, 1024:1280], start=(k == 0), stop=(k == 15))) for k in range(16)], reads=[r_wat, r_uT], writes=[r_pv])
                            P.op("dve", lambda e, kb=kb, bs=bs: e.tensor_copy(out=vatt[:bs, kb, :, 0:128], in_=pv[:bs, 0:256].rearrange("p (h d) -> p h d", h=2)), reads=[r_pv], writes=[r_vatt])
                            stop_here("tv%d_%d" % (p0 // 512, j))
                            P.op("pe", [(lambda e, k=k, j=j, bs=bs: e.matmul(out=pvo[:bs, :], lhsT=uT[:, k, 128 * j:128 * j + bs], rhs=wml[:, k, 512:1024], start=(k == 0), stop=(k == 15))) for k in range(16)], reads=[r_wml, r_uT], writes=[r_pvo])
                            mvt, r_mvt = mvst.next()
                            mot, r_mot = most.next()
                            P.op("act", lambda e, mvt=mvt, bs=bs: e.copy(out=mvt[:bs, 0:256], in_=pvo[:bs, 0:256]), reads=[r_pvo], writes=[r_mvt])
                            P.op("act", lambda e, mot=mot, bs=bs: e.activation(out=mot[:bs, :], in_=pvo[:bs, 256:512], func=AF.Sigmoid), reads=[r_pvo], writes=[r_mot])
                            P.dma("sp", lambda e, mvt=mvt, bs=bs, r0=r0: e.dma_start(out=mv_s[r0:r0 + bs, :], in_=mvt[:bs, :]), reads=[r_mvt], writes=[r_mv_s])
                            P.dma("sp", lambda e, mot=mot, bs=bs, r0=r0: e.dma_start(out=mo_s[r0:r0 + bs, :], in_=mot[:bs, :]), reads=[r_mot], writes=[r_mo_s])
                            stop_here("tm%d_%d" % (p0 // 512, j))
                            P.op("pe", [(lambda e, k=k, j=j, bs=bs: e.matmul(out=pg[:bs, 0:64], lhsT=uT[:, k, 128 * j:128 * j + bs], rhs=wml[:, k, 964:1028], start=(k == 0), stop=(k == 15))) for k in range(16)], reads=[r_wml, r_uT], writes=[r_pg])
                            stop_here("tgm%d_%d" % (p0 // 512, j))
                            gt_, r_gt = gst.next()
                            P.op("dve", lambda e, gt_=gt_, bs=bs: e.tensor_tensor(out=gt_[:bs, :], in0=pg[:bs, 60:64], in1=gbt[:bs, :], op=ALU.add), reads=[r_pg, r_gbt], writes=[r_gt])
                            stop_here("tga%d_%d" % (p0 // 512, j))
                            P.dma("sp", lambda e, gt_=gt_, bs=bs, r0=r0: e.dma_start(out=gates_s[r0:r0 + bs, :], in_=gt_[:bs, :]), reads=[r_gt], writes=[r_gates_s])
                    stop_here("a1t%d" % (p0 // 512))

                P.fence()
                stop_here("a1")
                with ExitStack() as s2:
                    lam_t = sb("lam_t", [128, 256], F32, s2); r_lam = Res()
                    lamw = sb("lamw", [128, 8], F32, s2); r_lamw = Res()
                    sg_t = sb("sg_t", [128, 128], F32, s2); r_sg = Res()
                    P.dma("act", lambda e: e.dma_start(out=lam_t[:], in_=lamv[:, :]), writes=[r_lam])
                    P.dma("act", lambda e: e.dma_start(out=sg_t[:], in_=sublng[:, :]), writes=[r_sg])
                    ljunk = sb("ljunk", [128, 64], F32, s2); r_lj = Res()
                    for i in range(2):
                        P.op("dve", lambda e, i=i: e.tensor_tensor(out=ljunk[:], in0=lam_t[:, 128 * i:128 * i + 64], in1=lam_t[:, 128 * i + 64:128 * i + 128], op=ALU.mult), reads=[r_lam], writes=[r_lj])
                        P.op("dve", lambda e, i=i: e.reduce_sum(out=lamw[:, i:i + 1], in_=ljunk[:], axis=AX.X), reads=[r_lj], writes=[r_lamw])
                    P.op("act", lambda e: e.activation(out=lamw[:, 2:4], in_=lamw[:, 0:2], func=AF.Exp), reads=[r_lamw], writes=[r_lamw])
                    P.op("dve", lambda e: e.tensor_tensor(out=lamw[:, 4:5], in0=lamw[:, 2:3], in1=lamw[:, 3:4], op=ALU.subtract), reads=[r_lamw], writes=[r_lamw])
                    P.op("dve", lambda e: e.tensor_scalar(out=lamw[:, 5:6], in0=lamw[:, 4:5], scalar1=0.2, scalar2=-1.0, op0=ALU.add, op1=ALU.mult), reads=[r_lamw], writes=[r_lamw])
                    P.op("dve", lambda e: e.tensor_scalar(out=sg_t[:], in0=sg_t[:], scalar1=0.8, scalar2=None, op0=ALU.mult), reads=[r_sg], writes=[r_sg])

                    stop_here("a2s")
                    pss = [Ring([ps(f"ps{m}{i}", [128, 512], F32, s2) for i in range(2)]) for m in range(2)]
                    pacc = [ps(f"pacc{i}", [128, 512], F32, s2) for i in range(3)]
                    r_pacc = Res()
                    ptr = ps("ptr", [128, 512], BF16, s2); r_ptr = Res()
                    ptile = [Ring([sb(f"pt{m}{i}", [128, 512], BF16, s2) for i in range(3)]) for m in range(2)]
                    rc = sb("rc", [128, 2], F32, s2); r_rc = Res()
                    t2 = sb("t2", [128, 128], F32, s2); r_t2 = Res()
                    ot = sb("ot", [128, 128], F32, s2); r_ot = Res()
                    oj = sb("oj", [128, 128], F32, s2); r_oj = Res()
                    os_ = sb("os_", [128, 1], F32, s2); r_os = Res()
                    bo = sb("bo", [128, 128], BF16, s2); r_bo = Res()
                    boT = Ring([sb(f"boT{i}", [128, 512], BF16, s2) for i in range(2)])

                    def acc(m, sub):
                        i = m * 4 + sub
                        return pacc[i // 3][:, (i % 3) * 129:(i % 3) * 129 + 129]

                    qtiles = [(512 * i, 512) for i in range(8)] + [(4096, 16)]
                    kblocks = [(128 * i, 128) for i in range(32)] + [(4096, 16)]
                    for h in range(2 if "A2" in STAGES else 0):
                        for (q0, nq) in qtiles:
                            nsub = (nq + 127) // 128
                            P.op("dve", [(lambda e, i=i: e.memset(pacc[i][:], 0.0)) for i in range(3)], writes=[r_pacc])
                            pend = []
                            for kbi, (k0, nk) in enumerate(kblocks):
                                cur = []
                                for m in range(2):
                                    pst, r_pst = pss[m].next()
                                    P.op("pe", lambda e, pst=pst, m=m, h=h, k0=k0, nk=nk, q0=q0, nq=nq: e.matmul(out=pst[:nk, :nq], lhsT=dqkT[64 * m:64 * m + 64, 2 + h, k0:k0 + nk], rhs=dqkT[64 * m:64 * m + 64, h, q0:q0 + nq], start=True, stop=True), reads=[r_dqkT], writes=[r_pst])
                                    cur.append((m, pst, r_pst))
                                newpend = []
                                for (m, pst, r_pst) in cur:
                                    pt, r_pt = ptile[m].next()
                                    P.op("act", lambda e, pst=pst, pt=pt, nk=nk, nq=nq: e.activation(out=pt[:nk, :nq], in_=pst[:nk, :nq], func=AF.Exp, scale=0.125), reads=[r_pst], writes=[r_pt])
                                    def pvf(pt=pt, r_pt=r_pt, m=m, nk=nk, kbi=kbi):
                                        P.op("pe", [(lambda e, pt=pt, m=m, sub=sub, nk=nk, kbi=kbi, h=h, qs=min(128, nq - 128 * sub): e.matmul(out=acc(m, sub)[:qs, :], lhsT=pt[:nk, 128 * sub:128 * sub + qs], rhs=vatt[:nk, kbi, h, :], start=False, stop=False, skip_group_check=True)) for sub in range(nsub)], reads=[r_pt, r_vatt], writes=[r_pacc])
                                    newpend.append(pvf)
                                for f in pend:
                                    f()
                                pend = newpend
                            for f in pend:
                                f()
                            bT, r_bT = boT.next()
                            for sub in range(nsub):
                                qs = min(128, nq - 128 * sub)
                                a1 = acc(0, sub); a2 = acc(1, sub)
                                P.op("dve", lambda e, a1=a1, qs=qs: e.reciprocal(out=rc[:qs, 0:1], in_=a1[:qs, 128:129]), reads=[r_pacc], writes=[r_rc])
                                P.op("dve", lambda e, a2=a2, qs=qs: e.reciprocal(out=rc[:qs, 1:2], in_=a2[:qs, 128:129]), reads=[r_pacc], writes=[r_rc])
                                P.op("dve", lambda e, a2=a2, qs=qs: e.tensor_scalar(out=t2[:qs, :], in0=a2[:qs, 0:128], scalar1=rc[:qs, 1:2], scalar2=lamw[:qs, 5:6], op0=ALU.mult, op1=ALU.mult), reads=[r_pacc, r_rc, r_lamw], writes=[r_t2])
                                P.op("dve", lambda e, a1=a1, qs=qs: e.scalar_tensor_tensor(out=ot[:qs, :], in0=a1[:qs, 0:128], scalar=rc[:qs, 0:1], in1=t2[:qs, :], op0=ALU.mult, op1=ALU.add), reads=[r_pacc, r_rc, r_t2], writes=[r_ot])
                                P.op("act", lambda e, qs=qs: e.activation(out=oj[:qs, :], in_=ot[:qs, :], func=AF.Square, accum_out=os_[:qs, :]), reads=[r_ot], writes=[r_oj, r_os])
                                P.op("dve", lambda e, qs=qs: e.tensor_scalar(out=os_[:qs, :], in0=os_[:qs, :], scalar1=1.0 / 128, scalar2=EPS, op0=ALU.mult, op1=ALU.add), reads=[r_os], writes=[r_os])
                                P.op("act", lambda e, qs=qs: e.activation(out=os_[:qs, :], in_=os_[:qs, :], func=AF.Sqrt), reads=[r_os], writes=[r_os])
                                P.op("dve", lambda e, qs=qs: e.reciprocal(out=os_[:qs, :], in_=os_[:qs, :]), reads=[r_os], writes=[r_os])
                                P.op("dve", lambda e, qs=qs: e.scalar_tensor_tensor(out=bo[:qs, :], in0=ot[:qs, :], scalar=os_[:qs, 0:1], in1=sg_t[:qs, :], op0=ALU.mult, op1=ALU.mult), reads=[r_ot, r_os, r_sg], writes=[r_bo])
                                P.op("pe", lambda e, sub=sub, qs=qs: e.transpose(out=ptr[:, 128 * sub:128 * sub + qs], in_=bo[:qs, :], identity=idb[:qs, :qs]), reads=[r_bo, r_idb], writes=[r_ptr])
                                P.op("act", lambda e, bT=bT, sub=sub, qs=qs: e.copy(out=bT[:, 128 * sub:128 * sub + qs], in_=ptr[:, 128 * sub:128 * sub + qs]), reads=[r_ptr], writes=[r_bT])
                            P.dma("sp", lambda e, bT=bT, h=h, q0=q0, nq=nq: e.dma_start(out=cc_in[256 + 128 * h:256 + 128 * h + 128, q0:q0 + nq], in_=bT[:, :nq]), reads=[r_bT], writes=[r_cc_in_b])

            if "X" in STAGES:
                exchange_half(1)
            P.fence()
            with ExitStack() as s3:
                mqkT = sb("mqkT", [128, 4, LP], BF16, s3); r_mqkT = Res()
                with ExitStack() as s3a:
                    raw = sb("raw", [128, 4, LP + 2], BF16, s3a); r_raw = Res()
                    cacc = sb("cacc", [128, LP], F32, s3a); r_cacc = Res()
                    mcw = sb("mcw", [128, 12], F32, s3a); r_mcw = Res()
                    P.dma("act", lambda e: e.dma_start(out=mcw[:], in_=mconv[:, :]), writes=[r_mcw])
                    P.dma("sp", lambda e: e.dma_start(out=raw[:], in_=mqk_raw.rearrange("(c p) n -> p c n", p=128)), reads=[r_mqk_raw], writes=[r_raw])
                    for c in range(4):
                        P.op("dve", lambda e, c=c: e.tensor_scalar(out=cacc[:], in0=raw[:, c, 0:LP], scalar1=mcw[:, 3 * c:3 * c + 1], scalar2=None, op0=ALU.mult), reads=[r_raw, r_mcw], writes=[r_cacc])
                        P.op("dve", lambda e, c=c: e.scalar_tensor_tensor(out=cacc[:], in0=raw[:, c, 1:LP + 1], scalar=mcw[:, 3 * c + 1:3 * c + 2], in1=cacc[:], op0=ALU.mult, op1=ALU.add), reads=[r_raw, r_mcw, r_cacc], writes=[r_cacc])
                        P.op("dve", lambda e, c=c: e.scalar_tensor_tensor(out=cacc[:], in0=raw[:, c, 2:LP + 2], scalar=mcw[:, 3 * c + 2:3 * c + 3], in1=cacc[:], op0=ALU.mult, op1=ALU.add), reads=[r_raw, r_mcw, r_cacc], writes=[r_cacc])
                        P.op("act", lambda e, c=c: e.activation(out=mqkT[:, c, :], in_=cacc[:], func=AF.Silu), reads=[r_cacc], writes=[r_mqkT])
                    P.op("dve", lambda e: e.memset(mqkT[:, :, 0:48], 0.0), writes=[r_mqkT])

                P.fence()
                stop_here("a3conv")
                gt = sb("gt", [64, 65, 4], F32, s3); r_gtb = Res()
                gl = sb("gl", [64, 4, 65], F32, s3); r_gl = Res()
                tru = sb("tru", [64, 64], F32, s3); r_tru = Res()
                trl = sb("trl", [64, 64], F32, s3); r_trl = Res()
                mku = sb("mku", [64, 64], F32, s3); r_mku = Res()
                mkl = sb("mkl", [64, 64], F32, s3); r_mkl = Res()
                ones64 = sb("ones64", [64, 128], F32, s3); r_ones = Res()
                ebt = sb("ebt", [64, 2, 65], F32, s3); r_ebt = Res()
                rft = sb("rft", [64, 2, 65], F32, s3); r_rft = Res()
                egt = sb("egt", [128, 2, 65], F32, s3); r_egt = Res()
                mngt = sb("mngt", [64, 256], F32, s3); r_mngt = Res()
                s3g = s3.enter_context(ExitStack())
                pgx = ps("pgx", [128, 512], F32, s3g); r_pgx = Res()
                for c5 in range(5):
                    P.dma("sp", lambda e, c5=c5: e.dma_start(out=gt[:, 13 * c5:13 * c5 + 13, :], in_=gates_s.rearrange("(c t) g -> t c g", t=64)[:, 13 * c5:13 * c5 + 13, :]), reads=[r_gates_s], writes=[r_gtb])
                P.dma("act", lambda e: e.dma_start(out=tru[:], in_=triu_d[:, :]), writes=[r_tru])
                P.dma("act", lambda e: e.dma_start(out=trl[:], in_=tril_d[:, :]), writes=[r_trl])
                P.dma("act", lambda e: e.dma_start(out=mngt[:], in_=mng[:, :]), writes=[r_mngt])
                P.op("dve", lambda e: e.tensor_scalar(out=mku[:], in0=tru[:], scalar1=1.0 / 16, scalar2=None, op0=ALU.mult), reads=[r_tru], writes=[r_mku])
                P.op("dve", lambda e: e.tensor_scalar(out=mkl[:], in0=trl[:], scalar1=1.0 / 16, scalar2=None, op0=ALU.mult), reads=[r_trl], writes=[r_mkl])
                P.op("dve", lambda e: e.memset(ones64[:], 1.0), writes=[r_ones])
                stop_here("g1")
                for gi in range(2):
                    P.op("act", lambda e, gi=gi: e.activation(out=gl[:, gi, :], in_=gt[:, :, gi], func=AF.Exp, scale=-1.0), reads=[r_gtb], writes=[r_gl])
                P.op("dve", lambda e: e.tensor_scalar(out=gl[:, 0:2, :], in0=gl[:, 0:2, :], scalar1=1.0, scalar2=None, op0=ALU.add), reads=[r_gl], writes=[r_gl])
                P.op("act", lambda e: e.activation(out=gl[:, 0:2, :], in_=gl[:, 0:2, :], func=AF.Ln), reads=[r_gl], writes=[r_gl])
                P.op("dve", lambda e: e.tensor_scalar(out=gl[:, 0:2, :], in0=gl[:, 0:2, :], scalar1=-1.0, scalar2=None, op0=ALU.mult), reads=[r_gl], writes=[r_gl])
                for gi in range(2):
                    P.op("dve", lambda e, gi=gi: e.tensor_copy(out=gl[:, 2 + gi, :], in_=gt[:, :, 2 + gi]), reads=[r_gtb], writes=[r_gl])
                stop_here("g2")
                P.op("dve", lambda e: e.memset(gl[0:48, :, 0:1], 0.0), writes=[r_gl])
                stop_here("g3")
                P.op("pe", lambda e: e.matmul(out=pgx[:64, 0:65], lhsT=tru[:], rhs=gl[:, 0, :], start=True, stop=True), reads=[r_tru, r_gl], writes=[r_pgx])
                P.op("pe", lambda e: e.matmul(out=pgx[:64, 65:130], lhsT=trl[:], rhs=gl[:, 1, :], start=True, stop=True), reads=[r_trl, r_gl], writes=[r_pgx])
                P.op("pe", lambda e: e.matmul(out=pgx[:, 130:260], lhsT=ones64[:], rhs=gl[:, 0:2, :].rearrange("p a c -> p (a c)"), start=True, stop=True), reads=[r_ones, r_gl], writes=[r_pgx])
                stop_here("g4")
                P.op("act", lambda e: e.activation(out=ebt[:].rearrange("p a c -> p (a c)"), in_=pgx[:64, 0:130], func=AF.Exp), reads=[r_pgx], writes=[r_ebt])
                P.op("act", lambda e: e.activation(out=egt[:].rearrange("p a c -> p (a c)"), in_=pgx[:, 130:260], func=AF.Exp), reads=[r_pgx], writes=[r_egt])
                P.op("dve", lambda e: e.tensor_tensor(out=rft[:].rearrange("p a c -> p (a c)"), in0=gl[:, 2:4, :].rearrange("p a c -> p (a c)"), in1=pgx[:64, 0:130], op=ALU.subtract), reads=[r_pgx, r_gl], writes=[r_rft])
                P.op("act", lambda e: e.activation(out=rft[:].rearrange("p a c -> p (a c)"), in_=rft[:].rearrange("p a c -> p (a c)"), func=AF.Exp), reads=[r_rft], writes=[r_rft])

                stop_here("a3g")
                s3g.close()
                P.fence()
                Zt = [sb(f"Zt{d}", [128, 2, 257], F32, s3) for d in range(2)]; r_Z = [Res(), Res()]
                Zbt = [sb(f"Zbt{d}", [128, 2, 257], BF16, s3) for d in range(2)]; r_Zb = [Res(), Res()]
                ztt = [sb(f"ztt{d}", [128, 2, 257], F32, s3) for d in range(2)]; r_ztmp = [Res(), Res()]
                nebt = sb("nebt", [64, 2, 65], F32, s3); r_nebt = Res()
                P.op("dve", lambda e: e.tensor_scalar(out=nebt[:].rearrange("p a c -> p (a c)"), in0=ebt[:].rearrange("p a c -> p (a c)"), scalar1=-1.0, scalar2=None, op0=ALU.mult), reads=[r_ebt], writes=[r_nebt])
                vres = sb("vres", [64, 65, 257], BF16, s3); r_vres = Res()
                for c5 in range(5):
                    P.dma("act", lambda e, c5=c5: e.dma_start(out=vres[:, 13 * c5:13 * c5 + 13, :], in_=mv_s.rearrange("(c t) v -> t c v", t=64)[:, 13 * c5:13 * c5 + 13, :]), reads=[r_mv_s], writes=[r_vres])
                vtr = Ring([sb(f"vt{i}", [64, 257], BF16, s3) for i in range(4)])
                ptmr = Ring([sb(f"ptm{i}", [64, 64], BF16, s3) for i in range(4)])
                ktokr = Ring([sb(f"ktok{i}", [64, 256], BF16, s3) for i in range(4)])
                dnr = Ring([sb(f"dn{i}", [64, 4], F32, s3) for i in range(4)])
                hring = Ring([sb(f"hch{i}", [64, 256], F32, s3) for i in range(4)])
                with ExitStack() as s3s:
                    p_sr = Ring([ps(f"p_s{i}", [128, 512], F32, s3s) for i in range(2)])
                    p_kr = Ring([ps(f"p_k{i}", [128, 1024], BF16, s3s) for i in range(2)])
                    p_or = Ring([ps(f"p_o{i}", [128, 512], F32, s3s) for i in range(2)])
                    p_cc = ps("p_cc", [128, 1024], F32, s3s); r_p_c = Res()

                    def phase_I(d, c):
                        c0 = 64 * c
                        mask, r_mask = (mku, r_mku) if d == 0 else (mkl, r_mkl)
                        p_s, r_p_s = p_sr.next()
                        p_k, r_p_k = p_kr.next()
                        ptm, r_ptm = ptmr.next()
                        ktok, r_ktok = ktokr.next()
                        vt, r_vt = vtr.next()
                        P.op("pe", [(lambda e, dc=dc, c0=c0, p_s=p_s: e.matmul(out=p_s[:64, 0:64], lhsT=mqkT[:, 2 + dc, c0:c0 + 64], rhs=mqkT[:, dc, c0:c0 + 64], start=(dc == 0), stop=(dc == 1))) for dc in range(2)], reads=[r_mqkT], writes=[r_p_s])
                        P.op("dve", lambda e, mask=mask, p_s=p_s, ptm=ptm: e.tensor_tensor(out=ptm[:], in0=p_s[:64, 0:64], in1=mask[:], op=ALU.mult), reads=[r_p_s, r_mask], writes=[r_ptm])
                        P.op("pe", [(lambda e, dc=dc, c0=c0, p_k=p_k: e.transpose(out=p_k[:64, 128 * dc:128 * dc + 128], in_=mqkT[:, 2 + dc, c0:c0 + 64], identity=idb[:])) for dc in range(2)], reads=[r_mqkT, r_idb], writes=[r_p_k])
                        P.op("act", lambda e, ktok=ktok, p_k=p_k: e.copy(out=ktok[:], in_=p_k[:64, 0:256]), reads=[r_p_k], writes=[r_ktok])
                        P.op("dve", lambda e, vt=vt, d=d, c=c: e.tensor_scalar(out=vt[:], in0=vres[:, c, :], scalar1=rft[:, d, c:c + 1], scalar2=None, op0=ALU.mult), reads=[r_vres, r_rft], writes=[r_vt])
                        return dict(c=c, c0=c0, d=d, ptm=ptm, r_ptm=r_ptm, ktok=ktok, r_ktok=r_ktok, vt=vt, r_vt=r_vt)

                    def phase_D(x):
                        d, c, c0 = x["d"], x["c"], x["c0"]
                        ptm, r_ptm, ktok, r_ktok, vt, r_vt = x["ptm"], x["r_ptm"], x["ktok"], x["r_ktok"], x["vt"], x["r_vt"]
                        p_o, r_p_o = p_or.next()
                        for dc in range(2):
                            P.op("pe", lambda e, dc=dc, ktok=ktok, vt=vt: e.matmul(out=p_cc[:, 512 * dc:512 * dc + 257], lhsT=ktok[:, 128 * dc:128 * dc + 128], rhs=vt[:], start=True, stop=True), reads=[r_ktok, r_vt], writes=[r_p_c])
                        P.op("pe", [lambda e, ptm=ptm, vt=vt, p_o=p_o: e.matmul(out=p_o[:64, 0:257], lhsT=ptm[:], rhs=vt[:], start=True, stop=False)] +
                             [(lambda e, dc=dc, c0=c0, p_o=p_o, d=d: e.matmul(out=p_o[:64, 0:257], lhsT=mqkT[:, dc, c0:c0 + 64], rhs=Zbt[d][:, dc, :], start=False, stop=(dc == 1))) for dc in range(2)],
                             reads=[r_ptm, r_vt, r_mqkT, r_Zb[d]], writes=[r_p_o])
                        P.op("dve", lambda e, d=d: e.scalar_tensor_tensor(out=ztt[d][:], in0=p_cc[:].rearrange("p (a n) -> p a n", a=2)[:, :, 0:257], scalar=1.0 / 16, in1=Zt[d][:], op0=ALU.mult, op1=ALU.add), reads=[r_p_c, r_Z[d]], writes=[r_ztmp[d]])
                        P.op("act", lambda e, d=d, c=c: e.activation(out=Zbt[d][:], in_=ztt[d][:], func=AF.Identity, scale=egt[:, d, c:c + 1]), reads=[r_ztmp[d], r_egt], writes=[r_Zb[d]])
                        P.op("act", lambda e, d=d, c=c: e.activation(out=Zt[d][:], in_=ztt[d][:], func=AF.Identity, scale=egt[:, d, c:c + 1]), reads=[r_ztmp[d], r_egt], writes=[r_Z[d]])
                        hch, r_hch = hring.next()
                        dn, r_dn = dnr.next()
                        P.op("dve", lambda e, d=d, c=c, dn=dn, p_o=p_o: e.tensor_scalar(out=dn[:, 0:1], in0=p_o[:64, 256:257], scalar1=ebt[:, d, c:c + 1], scalar2=1.0, op0=ALU.mult, op1=ALU.max), reads=[r_p_o, r_ebt], writes=[r_dn])
                        P.op("dve", lambda e, d=d, c=c, dn=dn, p_o=p_o: e.scalar_tensor_tensor(out=dn[:, 1:2], in0=p_o[:64, 256:257], scalar=nebt[:, d, c:c + 1], in1=dn[:, 0:1], op0=ALU.mult, op1=ALU.max), reads=[r_p_o, r_nebt, r_dn], writes=[r_dn])
                        P.op("dve", lambda e, dn=dn: e.reciprocal(out=dn[:, 2:3], in_=dn[:, 1:2]), reads=[r_dn], writes=[r_dn])
                        P.op("dve", lambda e, hch=hch, dn=dn, p_o=p_o, d=d, c=c: e.tensor_scalar(out=hch[:], in0=p_o[:64, 0:256], scalar1=dn[:, 2:3], scalar2=ebt[:, d, c:c + 1], op0=ALU.mult, op1=ALU.mult), reads=[r_p_o, r_dn, r_ebt], writes=[r_hch])
                        dst, r_dst = (hf_s, r_hf_s) if d == 0 else (hb_s, r_hb_s)
                        P.dma("sp", lambda e, hch=hch, c0=c0, dst=dst: e.dma_start(out=dst[c0:c0 + 64, :], in_=hch[:]), reads=[r_hch], writes=[r_dst])

                    if "A3" in STAGES:
                        for d in range(2):
                            P.op("dve", lambda e, d=d: e.memset(Zt[d][:], 0.0), writes=[r_Z[d]])
                            P.op("dve", lambda e, d=d: e.memset(Zbt[d][:], 0.0), writes=[r_Zb[d]])
                        nxt = [phase_I(0, 0), phase_I(1, 64)]
                        for i in range(65):
                            cur = nxt
                            if i + 1 < 65:
                                nxt = [phase_I(0, i + 1), phase_I(1, 63 - i)]
                            phase_D(cur[0])
                            phase_D(cur[1])
                P.fence()
                with ExitStack() as s3e:
                    NR = 8
                    hfr = Ring([sb(f"ehf{i}", [128, 256], F32, s3e) for i in range(NR)])
                    hbr = Ring([sb(f"ehb{i}", [128, 256], F32, s3e) for i in range(NR)])
                    mor = Ring([sb(f"emo{i}", [128, 256], F32, s3e) for i in range(NR)])
                    hsr = Ring([sb(f"ehs{i}", [128, 256], F32, s3e) for i in range(NR)])
                    bsr = Ring([sb(f"ebs{i}", [128, 8], F32, s3e) for i in range(NR)])
                    aor = Ring([sb(f"eao{i}", [128, 256], BF16, s3e) for i in range(NR)])
                    aTr = Ring([sb(f"eaT{i}", [128, 2, 128], BF16, s3e) for i in range(NR)])
                    p_tr = Ring([ps(f"ep_t{i}", [128, 1024], BF16, s3e) for i in range(3)])
                    mng128 = sb("mng128", [128, 256], F32, s3e); r_mng128 = Res()
                    P.dma("act", lambda e: e.dma_start(out=mng128[0:64, :], in_=mng[:, :]), writes=[r_mng128])
                    P.dma("act", lambda e: e.dma_start(out=mng128[64:128, :], in_=mng[:, :]), writes=[r_mng128])
                    eblocks = [(128 * j, 128) for j in range(32)] + [(4096, 64)]

                    def st0(j):
                        r0, n = eblocks[j]
                        hf, r_hf = hfr.next(); hb, r_hb = hbr.next(); mo, r_mo = mor.next()
                        P.dma("sp", lambda e, hf=hf, r0=r0, n=n: e.dma_start(out=hf[:n, :], in_=hf_s[r0:r0 + n, :]), reads=[r_hf_s], writes=[r_hf])
                        P.dma("sp", lambda e, hb=hb, r0=r0, n=n: e.dma_start(out=hb[:n, :], in_=hb_s[r0:r0 + n, :]), reads=[r_hb_s], writes=[r_hb])
                        P.dma("act", lambda e, mo=mo, r0=r0, n=n: e.dma_start(out=mo[:n, :], in_=mo_s[r0:r0 + n, :]), reads=[r_mo_s], writes=[r_mo])
                        return dict(r0=r0, n=n, hf=hf, r_hf=r_hf, hb=hb, r_hb=r_hb, mo=mo, r_mo=r_mo)

                    def st1(x):
                        n = x["n"]
                        hs, r_hs = hsr.next(); bs_, r_bs = bsr.next()
                        hf, hb = x["hf"], x["hb"]
                        P.op("dve", lambda e, hf=hf, hb=hb, hs=hs, n=n: e.tensor_tensor(out=hs[:n, :], in0=hf[:n, :], in1=hb[:n, :], op=ALU.add), reads=[x["r_hf"], x["r_hb"]], writes=[r_hs])
                        P.op("dve", lambda e, hs=hs, bs_=bs_, n=n: e.bn_stats(out=bs_[:n, 0:6], in_=hs[:n, :]), reads=[r_hs], writes=[r_bs])
                        P.op("dve", lambda e, bs_=bs_, n=n: e.bn_aggr(out=bs_[:n, 6:8], in_=bs_[:n, 0:6]), reads=[r_bs], writes=[r_bs])
                        P.op("dve", lambda e, bs_=bs_, n=n: e.tensor_scalar(out=bs_[:n, 7:8], in0=bs_[:n, 7:8], scalar1=EPS, scalar2=None, op0=ALU.add), reads=[r_bs], writes=[r_bs])
                        x.update(hs=hs, r_hs=r_hs, bs=bs_, r_bs=r_bs)

                    def st2(x):
                        n, bs_, r_bs = x["n"], x["bs"], x["r_bs"]
                        P.op("act", lambda e, bs_=bs_, n=n: e.activation(out=bs_[:n, 7:8], in_=bs_[:n, 7:8], func=AF.Sqrt), reads=[r_bs], writes=[r_bs])

                    def st3(x):
                        n, bs_, r_bs, hs, r_hs = x["n"], x["bs"], x["r_bs"], x["hs"], x["r_hs"]
                        P.op("dve", lambda e, bs_=bs_, n=n: e.reciprocal(out=bs_[:n, 7:8], in_=bs_[:n, 7:8]), reads=[r_bs], writes=[r_bs])
                        P.op("dve", lambda e, bs_=bs_, hs=hs, n=n: e.tensor_scalar(out=hs[:n, :], in0=hs[:n, :], scalar1=bs_[:n, 6:7], scalar2=bs_[:n, 7:8], op0=ALU.subtract, op1=ALU.mult), reads=[r_hs, r_bs], writes=[r_hs])

                    def st4(x):
                        n, hs, r_hs, mo, r_mo = x["n"], x["hs"], x["r_hs"], x["mo"], x["r_mo"]
                        ao, r_ao = aor.next()
                        P.op("pool", lambda e, hs=hs, n=n: e.tensor_tensor(out=hs[:n, :], in0=hs[:n, :], in1=mng128[:n, :], op=ALU.mult), reads=[r_hs, r_mng128], writes=[r_hs])
                        P.op("pool", lambda e, hs=hs, mo=mo, ao=ao, n=n: e.tensor_tensor(out=ao[:n, :], in0=hs[:n, :], in1=mo[:n, :], op=ALU.mult), reads=[r_hs, r_mo], writes=[r_ao])
                        x.update(ao=ao, r_ao=r_ao)

                    def st5(x):
                        n, ao, r_ao = x["n"], x["ao"], x["r_ao"]
                        p_t, r_p_t = p_tr.next()
                        P.op("pe", [(lambda e, dc=dc, ao=ao, p_t=p_t, n=n: e.transpose(out=p_t[:, 128 * dc:128 * dc + n], in_=ao[:n, 128 * dc:128 * dc + 128], identity=idb[:n, :n])) for dc in range(2)], reads=[r_ao, r_idb], writes=[r_p_t])
                        x.update(p_t=p_t, r_p_t=r_p_t)

                    def st6(x):
                        r0, n, p_t, r_p_t = x["r0"], x["n"], x["p_t"], x["r_p_t"]
                        aT, r_aT = aTr.next()
                        P.op("act", lambda e, aT=aT, p_t=p_t, n=n: e.copy(out=aT[:, :, :n], in_=p_t[:, 0:256].rearrange("p (a t) -> p a t", a=2)[:, :, :n]), reads=[r_p_t], writes=[r_aT])
                        if r0 == 0:
                            P.dma("sp", lambda e, aT=aT: e.dma_start(out=cc_in[0:256, 0:80].rearrange("(a p) t -> p a t", p=128), in_=aT[:, :, 48:128]), reads=[r_aT], writes=[r_cc_in])
                        else:
                            pos0 = r0 - 48
                            P.dma("sp", lambda e, aT=aT, pos0=pos0, n=n: e.dma_start(out=cc_in[0:256, pos0:pos0 + n].rearrange("(a p) t -> p a t", p=128), in_=aT[:, :, :n]), reads=[r_aT], writes=[r_cc_in])

                    if "A3" in STAGES:
                        stages = [st1, st2, st3, st4, st5, st6]
                        xs = {}
                        nb = len(eblocks)
                        for i in range(nb + len(stages) + 1):
                            if i < nb:
                                xs[i] = st0(i)
                            for si, stf in enumerate(stages):
                                jj = i - 1 - si
                                if 0 <= jj < nb:
                                    stf(xs[jj])
                                    if stf is st6 and "X" in STAGES and jj in (8, 16, 24, 32):
                                        exchange_half(0, js=(jj // 8 - 1,))

            P.fence()
            if "X" in STAGES and "A3" not in STAGES:
                exchange_half(0)
            if DEBUG:
                P.stopped = False
                dbg_holder.append(P.dma("pool", lambda e: e.dma_start(out=dbg_cc[:, :], in_=cc_in[:, :]), reads=[r_cc_in, r_cc_in_b]))
                stop_here(STOP_AT)

            if "B" in STAGES:
                TS = [(342 * i, 342) for i in range(3)]
                with ExitStack() as sB:
                    hT = sb("hT", [128, 16, NW], F32, sB); r_hT = Res()
                    uT2 = sb("uT2", [128, 16, NW], BF16, sB); r_uT2 = Res()
                    g2t = sb("g2t", [128, 16], F32, sB); r_g2t = Res()
                    P.dma("act", lambda e: e.dma_start(out=g2t[:], in_=g2c[:, :]), writes=[r_g2t])
                    onesf = sb("onesf", [128, 128], F32, sB); r_onesf = Res()
                    P.op("dve", lambda e: e.memset(onesf[:], 1.0), writes=[r_onesf])
                    with ExitStack() as sB1:
                        abT = sb("abT", [128, 16, NW], BF16, sB1); r_abT = Res()
                        mT = sb("mT", [128, 16, NW], BF16, sB1); r_mT = Res()
                        rank_cache = {}
                        for k in range(16):
                            def load_ab(e, k=k):
                                if "r" not in rank_cache:
                                    rank_cache["r"] = e.partition_id() % 4
                                rank = rank_cache["r"]
                                return e.dma_start(out=abT[:, k:k + 1, :], in_=cc_out.rearrange("(j r) t -> r j t", j=4)[k * 128:(k + 1) * 128, bass.ds(rank, 1), :])
                            P.dma("pool", load_ab, reads=[r_cc_out[0 if k < 8 else 1]], writes=[r_abT])
                        with ExitStack() as sB0:
                            g1t_b = sb("g1t2", [128, D], F32, sB0); r_g1t_b = Res()
                            P.dma("act", lambda e: e.dma_start(out=g1t_b[:], in_=g1b[:, :]), writes=[r_g1t_b])
                            xring_b = Ring([sb(f"xw{i}", [128, D], F32, sB0) for i in range(2)])
                            junk_b = sb("junk2", [128, D], BF16, sB0); r_junk_b = Res()
                            ss_b = sb("ss2", [128, 1], F32, sB0); r_ss_b = Res()
                            u_b = sb("u2", [128, D], BF16, sB0); r_u_b = Res()
                            pT_b = ps("pT2", [128, D], BF16, sB0); r_pT_b = Res()
                            pX = [ps(f"pX{i}", [128, 1024], F32, sB0) for i in range(2)]; r_pX = [Res(), Res()]
                            for j in range(9):
                                bs = 128 if j < 8 else 2
                                xt_b, r_xt_b = xring_b.next()
                                P.dma("sp", lambda e, xt_b=xt_b, bs=bs, j=j: e.dma_start(out=xt_b[:bs, :], in_=xwin[128 * j:128 * j + bs, :]), writes=[r_xt_b])
                                P.op("act", lambda e, xt_b=xt_b, bs=bs: e.activation(out=junk_b[:bs, :], in_=xt_b[:bs, :], func=AF.Square, accum_out=ss_b[:bs, :]), reads=[r_xt_b], writes=[r_junk_b, r_ss_b])
                                P.op("dve", lambda e, bs=bs: e.tensor_scalar(out=ss_b[:bs, :], in0=ss_b[:bs, :], scalar1=1.0 / D, scalar2=EPS, op0=ALU.mult, op1=ALU.add), reads=[r_ss_b], writes=[r_ss_b])
                                P.op("act", lambda e, bs=bs: e.activation(out=ss_b[:bs, :], in_=ss_b[:bs, :], func=AF.Sqrt), reads=[r_ss_b], writes=[r_ss_b])
                                P.op("dve", lambda e, bs=bs: e.reciprocal(out=ss_b[:bs, :], in_=ss_b[:bs, :]), reads=[r_ss_b], writes=[r_ss_b])
                                P.op("dve", lambda e, xt_b=xt_b, bs=bs: e.scalar_tensor_tensor(out=u_b[:bs, :], in0=xt_b[:bs, :], scalar=ss_b[:bs, 0:1], in1=g1t_b[:bs, :], op0=ALU.mult, op1=ALU.mult), reads=[r_xt_b, r_ss_b, r_g1t_b], writes=[r_u_b])
                                P.op("pe", [(lambda e, k=k, bs=bs: e.transpose(out=pT_b[:, k * 128:k * 128 + bs], in_=u_b[:bs, k * 128:(k + 1) * 128], identity=idb[:bs, :bs])) for k in range(16)], reads=[r_u_b, r_idb], writes=[r_pT_b])
                                P.op("act", lambda e, j=j, bs=bs: e.copy(out=uT2[:, :, 128 * j:128 * j + bs], in_=pT_b[:].rearrange("p (k n) -> p k n", k=16)[:, :, :bs]), reads=[r_pT_b], writes=[r_uT2])
                                for hh in range(2):
                                    P.op("pe", [(lambda e, k=k, hh=hh, xt_b=xt_b, bs=bs: e.transpose(out=pX[hh][:, (k % 8) * 128:(k % 8) * 128 + bs], in_=xt_b[:bs, k * 128:(k + 1) * 128], identity=idf[:bs, :bs])) for k in range(8 * hh, 8 * hh + 8)], reads=[r_xt_b, r_idf], writes=[r_pX[hh]])
                                    P.op("dve", lambda e, hh=hh, j=j, bs=bs: e.tensor_copy(out=hT[:, 8 * hh:8 * hh + 8, 128 * j:128 * j + bs], in_=pX[hh][:].rearrange("p (k n) -> p k n", k=8)[:, :, :bs]), reads=[r_pX[hh]], writes=[r_hT])

                        P.fence()
                        with ExitStack() as sB1b:
                            wgr = Ring([sb(f"wg{i}", [128, 16, 256], BF16, sB1b) for i in range(2)])
                            war = Ring([sb(f"wa{i}", [128, 8, 256], BF16, sB1b) for i in range(2)])
                            sgm = sb("sgm", [128, 342], F32, sB1b); r_sgm = Res()
                            sgd = sb("sgd", [128, 342], F32, sB1b); r_sgd = Res()
                            tA = sb("tA", [128, 342], F32, sB1b); r_tA = Res()
                            tB = sb("tB", [128, 342], F32, sB1b); r_tB = Res()
                            pq = [Ring([ps(f"pq{q}{i}", [128, 512], F32, sB1b) for i in range(2)]) for q in range(4)]
                            for c in range(16):
                                wg, r_wg = wgr.next()
                                P.dma("pool", lambda e, wg=wg, c=c: e.dma_start(out=wg[:, :, 0:128], in_=w_g[:, 128 * c:128 * c + 128].rearrange("(k p) n -> p k n", p=128)), writes=[r_wg])
                                for (t0, tn) in TS:
                                    p0_, r0_ = pq[0].next()
                                    P.op("pe", [(lambda e, k=k, p0_=p0_, wg=wg, t0=t0, tn=tn: e.matmul(out=p0_[:, :tn], lhsT=wg[:, k, 0:128], rhs=uT2[:, k, t0:t0 + tn], start=(k == 0), stop=(k == 15))) for k in range(16)], reads=[r_wg, r_uT2], writes=[r0_])
                                    P.op("act", lambda e, p0_=p0_, c=c, t0=t0, tn=tn: e.activation(out=mT[:, c, t0:t0 + tn], in_=p0_[:, :tn], func=AF.Sigmoid), reads=[r0_], writes=[r_mT])
                            for c in range(16):
                                wg, r_wg = wgr.next()
                                wa, r_wa = war.next()
                                P.dma("pool", lambda e, wg=wg, c=c: e.dma_start(out=wg[:, :, 128:256], in_=w_g[:, 2048 + 128 * c:2048 + 128 * c + 128].rearrange("(k p) n -> p k n", p=128)), writes=[r_wg])
                                P.dma("pool", lambda e, wa=wa, c=c: e.dma_start(out=wa[:, :, 0:128], in_=w_a[:, 128 * c:128 * c + 128].rearrange("(k p) n -> p k n", p=128)), writes=[r_wa])
                                P.dma("pool", lambda e, wa=wa, c=c: e.dma_start(out=wa[:, :, 128:256], in_=w_b[:, 128 * c:128 * c + 128].rearrange("(k p) n -> p k n", p=128)), writes=[r_wa])
                                for (t0, tn) in TS:
                                    p1_, r1_ = pq[1].next(); p2_, r2_ = pq[2].next(); p3_, r3_ = pq[3].next()
                                    P.op("pe", [(lambda e, k=k, p2_=p2_, wg=wg, t0=t0, tn=tn: e.matmul(out=p2_[:, :tn], lhsT=wg[:, k, 128:256], rhs=uT2[:, k, t0:t0 + tn], start=(k == 0), stop=(k == 15))) for k in range(16)], reads=[r_wg, r_uT2], writes=[r2_])
                                    P.op("pe", [(lambda e, k=k, p1_=p1_, wa=wa, t0=t0, tn=tn: e.matmul(out=p1_[:, :tn], lhsT=wa[:, k, 0:128], rhs=abT[:, k, t0:t0 + tn], start=(k == 0), stop=(k == 7))) for k in range(8)], reads=[r_wa, r_abT], writes=[r1_])
                                    P.op("pe", [(lambda e, k=k, p3_=p3_, wa=wa, t0=t0, tn=tn: e.matmul(out=p3_[:, :tn], lhsT=wa[:, k, 128:256], rhs=abT[:, 8 + k, t0:t0 + tn], start=(k == 0), stop=(k == 7))) for k in range(8)], reads=[r_wa, r_abT], writes=[r3_])
                                    P.op("act", lambda e, p2_=p2_, tn=tn: e.activation(out=sgd[:, :tn], in_=p2_[:, :tn], func=AF.Sigmoid), reads=[r2_], writes=[r_sgd])
                                    P.op("dve", lambda e, p1_=p1_, c=c, t0=t0, tn=tn: e.tensor_tensor(out=tA[:, :tn], in0=p1_[:, :tn], in1=mT[:, c, t0:t0 + tn], op=ALU.mult), reads=[r1_, r_mT], writes=[r_tA])
                                    P.op("dve", lambda e, p3_=p3_, tn=tn: e.tensor_tensor(out=tB[:, :tn], in0=p3_[:, :tn], in1=sgd[:, :tn], op=ALU.mult), reads=[r3_, r_sgd], writes=[r_tB])
                                    P.op("dve", lambda e, c=c, t0=t0, tn=tn: e.tensor_tensor(out=mT[:, c, t0:t0 + tn], in0=tA[:, :tn], in1=tB[:, :tn], op=ALU.add), reads=[r_tA, r_tB], writes=[r_mT])
                        P.fence()
                        with ExitStack() as sB2:
                            wor = Ring([sb(f"wo{i}", [128, 16, 128], BF16, sB2) for i in range(2)])
                            po = Ring([ps(f"po{i}", [128, 512], F32, sB2) for i in range(4)])
                            for c in range(16):
                                wo, r_wo = wor.next()
                                P.dma("pool", lambda e, wo=wo, c=c: e.dma_start(out=wo[:], in_=w_out[:, 128 * c:128 * c + 128].rearrange("(k p) n -> p k n", p=128)), writes=[r_wo])
                                for (t0, tn) in TS:
                                    pp, rp = po.next()
                                    P.op("pe", [(lambda e, k=k, pp=pp, wo=wo, t0=t0, tn=tn: e.matmul(out=pp[:, :tn], lhsT=wo[:, k, :], rhs=mT[:, k, t0:t0 + tn], start=(k == 0), stop=(k == 15))) for k in range(16)], reads=[r_wo, r_mT], writes=[rp])
                                    P.op("dve", lambda e, pp=pp, c=c, t0=t0, tn=tn: e.tensor_tensor(out=hT[:, c, t0:t0 + tn], in0=hT[:, c, t0:t0 + tn], in1=pp[:, :tn], op=ALU.add), reads=[rp, r_hT], writes=[r_hT])

                    P.fence()
                    with ExitStack() as sB3:
                        sq = Ring([sb(f"sq{i}", [128, 342], F32, sB3) for i in range(2)])
                        rstd = sb("rstd", [128, NW], F32, sB3); r_rstd = Res()
                        wm = sb("wm", [128, NW], F32, sB3); r_wm = Res()
                        P.dma("act", lambda e: e.dma_start(out=wm[:], in_=wmask_d[:, :]), writes=[r_wm])
                        pss3 = [ps(f"pss3{i}", [128, 512], F32, sB3) for i in range(3)]; r_pss3 = [Res() for _ in range(3)]
                        for ti, (t0, tn) in enumerate(TS):
                            fns = []
                            for c in range(16):
                                s_, r_s = sq.next()
                                P.op("act", lambda e, s_=s_, c=c, t0=t0, tn=tn: e.activation(out=s_[:, :tn], in_=hT[:, c, t0:t0 + tn], func=AF.Square), reads=[r_hT], writes=[r_s])
                                P.op("pe", lambda e, s_=s_, c=c, ti=ti, tn=tn: e.matmul(out=pss3[ti][:, :tn], lhsT=onesf[:], rhs=s_[:, :tn], start=(c == 0), stop=(c == 15), skip_group_check=True), reads=[r_s, r_onesf], writes=[r_pss3[ti]])
                            P.op("dve", lambda e, ti=ti, t0=t0, tn=tn: e.tensor_scalar(out=rstd[:, t0:t0 + tn], in0=pss3[ti][:, :tn], scalar1=1.0 / D, scalar2=EPS, op0=ALU.mult, op1=ALU.add), reads=[r_pss3[ti]], writes=[r_rstd])
                        P.op("act", lambda e: e.activation(out=rstd[:], in_=rstd[:], func=AF.Sqrt), reads=[r_rstd], writes=[r_rstd])
                        P.op("dve", lambda e: e.reciprocal(out=rstd[:], in_=rstd[:]), reads=[r_rstd], writes=[r_rstd])
                        P.op("dve", lambda e: e.tensor_tensor(out=rstd[:], in0=rstd[:], in1=wm[:], op=ALU.mult), reads=[r_rstd, r_wm], writes=[r_rstd])
                        for c in range(16):
                            P.op("dve", lambda e, c=c: e.scalar_tensor_tensor(out=uT2[:, c, :], in0=hT[:, c, :], scalar=g2t[:, c:c + 1], in1=rstd[:], op0=ALU.mult, op1=ALU.mult), reads=[r_hT, r_g2t, r_rstd], writes=[r_uT2])

                    P.fence()
                    with ExitStack() as sB4:
                        fcw = sb("fcw", [128, 264], F32, sB4); r_fcw = Res()
                        P.dma("act", lambda e: e.dma_start(out=fcw[:], in_=fconv[:, :]), writes=[r_fcw])
                        actT = sb("actT", [128, 22, 1024], BF16, sB4); r_actT = Res()
                        wur = Ring([sb(f"wu{i}", [128, 16, 256], BF16, sB4) for i in range(2)])
                        wdr = Ring([sb(f"wd{i}", [128, 22, 128], BF16, sB4) for i in range(2)])
                        upg = sb("upg", [128, NW], F32, sB4); r_upg = Res()
                        upv = sb("upv", [128, NW], F32, sB4); r_upv = Res()
                        cg = sb("cg", [128, 1024], F32, sB4); r_cg = Res()
                        cv = sb("cv", [128, 1024], F32, sB4); r_cv = Res()
                        sgl = sb("sgl", [128, 1024], F32, sB4); r_sgl = Res()
                        pu = Ring([ps(f"pu{i}", [128, 512], F32, sB4) for i in range(4)])
                        pd = Ring([ps(f"pd{i}", [128, 512], F32, sB4) for i in range(4)])
                        for half in range(2):
                            for fc in range(22):
                                f = half * 22 + fc
                                wu, r_wu = wur.next()
                                P.dma("pool", lambda e, wu=wu, f=f: e.dma_start(out=wu[:, :, 0:128], in_=w_up[:, 128 * f:128 * f + 128].rearrange("(k p) n -> p k n", p=128)), writes=[r_wu])
                                P.dma("pool", lambda e, wu=wu, f=f: e.dma_start(out=wu[:, :, 128:256], in_=w_up[:, FFN + 128 * f:FFN + 128 * f + 128].rearrange("(k p) n -> p k n", p=128)), writes=[r_wu])
                                for (t0, tn) in TS:
                                    pg_, rg_ = pu.next()
                                    P.op("pe", [(lambda e, k=k, pg_=pg_, wu=wu, t0=t0, tn=tn: e.matmul(out=pg_[:, :tn], lhsT=wu[:, k, 0:128], rhs=uT2[:, k, t0:t0 + tn], start=(k == 0), stop=(k == 15))) for k in range(16)], reads=[r_wu, r_uT2], writes=[rg_])
                                    P.op("act", lambda e, pg_=pg_, t0=t0, tn=tn: e.copy(out=upg[:, t0:t0 + tn], in_=pg_[:, :tn]), reads=[rg_], writes=[r_upg])
                                    pv_, rv_ = pu.next()
                                    P.op("pe", [(lambda e, k=k, pv_=pv_, wu=wu, t0=t0, tn=tn: e.matmul(out=pv_[:, :tn], lhsT=wu[:, k, 128:256], rhs=uT2[:, k, t0:t0 + tn], start=(k == 0), stop=(k == 15))) for k in range(16)], reads=[r_wu, r_uT2], writes=[rv_])
                                    P.op("act", lambda e, pv_=pv_, t0=t0, tn=tn: e.copy(out=upv[:, t0:t0 + tn], in_=pv_[:, :tn]), reads=[rv_], writes=[r_upv])
                                for (src, r_src, dst, r_dst, ci) in ((upg, r_upg, cg, r_cg, f), (upv, r_upv, cv, r_cv, 44 + f)):
                                    P.op("dve", lambda e, src=src, dst=dst, ci=ci: e.tensor_scalar(out=dst[:], in0=src[:, 0:1024], scalar1=fcw[:, 3 * ci:3 * ci + 1], scalar2=None, op0=ALU.mult), reads=[r_src, r_fcw], writes=[r_dst])
                                    P.op("dve", lambda e, src=src, dst=dst, ci=ci: e.scalar_tensor_tensor(out=dst[:], in0=src[:, 1:1025], scalar=fcw[:, 3 * ci + 1:3 * ci + 2], in1=dst[:], op0=ALU.mult, op1=ALU.add), reads=[r_src, r_fcw, r_dst], writes=[r_dst])
                                    P.op("dve", lambda e, src=src, dst=dst, ci=ci: e.scalar_tensor_tensor(out=dst[:], in0=src[:, 2:1026], scalar=fcw[:, 3 * ci + 2:3 * ci + 3], in1=dst[:], op0=ALU.mult, op1=ALU.add), reads=[r_src, r_fcw, r_dst], writes=[r_dst])
                                P.op("act", lambda e: e.activation(out=sgl[:], in_=cg[:], func=AF.Silu), reads=[r_cg], writes=[r_sgl])
                                P.op("dve", lambda e, fc=fc: e.tensor_tensor(out=actT[:, fc, :], in0=sgl[:], in1=cv[:], op=ALU.mult), reads=[r_sgl, r_cv], writes=[r_actT])
                            for c in range(16):
                                wd, r_wd = wdr.next()
                                P.dma("pool", lambda e, wd=wd, c=c, half=half: e.dma_start(out=wd[:], in_=w_down[2816 * half:2816 * half + 2816, 128 * c:128 * c + 128].rearrange("(k p) n -> p k n", p=128)), writes=[r_wd])
                                for t2_ in range(2):
                                    pp, rp = pd.next()
                                    P.op("pe", [(lambda e, k=k, pp=pp, wd=wd, t2_=t2_: e.matmul(out=pp[:, :], lhsT=wd[:, k, :], rhs=actT[:, k, 512 * t2_:512 * t2_ + 512], start=(k == 0), stop=(k == 21))) for k in range(22)], reads=[r_wd, r_actT], writes=[rp])
                                    P.op("dve", lambda e, pp=pp, c=c, t2_=t2_: e.tensor_tensor(out=hT[:, c, 1 + 512 * t2_:1 + 512 * t2_ + 512], in0=hT[:, c, 1 + 512 * t2_:1 + 512 * t2_ + 512], in1=pp[:, :], op=ALU.add), reads=[rp, r_hT], writes=[r_hT])

                    P.fence()
                    with ExitStack() as sB5:
                        gft = sb("gft", [128, D], F32, sB5); r_gft = Res()
                        P.dma("act", lambda e: e.dma_start(out=gft[:], in_=gfb[:, :]), writes=[r_gft])
                        pF = [ps(f"pF{i}", [128, 1024], F32, sB5) for i in range(2)]; r_pF = [Res(), Res()]
                        oring = Ring([sb(f"ob{i}", [128, D], F32, sB5) for i in range(2)])
                        fj = sb("fj", [128, 1024], F32, sB5); r_fj = Res()
                        fs = sb("fs", [128, 4], F32, sB5); r_fs = Res()
                        for j in range(8):
                            for hh in range(2):
                                P.op("pe", [(lambda e, k=k, hh=hh, j=j: e.transpose(out=pF[hh][:, (k % 8) * 128:(k % 8) * 128 + 128], in_=hT[:, k, 1 + 128 * j:1 + 128 * j + 128], identity=idf[:])) for k in range(8 * hh, 8 * hh + 8)], reads=[r_hT, r_idf], writes=[r_pF[hh]])
                                P.op("act", lambda e, hh=hh: e.activation(out=fj[:], in_=pF[hh][:], func=AF.Square, accum_out=fs[:, hh:hh + 1]), reads=[r_pF[hh]], writes=[r_fj, r_fs])
                            P.op("dve", lambda e: e.tensor_tensor(out=fs[:, 2:3], in0=fs[:, 0:1], in1=fs[:, 1:2], op=ALU.add), reads=[r_fs], writes=[r_fs])
                            P.op("dve", lambda e: e.tensor_scalar(out=fs[:, 2:3], in0=fs[:, 2:3], scalar1=1.0 / D, scalar2=EPS, op0=ALU.mult, op1=ALU.add), reads=[r_fs], writes=[r_fs])
                            P.op("act", lambda e: e.activation(out=fs[:, 2:3], in_=fs[:, 2:3], func=AF.Sqrt), reads=[r_fs], writes=[r_fs])
                            P.op("dve", lambda e: e.reciprocal(out=fs[:, 3:4], in_=fs[:, 2:3]), reads=[r_fs], writes=[r_fs])
                            ob, r_ob = oring.next()
                            for hh in range(2):
                                P.op("dve", lambda e, hh=hh, ob=ob: e.scalar_tensor_tensor(out=ob[:, 1024 * hh:1024 * hh + 1024], in0=pF[hh][:], scalar=fs[:, 3:4], in1=gft[:, 1024 * hh:1024 * hh + 1024], op0=ALU.mult, op1=ALU.mult), reads=[r_pF[hh], r_fs, r_gft], writes=[r_ob])
                            final_toks.append(P.dma("sp", lambda e, ob=ob, j=j: e.dma_start(out=out_d[128 * j:128 * j + 128, :], in_=ob[:]), reads=[r_ob]))
        except _Stop:
            pass
        if True:
            if DEBUG and dbg_holder:
                final_toks.append(dbg_holder[0])
            P.finish(final_toks)
    return nc


_NC_CACHE = {}


def _rope_tables():
    inv_freq = (500000.0 ** (-np.arange(0, 16, 2, dtype=np.float32) / 16)).astype(np.float32)
    ang = np.arange(L, dtype=np.float32)[:, None] * inv_freq[None, :]
    cos = np.cos(ang).astype(np.float32).T
    sin = np.sin(ang).astype(np.float32).T
    cosF = np.ones((128, L), np.float32)
    sinF = np.zeros((128, L), np.float32)
    for mp in range(2):
        b0 = 64 * mp
        cosF[b0:b0 + 8] = cos
        cosF[b0 + 8:b0 + 16] = cos
        sinF[b0:b0 + 8] = -sin
        sinF[b0 + 8:b0 + 16] = sin
    return cosF, sinF


def kernel(x, meta_tokens, norm1_g, w_in, mlstm_conv_w, mlstm_gate_bias, mlstm_norm_g,
           lambda_q1, lambda_k1, lambda_q2, lambda_k2, diff_subln_g, w_branch_m, w_branch_d,
           w_out, norm2_g, w_up, ffn_conv_w, w_down, norm_f_g):
    f32 = np.float32
    x = np.asarray(x, f32)
    w_in0 = np.asarray(w_in, f32)[0]
    B = x.shape[0]
    cosF, sinF = _rope_tables()
    o_mqk, o_mv, o_mo, o_gates, o_dq, o_dk, o_dv, o_gm = 0, 2048, 3072, 4096, 4112, 5136, 6160, 7184
    rotperm = np.arange(128)
    for mp in range(2):
        b0 = 64 * mp
        rotperm[b0:b0 + 8] = np.arange(b0 + 8, b0 + 16)
        rotperm[b0 + 8:b0 + 16] = np.arange(b0, b0 + 8)
    ident = np.eye(128, dtype=f32)
    triu = np.triu(np.ones((64, 64), f32))
    tril = np.tril(np.ones((64, 64), f32))
    common = {
        "w_g": np.ascontiguousarray(w_in0[:, o_gm:o_gm + 4096]),
        "w_a": np.ascontiguousarray(np.asarray(w_branch_m, f32)[0]),
        "w_b": np.ascontiguousarray(np.asarray(w_branch_d, f32)[0]),
        "w_out": np.ascontiguousarray(np.asarray(w_out, f32)[0]),
        "w_up": np.ascontiguousarray(np.asarray(w_up, f32)[0]),
        "w_down": np.ascontiguousarray(np.asarray(w_down, f32)[0]),
        "g1b": np.ascontiguousarray(np.broadcast_to(np.asarray(norm1_g, f32)[0], (128, D))),
        "gfb": np.ascontiguousarray(np.broadcast_to(np.asarray(norm_f_g, f32), (128, D))),
        "g2c": np.ascontiguousarray(np.asarray(norm2_g, f32)[0].reshape(16, 128).T),
        "cosf": cosF, "sinf": sinF,
        "fconv": np.ascontiguousarray(np.asarray(ffn_conv_w, f32)[0].reshape(3, 88, 128).transpose(2, 1, 0).reshape(128, 264)),
        "lamv": np.ascontiguousarray(np.broadcast_to(np.concatenate([np.asarray(a, f32)[0] for a in (lambda_q1, lambda_k1, lambda_q2, lambda_k2)]), (128, 256))),
        "sublng": np.ascontiguousarray(np.broadcast_to(np.asarray(diff_subln_g, f32)[0], (128, 128))),
        "ident": ident, "triu": triu, "tril": tril,
    }
    if "B" not in STAGES:
        for nm in ("w_g", "w_a", "w_b", "w_out", "w_up", "w_down"):
            common[nm] = np.zeros((128, 128), f32)
    in_maps = []
    mcw_full = np.asarray(mlstm_conv_w, f32)[0]
    gb_full = np.asarray(mlstm_gate_bias, f32)[0]
    for c in range(8):
        b, g = c // 4, c % 4
        hfull = np.concatenate([np.asarray(meta_tokens, f32), x[b]], axis=0)
        s0 = 15 + 1024 * g
        xwin = np.zeros((NW, D), f32)
        e0 = min(s0 + NW, L)
        xwin[:e0 - s0] = hfull[s0:e0]
        wmask = np.ones((128, NW), f32)
        if e0 - s0 < NW:
            wmask[:, e0 - s0:] = 0.0
        cols = []
        for base in (o_dq, o_dk):
            for hh in range(2):
                head = 2 * g + hh
                cols.append(base + 128 * head + np.arange(128))
            for hh in range(2):
                head = 2 * g + hh
                cols.append(base + 128 * head + rotperm)
        for hh in range(2):
            head = 2 * g + hh
            cols.append(o_dv + 128 * head + np.arange(128))
        w_attn = np.ascontiguousarray(w_in0[:, np.concatenate(cols)])
        qc = o_mqk + 256 * g + np.arange(256)
        kc = o_mqk + 1024 + 256 * g + np.arange(256)
        vc = o_mv + 256 * g + np.arange(256)
        oc = o_mo + 256 * g + np.arange(256)
        gc = o_gates + np.array([4 + g, 12 + g, 0 + g, 8 + g])
        w_ml = np.ascontiguousarray(w_in0[:, np.concatenate([qc, kc, vc, oc, gc])])
        mconv = np.ascontiguousarray(mcw_full[:, np.concatenate([qc, kc])].reshape(3, 4, 128).transpose(2, 1, 0).reshape(128, 12))
        gbias = np.ascontiguousarray(np.broadcast_to(gb_full[[4 + g, 12 + g, 0 + g, 8 + g]], (128, 4)))
        mngv = np.ascontiguousarray(np.broadcast_to(np.asarray(mlstm_norm_g, f32)[0][256 * g:256 * g + 256], (64, 256)))
        m = dict(common)
        m.update({"hfull": hfull, "xwin": xwin, "w_attn": w_attn, "w_ml": w_ml, "mconv": mconv,
                  "gbias": gbias, "mng": mngv, "wmask": wmask})
        in_maps.append(m)
    if "nc" not in _NC_CACHE:
        _NC_CACHE["nc"] = build_program()
    nc = _NC_CACHE["nc"]
    res = run_bass_kernel_spmd(nc, in_maps, core_ids=list(range(8)))
    out = np.empty((B, 4096, D), f32)
    for c in range(8):
        b, g = c // 4, c % 4
        out[b, 1024 * g:1024 * g + 1024] = res.results[c]["out"]
    if DEBUG:
        kernel.dbg = [res.results[c] for c in range(8)]
    return out
```

```python
import numpy as np
from contextlib import ExitStack
import concourse.bass as bass
import concourse.mybir as mybir
from concourse.bass_utils import run_bass_kernel_spmd

F32 = mybir.dt.float32
BF16 = mybir.dt.bfloat16
AF = mybir.ActivationFunctionType
ALU = mybir.AluOpType
AX = mybir.AxisListType

SAME_ENGINE_SYNC = True
DEBUG = False
STAGES = ("A1", "A2", "A3", "X", "B")

D = 2048
L = 4112
LP = 4160
NMETA = 16
NW = 1026
FFN = 5632
EPS = 1e-6


class _Stop(Exception):
    pass


STOP_AT = None


_PROG = []


def stop_here(tag):
    if STOP_AT == tag:
        _PROG[0].stopped = True


_FENCE = []


class Res:
    __slots__ = ("name", "w", "r")

    def __init__(self, name=""):
        self.name = name
        self.w = None
        self.r = dict(_FENCE)


class Prog:
    ENGS = ("pe", "act", "dve", "pool", "sp")

    def __init__(self, nc, stack, n_dma_sems=8):
        self.nc = nc
        self.stack = stack
        self.streams = {e: [] for e in self.ENGS}
        self.sems = {}
        self.count = {}
        self.waited = {e: {} for e in self.ENGS}
        for e in self.ENGS:
            self.sems["c_" + e] = stack.enter_context(nc.semaphore("c_" + e))
            self.count["c_" + e] = 0
        self.dma_pool = {}
        self.dma_next = {}
        for e in ("sp", "act", "pool"):
            keys = []
            for i in range(n_dma_sems):
                k = f"d_{e}{i}"
                self.sems[k] = stack.enter_context(nc.semaphore(k))
                self.count[k] = 0
                keys.append(k)
            self.dma_pool[e] = keys
            self.dma_next[e] = 0
        self.sems["cc"] = stack.enter_context(nc.semaphore("cc"))
        self.count["cc"] = 0
        self.stopped = False
        _PROG[:] = [self]
        _FENCE[:] = []

    def _need(self, eng, dep):
        if dep is None:
            return
        key, val = dep
        if key == "c_" + eng and (eng == "pe" or not SAME_ENGINE_SYNC):
            return
        if self.waited[eng].get(key, 0) >= val:
            return
        self.waited[eng][key] = val
        self.streams[eng].append(("wait", key, val))

    def _deps(self, eng, reads, writes):
        own = "c_" + eng
        for r in reads:
            self._need(eng, r.w)
            for k, v in r.r.items():
                if k != own:
                    self._need(eng, (k, v))
        for w in writes:
            self._need(eng, w.w)
            for k, v in w.r.items():
                self._need(eng, (k, v))

    def _commit(self, reads, writes, tok):
        for r in reads:
            if r.r.get(tok[0], 0) < tok[1]:
                r.r[tok[0]] = tok[1]
        for w in writes:
            w.w = tok
            w.r = {}

    def op(self, eng, fns, reads=(), writes=()):
        if self.stopped:
            return None
        if not isinstance(fns, (list, tuple)):
            fns = [fns]
        self._deps(eng, reads, writes)
        key = "c_" + eng
        self.count[key] += 1
        tok = (key, self.count[key])
        self.streams[eng].append(("op", fns, key, 1))
        self._commit(reads, writes, tok)
        return tok

    def dma(self, eng, fn, reads=(), writes=()):
        if self.stopped:
            return None
        pool = self.dma_pool[eng]
        key = pool[self.dma_next[eng] % len(pool)]
        self.dma_next[eng] += 1
        if self.count[key] > 0:
            self._need(eng, (key, self.count[key]))
        self._deps(eng, reads, writes)
        self.count[key] += 16
        tok = (key, self.count[key])
        self.streams[eng].append(("op", [fn], key, 16))
        self._commit(reads, writes, tok)
        return tok

    def fence(self):
        _FENCE[:] = [(k, c) for k, c in self.count.items() if c > 0 and k != "cc"]

    def cc(self, fn, reads=(), writes=()):
        if self.stopped:
            return None
        eng = "pool"
        self._deps(eng, reads, writes)
        self.count["cc"] += 1
        tok = ("cc", self.count["cc"])
        self.streams[eng].append(("cc", fn, "cc"))
        self._commit(reads, writes, tok)
        return tok

    def finish(self, final_tokens):
        for t in final_tokens:
            self._need("sp", t)
        for k, c in self.count.items():
            if c > 0:
                self._need("sp", (k, c))
        nc = self.nc
        with nc.Block() as block:
            def mk(ename):
                def body(e):
                    for item in self.streams[ename]:
                        if item[0] == "wait":
                            e.wait_ge(self.sems[item[1]], item[2])
                        elif item[0] == "op":
                            fns, key, inc = item[1], item[2], item[3]
                            for f in fns[:-1]:
                                f(e)
                            fns[-1](e).then_inc(self.sems[key], inc)
                        elif item[0] == "cc":
                            item[1](e).then_inc(self.sems[item[2]])
                return body
            block.tensor(mk("pe"))
            block.scalar(mk("act"))
            block.vector(mk("dve"))
            block.gpsimd(mk("pool"))
            block.sync(mk("sp"))


class Ring:
    def __init__(self, tiles):
        self.tiles = tiles
        self.res = [Res() for _ in tiles]
        self.i = 0

    def next(self):
        k = self.i % len(self.tiles)
        self.i += 1
        return self.tiles[k], self.res[k]


def build_program():
    nc = bass.Bass("TRN2", target_bir_lowering=False)
    dt_in = lambda name, shape, dt=F32: nc.dram_tensor(name, shape, dt, kind="ExternalInput").ap()
    dt_int = lambda name, shape, dt: nc.dram_tensor(name, shape, dt, kind="Internal").ap()

    hfull = dt_in("hfull", [L, D])
    xwin = dt_in("xwin", [NW, D])
    w_attn = dt_in("w_attn", [D, 1280])
    w_ml = dt_in("w_ml", [D, 1028])
    w_g = dt_in("w_g", [D, 4096] if "B" in STAGES else [128, 128])
    w_a = dt_in("w_a", [1024, D] if "B" in STAGES else [128, 128])
    w_b = dt_in("w_b", [1024, D] if "B" in STAGES else [128, 128])
    w_out = dt_in("w_out", [D, D] if "B" in STAGES else [128, 128])
    w_up = dt_in("w_up", [D, 2 * FFN] if "B" in STAGES else [128, 128])
    w_down = dt_in("w_down", [FFN, D] if "B" in STAGES else [128, 128])
    g1b = dt_in("g1b", [128, D])
    gfb = dt_in("gfb", [128, D])
    g2c = dt_in("g2c", [128, 16])
    cosf = dt_in("cosf", [128, L])
    sinf = dt_in("sinf", [128, L])
    mconv = dt_in("mconv", [128, 12])
    fconv = dt_in("fconv", [128, 88 * 3])
    gbias = dt_in("gbias", [128, 4])
    mng = dt_in("mng", [64, 256])
    lamv = dt_in("lamv", [128, 4 * 64])
    sublng = dt_in("sublng", [128, 128])
    ident_d = dt_in("ident", [128, 128])
    triu_d = dt_in("triu", [64, 64])
    tril_d = dt_in("tril", [64, 64])
    wmask_d = dt_in("wmask", [128, NW])
    out_d = nc.dram_tensor("out", [1024, D], F32, kind="ExternalOutput").ap()
    if DEBUG:
        dbg_cc = nc.dram_tensor("dbg_cc", [512, 4114], BF16, kind="ExternalOutput").ap()

    mqk_raw = dt_int("mqk_raw", [512, LP + 2], BF16)
    mv_s = dt_int("mv_s", [LP, 257], BF16)
    mo_s = dt_int("mo_s", [LP, 256], F32)
    gates_s = dt_int("gates_s", [LP, 4], F32)
    hf_s = dt_int("hf_s", [LP, 256], F32)
    hb_s = dt_int("hb_s", [LP, 256], F32)
    cc_in = dt_int("cc_in", [512, 4114], BF16)
    cc_win = dt_int("cc_win", [8 * 256, NW], BF16)
    cc_out = dt_int("cc_out", [8 * 1024, NW], BF16)

    with ExitStack() as st:
        P = Prog(nc, st)

        def sb(name, shape, dt, stack=st):
            return stack.enter_context(nc.sbuf_tensor(name, shape, dt))

        def ps(name, shape, dt, stack=st):
            return stack.enter_context(nc.psum_tensor(name, shape, dt))

        final_toks = []
        dbg_holder = []
        try:
            idf = sb("idf", [128, 128], F32); r_idf = Res()
            idb = sb("idb", [128, 128], BF16); r_idb = Res()
            zt = sb("zt", [128, 512], BF16); r_zt = Res()
            P.dma("sp", lambda e: e.dma_start(out=idf[:], in_=ident_d[:, :]), writes=[r_idf])
            P.op("dve", lambda e: e.tensor_copy(out=idb[:], in_=idf[:]), reads=[r_idf], writes=[r_idb])
            P.op("dve", lambda e: e.memset(zt[:], 0.0), writes=[r_zt])

            r_mqk_raw = Res(); r_mv_s = Res(); r_mo_s = Res(); r_gates_s = Res(); r_hf_s = Res(); r_hb_s = Res(); r_cc_in = Res(); r_cc_in_b = Res()
            r_cc_win = [Res(), Res()]; r_cc_out = [Res(), Res()]
            P.dma("sp", lambda e: e.dma_start(out=mqk_raw.rearrange("(c p) n -> p c n", p=128)[:, :, 0:49], in_=zt[:, 0:196].rearrange("p (c n) -> p c n", c=4)), reads=[r_zt], writes=[r_mqk_raw])
            P.dma("sp", lambda e: e.dma_start(out=mqk_raw.rearrange("(c p) n -> p c n", p=128)[:, :, LP + 1:LP + 2], in_=zt[:, 0:4].rearrange("p (c n) -> p c n", c=4), allow_slow_non_contiguous=True), reads=[r_zt], writes=[r_mqk_raw])
            P.dma("sp", lambda e: e.dma_start(out=mv_s[0:48, :], in_=zt[0:48, 0:257]), reads=[r_zt], writes=[r_mv_s])
            ztf = sb("ztf", [64, 256], F32); r_ztf = Res()
            P.op("dve", lambda e: e.memset(ztf[:], 0.0), writes=[r_ztf])
            P.dma("sp", lambda e: e.dma_start(out=mo_s[0:48, :], in_=ztf[0:48, :]), reads=[r_ztf], writes=[r_mo_s])
            P.dma("sp", lambda e: e.dma_start(out=gates_s[0:48, :], in_=ztf[0:48, 0:4]), reads=[r_ztf], writes=[r_gates_s])
            P.dma("sp", lambda e: e.dma_start(out=cc_in[:, 4112:4114].rearrange("(c p) n -> p c n", p=128), in_=zt[:, 0:8].rearrange("p (c n) -> p c n", c=4)), reads=[r_zt], writes=[r_cc_in, r_cc_in_b])


            def exchange_half(half, js=(0, 1, 2, 3)):
                r_src = r_cc_in if half == 0 else r_cc_in_b
                for j in js:
                    i = j * 2 + half
                    P.dma("sp", lambda e, i=i, j=j, half=half: e.dma_start(out=cc_win[i * 256:(i + 1) * 256, :], in_=cc_in[half * 256:(half + 1) * 256, 15 + 1024 * j:15 + 1024 * j + NW]), reads=[r_src], writes=[r_cc_win[half]])
                for j in js:
                    i = j * 2 + half
                    P.cc(lambda e, i=i: e.collective_compute("AllGather", ALU.bypass, replica_groups=[[0, 1, 2, 3], [4, 5, 6, 7]], ins=[cc_win[i * 256:(i + 1) * 256, :]], outs=[cc_out[i * 1024:(i + 1) * 1024, :]]), reads=[r_cc_win[half]], writes=[r_cc_out[half]])

            stop_here("c0")
            with ExitStack() as sa:
                dqkT = sb("dqkT", [128, 4, L], BF16, sa); r_dqkT = Res()
                vatt = sb("vatt", [128, 33, 2, 129], BF16, sa); r_vatt = Res()
                with ExitStack() as s1:
                    wat = sb("wat", [128, 16, 1280], BF16, s1); r_wat = Res()
                    wml = sb("wml", [128, 16, 1028], BF16, s1); r_wml = Res()
                    g1t = sb("g1t", [128, D], F32, s1); r_g1t = Res()
                    gbt = sb("gbt", [128, 4], F32, s1); r_gbt = Res()
                    for kq in range(4):
                        P.dma("pool", lambda e, kq=kq: e.dma_start(out=wat[:, 4 * kq:4 * kq + 4, :], in_=w_attn[512 * kq:512 * kq + 512, :].rearrange("(k p) n -> p k n", p=128)), writes=[r_wat])
                        P.dma("pool", lambda e, kq=kq: e.dma_start(out=wml[:, 4 * kq:4 * kq + 4, :], in_=w_ml[512 * kq:512 * kq + 512, :].rearrange("(k p) n -> p k n", p=128)), writes=[r_wml])
                    P.dma("act", lambda e: e.dma_start(out=g1t[:], in_=g1b[:, :]), writes=[r_g1t])
                    P.dma("act", lambda e: e.dma_start(out=gbt[:], in_=gbias[:, :]), writes=[r_gbt])
                    P.op("dve", lambda e: e.memset(vatt[:, :, :, 128:129], 1.0), writes=[r_vatt])

                    xring = Ring([sb(f"xt{i}", [128, D], F32, s1) for i in range(3)])
                    ssr = Ring([sb(f"ss{i}", [128, 1], F32, s1) for i in range(2)])
                    ur = Ring([sb(f"u{i}", [128, D], BF16, s1) for i in range(2)])
                    uT = sb("uT", [128, 16, 512], BF16, s1); r_uT = Res()
                    csr = Ring([sb(f"cs{i}", [128, 2, 512], F32, s1) for i in range(2)])
                    rt1 = sb("rt1", [128, 512], F32, s1); r_rt1 = Res()
                    rt2 = sb("rt2", [128, 512], F32, s1); r_rt2 = Res()
                    mstage = Ring([sb(f"mst{i}", [128, 4, 512], BF16, s1) for i in range(2)])
                    mvst = Ring([sb(f"mvst{i}", [128, 257], BF16, s1) for i in range(2)])
                    most = Ring([sb(f"most{i}", [128, 256], F32, s1) for i in range(2)])
                    gst = Ring([sb(f"gst{i}", [128, 4], F32, s1) for i in range(2)])
                    pT = ps("pT", [128, D], BF16, s1); r_pT = Res()
                    pa = ps("pa", [128, 512], F32, s1); r_pa = Res()
                    pb = ps("pb", [128, 512], F32, s1); r_pb = Res()
                    pm = ps("pm", [128, 512], F32, s1); r_pm = Res()
                    pv = ps("pv", [128, 512], F32, s1); r_pv = Res()
                    pvo = ps("pvo", [128, 512], F32, s1); r_pvo = Res()
                    pg = ps("pg", [128, 512], F32, s1); r_pg = Res()
                    for rr, rres in zip(mvst.tiles, mvst.res):
                        P.op("dve", lambda e, rr=rr: e.memset(rr[:, 256:257], 1.0), writes=[rres])

                    def norm_block(xt, r_xt, bs, gtile, r_g):
                        ss, r_ss = ssr.next()
                        u, r_u = ur.next()
                        P.op("act", lambda e: e.activation(out=u[:bs, :], in_=xt[:bs, :], func=AF.Square, accum_out=ss[:bs, :]), reads=[r_xt], writes=[r_u, r_ss])
                        P.op("dve", lambda e: e.tensor_scalar(out=ss[:bs, :], in0=ss[:bs, :], scalar1=1.0 / D, scalar2=EPS, op0=ALU.mult, op1=ALU.add), reads=[r_ss], writes=[r_ss])
                        P.op("act", lambda e: e.activation(out=ss[:bs, :], in_=ss[:bs, :], func=AF.Sqrt), reads=[r_ss], writes=[r_ss])
                        P.op("dve", lambda e: e.reciprocal(out=ss[:bs, :], in_=ss[:bs, :]), reads=[r_ss], writes=[r_ss])
                        P.op("dve", lambda e: e.scalar_tensor_tensor(out=u[:bs, :], in0=xt[:bs, :], scalar=ss[:bs, 0:1], in1=gtile[:bs, :], op0=ALU.mult, op1=ALU.mult), reads=[r_xt, r_ss, r_g], writes=[r_u])
                        return u, r_u

                    stop_here("a1w")
                    tiles = [(512 * i, 512) for i in range(8)] + [(4096, 16)]
                    for (p0, n) in (tiles if "A1" in STAGES else []):
                        nblk = (n + 127) // 128
                        cs, r_cs = csr.next()
                        P.dma("act", lambda e, cs=cs, p0=p0, n=n: e.dma_start(out=cs[:, 0, :n], in_=cosf[:, p0:p0 + n]), writes=[r_cs])
                        P.dma("act", lambda e, cs=cs, p0=p0, n=n: e.dma_start(out=cs[:, 1, :n], in_=sinf[:, p0:p0 + n]), writes=[r_cs])
                        for j in range(nblk):
                            bs = min(128, n - 128 * j)
                            xt, r_xt = xring.next()
                            P.dma("pool", lambda e, xt=xt, bs=bs, r0=p0 + 128 * j: e.dma_start(out=xt[:bs, :], in_=hfull[r0:r0 + bs, :]), writes=[r_xt])
                            u, r_u = norm_block(xt, r_xt, bs, g1t, r_g1t)
                            P.op("pe", [(lambda e, k=k, bs=bs, u=u: e.transpose(out=pT[:, k * 128:k * 128 + bs], in_=u[:bs, k * 128:(k + 1) * 128], identity=idb[:bs, :bs])) for k in range(16)], reads=[r_u, r_idb], writes=[r_pT])
                            P.op("act", lambda e, j=j, bs=bs: e.copy(out=uT[:, :, 128 * j:128 * j + bs], in_=pT[:].rearrange("p (k n) -> p k n", k=16)[:, :, :bs]), reads=[r_pT], writes=[r_uT])
                        stop_here("blk%d" % (p0 // 512))
                        for c in range(4):
                            cm = (c % 2) * 128 + (c // 2) * 512
                            cr = cm + 256
                            P.op("pe", [(lambda e, k=k, cm=cm, n=n: e.matmul(out=pa[:, :n], lhsT=wat[:, k, cm:cm + 128], rhs=uT[:, k, :n], start=(k == 0), stop=(k == 15))) for k in range(16)], reads=[r_wat, r_uT], writes=[r_pa])
                            P.op("pe", [(lambda e, k=k, cr=cr, n=n: e.matmul(out=pb[:, :n], lhsT=wat[:, k, cr:cr + 128], rhs=uT[:, k, :n], start=(k == 0), stop=(k == 15))) for k in range(16)], reads=[r_wat, r_uT], writes=[r_pb])
                            P.op("dve", lambda e, cs=cs, n=n: e.tensor_tensor(out=rt1[:, :n], in0=pa[:, :n], in1=cs[:, 0, :n], op=ALU.mult), reads=[r_pa, r_cs], writes=[r_rt1])
                            P.op("dve", lambda e, cs=cs, n=n: e.tensor_tensor(out=rt2[:, :n], in0=pb[:, :n], in1=cs[:, 1, :n], op=ALU.mult), reads=[r_pb, r_cs], writes=[r_rt2])
                            P.op("dve", lambda e, c=c, p0=p0, n=n: e.tensor_tensor(out=dqkT[:, c, p0:p0 + n], in0=rt1[:, :n], in1=rt2[:, :n], op=ALU.add), reads=[r_rt1, r_rt2], writes=[r_dqkT])
                        stop_here("fm%d" % (p0 // 512))
                        mst, r_mst = mstage.next()
                        for c in range(4):
                            P.op("pe", [(lambda e, k=k, c=c, n=n: e.matmul(out=pm[:, :n], lhsT=wml[:, k, c * 128:(c + 1) * 128], rhs=uT[:, k, :n], start=(k == 0), stop=(k == 15))) for k in range(16)], reads=[r_wml, r_uT], writes=[r_pm])
                            P.op("act", lambda e, mst=mst, c=c, n=n: e.copy(out=mst[:, c, :n], in_=pm[:, :n]), reads=[r_pm], writes=[r_mst])
                        P.dma("sp", lambda e, mst=mst, p0=p0, n=n: e.dma_start(out=mqk_raw.rearrange("(c p) n -> p c n", p=128)[:, :, 49 + p0:49 + p0 + n], in_=mst[:, :, :n]), reads=[r_mst], writes=[r_mqk_raw])
                        stop_here("ml%d" % (p0 // 512))
                        for j in range(nblk):
                            bs = min(128, n - 128 * j)
                            kb = (p0 + 128 * j) // 128
                            r0 = 48 + p0 + 128 * j
                            P.op("pe", [(lambda e, k=k, j=j, bs=bs: e.matmul(out=pv[:bs, 0:256], lhsT=uT[:, k, 128 * j:128 * j + bs], rhs=wat[:, k, 1024:1280], start=(k == 0), stop=(k == 15))) for k in range(16)], reads=[r_wat, r_uT], writes=[r_pv])
                            P.op("dve", lambda e, kb=kb, bs=bs: e.tensor_copy(out=vatt[:bs, kb, :, 0:128], in_=pv[:bs, 0:256].rearrange("p (h d) -> p h d", h=2)), reads=[r_pv], writes=[r_vatt])
                            stop_here("tv%d_%d" % (p0 // 512, j))
                            P.op("pe", [(lambda e, k=k, j=j, bs=bs: e.matmul(out=pvo[:bs, :], lhsT=uT[:, k, 128 * j:128 * j + bs], rhs=wml[:, k, 512:1024], start=(k == 0), stop=(k == 15))) for k in range(16)], reads=[r_wml, r_uT], writes=[r_pvo])
                            mvt, r_mvt = mvst.next()
                            mot, r_mot = most.next()
                            P.op("act", lambda e, mvt=mvt, bs=bs: e.copy(out=mvt[:bs, 0:256], in_=pvo[:bs, 0:256]), reads=[r_pvo], writes=[r_mvt])
                            P.op("act", lambda e, mot=mot, bs=bs: e.activation(out=mot[:bs, :], in_=pvo[:bs, 256:512], func=AF.Sigmoid), reads=[r_pvo], writes=[r_mot])
                            P.dma("sp", lambda e, mvt=mvt, bs=bs, r0=r0: e.dma_start(out=mv_s[r0:r0 + bs, :], in_=mvt[:bs, :]), reads=[r_mvt], writes=[r_mv_s])
                            P.dma("sp", lambda e, mot=mot, bs=bs, r0=r0: e.dma_start(out=mo_s[r0:r0 + bs, :], in_=mot[:bs, :]), reads=[r_mot], writes=[r_mo_s])
                            stop_here("tm%d_%d" % (p0 // 512, j))
                            P.op("pe", [(lambda e, k=k, j=j, bs=bs: e.matmul(out=pg[:bs, 0:64], lhsT=uT[:, k, 128 * j:128 * j + bs], rhs=wml[:, k, 964:1028], start=(k == 0), stop=(k == 15))) for k in range(16)], reads=[r_wml, r_uT], writes=[r_pg])
                            stop_here("tgm%d_%d" % (p0 // 512, j))
                            gt_, r_gt = gst.next()
                            P.op("dve", lambda e, gt_=gt_, bs=bs: e.tensor_tensor(out=gt_[:bs, :], in0=pg[:bs, 60:64], in1=gbt[:bs, :], op=ALU.add), reads=[r_pg, r_gbt], writes=[r_gt])
                            stop_here("tga%d_%d" % (p0 // 512, j))
                            P.dma("sp", lambda e, gt_=gt_, bs=bs, r0=r0: e.dma_start(out=gates_s[r0:r0 + bs, :], in_=gt_[:bs, :]), reads=[r_gt], writes=[r_gates_s])
                    stop_here("a1t%d" % (p0 // 512))

                P.fence()
                stop_here("a1")
                with ExitStack() as s2:
                    lam_t = sb("lam_t", [128, 256], F32, s2); r_lam = Res()
                    lamw = sb("lamw", [128, 8], F32, s2); r_lamw = Res()
                    sg_t = sb("sg_t", [128, 128], F32, s2); r_sg = Res()
                    P.dma("act", lambda e: e.dma_start(out=lam_t[:], in_=lamv[:, :]), writes=[r_lam])
                    P.dma("act", lambda e: e.dma_start(out=sg_t[:], in_=sublng[:, :]), writes=[r_sg])
                    ljunk = sb("ljunk", [128, 64], F32, s2); r_lj = Res()
                    for i in range(2):
                        P.op("dve", lambda e, i=i: e.tensor_tensor(out=ljunk[:], in0=lam_t[:, 128 * i:128 * i + 64], in1=lam_t[:, 128 * i + 64:128 * i + 128], op=ALU.mult), reads=[r_lam], writes=[r_lj])
                        P.op("dve", lambda e, i=i: e.reduce_sum(out=lamw[:, i:i + 1], in_=ljunk[:], axis=AX.X), reads=[r_lj], writes=[r_lamw])
                    P.op("act", lambda e: e.activation(out=lamw[:, 2:4], in_=lamw[:, 0:2], func=AF.Exp), reads=[r_lamw], writes=[r_lamw])
                    P.op("dve", lambda e: e.tensor_tensor(out=lamw[:, 4:5], in0=lamw[:, 2:3], in1=lamw[:, 3:4], op=ALU.subtract), reads=[r_lamw], writes=[r_lamw])
                    P.op("dve", lambda e: e.tensor_scalar(out=lamw[:, 5:6], in0=lamw[:, 4:5], scalar1=0.2, scalar2=-1.0, op0=ALU.add, op1=ALU.mult), reads=[r_lamw], writes=[r_lamw])
                    P.op("dve", lambda e: e.tensor_scalar(out=sg_t[:], in0=sg_t[:], scalar1=0.8, scalar2=None, op0=ALU.mult), reads=[r_sg], writes=[r_sg])

                    stop_here("a2s")
                    pss = [Ring([ps(f"ps{m}{i}", [128, 512], F32, s2) for i in range(2)]) for m in range(2)]
                    pacc = [ps(f"pacc{i}", [128, 512], F32, s2) for i in range(3)]
                    r_pacc = Res()
                    ptr = ps("ptr", [128, 512], BF16, s2); r_ptr = Res()
                    ptile = [Ring([sb(f"pt{m}{i}", [128, 512], BF16, s2) for i in range(3)]) for m in range(2)]
                    rc = sb("rc", [128, 2], F32, s2); r_rc = Res()
                    t2 = sb("t2", [128, 128], F32, s2); r_t2 = Res()
                    ot = sb("ot", [128, 128], F32, s2); r_ot = Res()
                    oj = sb("oj", [128, 128], F32, s2); r_oj = Res()
                    os_ = sb("os_", [128, 1], F32, s2); r_os = Res()
                    bo = sb("bo", [128, 128], BF16, s2); r_bo = Res()
                    boT = Ring([sb(f"boT{i}", [128, 512], BF16, s2) for i in range(2)])

                    def acc(m, sub):
                        i = m * 4 + sub
                        return pacc[i // 3][:, (i % 3) * 129:(i % 3) * 129 + 129]

                    qtiles = [(512 * i, 512) for i in range(8)] + [(4096, 16)]
                    kblocks = [(128 * i, 128) for i in range(32)] + [(4096, 16)]
                    for h in range(2 if "A2" in STAGES else 0):
                        for (q0, nq) in qtiles:
                            nsub = (nq + 127) // 128
                            P.op("dve", [(lambda e, i=i: e.memset(pacc[i][:], 0.0)) for i in range(3)], writes=[r_pacc])
                            pend = []
                            for kbi, (k0, nk) in enumerate(kblocks):
                                cur = []
                                for m in range(2):
                                    pst, r_pst = pss[m].next()
                                    P.op("pe", lambda e, pst=pst, m=m, h=h, k0=k0, nk=nk, q0=q0, nq=nq: e.matmul(out=pst[:nk, :nq], lhsT=dqkT[64 * m:64 * m + 64, 2 + h, k0:k0 + nk], rhs=dqkT[64 * m:64 * m + 64, h, q0:q0 + nq], start=True, stop=True), reads=[r_dqkT], writes=[r_pst])
                                    cur.append((m, pst, r_pst))
                                newpend = []
                                for (m, pst, r_pst) in cur:
                                    pt, r_pt = ptile[m].next()
                                    P.op("act", lambda e, pst=pst, pt=pt, nk=nk, nq=nq: e.activation(out=pt[:nk, :nq], in_=pst[:nk, :nq], func=AF.Exp, scale=0.125), reads=[r_pst], writes=[r_pt])
                                    def pvf(pt=pt, r_pt=r_pt, m=m, nk=nk, kbi=kbi):
                                        P.op("pe", [(lambda e, pt=pt, m=m, sub=sub, nk=nk, kbi=kbi, h=h, qs=min(128, nq - 128 * sub): e.matmul(out=acc(m, sub)[:qs, :], lhsT=pt[:nk, 128 * sub:128 * sub + qs], rhs=vatt[:nk, kbi, h, :], start=False, stop=False, skip_group_check=True)) for sub in range(nsub)], reads=[r_pt, r_vatt], writes=[r_pacc])
                                    newpend.append(pvf)
                                for f in pend:
                                    f()
                                pend = newpend
                            for f in pend:
                                f()
                            bT, r_bT = boT.next()
                            for sub in range(nsub):
                                qs = min(128, nq - 128 * sub)
                                a1 = acc(0, sub); a2 = acc(1, sub)
                                P.op("dve", lambda e, a1=a1, qs=qs: e.reciprocal(out=rc[:qs, 0:1], in_=a1[:qs, 128:129]), reads=[r_pacc], writes=[r_rc])
                                P.op("dve", lambda e, a2=a2, qs=qs: e.reciprocal(out=rc[:qs, 1:2], in_=a2[:qs, 128:129]), reads=[r_pacc], writes=[r_rc])
                                P.op("dve", lambda e, a2=a2, qs=qs: e.tensor_scalar(out=t2[:qs, :], in0=a2[:qs, 0:128], scalar1=rc[:qs, 1:2], scalar2=lamw[:qs, 5:6], op0=ALU.mult, op1=ALU.mult), reads=[r_pacc, r_rc, r_lamw], writes=[r_t2])
                                P.op("dve", lambda e, a1=a1, qs=qs: e.scalar_tensor_tensor(out=ot[:qs, :], in0=a1[:qs, 0:128], scalar=rc[:qs, 0:1], in1=t2[:qs, :], op0=ALU.mult, op1=ALU.add), reads=[r_pacc, r_rc, r_t2], writes=[r_ot])
                                P.op("act", lambda e, qs=qs: e.activation(out=oj[:qs, :], in_=ot[:qs, :], func=AF.Square, accum_out=os_[:qs, :]), reads=[r_ot], writes=[r_oj, r_os])
                                P.op("dve", lambda e, qs=qs: e.tensor_scalar(out=os_[:qs, :], in0=os_[:qs, :], scalar1=1.0 / 128, scalar2=EPS, op0=ALU.mult, op1=ALU.add), reads=[r_os], writes=[r_os])
                                P.op("act", lambda e, qs=qs: e.activation(out=os_[:qs, :], in_=os_[:qs, :], func=AF.Sqrt), reads=[r_os], writes=[r_os])
                                P.op("dve", lambda e, qs=qs: e.reciprocal(out=os_[:qs, :], in_=os_[:qs, :]), reads=[r_os], writes=[r_os])
                                P.op("dve", lambda e, qs=qs: e.scalar_tensor_tensor(out=bo[:qs, :], in0=ot[:qs, :], scalar=os_[:qs, 0:1], in1=sg_t[:qs, :], op0=ALU.mult, op1=ALU.mult), reads=[r_ot, r_os, r_sg], writes=[r_bo])
                                P.op("pe", lambda e, sub=sub, qs=qs: e.transpose(out=ptr[:, 128 * sub:128 * sub + qs], in_=bo[:qs, :], identity=idb[:qs, :qs]), reads=[r_bo, r_idb], writes=[r_ptr])
                                P.op("act", lambda e, bT=bT, sub=sub, qs=qs: e.copy(out=bT[:, 128 * sub:128 * sub + qs], in_=ptr[:, 128 * sub:128 * sub + qs]), reads=[r_ptr], writes=[r_bT])
                            P.dma("sp", lambda e, bT=bT, h=h, q0=q0, nq=nq: e.dma_start(out=cc_in[256 + 128 * h:256 + 128 * h + 128, q0:q0 + nq], in_=bT[:, :nq]), reads=[r_bT], writes=[r_cc_in_b])

            if "X" in STAGES:
                exchange_half(1)
            P.fence()
            with ExitStack() as s3:
                mqkT = sb("mqkT", [128, 4, LP], BF16, s3); r_mqkT = Res()
                with ExitStack() as s3a:
                    raw = sb("raw", [128, 4, LP + 2], BF16, s3a); r_raw = Res()
                    cacc = sb("cacc", [128, LP], F32, s3a); r_cacc = Res()
                    mcw = sb("mcw", [128, 12], F32, s3a); r_mcw = Res()
                    P.dma("act", lambda e: e.dma_start(out=mcw[:], in_=mconv[:, :]), writes=[r_mcw])
                    P.dma("sp", lambda e: e.dma_start(out=raw[:], in_=mqk_raw.rearrange("(c p) n -> p c n", p=128)), reads=[r_mqk_raw], writes=[r_raw])
                    for c in range(4):
                        P.op("dve", lambda e, c=c: e.tensor_scalar(out=cacc[:], in0=raw[:, c, 0:LP], scalar1=mcw[:, 3 * c:3 * c + 1], scalar2=None, op0=ALU.mult), reads=[r_raw, r_mcw], writes=[r_cacc])
                        P.op("dve", lambda e, c=c: e.scalar_tensor_tensor(out=cacc[:], in0=raw[:, c, 1:LP + 1], scalar=mcw[:, 3 * c + 1:3 * c + 2], in1=cacc[:], op0=ALU.mult, op1=ALU.add), reads=[r_raw, r_mcw, r_cacc], writes=[r_cacc])
                        P.op("dve", lambda e, c=c: e.scalar_tensor_tensor(out=cacc[:], in0=raw[:, c, 2:LP + 2], scalar=mcw[:, 3 * c + 2:3 * c + 3], in1=cacc[:], op0=ALU.mult, op1=ALU.add), reads=[r_raw, r_mcw, r_cacc], writes=[r_cacc])
                        P.op("act", lambda e, c=c: e.activation(out=mqkT[:, c, :], in_=cacc[:], func=AF.Silu), reads=[r_cacc], writes=[r_mqkT])
                    P.op("dve", lambda e: e.memset(mqkT[:, :, 0:48], 0.0), writes=[r_mqkT])

                P.fence()
                stop_here("a3conv")
                gt = sb("gt", [64, 65, 4], F32, s3); r_gtb = Res()
                gl = sb("gl", [64, 4, 65], F32, s3); r_gl = Res()
                tru = sb("tru", [64, 64], F32, s3); r_tru = Res()
                trl = sb("trl", [64, 64], F32, s3); r_trl = Res()
                mku = sb("mku", [64, 64], F32, s3); r_mku = Res()
                mkl = sb("mkl", [64, 64], F32, s3); r_mkl = Res()
                ones64 = sb("ones64", [64, 128], F32, s3); r_ones = Res()
                ebt = sb("ebt", [64, 2, 65], F32, s3); r_ebt = Res()
                rft = sb("rft", [64, 2, 65], F32, s3); r_rft = Res()
                egt = sb("egt", [128, 2, 65], F32, s3); r_egt = Res()
                mngt = sb("mngt", [64, 256], F32, s3); r_mngt = Res()
                s3g = s3.enter_context(ExitStack())
                pgx = ps("pgx", [128, 512], F32, s3g); r_pgx = Res()
                for c5 in range(5):
                    P.dma("sp", lambda e, c5=c5: e.dma_start(out=gt[:, 13 * c5:13 * c5 + 13, :], in_=gates_s.rearrange("(c t) g -> t c g", t=64)[:, 13 * c5:13 * c5 + 13, :]), reads=[r_gates_s], writes=[r_gtb])
                P.dma("act", lambda e: e.dma_start(out=tru[:], in_=triu_d[:, :]), writes=[r_tru])
                P.dma("act", lambda e: e.dma_start(out=trl[:], in_=tril_d[:, :]), writes=[r_trl])
                P.dma("act", lambda e: e.dma_start(out=mngt[:], in_=mng[:, :]), writes=[r_mngt])
                P.op("dve", lambda e: e.tensor_scalar(out=mku[:], in0=tru[:], scalar1=1.0 / 16, scalar2=None, op0=ALU.mult), reads=[r_tru], writes=[r_mku])
                P.op("dve", lambda e: e.tensor_scalar(out=mkl[:], in0=trl[:], scalar1=1.0 / 16, scalar2=None, op0=ALU.mult), reads=[r_trl], writes=[r_mkl])
                P.op("dve", lambda e: e.memset(ones64[:], 1.0), writes=[r_ones])
                stop_here("g1")
                for gi in range(2):
                    P.op("act", lambda e, gi=gi: e.activation(out=gl[:, gi, :], in_=gt[:, :, gi], func=AF.Exp, scale=-1.0), reads=[r_gtb], writes=[r_gl])
                P.op("dve", lambda e: e.tensor_scalar(out=gl[:, 0:2, :], in0=gl[:, 0:2, :], scalar1=1.0, scalar2=None, op0=ALU.add), reads=[r_gl], writes=[r_gl])
                P.op("act", lambda e: e.activation(out=gl[:, 0:2, :], in_=gl[:, 0:2, :], func=AF.Ln), reads=[r_gl], writes=[r_gl])
                P.op("dve", lambda e: e.tensor_scalar(out=gl[:, 0:2, :], in0=gl[:, 0:2, :], scalar1=-1.0, scalar2=None, op0=ALU.mult), reads=[r_gl], writes=[r_gl])
                for gi in range(2):
                    P.op("dve", lambda e, gi=gi: e.tensor_copy(out=gl[:, 2 + gi, :], in_=gt[:, :, 2 + gi]), reads=[r_gtb], writes=[r_gl])
                stop_here("g2")
                P.op("dve", lambda e: e.memset(gl[0:48, :, 0:1], 0.0), writes=[r_gl])
                stop_here("g3")
                P.op("pe", lambda e: e.matmul(out=pgx[:64, 0:65], lhsT=tru[:], rhs=gl[:, 0, :], start=True, stop=True), reads=[r_tru, r_gl], writes=[r_pgx])
                P.op("pe", lambda e: e.matmul(out=pgx[:64, 65:130], lhsT=trl[:], rhs=gl[:, 1, :], start=True, stop=True), reads=[r_trl, r_gl], writes=[r_pgx])
                P.op("pe", lambda e: e.matmul(out=pgx[:, 130:260], lhsT=ones64[:], rhs=gl[:, 0:2, :].rearrange("p a c -> p (a c)"), start=True, stop=True), reads=[r_ones, r_gl], writes=[r_pgx])
                stop_here("g4")
                P.op("act", lambda e: e.activation(out=ebt[:].rearrange("p a c -> p (a c)"), in_=pgx[:64, 0:130], func=AF.Exp), reads=[r_pgx], writes=[r_ebt])
                P.op("act", lambda e: e.activation(out=egt[:].rearrange("p a c -> p (a c)"), in_=pgx[:, 130:260], func=AF.Exp), reads=[r_pgx], writes=[r_egt])
                P.op("dve", lambda e: e.tensor_tensor(out=rft[:].rearrange("p a c -> p (a c)"), in0=gl[:, 2:4, :].rearrange("p a c -> p (a c)"), in1=pgx[:64, 0:130], op=ALU.subtract), reads=[r_pgx, r_gl], writes=[r_rft])
                P.op("act", lambda e: e.activation(out=rft[:].rearrange("p a c -> p (a c)"), in_=rft[:].rearrange("p a c -> p (a c)"), func=AF.Exp), reads=[r_rft], writes=[r_rft])

                stop_here("a3g")
                s3g.close()
                P.fence()
                Zt = [sb(f"Zt{d}", [128, 2, 257], F32, s3) for d in range(2)]; r_Z = [Res(), Res()]
                Zbt = [sb(f"Zbt{d}", [128, 2, 257], BF16, s3) for d in range(2)]; r_Zb = [Res(), Res()]
                ztt = [sb(f"ztt{d}", [128, 2, 257], F32, s3) for d in range(2)]; r_ztmp = [Res(), Res()]
                nebt = sb("nebt", [64, 2, 65], F32, s3); r_nebt = Res()
                P.op("dve", lambda e: e.tensor_scalar(out=nebt[:].rearrange("p a c -> p (a c)"), in0=ebt[:].rearrange("p a c -> p (a c)"), scalar1=-1.0, scalar2=None, op0=ALU.mult), reads=[r_ebt], writes=[r_nebt])
                vres = sb("vres", [64, 65, 257], BF16, s3); r_vres = Res()
                for c5 in range(5):
                    P.dma("act", lambda e, c5=c5: e.dma_start(out=vres[:, 13 * c5:13 * c5 + 13, :], in_=mv_s.rearrange("(c t) v -> t c v", t=64)[:, 13 * c5:13 * c5 + 13, :]), reads=[r_mv_s], writes=[r_vres])
                vtr = Ring([sb(f"vt{i}", [64, 257], BF16, s3) for i in range(4)])
                ptmr = Ring([sb(f"ptm{i}", [64, 64], BF16, s3) for i in range(4)])
                ktokr = Ring([sb(f"ktok{i}", [64, 256], BF16, s3) for i in range(4)])
                dnr = Ring([sb(f"dn{i}", [64, 4], F32, s3) for i in range(4)])
                hring = Ring([sb(f"hch{i}", [64, 256], F32, s3) for i in range(4)])
                with ExitStack() as s3s:
                    p_sr = Ring([ps(f"p_s{i}", [128, 512], F32, s3s) for i in range(2)])
                    p_kr = Ring([ps(f"p_k{i}", [128, 1024], BF16, s3s) for i in range(2)])
                    p_or = Ring([ps(f"p_o{i}", [128, 512], F32, s3s) for i in range(2)])
                    p_cc = ps("p_cc", [128, 1024], F32, s3s); r_p_c = Res()

                    def phase_I(d, c):
                        c0 = 64 * c
                        mask, r_mask = (mku, r_mku) if d == 0 else (mkl, r_mkl)
                        p_s, r_p_s = p_sr.next()
                        p_k, r_p_k = p_kr.next()
                        ptm, r_ptm = ptmr.next()
                        ktok, r_ktok = ktokr.next()
                        vt, r_vt = vtr.next()
                        P.op("pe", [(lambda e, dc=dc, c0=c0, p_s=p_s: e.matmul(out=p_s[:64, 0:64], lhsT=mqkT[:, 2 + dc, c0:c0 + 64], rhs=mqkT[:, dc, c0:c0 + 64], start=(dc == 0), stop=(dc == 1))) for dc in range(2)], reads=[r_mqkT], writes=[r_p_s])
                        P.op("dve", lambda e, mask=mask, p_s=p_s, ptm=ptm: e.tensor_tensor(out=ptm[:], in0=p_s[:64, 0:64], in1=mask[:], op=ALU.mult), reads=[r_p_s, r_mask], writes=[r_ptm])
                        P.op("pe", [(lambda e, dc=dc, c0=c0, p_k=p_k: e.transpose(out=p_k[:64, 128 * dc:128 * dc + 128], in_=mqkT[:, 2 + dc, c0:c0 + 64], identity=idb[:])) for dc in range(2)], reads=[r_mqkT, r_idb], writes=[r_p_k])
                        P.op("act", lambda e, ktok=ktok, p_k=p_k: e.copy(out=ktok[:], in_=p_k[:64, 0:256]), reads=[r_p_k], writes=[r_ktok])
                        P.op("dve", lambda e, vt=vt, d=d, c=c: e.tensor_scalar(out=vt[:], in0=vres[:, c, :], scalar1=rft[:, d, c:c + 1], scalar2=None, op0=ALU.mult), reads=[r_vres, r_rft], writes=[r_vt])
                        return dict(c=c, c0=c0, d=d, ptm=ptm, r_ptm=r_ptm, ktok=ktok, r_ktok=r_ktok, vt=vt, r_vt=r_vt)

                    def phase_D(x):
                        d, c, c0 = x["d"], x["c"], x["c0"]
                        ptm, r_ptm, ktok, r_ktok, vt, r_vt = x["ptm"], x["r_ptm"], x["ktok"], x["r_ktok"], x["vt"], x["r_vt"]
                        p_o, r_p_o = p_or.next()
                        for dc in range(2):
                            P.op("pe", lambda e, dc=dc, ktok=ktok, vt=vt: e.matmul(out=p_cc[:, 512 * dc:512 * dc + 257], lhsT=ktok[:, 128 * dc:128 * dc + 128], rhs=vt[:], start=True, stop=True), reads=[r_ktok, r_vt], writes=[r_p_c])
                        P.op("pe", [lambda e, ptm=ptm, vt=vt, p_o=p_o: e.matmul(out=p_o[:64, 0:257], lhsT=ptm[:], rhs=vt[:], start=True, stop=False)] +
                             [(lambda e, dc=dc, c0=c0, p_o=p_o, d=d: e.matmul(out=p_o[:64, 0:257], lhsT=mqkT[:, dc, c0:c0 + 64], rhs=Zbt[d][:, dc, :], start=False, stop=(dc == 1))) for dc in range(2)],
                             reads=[r_ptm, r_vt, r_mqkT, r_Zb[d]], writes=[r_p_o])
                        P.op("dve", lambda e, d=d: e.scalar_tensor_tensor(out=ztt[d][:], in0=p_cc[:].rearrange("p (a n) -> p a n", a=2)[:, :, 0:257], scalar=1.0 / 16, in1=Zt[d][:], op0=ALU.mult, op1=ALU.add), reads=[r_p_c, r_Z[d]], writes=[r_ztmp[d]])
                        P.op("act", lambda e, d=d, c=c: e.activation(out=Zbt[d][:], in_=ztt[d][:], func=AF.Identity, scale=egt[:, d, c:c + 1]), reads=[r_ztmp[d], r_egt], writes=[r_Zb[d]])
                        P.op("act", lambda e, d=d, c=c: e.activation(out=Zt[d][:], in_=ztt[d][:], func=AF.Identity, scale=egt[:, d, c:c + 1]), reads=[r_ztmp[d], r_egt], writes=[r_Z[d]])
                        hch, r_hch = hring.next()
                        dn, r_dn = dnr.next()
                        P.op("dve", lambda e, d=d, c=c, dn=dn, p_o=p_o: e.tensor_scalar(out=dn[:, 0:1], in0=p_o[:64, 256:257], scalar1=ebt[:, d, c:c + 1], scalar2=1.0, op0=ALU.mult, op1=ALU.max), reads=[r_p_o, r_ebt], writes=[r_dn])
                        P.op("dve", lambda e, d=d, c=c, dn=dn, p_o=p_o: e.scalar_tensor_tensor(out=dn[:, 1:2], in0=p_o[:64, 256:257], scalar=nebt[:, d, c:c + 1], in1=dn[:, 0:1], op0=ALU.mult, op1=ALU.max), reads=[r_p_o, r_nebt, r_dn], writes=[r_dn])
                        P.op("dve", lambda e, dn=dn: e.reciprocal(out=dn[:, 2:3], in_=dn[:, 1:2]), reads=[r_dn], writes=[r_dn])
                        P.op("dve", lambda e, hch=hch, dn=dn, p_o=p_o, d=d, c=c: e.tensor_scalar(out=hch[:], in0=p_o[:64, 0:256], scalar1=dn[:, 2:3], scalar2=ebt[:, d, c:c + 1], op0=ALU.mult, op1=ALU.mult), reads=[r_p_o, r_dn, r_ebt], writes=[r_hch])
                        dst, r_dst = (hf_s, r_hf_s) if d == 0 else (hb_s, r_hb_s)
                        P.dma("sp", lambda e, hch=hch, c0=c0, dst=dst: e.dma_start(out=dst[c0:c0 + 64, :], in_=hch[:]), reads=[r_hch], writes=[r_dst])

                    if "A3" in STAGES:
                        for d in range(2):
                            P.op("dve", lambda e, d=d: e.memset(Zt[d][:], 0.0), writes=[r_Z[d]])
                            P.op("dve", lambda e, d=d: e.memset(Zbt[d][:], 0.0), writes=[r_Zb[d]])
                        nxt = [phase_I(0, 0), phase_I(1, 64)]
                        for i in range(65):
                            cur = nxt
                            if i + 1 < 65:
                                nxt = [phase_I(0, i + 1), phase_I(1, 63 - i)]
                            phase_D(cur[0])
                            phase_D(cur[1])
                P.fence()
                with ExitStack() as s3e:
                    NR = 8
                    hfr = Ring([sb(f"ehf{i}", [128, 256], F32, s3e) for i in range(NR)])
                    hbr = Ring([sb(f"ehb{i}", [128, 256], F32, s3e) for i in range(NR)])
                    mor = Ring([sb(f"emo{i}", [128, 256], F32, s3e) for i in range(NR)])
                    hsr = Ring([sb(f"ehs{i}", [128, 256], F32, s3e) for i in range(NR)])
                    bsr = Ring([sb(f"ebs{i}", [128, 8], F32, s3e) for i in range(NR)])
                    aor = Ring([sb(f"eao{i}", [128, 256], BF16, s3e) for i in range(NR)])
                    aTr = Ring([sb(f"eaT{i}", [128, 2, 128], BF16, s3e) for i in range(NR)])
                    p_tr = Ring([ps(f"ep_t{i}", [128, 1024], BF16, s3e) for i in range(3)])
                    mng128 = sb("mng128", [128, 256], F32, s3e); r_mng128 = Res()
                    P.dma("act", lambda e: e.dma_start(out=mng128[0:64, :], in_=mng[:, :]), writes=[r_mng128])
                    P.dma("act", lambda e: e.dma_start(out=mng128[64:128, :], in_=mng[:, :]), writes=[r_mng128])
                    eblocks = [(128 * j, 128) for j in range(32)] + [(4096, 64)]

                    def st0(j):
                        r0, n = eblocks[j]
                        hf, r_hf = hfr.next(); hb, r_hb = hbr.next(); mo, r_mo = mor.next()
                        P.dma("sp", lambda e, hf=hf, r0=r0, n=n: e.dma_start(out=hf[:n, :], in_=hf_s[r0:r0 + n, :]), reads=[r_hf_s], writes=[r_hf])
                        P.dma("sp", lambda e, hb=hb, r0=r0, n=n: e.dma_start(out=hb[:n, :], in_=hb_s[r0:r0 + n, :]), reads=[r_hb_s], writes=[r_hb])
                        P.dma("act", lambda e, mo=mo, r0=r0, n=n: e.dma_start(out=mo[:n, :], in_=mo_s[r0:r0 + n, :]), reads=[r_mo_s], writes=[r_mo])
                        return dict(r0=r0, n=n, hf=hf, r_hf=r_hf, hb=hb, r_hb=r_hb, mo=mo, r_mo=r_mo)

                    def st1(x):
                        n = x["n"]
                        hs, r_hs = hsr.next(); bs_, r_bs = bsr.next()
                        hf, hb = x["hf"], x["hb"]
                        P.op("dve", lambda e, hf=hf, hb=hb, hs=hs, n=n: e.tensor_tensor(out=hs[:n, :], in0=hf[:n, :], in1=hb[:n, :], op=ALU.add), reads=[x["r_hf"], x["r_hb"]], writes=[r_hs])
                        P.op("dve", lambda e, hs=hs, bs_=bs_, n=n: e.bn_stats(out=bs_[:n, 0:6], in_=hs[:n, :]), reads=[r_hs], writes=[r_bs])
                        P.op("dve", lambda e, bs_=bs_, n=n: e.bn_aggr(out=bs_[:n, 6:8], in_=bs_[:n, 0:6]), reads=[r_bs], writes=[r_bs])
                        P.op("dve", lambda e, bs_=bs_, n=n: e.tensor_scalar(out=bs_[:n, 7:8], in0=bs_[:n, 7:8], scalar1=EPS, scalar2=None, op0=ALU.add), reads=[r_bs], writes=[r_bs])
                        x.update(hs=hs, r_hs=r_hs, bs=bs_, r_bs=r_bs)

                    def st2(x):
                        n, bs_, r_bs = x["n"], x["bs"], x["r_bs"]
                        P.op("act", lambda e, bs_=bs_, n=n: e.activation(out=bs_[:n, 7:8], in_=bs_[:n, 7:8], func=AF.Sqrt), reads=[r_bs], writes=[r_bs])

                    def st3(x):
                        n, bs_, r_bs, hs, r_hs = x["n"], x["bs"], x["r_bs"], x["hs"], x["r_hs"]
                        P.op("dve", lambda e, bs_=bs_, n=n: e.reciprocal(out=bs_[:n, 7:8], in_=bs_[:n, 7:8]), reads=[r_bs], writes=[r_bs])
                        P.op("dve", lambda e, bs_=bs_, hs=hs, n=n: e.tensor_scalar(out=hs[:n, :], in0=hs[:n, :], scalar1=bs_[:n, 6:7], scalar2=bs_[:n, 7:8], op0=ALU.subtract, op1=ALU.mult), reads=[r_hs, r_bs], writes=[r_hs])

                    def st4(x):
                        n, hs, r_hs, mo, r_mo = x["n"], x["hs"], x["r_hs"], x["mo"], x["r_mo"]
                        ao, r_ao = aor.next()
                        P.op("pool", lambda e, hs=hs, n=n: e.tensor_tensor(out=hs[:n, :], in0=hs[:n, :], in1=mng128[:n, :], op=ALU.mult), reads=[r_hs, r_mng128], writes=[r_hs])
                        P.op("pool", lambda e, hs=hs, mo=mo, ao=ao, n=n: e.tensor_tensor(out=ao[:n, :], in0=hs[:n, :], in1=mo[:n, :], op=ALU.mult), reads=[r_hs, r_mo], writes=[r_ao])
                        x.update(ao=ao, r_ao=r_ao)

                    def st5(x):
                        n, ao, r_ao = x["n"], x["ao"], x["r_ao"]
                        p_t, r_p_t = p_tr.next()
                        P.op("pe", [(lambda e, dc=dc, ao=ao, p_t=p_t, n=n: e.transpose(out=p_t[:, 128 * dc:128 * dc + n], in_=ao[:n, 128 * dc:128 * dc + 128], identity=idb[:n, :n])) for dc in range(2)], reads=[r_ao, r_idb], writes=[r_p_t])
                        x.update(p_t=p_t, r_p_t=r_p_t)

                    def st6(x):
                        r0, n, p_t, r_p_t = x["r0"], x["n"], x["p_t"], x["r_p_t"]
                        aT, r_aT = aTr.next()
                        P.op("act", lambda e, aT=aT, p_t=p_t, n=n: e.copy(out=aT[:, :, :n], in_=p_t[:, 0:256].rearrange("p (a t) -> p a t", a=2)[:, :, :n]), reads=[r_p_t], writes=[r_aT])
                        if r0 == 0:
                            P.dma("sp", lambda e, aT=aT: e.dma_start(out=cc_in[0:256, 0:80].rearrange("(a p) t -> p a t", p=128), in_=aT[:, :, 48:128]), reads=[r_aT], writes=[r_cc_in])
                        else:
                            pos0 = r0 - 48
                            P.dma("sp", lambda e, aT=aT, pos0=pos0, n=n: e.dma_start(out=cc_in[0:256, pos0:pos0 + n].rearrange("(a p) t -> p a t", p=128), in_=aT[:, :, :n]), reads=[r_aT], writes=[r_cc_in])

                    if "A3" in STAGES:
                        stages = [st1, st2, st3, st4, st5, st6]
                        xs = {}
                        nb = len(eblocks)
                        for i in range(nb + len(stages) + 1):
                            if i < nb:
                                xs[i] = st0(i)
                            for si, stf in enumerate(stages):
                                jj = i - 1 - si
                                if 0 <= jj < nb:
                                    stf(xs[jj])
                                    if stf is st6 and "X" in STAGES and jj in (8, 16, 24, 32):
                                        exchange_half(0, js=(jj // 8 - 1,))

            P.fence()
            if "X" in STAGES and "A3" not in STAGES:
                exchange_half(0)
            if DEBUG:
                P.stopped = False
                dbg_holder.append(P.dma("pool", lambda e: e.dma_start(out=dbg_cc[:, :], in_=cc_in[:, :]), reads=[r_cc_in, r_cc_in_b]))
                stop_here(STOP_AT)

            if "B" in STAGES:
                TS = [(342 * i, 342) for i in range(3)]
                with ExitStack() as sB:
                    hT = sb("hT", [128, 16, NW], F32, sB); r_hT = Res()
                    uT2 = sb("uT2", [128, 16, NW], BF16, sB); r_uT2 = Res()
                    g2t = sb("g2t", [128, 16], F32, sB); r_g2t = Res()
                    P.dma("act", lambda e: e.dma_start(out=g2t[:], in_=g2c[:, :]), writes=[r_g2t])
                    onesf = sb("onesf", [128, 128], F32, sB); r_onesf = Res()
                    P.op("dve", lambda e: e.memset(onesf[:], 1.0), writes=[r_onesf])
                    with ExitStack() as sB1:
                        abT = sb("abT", [128, 16, NW], BF16, sB1); r_abT = Res()
                        mT = sb("mT", [128, 16, NW], BF16, sB1); r_mT = Res()
                        rank_cache = {}
                        for k in range(16):
                            def load_ab(e, k=k):
                                if "r" not in rank_cache:
                                    rank_cache["r"] = e.partition_id() % 4
                                rank = rank_cache["r"]
                                return e.dma_start(out=abT[:, k:k + 1, :], in_=cc_out.rearrange("(j r) t -> r j t", j=4)[k * 128:(k + 1) * 128, bass.ds(rank, 1), :])
                            P.dma("pool", load_ab, reads=[r_cc_out[0 if k < 8 else 1]], writes=[r_abT])
                        with ExitStack() as sB0:
                            g1t_b = sb("g1t2", [128, D], F32, sB0); r_g1t_b = Res()
                            P.dma("act", lambda e: e.dma_start(out=g1t_b[:], in_=g1b[:, :]), writes=[r_g1t_b])
                            xring_b = Ring([sb(f"xw{i}", [128, D], F32, sB0) for i in range(2)])
                            junk_b = sb("junk2", [128, D], BF16, sB0); r_junk_b = Res()
                            ss_b = sb("ss2", [128, 1], F32, sB0); r_ss_b = Res()
                            u_b = sb("u2", [128, D], BF16, sB0); r_u_b = Res()
                            pT_b = ps("pT2", [128, D], BF16, sB0); r_pT_b = Res()
                            pX = [ps(f"pX{i}", [128, 1024], F32, sB0) for i in range(2)]; r_pX = [Res(), Res()]
                            for j in range(9):
                                bs = 128 if j < 8 else 2
                                xt_b, r_xt_b = xring_b.next()
                                P.dma("sp", lambda e, xt_b=xt_b, bs=bs, j=j: e.dma_start(out=xt_b[:bs, :], in_=xwin[128 * j:128 * j + bs, :]), writes=[r_xt_b])
                                P.op("act", lambda e, xt_b=xt_b, bs=bs: e.activation(out=junk_b[:bs, :], in_=xt_b[:bs, :], func=AF.Square, accum_out=ss_b[:bs, :]), reads=[r_xt_b], writes=[r_junk_b, r_ss_b])
                                P.op("dve", lambda e, bs=bs: e.tensor_scalar(out=ss_b[:bs, :], in0=ss_b[:bs, :], scalar1=1.0 / D, scalar2=EPS, op0=ALU.mult, op1=ALU.add), reads=[r_ss_b], writes=[r_ss_b])
                                P.op("act", lambda e, bs=bs: e.activation(out=ss_b[:bs, :], in_=ss_b[:bs, :], func=AF.Sqrt), reads=[r_ss_b], writes=[r_ss_b])
                                P.op("dve", lambda e, bs=bs: e.reciprocal(out=ss_b[:bs, :], in_=ss_b[:bs, :]), reads=[r_ss_b], writes=[r_ss_b])
                                P.op("dve", lambda e, xt_b=xt_b, bs=bs: e.scalar_tensor_tensor(out=u_b[:bs, :], in0=xt_b[:bs, :], scalar=ss_b[:bs, 0:1], in1=g1t_b[:bs, :], op0=ALU.mult, op1=ALU.mult), reads=[r_xt_b, r_ss_b, r_g1t_b], writes=[r_u_b])
                                P.op("pe", [(lambda e, k=k, bs=bs: e.transpose(out=pT_b[:, k * 128:k * 128 + bs], in_=u_b[:bs, k * 128:(k + 1) * 128], identity=idb[:bs, :bs])) for k in range(16)], reads=[r_u_b, r_idb], writes=[r_pT_b])
                                P.op("act", lambda e, j=j, bs=bs: e.copy(out=uT2[:, :, 128 * j:128 * j + bs], in_=pT_b[:].rearrange("p (k n) -> p k n", k=16)[:, :, :bs]), reads=[r_pT_b], writes=[r_uT2])
                                for hh in range(2):
                                    P.op("pe", [(lambda e, k=k, hh=hh, xt_b=xt_b, bs=bs: e.transpose(out=pX[hh][:, (k % 8) * 128:(k % 8) * 128 + bs], in_=xt_b[:bs, k * 128:(k + 1) * 128], identity=idf[:bs, :bs])) for k in range(8 * hh, 8 * hh + 8)], reads=[r_xt_b, r_idf], writes=[r_pX[hh]])
                                    P.op("dve", lambda e, hh=hh, j=j, bs=bs: e.tensor_copy(out=hT[:, 8 * hh:8 * hh + 8, 128 * j:128 * j + bs], in_=pX[hh][:].rearrange("p (k n) -> p k n", k=8)[:, :, :bs]), reads=[r_pX[hh]], writes=[r_hT])

                        P.fence()
                        with ExitStack() as sB1b:
                            wgr = Ring([sb(f"wg{i}", [128, 16, 256], BF16, sB1b) for i in range(2)])
                            war = Ring([sb(f"wa{i}", [128, 8, 256], BF16, sB1b) for i in range(2)])
                            sgm = sb("sgm", [128, 342], F32, sB1b); r_sgm = Res()
                            sgd = sb("sgd", [128, 342], F32, sB1b); r_sgd = Res()
                            tA = sb("tA", [128, 342], F32, sB1b); r_tA = Res()
                            tB = sb("tB", [128, 342], F32, sB1b); r_tB = Res()
                            pq = [Ring([ps(f"pq{q}{i}", [128, 512], F32, sB1b) for i in range(2)]) for q in range(4)]
                            for c in range(16):
                                wg, r_wg = wgr.next()
                                P.dma("pool", lambda e, wg=wg, c=c: e.dma_start(out=wg[:, :, 0:128], in_=w_g[:, 128 * c:128 * c + 128].rearrange("(k p) n -> p k n", p=128)), writes=[r_wg])
                                for (t0, tn) in TS:
                                    p0_, r0_ = pq[0].next()
                                    P.op("pe", [(lambda e, k=k, p0_=p0_, wg=wg, t0=t0, tn=tn: e.matmul(out=p0_[:, :tn], lhsT=wg[:, k, 0:128], rhs=uT2[:, k, t0:t0 + tn], start=(k == 0), stop=(k == 15))) for k in range(16)], reads=[r_wg, r_uT2], writes=[r0_])
                                    P.op("act", lambda e, p0_=p0_, c=c, t0=t0, tn=tn: e.activation(out=mT[:, c, t0:t0 + tn], in_=p0_[:, :tn], func=AF.Sigmoid), reads=[r0_], writes=[r_mT])
                            for c in range(16):
                                wg, r_wg = wgr.next()
                                wa, r_wa = war.next()
                                P.dma("pool", lambda e, wg=wg, c=c: e.dma_start(out=wg[:, :, 128:256], in_=w_g[:, 2048 + 128 * c:2048 + 128 * c + 128].rearrange("(k p) n -> p k n", p=128)), writes=[r_wg])
                                P.dma("pool", lambda e, wa=wa, c=c: e.dma_start(out=wa[:, :, 0:128], in_=w_a[:, 128 * c:128 * c + 128].rearrange("(k p) n -> p k n", p=128)), writes=[r_wa])
                                P.dma("pool", lambda e, wa=wa, c=c: e.dma_start(out=wa[:, :, 128:256], in_=w_b[:, 128 * c:128 * c + 128].rearrange("(k p) n -> p k n", p=128)), writes=[r_wa])
                                for (t0, tn) in TS:
                                    p1_, r1_ = pq[1].next(); p2_, r2_ = pq[2].next(); p3_, r3_ = pq[3].next()
                                    P.op("pe", [(lambda e, k=k, p2_=p2_, wg=wg, t0=t0, tn=tn: e.matmul(out=p2_[:, :tn], lhsT=wg[:, k, 128:256], rhs=uT2[:, k, t0:t0 + tn], start=(k == 0), stop=(k == 15))) for k in range(16)], reads=[r_wg, r_uT2], writes=[r2_])
                                    P.op("pe", [(lambda e, k=k, p1_=p1_, wa=wa, t0=t0, tn=tn: e.matmul(out=p1_[:, :tn], lhsT=wa[:, k, 0:128], rhs=abT[:, k, t0:t0 + tn], start=(k == 0), stop=(k == 7))) for k in range(8)], reads=[r_wa, r_abT], writes=[r1_])
                                    P.op("pe", [(lambda e, k=k, p3_=p3_, wa=wa, t0=t0, tn=tn: e.matmul(out=p3_[:, :tn], lhsT=wa[:, k, 128:256], rhs=abT[:, 8 + k, t0:t0 + tn], start=(k == 0), stop=(k == 7))) for k in range(8)], reads=[r_wa, r_abT], writes=[r3_])
                                    P.op("act", lambda e, p2_=p2_, tn=tn: e.activation(out=sgd[:, :tn], in_=p2_[:, :tn], func=AF.Sigmoid), reads=[r2_], writes=[r_sgd])
                                    P.op("dve", lambda e, p1_=p1_, c=c, t0=t0, tn=tn: e.tensor_tensor(out=tA[:, :tn], in0=p1_[:, :tn], in1=mT[:, c, t0:t0 + tn], op=ALU.mult), reads=[r1_, r_mT], writes=[r_tA])
                                    P.op("dve", lambda e, p3_=p3_, tn=tn: e.tensor_tensor(out=tB[:, :tn], in0=p3_[:, :tn], in1=sgd[:, :tn], op=ALU.mult), reads=[r3_, r_sgd], writes=[r_tB])
                                    P.op("dve", lambda e, c=c, t0=t0, tn=tn: e.tensor_tensor(out=mT[:, c, t0:t0 + tn], in0=tA[:, :tn], in1=tB[:, :tn], op=ALU.add), reads=[r_tA, r_tB], writes=[r_mT])
                        P.fence()
                        with ExitStack() as sB2:
                            wor = Ring([sb(f"wo{i}", [128, 16, 128], BF16, sB2) for i in range(2)])
                            po = Ring([ps(f"po{i}", [128, 512], F32, sB2) for i in range(4)])
                            for c in range(16):
                                wo, r_wo = wor.next()
                                P.dma("pool", lambda e, wo=wo, c=c: e.dma_start(out=wo[:], in_=w_out[:, 128 * c:128 * c + 128].rearrange("(k p) n -> p k n", p=128)), writes=[r_wo])
                                for (t0, tn) in TS:
                                    pp, rp = po.next()
                                    P.op("pe", [(lambda e, k=k, pp=pp, wo=wo, t0=t0, tn=tn: e.matmul(out=pp[:, :tn], lhsT=wo[:, k, :], rhs=mT[:, k, t0:t0 + tn], start=(k == 0), stop=(k == 15))) for k in range(16)], reads=[r_wo, r_mT], writes=[rp])
                                    P.op("dve", lambda e, pp=pp, c=c, t0=t0, tn=tn: e.tensor_tensor(out=hT[:, c, t0:t0 + tn], in0=hT[:, c, t0:t0 + tn], in1=pp[:, :tn], op=ALU.add), reads=[rp, r_hT], writes=[r_hT])

                    P.fence()
                    with ExitStack() as sB3:
                        sq = Ring([sb(f"sq{i}", [128, 342], F32, sB3) for i in range(2)])
                        rstd = sb("rstd", [128, NW], F32, sB3); r_rstd = Res()
                        wm = sb("wm", [128, NW], F32, sB3); r_wm = Res()
                        P.dma("act", lambda e: e.dma_start(out=wm[:], in_=wmask_d[:, :]), writes=[r_wm])
                        pss3 = [ps(f"pss3{i}", [128, 512], F32, sB3) for i in range(3)]; r_pss3 = [Res() for _ in range(3)]
                        for ti, (t0, tn) in enumerate(TS):
                            fns = []
                            for c in range(16):
                                s_, r_s = sq.next()
                                P.op("act", lambda e, s_=s_, c=c, t0=t0, tn=tn: e.activation(out=s_[:, :tn], in_=hT[:, c, t0:t0 + tn], func=AF.Square), reads=[r_hT], writes=[r_s])
                                P.op("pe", lambda e, s_=s_, c=c, ti=ti, tn=tn: e.matmul(out=pss3[ti][:, :tn], lhsT=onesf[:], rhs=s_[:, :tn], start=(c == 0), stop=(c == 15), skip_group_check=True), reads=[r_s, r_onesf], writes=[r_pss3[ti]])
                            P.op("dve", lambda e, ti=ti, t0=t0, tn=tn: e.tensor_scalar(out=rstd[:, t0:t0 + tn], in0=pss3[ti][:, :tn], scalar1=1.0 / D, scalar2=EPS, op0=ALU.mult, op1=ALU.add), reads=[r_pss3[ti]], writes=[r_rstd])
                        P.op("act", lambda e: e.activation(out=rstd[:], in_=rstd[:], func=AF.Sqrt), reads=[r_rstd], writes=[r_rstd])
                        P.op("dve", lambda e: e.reciprocal(out=rstd[:], in_=rstd[:]), reads=[r_rstd], writes=[r_rstd])
                        P.op("dve", lambda e: e.tensor_tensor(out=rstd[:], in0=rstd[:], in1=wm[:], op=ALU.mult), reads=[r_rstd, r_wm], writes=[r_rstd])
                        for c in range(16):
                            P.op("dve", lambda e, c=c: e.scalar_tensor_tensor(out=uT2[:, c, :], in0=hT[:, c, :], scalar=g2t[:, c:c + 1], in1=rstd[:], op0=ALU.mult, op1=ALU.mult), reads=[r_hT, r_g2t, r_rstd], writes=[r_uT2])

                    P.fence()
                    with ExitStack() as sB4:
                        fcw = sb("fcw", [128, 264], F32, sB4); r_fcw = Res()
                        P.dma("act", lambda e: e.dma_start(out=fcw[:], in_=fconv[:, :]), writes=[r_fcw])
                        actT = sb("actT", [128, 22, 1024], BF16, sB4); r_actT = Res()
                        wur = Ring([sb(f"wu{i}", [128, 16, 256], BF16, sB4) for i in range(2)])
                        wdr = Ring([sb(f"wd{i}", [128, 22, 128], BF16, sB4) for i in range(2)])
                        upg = sb("upg", [128, NW], F32, sB4); r_upg = Res()
                        upv = sb("upv", [128, NW], F32, sB4); r_upv = Res()
                        cg = sb("cg", [128, 1024], F32, sB4); r_cg = Res()
                        cv = sb("cv", [128, 1024], F32, sB4); r_cv = Res()
                        sgl = sb("sgl", [128, 1024], F32, sB4); r_sgl = Res()
                        pu = Ring([ps(f"pu{i}", [128, 512], F32, sB4) for i in range(4)])
                        pd = Ring([ps(f"pd{i}", [128, 512], F32, sB4) for i in range(4)])
                        for half in range(2):
                            for fc in range(22):
                                f = half * 22 + fc
                                wu, r_wu = wur.next()
                                P.dma("pool", lambda e, wu=wu, f=f: e.dma_start(out=wu[:, :, 0:128], in_=w_up[:, 128 * f:128 * f + 128].rearrange("(k p) n -> p k n", p=128)), writes=[r_wu])
                                P.dma("pool", lambda e, wu=wu, f=f: e.dma_start(out=wu[:, :, 128:256], in_=w_up[:, FFN + 128 * f:FFN + 128 * f + 128].rearrange("(k p) n -> p k n", p=128)), writes=[r_wu])
                                for (t0, tn) in TS:
                                    pg_, rg_ = pu.next()
                                    P.op("pe", [(lambda e, k=k, pg_=pg_, wu=wu, t0=t0, tn=tn: e.matmul(out=pg_[:, :tn], lhsT=wu[:, k, 0:128], rhs=uT2[:, k, t0:t0 + tn], start=(k == 0), stop=(k == 15))) for k in range(16)], reads=[r_wu, r_uT2], writes=[rg_])
                                    P.op("act", lambda e, pg_=pg_, t0=t0, tn=tn: e.copy(out=upg[:, t0:t0 + tn], in_=pg_[:, :tn]), reads=[rg_], writes=[r_upg])
                                    pv_, rv_ = pu.next()
                                    P.op("pe", [(lambda e, k=k, pv_=pv_, wu=wu, t0=t0, tn=tn: e.matmul(out=pv_[:, :tn], lhsT=wu[:, k, 128:256], rhs=uT2[:, k, t0:t0 + tn], start=(k == 0), stop=(k == 15))) for k in range(16)], reads=[r_wu, r_uT2], writes=[rv_])
                                    P.op("act", lambda e, pv_=pv_, t0=t0, tn=tn: e.copy(out=upv[:, t0:t0 + tn], in_=pv_[:, :tn]), reads=[rv_], writes=[r_upv])
                                for (src, r_src, dst, r_dst, ci) in ((upg, r_upg, cg, r_cg, f), (upv, r_upv, cv, r_cv, 44 + f)):
                                    P.op("dve", lambda e, src=src, dst=dst, ci=ci: e.tensor_scalar(out=dst[:], in0=src[:, 0:1024], scalar1=fcw[:, 3 * ci:3 * ci + 1], scalar2=None, op0=ALU.mult), reads=[r_src, r_fcw], writes=[r_dst])
                                    P.op("dve", lambda e, src=src, dst=dst, ci=ci: e.scalar_tensor_tensor(out=dst[:], in0=src[:, 1:1025], scalar=fcw[:, 3 * ci + 1:3 * ci + 2], in1=dst[:], op0=ALU.mult, op1=ALU.add), reads=[r_src, r_fcw, r_dst], writes=[r_dst])
                                    P.op("dve", lambda e, src=src, dst=dst, ci=ci: e.scalar_tensor_tensor(out=dst[:], in0=src[:, 2:1026], scalar=fcw[:, 3 * ci + 2:3 * ci + 3], in1=dst[:], op0=ALU.mult, op1=ALU.add), reads=[r_src, r_fcw, r_dst], writes=[r_dst])
                                P.op("act", lambda e: e.activation(out=sgl[:], in_=cg[:], func=AF.Silu), reads=[r_cg], writes=[r_sgl])
                                P.op("dve", lambda e, fc=fc: e.tensor_tensor(out=actT[:, fc, :], in0=sgl[:], in1=cv[:], op=ALU.mult), reads=[r_sgl, r_cv], writes=[r_actT])
                            for c in range(16):
                                wd, r_wd = wdr.next()
                                P.dma("pool", lambda e, wd=wd, c=c, half=half: e.dma_start(out=wd[:], in_=w_down[2816 * half:2816 * half + 2816, 128 * c:128 * c + 128].rearrange("(k p) n -> p k n", p=128)), writes=[r_wd])
                                for t2_ in range(2):
                                    pp, rp = pd.next()
                                    P.op("pe", [(lambda e, k=k, pp=pp, wd=wd, t2_=t2_: e.matmul(out=pp[:, :], lhsT=wd[:, k, :], rhs=actT[:, k, 512 * t2_:512 * t2_ + 512], start=(k == 0), stop=(k == 21))) for k in range(22)], reads=[r_wd, r_actT], writes=[rp])
                                    P.op("dve", lambda e, pp=pp, c=c, t2_=t2_: e.tensor_tensor(out=hT[:, c, 1 + 512 * t2_:1 + 512 * t2_ + 512], in0=hT[:, c, 1 + 512 * t2_:1 + 512 * t2_ + 512], in1=pp[:, :], op=ALU.add), reads=[rp, r_hT], writes=[r_hT])

                    P.fence()
                    with ExitStack() as sB5:
                        gft = sb("gft", [128, D], F32, sB5); r_gft = Res()
                        P.dma("act", lambda e: e.dma_start(out=gft[:], in_=gfb[:, :]), writes=[r_gft])
                        pF = [ps(f"pF{i}", [128, 1024], F32, sB5) for i in range(2)]; r_pF = [Res(), Res()]
                        oring = Ring([sb(f"ob{i}", [128, D], F32, sB5) for i in range(2)])
                        fj = sb("fj", [128, 1024], F32, sB5); r_fj = Res()
                        fs = sb("fs", [128, 4], F32, sB5); r_fs = Res()
                        for j in range(8):
                            for hh in range(2):
                                P.op("pe", [(lambda e, k=k, hh=hh, j=j: e.transpose(out=pF[hh][:, (k % 8) * 128:(k % 8) * 128 + 128], in_=hT[:, k, 1 + 128 * j:1 + 128 * j + 128], identity=idf[:])) for k in range(8 * hh, 8 * hh + 8)], reads=[r_hT, r_idf], writes=[r_pF[hh]])
                                P.op("act", lambda e, hh=hh: e.activation(out=fj[:], in_=pF[hh][:], func=AF.Square, accum_out=fs[:, hh:hh + 1]), reads=[r_pF[hh]], writes=[r_fj, r_fs])
                            P.op("dve", lambda e: e.tensor_tensor(out=fs[:, 2:3], in0=fs[:, 0:1], in1=fs[:, 1:2], op=ALU.add), reads=[r_fs], writes=[r_fs])
                            P.op("dve", lambda e: e.tensor_scalar(out=fs[:, 2:3], in0=fs[:, 2:3], scalar1=1.0 / D, scalar2=EPS, op0=ALU.mult, op1=ALU.add), reads=[r_fs], writes=[r_fs])
                            P.op("act", lambda e: e.activation(out=fs[:, 2:3], in_=fs[:, 2:3], func=AF.Sqrt), reads=[r_fs], writes=[r_fs])
                            P.op("dve", lambda e: e.reciprocal(out=fs[:, 3:4], in_=fs[:, 2:3]), reads=[r_fs], writes=[r_fs])
                            ob, r_ob = oring.next()
                            for hh in range(2):
                                P.op("dve", lambda e, hh=hh, ob=ob: e.scalar_tensor_tensor(out=ob[:, 1024 * hh:1024 * hh + 1024], in0=pF[hh][:], scalar=fs[:, 3:4], in1=gft[:, 1024 * hh:1024 * hh + 1024], op0=ALU.mult, op1=ALU.mult), reads=[r_pF[hh], r_fs, r_gft], writes=[r_ob])
                            final_toks.append(P.dma("sp", lambda e, ob=ob, j=j: e.dma_start(out=out_d[128 * j:128 * j + 128, :], in_=ob[:]), reads=[r_ob]))
        except _Stop:
            pass
        if True:
            if DEBUG and dbg_holder:
                final_toks.append(dbg_holder[0])
            P.finish(final_toks)
    return nc


_NC_CACHE = {}


def _rope_tables():
    inv_freq = (500000.0 ** (-np.arange(0, 16, 2, dtype=np.float32) / 16)).astype(np.float32)
    ang = np.arange(L, dtype=np.float32)[:, None] * inv_freq[None, :]
    cos = np.cos(ang).astype(np.float32).T
    sin = np.sin(ang).astype(np.float32).T
    cosF = np.ones((128, L), np.float32)
    sinF = np.zeros((128, L), np.float32)
    for mp in range(2):
        b0 = 64 * mp
        cosF[b0:b0 + 8] = cos
        cosF[b0 + 8:b0 + 16] = cos
        sinF[b0:b0 + 8] = -sin
        sinF[b0 + 8:b0 + 16] = sin
    return cosF, sinF


def kernel(x, meta_tokens, norm1_g, w_in, mlstm_conv_w, mlstm_gate_bias, mlstm_norm_g,
           lambda_q1, lambda_k1, lambda_q2, lambda_k2, diff_subln_g, w_branch_m, w_branch_d,
           w_out, norm2_g, w_up, ffn_conv_w, w_down, norm_f_g):
    f32 = np.float32
    x = np.asarray(x, f32)
    w_in0 = np.asarray(w_in, f32)[0]
    B = x.shape[0]
    cosF, sinF = _rope_tables()
    o_mqk, o_mv, o_mo, o_gates, o_dq, o_dk, o_dv, o_gm = 0, 2048, 3072, 4096, 4112, 5136, 6160, 7184
    rotperm = np.arange(128)
    for mp in range(2):
        b0 = 64 * mp
        rotperm[b0:b0 + 8] = np.arange(b0 + 8, b0 + 16)
        rotperm[b0 + 8:b0 + 16] = np.arange(b0, b0 + 8)
    ident = np.eye(128, dtype=f32)
    triu = np.triu(np.ones((64, 64), f32))
    tril = np.tril(np.ones((64, 64), f32))
    common = {
        "w_g": np.ascontiguousarray(w_in0[:, o_gm:o_gm + 4096]),
        "w_a": np.ascontiguousarray(np.asarray(w_branch_m, f32)[0]),
        "w_b": np.ascontiguousarray(np.asarray(w_branch_d, f32)[0]),
        "w_out": np.ascontiguousarray(np.asarray(w_out, f32)[0]),
        "w_up": np.ascontiguousarray(np.asarray(w_up, f32)[0]),
        "w_down": np.ascontiguousarray(np.asarray(w_down, f32)[0]),
        "g1b": np.ascontiguousarray(np.broadcast_to(np.asarray(norm1_g, f32)[0], (128, D))),
        "gfb": np.ascontiguousarray(np.broadcast_to(np.asarray(norm_f_g, f32), (128, D))),
        "g2c": np.ascontiguousarray(np.asarray(norm2_g, f32)[0].reshape(16, 128).T),
        "cosf": cosF, "sinf": sinF,
        "fconv": np.ascontiguousarray(np.asarray(ffn_conv_w, f32)[0].reshape(3, 88, 128).transpose(2, 1, 0).reshape(128, 264)),
        "lamv": np.ascontiguousarray(np.broadcast_to(np.concatenate([np.asarray(a, f32)[0] for a in (lambda_q1, lambda_k1, lambda_q2, lambda_k2)]), (128, 256))),
        "sublng": np.ascontiguousarray(np.broadcast_to(np.asarray(diff_subln_g, f32)[0], (128, 128))),
        "ident": ident, "triu": triu, "tril": tril,
    }
    if "B" not in STAGES:
        for nm in ("w_g", "w_a", "w_b", "w_out", "w_up", "w_down"):
            common[nm] = np.zeros((128, 128), f32)
    in_maps = []
    mcw_full = np.asarray(mlstm_conv_w, f32)[0]
    gb_full = np.asarray(mlstm_gate_bias, f32)[0]
    for c in range(8):
        b, g = c // 4, c % 4
        hfull = np.concatenate([np.asarray(meta_tokens, f32), x[b]], axis=0)
        s0 = 15 + 1024 * g
        xwin = np.zeros((NW, D), f32)
        e0 = min(s0 + NW, L)
        xwin[:e0 - s0] = hfull[s0:e0]
        wmask = np.ones((128, NW), f32)
        if e0 - s0 < NW:
            wmask[:, e0 - s0:] = 0.0
        cols = []
        for base in (o_dq, o_dk):
            for hh in range(2):
                head = 2 * g + hh
                cols.append(base + 128 * head + np.arange(128))
            for hh in range(2):
                head = 2 * g + hh
                cols.append(base + 128 * head + rotperm)
        for hh in range(2):
            head = 2 * g + hh
            cols.append(o_dv + 128 * head + np.arange(128))
        w_attn = np.ascontiguousarray(w_in0[:, np.concatenate(cols)])
        qc = o_mqk + 256 * g + np.arange(256)
        kc = o_mqk + 1024 + 256 * g + np.arange(256)
        vc = o_mv + 256 * g + np.arange(256)
        oc = o_mo + 256 * g + np.arange(256)
        gc = o_gates + np.array([4 + g, 12 + g, 0 + g, 8 + g])
        w_ml = np.ascontiguousarray(w_in0[:, np.concatenate([qc, kc, vc, oc, gc])])
        mconv = np.ascontiguousarray(mcw_full[:, np.concatenate([qc, kc])].reshape(3, 4, 128).transpose(2, 1, 0).reshape(128, 12))
        gbias = np.ascontiguousarray(np.broadcast_to(gb_full[[4 + g, 12 + g, 0 + g, 8 + g]], (128, 4)))
        mngv = np.ascontiguousarray(np.broadcast_to(np.asarray(mlstm_norm_g, f32)[0][256 * g:256 * g + 256], (64, 256)))
        m = dict(common)
        m.update({"hfull": hfull, "xwin": xwin, "w_attn": w_attn, "w_ml": w_ml, "mconv": mconv,
                  "gbias": gbias, "mng": mngv, "wmask": wmask})
        in_maps.append(m)
    if "nc" not in _NC_CACHE:
        _NC_CACHE["nc"] = build_program()
    nc = _NC_CACHE["nc"]
    res = run_bass_kernel_spmd(nc, in_maps, core_ids=list(range(8)))
    out = np.empty((B, 4096, D), f32)
    for c in range(8):
        b, g = c // 4, c % 4
        out[b, 1024 * g:1024 * g + 1024] = res.results[c]["out"]
    if DEBUG:
        kernel.dbg = [res.results[c] for c in range(8)]
    return out
```
